# Optimizing a Trainium2 kernel written in Bass

```python
import math
import jax
import jax.numpy as jnp
from jax import lax
import numpy as np

D_MODEL = 1024
BATCH = 16
SEQ = 256
DEPTH = 4
DEC_BATCH = 8
DEC_SEQ = 2048
PAST_LEN = 256

GRID_W = 64
EPS = 1e-6
POOL_WIDTH = D_MODEL // 4
MLA_WIDTH = D_MODEL // 2
GMLP_WIDTH = D_MODEL // 4
MIX_WIDTH = POOL_WIDTH + MLA_WIDTH + GMLP_WIDTH
POOL_WINDOWS = (2, 4, 8, 16)
POOL_GROUPS = len(POOL_WINDOWS)
POOL_GDIM = POOL_WIDTH // POOL_GROUPS
QK_NOPE = 128
QK_ROPE = 64
V_DIM = 128
MLA_HEADS = MLA_WIDTH // V_DIM
Q_RANK = 3 * D_MODEL // 8
KV_RANK = D_MODEL // 4
ROPE_THETA = 10000.0
ATTN_BLOCK = 128
CHUNK = 128
GMLP_GROUPS = 4
GMLP_GDIM = GMLP_WIDTH // GMLP_GROUPS
D_FF = 4 * D_MODEL
OFF_Q = POOL_WIDTH
OFF_KV = OFF_Q + Q_RANK
OFF_R = OFF_KV + KV_RANK
OFF_G = OFF_R + QK_ROPE
IN_WIDTH = OFF_G + 2 * GMLP_WIDTH

kernel_name = 'hybrid_pool_mla_gmlp_diffusion_step'


def rmsnorm(x, g):
    xf = x.astype(jnp.float32)
    y = xf * lax.rsqrt(jnp.mean(xf * xf, axis=-1, keepdims=True) + EPS)
    return (y * g.astype(jnp.float32)).astype(x.dtype)


def pool_mix(h, w_pool, pool_scale):
    B, L, _ = h.shape
    hf = h.astype(jnp.float32)
    csum = jnp.concatenate([jnp.zeros((B, 1, POOL_WIDTH), jnp.float32),
                            lax.cumsum(hf, axis=1)], axis=1)
    t = jnp.arange(L)
    groups = []
    for gi, w in enumerate(POOL_WINDOWS):
        lo = jnp.clip(t - w // 2, 0, L)
        hi = jnp.clip(t + (w - w // 2), 0, L)
        cnt = (hi - lo).astype(jnp.float32)[None, :, None]
        sl = slice(gi * POOL_GDIM, (gi + 1) * POOL_GDIM)
        win = csum[:, hi, sl] - csum[:, lo, sl]
        groups.append(win / cnt - hf[:, :, sl])
    pooled = jnp.stack(groups, axis=2)
    y = jnp.einsum('blgc,gcd->blgd', pooled, w_pool.astype(jnp.float32)).reshape(B, L, POOL_WIDTH)
    return (y * pool_scale.astype(jnp.float32)).astype(h.dtype)


def axial_rope(L):
    rows = L // GRID_W
    row = jnp.repeat(jnp.arange(rows, dtype=jnp.float32), GRID_W)
    col = jnp.tile(jnp.arange(GRID_W, dtype=jnp.float32), rows)
    n_freq = QK_ROPE // 4
    inv = 1.0 / (ROPE_THETA ** (jnp.arange(n_freq, dtype=jnp.float32) / n_freq))
    ang = jnp.concatenate([row[:, None] * inv, col[:, None] * inv], axis=-1)
    return jnp.cos(ang), jnp.sin(ang)


def apply_rope(x, cos, sin):
    xf = x.astype(jnp.float32)
    x1, x2 = xf[..., :QK_ROPE // 2], xf[..., QK_ROPE // 2:]
    return jnp.concatenate([x1 * cos - x2 * sin, x2 * cos + x1 * sin], axis=-1).astype(x.dtype)


def expand_kv(ckv, w_ukv):
    B, L, _ = ckv.shape
    kv = (ckv @ w_ukv).reshape(B, L, MLA_HEADS, QK_NOPE + V_DIM)
    return kv[..., :QK_NOPE], kv[..., QK_NOPE:]


def block_attention(qn, qr, kn, kr, v):
    B, Lq, H, _ = qn.shape
    nb = Lq // ATTN_BLOCK
    scale = 1.0 / math.sqrt(QK_NOPE + QK_ROPE)

    def to_blocks(a):
        return a.reshape((B, nb, ATTN_BLOCK) + a.shape[2:]).swapaxes(0, 1)

    def one_block(qs):
        qn_b, qr_b = qs
        s = (jnp.einsum('bqhd,bkhd->bhqk', qn_b, kn, preferred_element_type=jnp.float32)
             + jnp.einsum('bqhr,bkr->bhqk', qr_b, kr, preferred_element_type=jnp.float32)) * scale
        p = jax.nn.softmax(s, axis=-1).astype(v.dtype)
        return jnp.einsum('bhqk,bkhd->bqhd', p, v)

    o = lax.map(one_block, (to_blocks(qn), to_blocks(qr)))
    return o.swapaxes(0, 1).reshape(B, Lq, H * V_DIM)


def chunk_gmlp(u, v, w_s, b_s):
    B, L, _ = u.shape
    n = L // CHUNK
    vr = v.reshape(B, n, CHUNK, GMLP_GROUPS, GMLP_GDIM)
    mixed = jnp.einsum('gpq,bnqgc->bnpgc', w_s, vr) + b_s.T[:, :, None]
    return u * mixed.reshape(B, L, GMLP_WIDTH)


def mixer(hn, p, rope, ctx):
    B, L, _ = hn.shape
    proj = hn @ p['w_in']
    hp = proj[..., :OFF_Q]
    cq = proj[..., OFF_Q:OFF_KV]
    ckv_raw = proj[..., OFF_KV:OFF_R]
    kr = proj[..., OFF_R:OFF_G]
    uv = proj[..., OFF_G:]
    y_pool = pool_mix(hp, p['w_pool'], p['pool_scale'])
    q = (rmsnorm(cq, p['g_q']) @ p['w_uq']).reshape(B, L, MLA_HEADS, QK_NOPE + QK_ROPE)
    qn, qr = q[..., :QK_NOPE], q[..., QK_NOPE:]
    ckv = rmsnorm(ckv_raw, p['g_kv'])
    kn, v = expand_kv(ckv, p['w_ukv'])
    kr_keys = kr
    if rope is not None:
        cos, sin = rope
        qr = apply_rope(qr, cos[:, None, :], sin[:, None, :])
        kr_keys = apply_rope(kr, cos, sin)
    if ctx is not None:
        ckv_c, kr_c = ctx
        kn_c, v_c = expand_kv(ckv_c, p['w_ukv'])
        kn = jnp.concatenate([kn_c, kn], axis=1)
        v = jnp.concatenate([v_c, v], axis=1)
        kr_keys = jnp.concatenate([kr_c, kr_keys], axis=1)
    y_mla = block_attention(qn, qr, kn, kr_keys, v)
    uv = jax.nn.gelu(uv)
    u, vg = uv[..., :GMLP_WIDTH], uv[..., GMLP_WIDTH:]
    y_g = chunk_gmlp(u, rmsnorm(vg, p['g_sgu']), p['w_s'], p['b_s'])
    y = jnp.concatenate([y_pool, y_mla, y_g], axis=-1) @ p['w_out']
    return y, ckv, kr


def trunk_layer(x, mod, p, rope, ctx):
    sh1, sc1, g1, sh2, sc2, g2 = jnp.split(mod, 6, axis=-1)
    h = rmsnorm(x, p['g_mix']) * (1 + sc1) + sh1
    y, ckv, kr = mixer(h, p, rope, ctx)
    x = x + g1 * y
    h = rmsnorm(x, p['g_ffn']) * (1 + sc2) + sh2
    f = jnp.square(jax.nn.relu(h @ p['w_ff1'])) @ p['w_ff2']
    return x + g2 * f, ckv, kr


def setup_inputs(seed: int = 0) -> dict:
    key = jax.random.key(seed)
    ks = jax.random.split(key, 24)
    f32 = jnp.float32

    def nrm(k, shape, scale=1.0):
        return jax.random.normal(k, shape, f32) * scale

    def gain(k, shape):
        return 1.0 + 0.02 * jax.random.normal(k, shape, f32)

    return {
        'x_prompt': nrm(ks[0], (BATCH, SEQ, D_MODEL)),
        'x_sample': nrm(ks[1], (DEC_BATCH, DEC_SEQ, D_MODEL)),
        'cache_ckv': nrm(ks[2], (DEC_BATCH, DEPTH, PAST_LEN, KV_RANK)),
        'cache_krope': nrm(ks[3], (DEC_BATCH, DEPTH, PAST_LEN, QK_ROPE)),
        'c': nrm(ks[4], (DEC_BATCH, D_MODEL)),
        'c_ctx': nrm(ks[5], (D_MODEL,)),
        'w_ada': nrm(ks[6], (DEPTH, D_MODEL, 6 * D_MODEL), 0.5 * D_MODEL ** -0.5),
        'b_ada': nrm(ks[7], (DEPTH, 6 * D_MODEL), 0.01),
        'g_mix': gain(ks[8], (DEPTH, D_MODEL)),
        'w_in': nrm(ks[9], (DEPTH, D_MODEL, IN_WIDTH), D_MODEL ** -0.5),
        'w_pool': nrm(ks[10], (DEPTH, POOL_GROUPS, POOL_GDIM, POOL_GDIM), POOL_GDIM ** -0.5),
        'pool_scale': gain(ks[11], (DEPTH, POOL_WIDTH)),
        'g_q': gain(ks[12], (DEPTH, Q_RANK)),
        'w_uq': nrm(ks[13], (DEPTH, Q_RANK, MLA_HEADS * (QK_NOPE + QK_ROPE)), Q_RANK ** -0.5),
        'g_kv': gain(ks[14], (DEPTH, KV_RANK)),
        'w_ukv': nrm(ks[15], (DEPTH, KV_RANK, MLA_HEADS * (QK_NOPE + V_DIM)), KV_RANK ** -0.5),
        'g_sgu': gain(ks[16], (DEPTH, GMLP_WIDTH)),
        'w_s': nrm(ks[17], (DEPTH, GMLP_GROUPS, CHUNK, CHUNK), CHUNK ** -0.5),
        'b_s': nrm(ks[18], (DEPTH, GMLP_GROUPS, CHUNK), 0.01),
        'w_out': nrm(ks[19], (DEPTH, MIX_WIDTH, D_MODEL), MIX_WIDTH ** -0.5),
        'g_ffn': gain(ks[20], (DEPTH, D_MODEL)),
        'w_ff1': nrm(ks[21], (DEPTH, D_MODEL, D_FF), D_MODEL ** -0.5),
        'w_ff2': nrm(ks[22], (DEPTH, D_FF, D_MODEL), D_FF ** -0.5),
        'g_final': gain(ks[23], (D_MODEL,)),
    }


def reference(x_prompt, x_sample, cache_ckv, cache_krope, c, c_ctx, w_ada, b_ada, g_mix,
              w_in, w_pool, pool_scale, g_q, w_uq, g_kv, w_ukv, g_sgu, w_s, b_s, w_out,
              g_ffn, w_ff1, w_ff2, g_final):
    rope = axial_rope(x_sample.shape[1])
    s_ctx = jax.nn.silu(c_ctx)
    s_lat = jax.nn.silu(c)
    xp, xs = x_prompt, x_sample
    ckv_list, kr_list = [], []
    for l in range(DEPTH):
        p = {
            'g_mix': g_mix[l], 'w_in': w_in[l], 'w_pool': w_pool[l], 'pool_scale': pool_scale[l],
            'g_q': g_q[l], 'w_uq': w_uq[l], 'g_kv': g_kv[l], 'w_ukv': w_ukv[l],
            'g_sgu': g_sgu[l], 'w_s': w_s[l], 'b_s': b_s[l], 'w_out': w_out[l],
            'g_ffn': g_ffn[l], 'w_ff1': w_ff1[l], 'w_ff2': w_ff2[l],
        }
        mod_ctx = (s_ctx @ w_ada[l] + b_ada[l])[None, None, :]
        mod_lat = (s_lat @ w_ada[l] + b_ada[l])[:, None, :]
        xp, ckv, kr = trunk_layer(xp, mod_ctx, p, None, None)
        ckv_list.append(ckv)
        kr_list.append(kr)
        xs, _, _ = trunk_layer(xs, mod_lat, p, rope, (cache_ckv[:, l], cache_krope[:, l]))
    y_prompt = rmsnorm(xp, g_final)
    y_sample = rmsnorm(xs, g_final)
    state_ckv = jnp.stack(ckv_list, axis=1)
    state_krope = jnp.stack(kr_list, axis=1)
    return (y_prompt, y_sample, state_ckv, state_krope)
```

```python
import math
from contextlib import ExitStack

import numpy as np
import concourse.bass as bass
import concourse.mybir as mybir
from concourse.bass_utils import run_bass_kernel_spmd

F32, BF16 = mybir.dt.float32, mybir.dt.bfloat16
AF = mybir.ActivationFunctionType
ALU = mybir.AluOpType

D = 1024
L_FULL = 4
T = 2560
TS = 2048
NSUB = 10
SUB = 256
NKEY = 2816
EPS = 1e-6
SCALE = 1.0 / math.sqrt(192.0)
NJ = 8
REORDER = True
PADZERO = True
ROPE128 = True
BST_BF16 = True
ENGS = ['pe', 'act', 'dve', 'pool', 'sp']


class Buf:
    __slots__ = ('name', 'w', 'r', 'dcount', 'sem', 'psum')

    def __init__(self, name, psum=False):
        self.name = name
        self.psum = psum
        self.w = None
        self.r = []
        self.dcount = 0
        self.sem = None


class Op:
    __slots__ = ('eng', 'fn', 'deps', 'idx', 'signal', 'sigval', 'dkey', 'dval')

    def __init__(self, eng, fn, deps, idx, dkey=None, dval=0):
        self.eng = eng
        self.fn = fn
        self.deps = deps
        self.idx = idx
        self.signal = False
        self.sigval = 0
        self.dkey = dkey
        self.dval = dval


class Sched:
    def __init__(self):
        self.ops = {e: [] for e in ENGS}
        self.dkeys = []
        self.bar_deps = set()
        self.bar_gen = 0
        self.eng_gen = {e: 0 for e in ENGS}

    def _deps(self, eng, reads, writes, is_dma):
        deps = set()
        for b in reads:
            if b.w is not None:
                deps.add(b.w)
            if b.psum:
                deps.update(t for t in b.r if not (t[0] == 'e' and t[1] == eng))
        for b in writes:
            if b.w is not None and not (is_dma and b.w[0] == 'd'):
                deps.add(b.w)
            deps.update(b.r)
        if self.eng_gen[eng] != self.bar_gen:
            deps |= self.bar_deps
            self.eng_gen[eng] = self.bar_gen
        return deps

    def op(self, eng, fn, reads=(), writes=()):
        deps = self._deps(eng, reads, writes, False)
        o = Op(eng, fn, deps, len(self.ops[eng]))
        self.ops[eng].append(o)
        tok = ('e', eng, o.idx)
        if eng == 'pe':
            o.deps = {d for d in deps if not (d[0] == 'e' and d[1] == 'pe')}
        for b in reads:
            self._add_reader(b, tok)
        for b in writes:
            b.w = tok
            b.r = []
        return tok

    @staticmethod
    def _add_reader(b, tok):
        b.r = [t for t in b.r if not (t[0] == tok[0] and t[1] == tok[1])]
        b.r.append(tok)

    def dma(self, eng, fn, key, reads=(), writes=()):
        deps = self._deps(eng, reads, writes, True)
        if key.sem is None:
            key.sem = True
            self.dkeys.append(key)
        key.dcount += 16
        o = Op(eng, fn, deps, len(self.ops[eng]), dkey=key, dval=key.dcount)
        self.ops[eng].append(o)
        tok = ('d', key, key.dcount)
        for b in reads:
            self._add_reader(b, tok)
        for b in writes:
            b.w = tok
            b.r = []
        return tok

    def barrier(self):
        deps = set()
        for e in ENGS:
            for o in reversed(self.ops[e]):
                if o.dkey is None:
                    deps.add(('e', e, o.idx))
                    break
        for k in self.dkeys:
            if k.dcount:
                deps.add(('d', k, k.dcount))
        self.bar_deps = deps
        self.bar_gen += 1

    def emit(self, nc, block, stack, final_waits=()):
        for e in ENGS:
            for o in self.ops[e]:
                for d in o.deps:
                    if d[0] == 'e':
                        self.ops[d[1]][d[2]].signal = True
        for d in final_waits:
            if d[0] == 'e':
                self.ops[d[1]][d[2]].signal = True
        esem = {}
        for e in ENGS:
            n = 0
            for o in self.ops[e]:
                if o.signal:
                    n += 1
                    o.sigval = n
            if n:
                esem[e] = stack.enter_context(nc.semaphore('s_' + e))
        for k in self.dkeys:
            k.sem = stack.enter_context(nc.semaphore('d_' + k.name))
        sched = self

        def run(e, engobj, extra=()):
            seen = {}

            def wait(tok):
                if tok[0] == 'e':
                    sem = esem[tok[1]]
                    val = sched.ops[tok[1]][tok[2]].sigval
                    key = tok[1]
                else:
                    sem = tok[1].sem
                    val = tok[2]
                    key = id(tok[1])
                if seen.get(key, 0) >= val:
                    return
                seen[key] = val
                engobj.wait_ge(sem, val)

            for o in sched.ops[e]:
                for d in sorted(o.deps, key=lambda t: (t[0], t[1] if t[0] == 'e' else t[1].name, t[2])):
                    wait(d)
                ins = o.fn(engobj)
                if o.dkey is not None:
                    ins.then_inc(o.dkey.sem, 16)
                elif o.signal:
                    ins.then_inc(esem[e], 1)
            for d in extra:
                wait(d)

        @block.tensor
        def _(eng):
            run('pe', eng)

        @block.scalar
        def _(eng):
            run('act', eng)

        @block.vector
        def _(eng):
            run('dve', eng)

        @block.gpsimd
        def _(eng):
            run('pool', eng)

        @block.sync
        def _(eng):
            run('sp', eng, extra=final_waits)


def build_program(L=L_FULL):
    nc = bass.Bass("TRN2", target_bir_lowering=False)
    S = Sched()

    def din(name, shape):
        return nc.dram_tensor(name, list(shape), F32, kind="ExternalInput").ap()

    def dout(name, shape):
        return nc.dram_tensor(name, list(shape), F32, kind="ExternalOutput").ap()

    xT_d = din("xT", [128, 8, T])
    cT_d = din("cT", [128, 8, 2])
    wada_d = din("wada", [L_FULL, 48, 128, 8, 128])
    bada_d = din("bada", [128, L_FULL, 48])
    gmix_d = din("gmix", [128, L_FULL, 8])
    gffn_d = din("gffn", [128, L_FULL, 8])
    gq_d = din("gq", [128, L_FULL, 3])
    gkv_d = din("gkv", [128, L_FULL, 2])
    gsgu_d = din("gsgu", [L_FULL, 128, 256])
    pscale_d = din("pscale", [128, L_FULL, 2])
    gfinal_d = din("gfinal", [128, 8])
    winA_d = din("winA", [L_FULL, 128, 8, 384])
    winB_d = din("winB", [L_FULL, 128, 8, 1152])
    wuq_d = din("wuq", [L_FULL, 128, 3, 1024])
    wukv_d = din("wukv", [L_FULL, 128, 2, 1024])
    wout_d = din("wout", [L_FULL, 128, 8, 1024])
    wpool_d = din("wpool", [L_FULL, 4, 64, 64])
    wsT_d = din("wsT", [L_FULL, 128, 4, 128])
    bsT_d = din("bsT", [L_FULL, 128, 2, 128])
    w1_d = din("w1", [L_FULL, NJ, 128, 8, 512])
    w2_d = din("w2", [L_FULL, NJ, 128, 4, 1024])
    cckv_d = din("cckv", [128, L_FULL, 2, 256])
    ckr_d = din("ckr", [64, L_FULL, 256])
    cos_d = din("cosT", [64, TS])
    sin_d = din("sinT", [64, TS])
    invw_d = din("invw", [128, 2])
    corrL_d = din("corrL", [128, 2, 8])
    corrR_d = din("corrR", [128, 2, 8])
    yT_d = dout("yT", [128, 8, T])
    ckvst_d = dout("ckvst", [L_FULL, 128, 2, 512])
    krst_d = dout("krst", [L_FULL, 64, 512])

    base0 = (nc.sbuf_base + 63) // 64 * 64
    top = nc.sbuf_top
    cur = [base0]
    ncnt = [0]

    def alloc(shape, dt, name=None):
        size = int(np.prod(shape[1:])) * (4 if dt == F32 else 2)
        size = (size + 31) // 32 * 32
        off = cur[0]
        cur[0] += size
        assert cur[0] <= top, ("SBUF overflow", name, cur[0] - top)
        ncnt[0] += 1
        t_ = nc.alloc_sbuf_tensor_at("%s_%d" % (name or "t", ncnt[0]), list(shape), dt, offset=off)
        offs[id(t_)] = off
        return t_

    offs = {}

    def alias(t_, shape, dt, name):
        ncnt[0] += 1
        return nc.alloc_sbuf_tensor_at("%s_%d" % (name, ncnt[0]), list(shape), dt, offset=offs[id(t_)])

    x = alloc([128, 8, T], F32, "x")
    ones = alloc([128, 128], BF16, "ones")
    gmix = alloc([128, L_FULL, 8], F32, "gmix")
    gffn = alloc([128, L_FULL, 8], F32, "gffn")
    gq = alloc([128, L_FULL, 3], F32, "gq")
    gkv = alloc([128, L_FULL, 2], F32, "gkv")
    pscale = alloc([128, L_FULL, 2], F32, "pscale")
    gfinal = alloc([128, 8], F32, "gfinal")
    bada = alloc([128, L_FULL, 48], F32, "bada")
    cT = alloc([128, 8, 2], F32, "cT")
    sT = alloc([128, 8, 2], BF16, "sT")
    invw = alloc([128, 2], F32, "invw")
    corrL = alloc([128, 2, 8], F32, "corrL")
    corrR = alloc([128, 2, 8], F32, "corrR")
    modp = [alloc([128, 48, 2], F32, "mod") for _ in range(2)]
    a1p = [alloc([128, 8, 2], F32, "a1") for _ in range(2)]
    a2p = [alloc([128, 8, 2], F32, "a2") for _ in range(2)]
    scr_base = cur[0]

    ps = [nc.alloc_psum_tensor("ps%d" % i, [128, 512], F32) for i in range(8)]
    psb = [Buf("ps%d" % i, psum=True) for i in range(8)]

    B = {}

    def bf(name):
        if name not in B:
            B[name] = Buf(name)
        return B[name]

    gbank_list = [0, 1, 2, 3]
    gctr = [0]

    def gb():
        i = gbank_list[gctr[0] % len(gbank_list)]
        gctr[0] += 1
        return ps[i], psb[i]

    def mm(out, lhsT, rhs, start, stop, reads, writes, skip=False):
        if skip:
            S.op('pe', lambda e: e.matmul(out, lhsT=lhsT, rhs=rhs, start=start, stop=stop, skip_group_check=True),
                 reads, writes)
        else:
            S.op('pe', lambda e: e.matmul(out, lhsT=lhsT, rhs=rhs, start=start, stop=stop), reads, writes)

    def act(out, in_, func, reads, writes, scale=1.0, bias=0.0):
        S.op('act', lambda e: e.activation(out=out, in_=in_, func=func, scale=scale, bias=bias), reads, writes)

    def tt(eng, out, in0, in1, op, reads, writes):
        S.op(eng, lambda e: e.tensor_tensor(out=out, in0=in0, in1=in1, op=op), reads, writes)

    def stt(out, in0, scalar, in1, op0, op1, reads, writes):
        S.op('dve', lambda e: e.scalar_tensor_tensor(out=out, in0=in0, scalar=scalar, in1=in1, op0=op0, op1=op1),
             reads, writes)

    def ts(eng, out, in0, s1, s2, op0, op1, reads, writes):
        if s2 is None:
            S.op(eng, lambda e: e.tensor_scalar(out=out, in0=in0, scalar1=s1, scalar2=None, op0=op0), reads, writes)
        else:
            S.op(eng, lambda e: e.tensor_scalar(out=out, in0=in0, scalar1=s1, scalar2=s2, op0=op0, op1=op1),
                 reads, writes)

    def cp(eng, out, in_, reads, writes):
        if eng == 'act':
            S.op('act', lambda e: e.activation(out=out, in_=in_, func=AF.Copy), reads, writes)
        else:
            S.op(eng, lambda e: e.tensor_copy(out=out, in_=in_), reads, writes)

    def memset(eng, ap, val, writes):
        S.op(eng, lambda e: e.memset(ap, val), (), writes)

    def ld(q, out, in_, key, writes=None):
        S.dma(q, lambda e: e.dma_start(out=out, in_=in_), key, (), writes if writes is not None else [key])

    def rstd_from_ps(psum_ap, n_feat, lnv, rstd, rd, wr_ln, wr_rs):
        act(lnv, psum_ap, AF.Ln, rd, [wr_ln], scale=1.0 / n_feat, bias=EPS)
        act(rstd, lnv, AF.Exp, [wr_ln], [wr_rs], scale=-0.5)

    bx = bf("x")
    for t in range(5):
        S.dma('sp', lambda e, t=t: e.dma_start(out=x[:, :, t * 512:(t + 1) * 512], in_=xT_d[:, :, t * 512:(t + 1) * 512]),
              bx, (), [bx])
    bconst = bf("const")
    for dst, src in [(gmix, gmix_d), (gffn, gffn_d), (gq, gq_d), (gkv, gkv_d), (pscale, pscale_d), (gfinal, gfinal_d),
                     (bada, bada_d), (cT, cT_d), (invw, invw_d), (corrL, corrL_d), (corrR, corrR_d)]:
        S.dma('sp', lambda e, dst=dst, src=src: e.dma_start(out=dst[:], in_=src), bconst, (), [bconst])
    bones = bf("ones")
    memset('dve', ones[:], 1.0, [bones])
    bsT_ = bf("sT")
    act(sT[:], cT[:], AF.Silu, [bconst], [bsT_])

    out_toks = []

    def tokrange(s):
        return s * SUB, (s + 1) * SUB

    for l in range(L):
        mod, a1, a2 = modp[l % 2], a1p[l % 2], a2p[l % 2]
        bmod = bf("mod%d" % (l % 2))

        def emit_mod(ll, ring, ringb, m0, m1):
            pm, pmb = ps[7], psb[7]
            for m in range(m0, m1):
                r, rb = ring[m % 4], ringb[m % 4]
                ld('pool', r[:], wada_d[ll, m], rb)
                for k in range(8):
                    mm(pm[:, 2 * m:2 * m + 2], r[:, k, :], sT[:, k, :], k == 0, k == 7, [rb, bsT_], [pmb])

        def finish_mod(ll):
            pm, pmb = ps[7], psb[7]
            md, aa1, aa2 = modp[ll % 2], a1p[ll % 2], a2p[ll % 2]
            bm = bf("mod%d" % (ll % 2))
            pmv = pm[:, 0:96].rearrange("p (m s) -> p m s", s=2)
            for s_ in range(2):
                tt('dve', md[:, :, s_], pmv[:, :, s_], bada[:, ll, :], ALU.add, [pmb, bconst], [bm])
            for s_ in range(2):
                stt(aa1[:, :, s_], md[:, 8:16, s_], 1.0, gmix[:, ll, :], ALU.add, ALU.mult, [bm, bconst], [bm])
                stt(aa2[:, :, s_], md[:, 32:40, s_], 1.0, gffn[:, ll, :], ALU.add, ALU.mult, [bm, bconst], [bm])

        if l == 0:
            cur[0] = scr_base
            ring0 = [alloc([128, 8, 128], BF16, "adar") for _ in range(4)]
            ringb0 = [bf("adar%d" % i) for i in range(4)]
            emit_mod(0, ring0, ringb0, 0, 48)
            finish_mod(0)
            S.barrier()

        def mtype(s):
            return 0 if s < 8 else 1

        def norm_tile(tok0, n, avec, bvec_off, mt, h, hb, tag, sq, tmpx, lnv, rstd, bsq=None, tmpx_extra=()):
            if bsq is None:
                bsq = bf(tag + "sq")
            bln, brs = bf(tag + "ln"), bf(tag + "rs")
            if lnv is rstd:
                bln = brs
            act(sq[:, :, 0:n], x[:, :, tok0:tok0 + n], AF.Square, [bx], [bsq])
            pb, pbb = gb()
            for k in range(8):
                mm(pb[:, 0:n], ones[:], sq[:, k, 0:n], k == 0, k == 7, [bones, bsq], [pbb])
            rstd_from_ps(pb[:, 0:n], float(D), lnv[:, 0:n], rstd[:, 0:n], [pbb], bln, brs)
            for c in range(8):
                tb = bf(tag + "tmpx%d" % (c % 2))
                tx = tmpx[c % 2]
                tt('dve', tx[:, 0:n], x[:, c, tok0:tok0 + n], rstd[:, 0:n], ALU.mult, [bx, brs], [tb] + list(tmpx_extra))
                sc_ap = avec[:, c, mt:mt + 1]
                bi_ap = mod[:, bvec_off + c, mt:mt + 1]
                o_ap, i_ap = h[:, c, 0:n], tx[:, 0:n]
                S.op('act', lambda e, sc_ap=sc_ap, bi_ap=bi_ap, o_ap=o_ap, i_ap=i_ap: e.activation(
                    out=o_ap, in_=i_ap, func=AF.Identity, scale=sc_ap, bias=bi_ap), [tb, bmod], [hb])

        def alloc_kv(nkey, with_hph):
            KV = {}
            KV['knT'] = alloc([128, 4, nkey], BF16, "knT")
            KV['krT'] = alloc([128, nkey], BF16, "krT")
            KV['V'] = alloc([128, nkey // 128, 512], BF16, "V")
            if with_hph:
                KV['hph'] = alloc([128, 2, 8, 16], F32, "hph")
            return KV

        def alloc_A(with_hp):
            TA = {}
            TA['wA'] = alloc([128, 8, 384], BF16, "wA")
            if with_hp:
                TA['whp'] = alloc([128, 8, 256], BF16, "whp")
            TA['wkv'] = alloc([128, 2, 1024], BF16, "wkv")
            TA['sq'] = alloc([128, 8, SUB], BF16, "sqA")
            TA['tmpx'] = [alloc([128, SUB], F32, "tmpxA") for _ in range(2)]
            TA['lnv'] = alloc([128, SUB], F32, "lnvA")
            TA['rstd'] = alloc([128, SUB], F32, "rstdA")
            TA['h1'] = alloc([128, 8, SUB], BF16, "h1A")
            TA['sqk'] = alloc([128, 2, SUB], BF16, "sqk")
            TA['lnk'] = alloc([128, SUB], F32, "lnk")
            TA['rsk'] = alloc([128, SUB], F32, "rsk")
            TA['ckvn'] = [alloc([128, 2, SUB], BF16, "ckvn") for _ in range(2)]
            if with_hp:
                TA['kt1'] = alloc([64, SUB], F32, "kt1")
                TA['kt2'] = alloc([64, SUB], F32, "kt2")
                TA['rope'] = [alloc([64, 2, SUB], F32, "ropeA") for _ in range(2)]
            else:
                TA['ckvst'] = alloc([128, 2, SUB], F32, "ckvst")
                TA['krst'] = alloc([64, SUB], F32, "krst")
            return TA

        bknT, bkrT, bV, bhph = bf("knT"), bf("krT"), bf("V"), bf("hph")
        bwA, bwhp, bwkv = bf("wA"), bf("whp"), bf("wkv")

        def load_A(TA, with_hp):
            ld('pool', TA['wA'][:], winA_d[l], bwA)
            ld('pool', TA['wkv'][:], wukv_d[l], bwkv)
            if with_hp:
                ld('pool', TA['whp'][:], winB_d[l, :, :, 0:256], bwhp)

        def kv_proj(KV, TA, cb, cbb, key0):
            wkv = TA['wkv']
            for hp_ in range(2):
                pb, pbb = gb()
                for hh in range(2):
                    h_ = 2 * hp_ + hh
                    for k in range(2):
                        mm(pb[:, hh * 256:(hh + 1) * 256], wkv[:, k, h_ * 128:(h_ + 1) * 128], cb[:, k, :],
                           k == 0, k == 1, [bwkv, cbb], [pbb])
                cp('act', KV['knT'][:, 2 * hp_:2 * hp_ + 2, key0:key0 + 256],
                   pb[:, 0:512].rearrange("p (h n) -> p h n", h=2), [pbb], [bknT])
            for tb_ in range(2):
                pb, pbb = gb()
                for k in range(2):
                    mm(pb[:, 0:512], cb[:, k, tb_ * 128:(tb_ + 1) * 128], wkv[:, k, 512:1024], k == 0, k == 1,
                       [bwkv, cbb], [pbb])
                cp('dve', KV['V'][:, key0 // 128 + tb_, :], pb[:, 0:512], [pbb], [bV])

        def phaseA_sub(s, KV, TA):
            tok0, tok1 = tokrange(s)
            mt = mtype(s)
            sample = s < 8
            key0 = 256 + tok0 if sample else (s - 8) * 256
            wA, h1A = TA['wA'], TA['h1']
            bh1 = bf("h1A")
            norm_tile(tok0, SUB, a1, 0, mt, h1A, bh1, "nA", TA['sq'], TA['tmpx'], TA['lnv'], TA['rstd'])
            pc, pcb = gb()
            for mc in range(2):
                for k in range(8):
                    mm(pc[:, mc * 256:(mc + 1) * 256], wA[:, k, mc * 128:(mc + 1) * 128], h1A[:, k, :], k == 0, k == 7,
                       [bwA, bh1], [pcb])
            pk, pkb = gb()
            nk = 2 if sample else 1
            for q_ in range(nk):
                for k in range(8):
                    mm(pk[0:64, q_ * 256:(q_ + 1) * 256], wA[:, k, 256 + q_ * 64:256 + (q_ + 1) * 64], h1A[:, k, :],
                       k == 0, k == 7, [bwA, bh1], [pkb])
            if sample:
                whp = TA['whp']
                ph, phb = gb()
                for mc in range(2):
                    for side in range(2):
                        c0 = 0 if side == 0 else SUB - 8
                        o0 = mc * 16 + side * 8
                        for k in range(8):
                            mm(ph[:, o0:o0 + 8], whp[:, k, mc * 128:(mc + 1) * 128], h1A[:, k, c0:c0 + 8], k == 0, k == 7,
                               [bwhp, bh1], [phb])
                cp('dve', KV['hph'][:, :, s, :], ph[:, 0:32].rearrange("p (m n) -> p m n", m=2), [phb], [bhph])
            pcv = pc[:, 0:512].rearrange("p (m n) -> p m n", m=2)
            sqk, lnk, rsk = TA['sqk'], TA['lnk'], TA['rsk']
            bsqk, blnk, brsk = bf("sqk"), bf("lnk"), bf("rsk")
            act(sqk[:], pcv, AF.Square, [pcb], [bsqk])
            pq, pqb = gb()
            for k in range(2):
                mm(pq[:, 0:256], ones[:], sqk[:, k, :], k == 0, k == 1, [bones, bsqk], [pqb])
            rstd_from_ps(pq[:, 0:256], 256.0, lnk[:], rsk[:], [pqb], blnk, brsk)
            cn, cnb = TA['ckvn'][s % 2], bf("ckvn%d" % (s % 2))
            if sample:
                for mc in range(2):
                    stt(cn[:, mc, :], pcv[:, mc, :], gkv[:, l, mc:mc + 1], rsk[:], ALU.mult, ALU.mult,
                        [pcb, brsk, bconst], [cnb])
            else:
                ckvst = TA['ckvst']
                bst = bf("ckvst")
                for mc in range(2):
                    stt(ckvst[:, mc, :], pcv[:, mc, :], gkv[:, l, mc:mc + 1], rsk[:], ALU.mult, ALU.mult,
                        [pcb, brsk, bconst], [bst])
                cp('act', cn[:], ckvst[:], [bst], [cnb])
                p0 = (s - 8) * 256
                out_toks.append(S.dma('sp', lambda e, p0=p0, l=l, ckvst=ckvst: e.dma_start(out=ckvst_d[l, :, :, p0:p0 + 256], in_=ckvst[:]),
                                      bst, [bst], []))
            krT = KV['krT']
            if sample:
                rp, rpb = TA['rope'][s % 2], bf("ropeA%d" % (s % 2))
                S.dma('sp', lambda e, rp=rp, tok0=tok0: e.dma_start(out=rp[:, 0, :], in_=cos_d[:, tok0:tok0 + SUB]), rpb, (), [rpb])
                S.dma('sp', lambda e, rp=rp, tok0=tok0: e.dma_start(out=rp[:, 1, :], in_=sin_d[:, tok0:tok0 + SUB]), rpb, (), [rpb])
                kt1, kt2 = TA['kt1'], TA['kt2']
                bk1, bk2 = bf("kt1"), bf("kt2")
                tt('dve', kt1[:], pk[0:64, 0:256], rp[:, 0, :], ALU.mult, [pkb, rpb], [bk1])
                tt('dve', kt2[:], pk[0:64, 256:512], rp[:, 1, :], ALU.mult, [pkb, rpb], [bk2])
                tt('dve', krT[0:64, key0:key0 + 256], kt1[:], kt2[:], ALU.add, [bk1, bk2], [bkrT])
            else:
                krst = TA['krst']
                bks = bf("krst")
                cp('dve', krst[:], pk[0:64, 0:256], [pkb], [bks])
                cp('act', krT[0:64, key0:key0 + 256], krst[:], [bks], [bkrT])
                p0 = (s - 8) * 256
                out_toks.append(S.dma('sp', lambda e, p0=p0, l=l, krst=krst: e.dma_start(out=krst_d[l, :, p0:p0 + 256], in_=krst[:]),
                                      bks, [bks], []))
            kv_proj(KV, TA, cn, cnb, key0)

        cur[0] = scr_base
        KVs = alloc_kv(NKEY, True)
        kv_end = cur[0]
        NA = 512
        wA = alloc([128, 8, 384], BF16, "wA")
        whp = alloc([128, 8, 256], BF16, "whp")
        wkvS = alloc([128, 2, 1024], BF16, "wkv")
        sqS = alloc([128, 8, NA], BF16, "sqS")
        tmpxS = [alloc([128, NA], F32, "tmpxS") for _ in range(2)]
        rstdS = alloc([128, NA], F32, "rstdS")
        h1S = [alloc([128, 8, NA], BF16, "h1S") for _ in range(2)]
        sqkS = alloc([128, 2, NA], BF16, "sqkS")
        rskS = alloc([128, NA], F32, "rskS")
        ckvnS = [alloc([128, 2, NA], BF16, "ckvnS") for _ in range(2)]
        kt1S = alloc([64, NA], F32, "kt1S")
        kt2S = alloc([64, NA], F32, "kt2S")
        ropeS = [alloc([64, 2, NA], F32, "ropeS") for _ in range(2)]
        ckvc = alloc([128, 2, 256], BF16, "ckvc")
        ckvstS = alloc([128, 2, NA], F32, "ckvstS")
        krstS = alloc([64, NA], F32, "krstS")
        TAs = {'wkv': wkvS}
        ld('pool', wA[:], winA_d[l], bwA)
        ld('pool', wkvS[:], wukv_d[l], bwkv)
        ld('pool', whp[:], winB_d[l, :, :, 0:256], bwhp)
        bckvc = bf("ckvc")
        ld('pool', ckvc[:], cckv_d[:, l], bckvc)
        if PADZERO:
            memset('dve', KVs['krT'][64:128, :], 0.0, [bkrT])
        ld('pool', KVs['krT'][0:64, 0:256], ckr_d[:, l, :], bf("krTc"), writes=[bkrT])
        gbank_list[:] = [0, 1, 2, 3, 4, 5, 6, 7]
        kv_proj(KVs, TAs, ckvc, bckvc, 0)

        def sa_norm(t):
            norm_tile(t * NA, NA, a1, 0, (0 if t < 4 else 1), h1S[t % 2], bf("h1S%d" % (t % 2)), "nS", sqS, tmpxS, rstdS, rstdS)

        def sa_proj(t):
            h1, bh1 = h1S[t % 2], bf("h1S%d" % (t % 2))
            st = {}
            pcs = []
            for mc in range(2):
                pc, pcb = gb()
                for k in range(8):
                    mm(pc[:, 0:NA], wA[:, k, mc * 128:(mc + 1) * 128], h1[:, k, :], k == 0, k == 7, [bwA, bh1], [pcb])
                pcs.append((pc, pcb))
            pks = []
            for q_ in range(2 if t < 4 else 1):
                pk, pkb = gb()
                for k in range(8):
                    mm(pk[0:64, 0:NA], wA[:, k, 256 + q_ * 64:256 + (q_ + 1) * 64], h1[:, k, :], k == 0, k == 7,
                       [bwA, bh1], [pkb])
                pks.append((pk, pkb))
            if t == 4:
                return pcs, pks
            ph, phb = gb()
            for sub in range(2):
                for mc in range(2):
                    for side in range(2):
                        c0 = sub * 256 + (0 if side == 0 else SUB - 8)
                        o0 = sub * 32 + mc * 16 + side * 8
                        for k in range(8):
                            mm(ph[:, o0:o0 + 8], whp[:, k, mc * 128:(mc + 1) * 128], h1[:, k, c0:c0 + 8], k == 0, k == 7,
                               [bwhp, bh1], [phb])
            for sub in range(2):
                cp('dve', KVs['hph'][:, :, 2 * t + sub, :],
                   ph[:, sub * 32:(sub + 1) * 32].rearrange("p (m n) -> p m n", m=2), [phb], [bhph])
            return pcs, pks

        def sa_chain(t, pcs, pks):
            tok0 = t * NA
            key0 = 256 + tok0
            bsqk, brsk = bf("sqkS"), bf("rskS")
            for mc in range(2):
                act(sqkS[:, mc, :], pcs[mc][0][:, 0:NA], AF.Square, [pcs[mc][1]], [bsqk])
            pq, pqb = gb()
            for k in range(2):
                mm(pq[:, 0:NA], ones[:], sqkS[:, k, :], k == 0, k == 1, [bones, bsqk], [pqb])
            rstd_from_ps(pq[:, 0:NA], 256.0, rskS[:], rskS[:], [pqb], brsk, brsk)
            cn, cnb = ckvnS[t % 2], bf("ckvnS%d" % (t % 2))
            if t < 4:
                for mc in range(2):
                    stt(cn[:, mc, :], pcs[mc][0][:, 0:NA], gkv[:, l, mc:mc + 1], rskS[:], ALU.mult, ALU.mult,
                        [pcs[mc][1], brsk, bconst], [cnb])
                rp, rpb = ropeS[t % 2], bf("ropeS%d" % (t % 2))
                S.dma('sp', lambda e, rp=rp, tok0=tok0: e.dma_start(out=rp[:, 0, :], in_=cos_d[:, tok0:tok0 + NA]), rpb, (), [rpb])
                S.dma('sp', lambda e, rp=rp, tok0=tok0: e.dma_start(out=rp[:, 1, :], in_=sin_d[:, tok0:tok0 + NA]), rpb, (), [rpb])
                bk1, bk2 = bf("kt1S"), bf("kt2S")
                tt('dve', kt1S[:], pks[0][0][0:64, 0:NA], rp[:, 0, :], ALU.mult, [pks[0][1], rpb], [bk1])
                tt('dve', kt2S[:], pks[1][0][0:64, 0:NA], rp[:, 1, :], ALU.mult, [pks[1][1], rpb], [bk2])
                tt('dve', KVs['krT'][0:64, key0:key0 + NA], kt1S[:], kt2S[:], ALU.add, [bk1, bk2], [bkrT])
            else:
                bst, bks = bf("ckvstS"), bf("krstS")
                for mc in range(2):
                    stt(ckvstS[:, mc, :], pcs[mc][0][:, 0:NA], gkv[:, l, mc:mc + 1], rskS[:], ALU.mult, ALU.mult,
                        [pcs[mc][1], brsk, bconst], [bst])
                cp('act', cn[:], ckvstS[:], [bst], [cnb])
                out_toks.append(S.dma('sp', lambda e, l=l, ckvstS=ckvstS: e.dma_start(out=ckvst_d[l], in_=ckvstS[:]), bst, [bst], []))
                cp('dve', krstS[:], pks[0][0][0:64, 0:NA], [pks[0][1]], [bks])
                cp('act', KVs['krT'][0:64, key0:key0 + NA], krstS[:], [bks], [bkrT])
                out_toks.append(S.dma('sp', lambda e, l=l, krstS=krstS: e.dma_start(out=krst_d[l], in_=krstS[:]), bks, [bks], []))
            for h_ in range(4):
                pb, pbb = gb()
                for k in range(2):
                    mm(pb[:, 0:NA], wkvS[:, k, h_ * 128:(h_ + 1) * 128], cn[:, k, :], k == 0, k == 1, [bwkv, cnb], [pbb])
                cp('act', KVs['knT'][:, h_, key0:key0 + NA], pb[:, 0:NA], [pbb], [bknT])
            for tb_ in range(4):
                pb, pbb = gb()
                for k in range(2):
                    mm(pb[:, 0:512], cn[:, k, tb_ * 128:(tb_ + 1) * 128], wkvS[:, k, 512:1024], k == 0, k == 1,
                       [bwkv, cnb], [pbb])
                cp('dve', KVs['V'][:, key0 // 128 + tb_, :], pb[:, 0:512], [pbb], [bV])

        sa_norm(0)
        for t in range(5):
            pcs, pks = sa_proj(t)
            if t + 1 < 5:
                sa_norm(t + 1)
            sa_chain(t, pcs, pks)
        gbank_list[:] = [0, 1, 2, 3]
        S.barrier()

        cur[0] = kv_end
        wB = alloc([128, 8, 1152], BF16, "wB")
        wq = alloc([128, 3, 1024], BF16, "wq")
        wo = alloc([128, 8, 1024], BF16, "wo")
        wpl = alloc([128, 2, 128], BF16, "wpl")
        wsT = alloc([128, 4, 128], BF16, "wsT")
        bsT = alloc([128, 2, 128], BF16 if BST_BF16 else F32, "bsT")
        gsg = alloc([128, 256], BF16, "gsg")
        yT = alloc([128, 8, SUB], BF16, "yT")
        sqB = None
        wmn = alloc([128, 2, SUB], F32, "wmn")
        tmpxB = [wmn[:, 0, :], wmn[:, 1, :]]
        rstdB = alloc([128, SUB], F32, "rstdB")
        lnvB = rstdB
        lnq, rsq = lnvB, rstdB
        h1B = alloc([128, 8, SUB], BF16, "h1B")
        P0 = alloc([128, 2, SUB + 16], F32, "P0")
        sqq = alias(P0, [128, 3, SUB], BF16, "sqq")
        S2 = alloc([128, 2, SUB + 16], F32, "S2")
        S4 = alloc([128, 2, SUB + 16], F32, "S4")
        pooled = alloc([128, 2, SUB], BF16, "pooled")
        cqg = alloc([128, 3, SUB], BF16, "cqg")
        qnT = alloc([128, 4, SUB], BF16, "qnT")
        qrT = alloc([128, 4, SUB], BF16, "qrT")
        ropeB = alloc([64, 2, SUB], F32, "ropeB")
        qt1f = alloc([128, SUB], F32, "qt1")
        qt1 = qt1f[0:64, :]
        qt2 = alloc([64, SUB], F32, "qt2")
        ug = pooled
        vss = alloc([128, 2], F32, "vss")
        vln = alloc([128, 2], F32, "vln")
        vrs = alloc([128, 2], F32, "vrs")
        vgA = alloc([128, 2, 256], BF16, "vgA")
        vgB = alloc([128, 2, 256], BF16, "vgB")
        gt = S2[:, :, 0:SUB]
        vt = S4[:, :, 0:SUB]
        PT = [alloc([128, 512], BF16, "PT") for _ in range(2)]
        denr = qt1f
        NPT = len(PT)

        bwB, bwq, bwo, bwsm = bf("wB"), bf("wq"), bf("wo"), bf("wsm")
        ld('pool', wB[:], winB_d[l], bwB)
        ld('pool', wq[:], wuq_d[l], bwq)
        memset('dve', wpl[:], 0.0, [bwsm])
        for g in range(4):
            c_, hf = g // 2, g % 2
            S.dma('pool', lambda e, g=g, c_=c_, hf=hf, l=l, wpl=wpl: e.dma_start(out=wpl[hf * 64:(hf + 1) * 64, c_, hf * 64:(hf + 1) * 64],
                                                                  in_=wpool_d[l, g]), bwsm, (), [bwsm])
        ld('pool', wsT[:], wsT_d[l], bwsm)
        if BST_BF16:
            S.dma('pool', lambda e, l=l, bsT=bsT: e.dma_start(out=bsT[:], in_=bsT_d[l]), bwsm, (), [bwsm])
        else:
            S.dma('sp', lambda e, l=l, bsT=bsT: e.dma_start(out=bsT[:], in_=bsT_d[l]), bf("gsgk"), (), [bf("gsgk"), bwsm])
        S.dma('pool', lambda e, l=l, gsg=gsg: e.dma_start(out=gsg[:], in_=gsgu_d[l]), bf("gsgk"), (), [bf("gsgk")])
        ld('pool', wo[:], wout_d[l], bwo)
        bvgA, bvgB = bf("vgA"), bf("vgB")
        memset('dve', vgA[:], 0.0, [bvgA])
        memset('dve', vgB[:], 0.0, [bvgB])
        if PADZERO:
            memset('dve', qrT[64:128, :, :], 0.0, [bf("qrT")])

        def normB(s):
            tok0, tok1 = tokrange(s)
            norm_tile(tok0, SUB, a1, 0, mtype(s), h1B, bf('h1B'), 'nB', h1B, tmpxB, lnvB, rstdB, bsq=bf('h1B'),
                      tmpx_extra=[bf('wmn')])

        def phaseB_sub(s, KV, prenormed=False, next_s=None):
            tok0, tok1 = tokrange(s)
            mt = mtype(s)
            sample = s < 8
            knT, krT, V = KV['knT'], KV['krT'], KV['V']
            bh1 = bf("h1B")
            byT = bf("yT")
            bwm = bf("wmn")
            blnq, brsq = bf("nBrs"), bf("nBrs")
            if not prenormed:
                normB(s)
            tbx = [bf("nBtmpx0"), bf("nBtmpx1")]
            pcq0, pcq0b = gb()
            for mc in range(2):
                for k in range(8):
                    mm(pcq0[:, mc * 256:(mc + 1) * 256], wB[:, k, 256 + mc * 128:256 + (mc + 1) * 128], h1B[:, k, :],
                       k == 0, k == 7, [bwB, bh1], [pcq0b])
            pcq1, pcq1b = gb()
            for k in range(8):
                mm(pcq1[:, 0:256], wB[:, k, 512:640], h1B[:, k, :], k == 0, k == 7, [bwB, bh1], [pcq1b])
            bsqq, bcqg = bf("P0"), bf("cqg")
            act(sqq[:, 0:2, :], pcq0[:, 0:512].rearrange("p (m n) -> p m n", m=2), AF.Square, [pcq0b], [bsqq])
            act(sqq[:, 2, :], pcq1[:, 0:256], AF.Square, [pcq1b], [bsqq])
            for mc in range(3):
                src = pcq0[:, mc * 256:(mc + 1) * 256] if mc < 2 else pcq1[:, 0:256]
                ts('dve', cqg[:, mc, :], src, gq[:, l, mc:mc + 1], None, ALU.mult, None,
                   [pcq0b if mc < 2 else pcq1b, bconst], [bcqg])
            pss, pssb = gb()
            for k in range(3):
                mm(pss[:, 0:256], ones[:], sqq[:, k, :], k == 0, k == 2, [bones, bsqq], [pssb])
            rstd_from_ps(pss[:, 0:256], 384.0, lnq[:], rsq[:], [pssb], blnq, brsq)
            yield 'cq'
            bqn, bqr = bf("qnT"), bf("qrT")
            for hp_ in range(2):
                pq, pqb = gb()
                for hh in range(2):
                    h_ = 2 * hp_ + hh
                    for k in range(3):
                        mm(pq[:, hh * 256:(hh + 1) * 256], wq[:, k, h_ * 128:(h_ + 1) * 128], cqg[:, k, :], k == 0, k == 2,
                           [bwq, bcqg], [pqb])
                tt('dve', qnT[:, 2 * hp_:2 * hp_ + 2, :], pq[:, 0:512].rearrange("p (h n) -> p h n", h=2),
                   rsq[:].unsqueeze(1).broadcast_to([128, 2, 256]), ALU.mult, [pqb, brsq], [bqn])
            brp2 = bf("ropeB2")
            if sample:
                brp = bf("ropeB")
                S.dma('sp', lambda e, tok0=tok0: e.dma_start(out=ropeB[:, 0, :], in_=cos_d[:, tok0:tok0 + SUB]), brp, (), [brp, brp2])
                S.dma('sp', lambda e, tok0=tok0: e.dma_start(out=ropeB[:, 1, :], in_=sin_d[:, tok0:tok0 + SUB]), brp, (), [brp, brp2])
                tt('dve', ropeB[:, 0, :], ropeB[:, 0, :], rsq[0:64, :], ALU.mult, [brp, brsq], [brp2])
                tt('dve', ropeB[:, 1, :], ropeB[:, 1, :], rsq[0:64, :], ALU.mult, [brp, brsq, brp2], [brp2])
            for h_ in range(4):
                pq, pqb = gb()
                nq = 2 if sample else 1
                for q_ in range(nq):
                    c0 = 512 + q_ * 256 + h_ * 64
                    for k in range(3):
                        mm(pq[0:64, q_ * 256:(q_ + 1) * 256], wq[:, k, c0:c0 + 64], cqg[:, k, :], k == 0, k == 2,
                           [bwq, bcqg], [pqb])
                if sample:
                    bq1, bq2 = bf("qt1"), bf("qt2")
                    tt('dve', qt1[:], pq[0:64, 0:256], ropeB[:, 0, :], ALU.mult, [pqb, brp2], [bq1])
                    tt('dve', qt2[:], pq[0:64, 256:512], ropeB[:, 1, :], ALU.mult, [pqb, brp2], [bq2])
                    tt('dve', qrT[0:64, h_, :], qt1[:], qt2[:], ALU.add, [bq1, bq2], [bqr])
                else:
                    tt('dve', qrT[0:64, h_, :], pq[0:64, 0:256], rsq[0:64, :], ALU.mult, [pqb, brsq], [bqr])

            yield 'q'

            def rest_front():
                php, phpb = gb()
                for mc in range(2):
                    for k in range(8):
                        mm(php[:, mc * 256:(mc + 1) * 256], wB[:, k, mc * 128:(mc + 1) * 128], h1B[:, k, :], k == 0, k == 7,
                           [bwB, bh1], [phpb])
                bP0, bS2, bS4, bpl = bf("P0"), bf("S2"), bf("S4"), bf("pooled")
                cp('dve', P0[:, :, 8:8 + SUB], php[:, 0:512].rearrange("p (m n) -> p m n", m=2), [phpb], [bP0])
                if sample and s > 0:
                    cp('dve', P0[:, :, 0:8], KV['hph'][:, :, s - 1, 8:16], [bhph], [bP0])
                else:
                    memset('dve', P0[:, :, 0:8], 0.0, [bP0])
                if sample and s < 7:
                    cp('dve', P0[:, :, 8 + SUB:16 + SUB], KV['hph'][:, :, s + 1, 0:8], [bhph], [bP0])
                else:
                    memset('dve', P0[:, :, 8 + SUB:16 + SUB], 0.0, [bP0])
                W_ = SUB + 16
                wmw = [bwm] + tbx
                tt('dve', S2[:, :, 1:W_], P0[:, :, 1:W_], P0[:, :, 0:W_ - 1], ALU.add, [bP0], [bS2])
                tt('dve', S4[:, :, 3:W_], S2[:, :, 3:W_], S2[:, :, 1:W_ - 2], ALU.add, [bS2], [bS4])
                ts('dve', wmn[0:64, 0, :], S2[0:64, 0, 8:8 + SUB], invw[0:64, 0:1], None, ALU.mult, None, [bS2, bconst], wmw)
                ts('dve', wmn[64:128, 0, :], S4[64:128, 0, 9:9 + SUB], invw[64:128, 0:1], None, ALU.mult, None, [bS4, bconst], wmw)
                bS8 = bf("S8")
                tt('dve', S2[:, 1, 7:W_], S4[:, 1, 7:W_], S4[:, 1, 3:W_ - 4], ALU.add, [bS4, bwm, bS2], [bS8, bS2])
                ts('dve', wmn[0:64, 1, :], S2[0:64, 1, 11:11 + SUB], invw[0:64, 1:2], None, ALU.mult, None, [bS8, bconst], wmw)
                bS16 = bf("S16")
                tt('dve', S4[64:128, 1, 15:W_], S2[64:128, 1, 15:W_], S2[64:128, 1, 7:W_ - 8], ALU.add, [bS8, bwm, bS4], [bS16, bS4])
                ts('dve', wmn[64:128, 1, :], S4[64:128, 1, 15:15 + SUB], invw[64:128, 1:2], None, ALU.mult, None,
                   [bS16, bconst], wmw)
                seq_start = (s == 0) or not sample
                seq_end = (s == 7) or not sample
                if seq_start:
                    tt('dve', wmn[:, :, 0:8], wmn[:, :, 0:8], corrL[:], ALU.mult, [bwm, bconst], wmw)
                if seq_end:
                    tt('dve', wmn[:, :, SUB - 8:SUB], wmn[:, :, SUB - 8:SUB], corrR[:], ALU.mult, [bwm, bconst], wmw)
                tt('dve', pooled[:], wmn[:], P0[:, :, 8:8 + SUB], ALU.subtract, [bwm, bP0], [bpl])
                ppl, pplb = gb()
                for mc in range(2):
                    mm(ppl[:, mc * 256:(mc + 1) * 256], wpl[:, mc, :], pooled[:, mc, :], True, True, [bwsm, bpl], [pplb])
                for mc in range(2):
                    ts('dve', yT[:, mc, :], ppl[:, mc * 256:(mc + 1) * 256], pscale[:, l, mc:mc + 1], None, ALU.mult, None,
                       [pplb, bconst], [byT])
                pu, pub = gb()
                for mc in range(2):
                    for k in range(8):
                        mm(pu[:, mc * 256:(mc + 1) * 256], wB[:, k, 640 + mc * 128:640 + (mc + 1) * 128], h1B[:, k, :],
                           k == 0, k == 7, [bwB, bh1], [pub])
                bug = bf("pooled")
                act(ug[:], pu[:, 0:512].rearrange("p (m n) -> p m n", m=2), AF.Gelu_apprx_tanh, [pub], [bug])
                pv, pvb = gb()
                for blk in range(2):
                    for k in range(8):
                        mm(pv[:, blk * 256:(blk + 1) * 256], h1B[:, k, blk * 128:(blk + 1) * 128], wB[:, k, 896:1152],
                           k == 0, k == 7, [bwB, bh1], [pvb])
                bvt, bvss, bvrs, bgt = bf("S4"), bf("vss"), bf("vrs"), bf("S2")
                act(vt[:], pv[:, 0:512].rearrange("p (m n) -> p m n", m=2), AF.Gelu_apprx_tanh, [pvb], [bvt, bf("S16")])
                for blk in range(2):
                    S.op('act', lambda e, blk=blk: e.activation(out=gt[:, 0, :], in_=vt[:, blk, :], func=AF.Square,
                                                                accum_out=vss[:, blk:blk + 1]), [bvt], [bvss, bgt, bf("S8")])
                act(vln[:], vss[:], AF.Ln, [bvss], [bf("vln")], scale=1.0 / 256.0, bias=EPS)
                act(vrs[:], vln[:], AF.Exp, [bf("vln")], [bvrs], scale=-0.5)
                for blk in range(2):
                    for hf, (vg, vgb) in enumerate([(vgA, bvgA), (vgB, bvgB)]):
                        o_ = vg[:, blk, :].rearrange("p (m h c) -> p m h c", m=2, h=2)[:, :, hf, :]
                        i0 = vt[:, blk, :].rearrange("p (m h c) -> p m h c", m=2, h=2)[:, :, hf, :]
                        i1 = gsg[:].rearrange("p (m h c) -> p m h c", m=2, h=2)[:, :, hf, :]
                        stt(o_, i0, vrs[:, blk:blk + 1], i1, ALU.mult, ALU.mult, [bvt, bvrs, bf("gsgk")], [vgb])
                pg, pgb = gb()
                for mc in range(2):
                    for blk in range(2):
                        o_ = pg[:, mc * 256 + blk * 128: mc * 256 + (blk + 1) * 128]
                        mm(o_, vgA[:, blk, mc * 128:(mc + 1) * 128], wsT[:, 2 * mc, :], True, False, [bvgA, bwsm], [pgb])
                        mm(o_, vgB[:, blk, mc * 128:(mc + 1) * 128], wsT[:, 2 * mc + 1, :], False, True, [bvgB, bwsm], [pgb])
                for mc in range(2):
                    tt('dve', gt[:, mc, :].rearrange("p (b n) -> p b n", b=2),
                       pg[:, mc * 256:(mc + 1) * 256].rearrange("p (b n) -> p b n", b=2),
                       bsT[:, mc, :].unsqueeze(1).broadcast_to([128, 2, 128]), ALU.add, [pgb, bwsm], [bgt, bf("S8")])
                tt('dve', yT[:, 6:8, :], gt[:], ug[:], ALU.mult, [bgt, bug], [byT])

            if sample:
                kbase, nkc = 0, 18
            else:
                kbase, nkc = 2304 + (s - 8) * 256, 2
            npair = nkc // 2
            pairs = [(h_, jp) for h_ in range(4) for jp in range(npair)]
            st_ = {}

            def emit_scores(i):
                h_, jp = pairs[i]
                psc, pscb = (ps[7], psb[7]) if i % 2 == 0 else gb()
                for sub in range(2):
                    j = 2 * jp + sub
                    k0 = kbase + j * 128
                    mm(psc[:, sub * 256:(sub + 1) * 256], knT[:, h_, k0:k0 + 128], qnT[:, h_, :], True, False,
                       [bknT, bqn], [pscb])
                    if ROPE128:
                        mm(psc[:, sub * 256:(sub + 1) * 256], krT[:, k0:k0 + 128], qrT[:, h_, :], False, True,
                           [bkrT, bqr], [pscb])
                    else:
                        mm(psc[:, sub * 256:(sub + 1) * 256], krT[0:64, k0:k0 + 128], qrT[0:64, h_, :], False, True,
                           [bkrT, bqr], [pscb])
                pt, ptb = PT[i % NPT], bf("PT%d" % (i % NPT))
                act(pt[:], psc[:, 0:512], AF.Exp, [pscb], [ptb], scale=SCALE)
                st_[i] = (pt, ptb)

            def emit_pv(i):
                h_, jp = pairs[i]
                pt, ptb = st_.pop(i)
                po, pob = ps[4 + (h_ % 2)], psb[4 + (h_ % 2)]
                for sub in range(2):
                    j = 2 * jp + sub
                    kc = (kbase // 128) + j
                    mm(po[:, 0:256], V[:, kc, h_ * 128:(h_ + 1) * 128], pt[:, sub * 256:(sub + 1) * 256],
                       j == 0, j == nkc - 1, [bV, ptb], [pob], skip=True)
                    mm(po[:, 256:512], ones[:], pt[:, sub * 256:(sub + 1) * 256], False, j == nkc - 1,
                       [bones, ptb], [pob], skip=True)
                if jp == npair - 1:
                    bdr = bf("qt1")
                    S.op('dve', lambda e, po=po: e.reciprocal(out=denr[:], in_=po[:, 256:512]), [pob], [bdr])
                    tt('dve', yT[:, 2 + h_, :], po[:, 0:256], denr[:], ALU.mult, [pob, bdr], [byT])

            rest_front()
            emit_scores(0)
            for i in range(len(pairs)):
                if i + 1 < len(pairs):
                    emit_scores(i + 1)
                emit_pv(i)
                if i == npair - 1 and next_s is not None:
                    normB(next_s)
            yield 'att'
            for mp in range(4):
                pw, pwb = gb()
                for mh in range(2):
                    m_ = 2 * mp + mh
                    for k in range(8):
                        mm(pw[:, mh * 256:(mh + 1) * 256], wo[:, k, m_ * 128:(m_ + 1) * 128], yT[:, k, :], k == 0, k == 7,
                           [bwo, byT], [pwb])
                for mh in range(2):
                    m_ = 2 * mp + mh
                    stt(x[:, m_, tok0:tok1], pw[:, mh * 256:(mh + 1) * 256], mod[:, 16 + m_, mt:mt + 1], x[:, m_, tok0:tok1],
                        ALU.mult, ALU.add, [pwb, bmod, bx], [bx])

        gbank_list[:] = [0, 1, 2, 3, 6]
        gens = {s: phaseB_sub(s, KVs, prenormed=(s > 0), next_s=(s + 1 if s < NSUB - 1 else None)) for s in range(NSUB)}
        next(gens[0])
        next(gens[0])
        for s in range(NSUB):
            next(gens[s])
            if s + 1 < NSUB:
                next(gens[s + 1])
            for _ in gens[s]:
                pass
            if s + 1 < NSUB:
                next(gens[s + 1])
        gbank_list[:] = [0, 1, 2, 3]
        S.barrier()

        cur[0] = scr_base
        h2 = alloc([128, 8, T], BF16, "h2")
        wslot = [(alloc([128, 8, 512], BF16, "w1s"), alloc([128, 4, 1024], BF16, "w2s")) for _ in range(2)]
        sqF = alloc([128, 8, 512], BF16, "sqF")
        tmpxF = [alloc([128, 512], F32, "tmpxF") for _ in range(2)]
        lnvF = alloc([128, 512], F32, "lnvF")
        rstdF = alloc([128, 512], F32, "rstdF")
        rl = [alloc([128, 512], F32, "rl") for _ in range(2)]
        hid = [alloc([128, 4, 512], BF16, "hid") for _ in range(2)]
        ybuf = alloc([128, 8, 512], F32, "ybuf")
        bh2 = [bf("h2_%d" % t) for t in range(5)]

        def load_w(j):
            w1s, w2s = wslot[j % 2]
            kb1, kb2 = bf("w1s%d" % (j % 2)), bf("w2s%d" % (j % 2))
            ld('pool', w1s[:], w1_d[l, j], kb1)
            ld('pool', w2s[:], w2_d[l, j], kb2)

        gbank_list[:] = [0, 1, 2, 3, 4, 5, 6]
        load_w(0)
        nxt = l + 1 < L
        if nxt:
            ringF = [alloc([128, 8, 128], BF16, "adarF") for _ in range(4)]
            ringFb = [bf("adarF%d" % i) for i in range(4)]

        def norm2(t):
            mt = 0 if t < 4 else 1
            norm_tile(t * 512, 512, a2, 24, mt, h2[:, :, t * 512:(t + 1) * 512], bh2[t], "nF", sqF, tmpxF, lnvF, rstdF)

        def ffn_tile(j, t):
            w1s, w2s = wslot[j % 2]
            kb1, kb2 = bf("w1s%d" % (j % 2)), bf("w2s%d" % (j % 2))
            mt = 0 if t < 4 else 1
            hd, hdb = hid[t % 2], bf("hid%d" % (t % 2))
            for m_ in range(4):
                pb, pbb = gb()
                for k in range(8):
                    mm(pb[:, 0:512], w1s[:, k, m_ * 128:(m_ + 1) * 128], h2[:, k, t * 512:(t + 1) * 512], k == 0, k == 7,
                       [kb1, bh2[t]], [pbb])
                r_, rb_ = rl[m_ % 2], bf("rl%d" % (m_ % 2))
                act(r_[:], pb[:, 0:512], AF.Relu, [pbb], [rb_])
                tt('pool', hd[:, m_, :], r_[:], r_[:], ALU.mult, [rb_], [hdb])
            for m_ in range(8):
                pb, pbb = gb()
                for k in range(4):
                    mm(pb[:, 0:512], w2s[:, k, m_ * 128:(m_ + 1) * 128], hd[:, k, :], k == 0, k == 3, [kb2, hdb], [pbb])
                stt(x[:, m_, t * 512:(t + 1) * 512], pb[:, 0:512], mod[:, 40 + m_, mt:mt + 1],
                    x[:, m_, t * 512:(t + 1) * 512], ALU.mult, ALU.add, [pbb, bmod, bx], [bx])

        norm2(0)
        norm2(1)
        for j in range(NJ):
            if j + 1 < NJ:
                load_w(j + 1)
            for t in range(5):
                ffn_tile(j, t)
                if j == 0 and t + 2 < 5:
                    norm2(t + 2)
            if nxt:
                emit_mod(l + 1, ringF, ringFb, j * 6, (j + 1) * 6)
        if nxt:
            finish_mod(l + 1)
        if l == L - 1:
            for t in range(5):
                bsq, bln, brs, byb = bf("fsq"), bf("fln"), bf("frs"), bf("ybuf")
                act(sqF[:], x[:, :, t * 512:(t + 1) * 512], AF.Square, [bx], [bsq])
                pb, pbb = gb()
                for k in range(8):
                    mm(pb[:, 0:512], ones[:], sqF[:, k, :], k == 0, k == 7, [bones, bsq], [pbb])
                rstd_from_ps(pb[:, 0:512], float(D), lnvF[:], rstdF[:], [pbb], bln, brs)
                for c in range(8):
                    stt(ybuf[:, c, :], x[:, c, t * 512:(t + 1) * 512], gfinal[:, c:c + 1], rstdF[:], ALU.mult, ALU.mult,
                        [bx, brs, bconst], [byb])
                out_toks.append(S.dma('sp', lambda e, t=t: e.dma_start(out=yT_d[:, :, t * 512:(t + 1) * 512], in_=ybuf[:]),
                                      byb, [byb], []))
        gbank_list[:] = [0, 1, 2, 3]
        S.barrier()

    with ExitStack() as st:
        with nc.Block() as block:
            S.emit(nc, block, st, final_waits=out_toks)
    return nc


def _pc(a, nchunk):
    return np.ascontiguousarray(a.reshape((nchunk, 128) + a.shape[1:]).swapaxes(0, 1))


def _vecs(g, nchunk):
    Ln = g.shape[0]
    return np.ascontiguousarray(g.reshape(Ln, nchunk, 128).transpose(2, 0, 1))


def _rope_tables():
    rows = TS // 64
    row = np.repeat(np.arange(rows, dtype=np.float32), 64)
    col = np.tile(np.arange(64, dtype=np.float32), rows)
    inv = (1.0 / (10000.0 ** (np.arange(16, dtype=np.float32) / 16))).astype(np.float32)
    ang = np.concatenate([row[:, None] * inv, col[:, None] * inv], axis=-1).astype(np.float32)
    cos, sin = np.cos(ang).astype(np.float32), np.sin(ang).astype(np.float32)
    cosT = np.concatenate([cos, cos], axis=1).T
    sinT = np.concatenate([-sin, sin], axis=1).T
    return np.ascontiguousarray(cosT), np.ascontiguousarray(sinT)


def _pool_tables():
    wins = (2, 4, 8, 16)
    invw = np.zeros((128, 2), np.float32)
    corrL = np.ones((128, 2, 8), np.float32)
    corrR = np.ones((128, 2, 8), np.float32)
    Ls = 256
    for g, w in enumerate(wins):
        c, hf = g // 2, g % 2
        sl = slice(hf * 64, (hf + 1) * 64)
        invw[sl, c] = 1.0 / w
        for i in range(8):
            t = i
            cnt = min(t + (w - w // 2), Ls) - max(t - w // 2, 0)
            corrL[sl, c, i] = w / cnt
            t = Ls - 8 + i
            cnt = min(t + (w - w // 2), Ls) - max(t - w // 2, 0)
            corrR[sl, c, i] = w / cnt
    return invw, corrL, corrR


def prepare_inputs(x_prompt, x_sample, cache_ckv, cache_krope, c, c_ctx, w_ada, b_ada, g_mix,
                   w_in, w_pool, pool_scale, g_q, w_uq, g_kv, w_ukv, g_sgu, w_s, b_s, w_out,
                   g_ffn, w_ff1, w_ff2, g_final):
    f = lambda a: np.asarray(a, dtype=np.float32)
    x_prompt, x_sample, cache_ckv, cache_krope, c, c_ctx = map(f, (x_prompt, x_sample, cache_ckv, cache_krope, c, c_ctx))
    w_ada, b_ada, g_mix, w_in, w_pool, pool_scale, g_q, w_uq, g_kv, w_ukv = map(
        f, (w_ada, b_ada, g_mix, w_in, w_pool, pool_scale, g_q, w_uq, g_kv, w_ukv))
    g_sgu, w_s, b_s, w_out, g_ffn, w_ff1, w_ff2, g_final = map(f, (g_sgu, w_s, b_s, w_out, g_ffn, w_ff1, w_ff2, g_final))
    Ln = w_in.shape[0]
    OFF_Q, OFF_KV, OFF_R, OFF_G = 256, 640, 896, 960
    shared = {}
    shared["wada"] = np.ascontiguousarray(w_ada.reshape(Ln, 8, 128, 48, 128).transpose(0, 3, 2, 1, 4))
    shared["bada"] = _vecs(b_ada, 48)
    shared["gmix"] = _vecs(g_mix, 8)
    shared["gffn"] = _vecs(g_ffn, 8)
    shared["gq"] = _vecs(g_q, 3)
    shared["gkv"] = _vecs(g_kv, 2)
    shared["gsgu"] = np.ascontiguousarray(np.broadcast_to(g_sgu[:, None, :], (Ln, 128, 256)))
    shared["pscale"] = _vecs(pool_scale, 2)
    shared["gfinal"] = np.ascontiguousarray(g_final.reshape(8, 128).T)
    kr = w_in[:, :, OFF_R:OFF_G]
    kr_swp = np.concatenate([kr[:, :, 32:64], kr[:, :, 0:32]], axis=2)
    winA = np.concatenate([w_in[:, :, OFF_KV:OFF_R], kr, kr_swp], axis=2)
    shared["winA"] = np.ascontiguousarray(winA.reshape(Ln, 8, 128, 384).transpose(0, 2, 1, 3))
    winB = np.concatenate([w_in[:, :, 0:OFF_Q], w_in[:, :, OFF_Q:OFF_KV], w_in[:, :, OFF_G:OFF_G + 512]], axis=2)
    shared["winB"] = np.ascontiguousarray(winB.reshape(Ln, 8, 128, 1152).transpose(0, 2, 1, 3))
    wq4 = w_uq.reshape(Ln, 384, 4, 192)
    qn = wq4[:, :, :, 0:128].reshape(Ln, 384, 512)
    qr = wq4[:, :, :, 128:192]
    qr_swp = np.concatenate([qr[..., 32:64], qr[..., 0:32]], axis=-1)
    wuq = np.concatenate([qn, qr.reshape(Ln, 384, 256), qr_swp.reshape(Ln, 384, 256)], axis=2)
    shared["wuq"] = np.ascontiguousarray(wuq.reshape(Ln, 3, 128, 1024).transpose(0, 2, 1, 3))
    wkv4 = w_ukv.reshape(Ln, 256, 4, 256)
    wukv = np.concatenate([wkv4[..., 0:128].reshape(Ln, 256, 512), wkv4[..., 128:256].reshape(Ln, 256, 512)], axis=2)
    shared["wukv"] = np.ascontiguousarray(wukv.reshape(Ln, 2, 128, 1024).transpose(0, 2, 1, 3))
    shared["wout"] = np.ascontiguousarray(w_out.reshape(Ln, 8, 128, 1024).transpose(0, 2, 1, 3))
    shared["wpool"] = np.ascontiguousarray(w_pool)
    shared["wsT"] = np.ascontiguousarray(w_s.transpose(0, 3, 1, 2))
    bsT = np.broadcast_to(b_s.reshape(Ln, 2, 2, 1, 128), (Ln, 2, 2, 64, 128))
    shared["bsT"] = np.ascontiguousarray(bsT.reshape(Ln, 2, 128, 128).transpose(0, 2, 1, 3))
    shared["w1"] = np.ascontiguousarray(w_ff1.reshape(Ln, 8, 128, NJ, 512).transpose(0, 3, 2, 1, 4))
    shared["w2"] = np.ascontiguousarray(w_ff2.reshape(Ln, NJ, 4, 128, 1024).transpose(0, 1, 3, 2, 4))
    cosT, sinT = _rope_tables()
    shared["cosT"], shared["sinT"] = cosT, sinT
    shared["invw"], shared["corrL"], shared["corrR"] = _pool_tables()
    in_maps = []
    for i in range(8):
        m = dict(shared)
        xs = np.concatenate([x_sample[i], x_prompt[2 * i], x_prompt[2 * i + 1]], axis=0)
        m["xT"] = _pc(np.ascontiguousarray(xs.T), 8)
        cc = np.stack([c[i], c_ctx], axis=1)
        m["cT"] = _pc(cc, 8)
        ck = cache_ckv[i].transpose(2, 0, 1)
        m["cckv"] = np.ascontiguousarray(ck.reshape(2, 128, Ln, 256).transpose(1, 2, 0, 3))
        m["ckr"] = np.ascontiguousarray(cache_krope[i].transpose(2, 0, 1))
        in_maps.append(m)
    return in_maps


_NC_CACHE = {}


def kernel(**inputs):
    in_maps = prepare_inputs(**inputs)
    if "nc" not in _NC_CACHE:
        _NC_CACHE["nc"] = build_program(L_FULL)
    nc = _NC_CACHE["nc"]
    res = run_bass_kernel_spmd(nc, in_maps, core_ids=list(range(8)))
    y_prompt = np.zeros((16, 256, D), np.float32)
    y_sample = np.zeros((8, TS, D), np.float32)
    state_ckv = np.zeros((16, L_FULL, 256, 256), np.float32)
    state_krope = np.zeros((16, L_FULL, 256, 64), np.float32)
    for i, r in enumerate(res.results):
        yT = np.asarray(r["yT"])
        y = yT.transpose(2, 1, 0).reshape(T, D)
        y_sample[i] = y[0:TS]
        y_prompt[2 * i] = y[TS:TS + 256]
        y_prompt[2 * i + 1] = y[TS + 256:TS + 512]
        ck = np.asarray(r["ckvst"])
        ck = ck.transpose(0, 3, 2, 1).reshape(L_FULL, 512, 256)
        kr = np.asarray(r["krst"]).transpose(0, 2, 1)
        for b in range(2):
            state_ckv[2 * i + b] = ck[:, b * 256:(b + 1) * 256, :]
            state_krope[2 * i + b] = kr[:, b * 256:(b + 1) * 256, :]
    return (y_prompt, y_sample, state_ckv, state_krope)
```

```python
import math
from contextlib import ExitStack

import numpy as np
import concourse.bass as bass
import concourse.mybir as mybir
from concourse.bass_utils import run_bass_kernel_spmd

F32, BF16 = mybir.dt.float32, mybir.dt.bfloat16
AF = mybir.ActivationFunctionType
ALU = mybir.AluOpType

D = 1024
L_FULL = 4
T = 2560
TS = 2048
NSUB = 10
SUB = 256
NKEY = 2816
EPS = 1e-6
SCALE = 1.0 / math.sqrt(192.0)
NJ = 8
REORDER = True
PADZERO = True
ROPE128 = True
BST_BF16 = True
ENGS = ['pe', 'act', 'dve', 'pool', 'sp']


class Buf:
    __slots__ = ('name', 'w', 'r', 'dcount', 'sem', 'psum')

    def __init__(self, name, psum=False):
        self.name = name
        self.psum = psum
        self.w = None
        self.r = []
        self.dcount = 0
        self.sem = None


class Op:
    __slots__ = ('eng', 'fn', 'deps', 'idx', 'signal', 'sigval', 'dkey', 'dval')

    def __init__(self, eng, fn, deps, idx, dkey=None, dval=0):
        self.eng = eng
        self.fn = fn
        self.deps = deps
        self.idx = idx
        self.signal = False
        self.sigval = 0
        self.dkey = dkey
        self.dval = dval


class Sched:
    def __init__(self):
        self.ops = {e: [] for e in ENGS}
        self.dkeys = []
        self.bar_deps = set()
        self.bar_gen = 0
        self.eng_gen = {e: 0 for e in ENGS}

    def _deps(self, eng, reads, writes, is_dma):
        deps = set()
        for b in reads:
            if b.w is not None:
                deps.add(b.w)
            if b.psum:
                deps.update(t for t in b.r if not (t[0] == 'e' and t[1] == eng))
        for b in writes:
            if b.w is not None and not (is_dma and b.w[0] == 'd'):
                deps.add(b.w)
            deps.update(b.r)
        if self.eng_gen[eng] != self.bar_gen:
            deps |= self.bar_deps
            self.eng_gen[eng] = self.bar_gen
        return deps

    def op(self, eng, fn, reads=(), writes=()):
        deps = self._deps(eng, reads, writes, False)
        o = Op(eng, fn, deps, len(self.ops[eng]))
        self.ops[eng].append(o)
        tok = ('e', eng, o.idx)
        if eng == 'pe':
            o.deps = {d for d in deps if not (d[0] == 'e' and d[1] == 'pe')}
        for b in reads:
            self._add_reader(b, tok)
        for b in writes:
            b.w = tok
            b.r = []
        return tok

    @staticmethod
    def _add_reader(b, tok):
        b.r = [t for t in b.r if not (t[0] == tok[0] and t[1] == tok[1])]
        b.r.append(tok)

    def dma(self, eng, fn, key, reads=(), writes=()):
        deps = self._deps(eng, reads, writes, True)
        if key.sem is None:
            key.sem = True
            self.dkeys.append(key)
        key.dcount += 16
        o = Op(eng, fn, deps, len(self.ops[eng]), dkey=key, dval=key.dcount)
        self.ops[eng].append(o)
        tok = ('d', key, key.dcount)
        for b in reads:
            self._add_reader(b, tok)
        for b in writes:
            b.w = tok
            b.r = []
        return tok

    def barrier(self):
        deps = set()
        for e in ENGS:
            for o in reversed(self.ops[e]):
                if o.dkey is None:
                    deps.add(('e', e, o.idx))
                    break
        for k in self.dkeys:
            if k.dcount:
                deps.add(('d', k, k.dcount))
        self.bar_deps = deps
        self.bar_gen += 1

    def emit(self, nc, block, stack, final_waits=()):
        for e in ENGS:
            for o in self.ops[e]:
                for d in o.deps:
                    if d[0] == 'e':
                        self.ops[d[1]][d[2]].signal = True
        for d in final_waits:
            if d[0] == 'e':
                self.ops[d[1]][d[2]].signal = True
        esem = {}
        for e in ENGS:
            n = 0
            for o in self.ops[e]:
                if o.signal:
                    n += 1
                    o.sigval = n
            if n:
                esem[e] = stack.enter_context(nc.semaphore('s_' + e))
        for k in self.dkeys:
            k.sem = stack.enter_context(nc.semaphore('d_' + k.name))
        sched = self

        def run(e, engobj, extra=()):
            seen = {}

            def wait(tok):
                if tok[0] == 'e':
                    sem = esem[tok[1]]
                    val = sched.ops[tok[1]][tok[2]].sigval
                    key = tok[1]
                else:
                    sem = tok[1].sem
                    val = tok[2]
                    key = id(tok[1])
                if seen.get(key, 0) >= val:
                    return
                seen[key] = val
                engobj.wait_ge(sem, val)

            for o in sched.ops[e]:
                for d in sorted(o.deps, key=lambda t: (t[0], t[1] if t[0] == 'e' else t[1].name, t[2])):
                    wait(d)
                ins = o.fn(engobj)
                if o.dkey is not None:
                    ins.then_inc(o.dkey.sem, 16)
                elif o.signal:
                    ins.then_inc(esem[e], 1)
            for d in extra:
                wait(d)

        @block.tensor
        def _(eng):
            run('pe', eng)

        @block.scalar
        def _(eng):
            run('act', eng)

        @block.vector
        def _(eng):
            run('dve', eng)

        @block.gpsimd
        def _(eng):
            run('pool', eng)

        @block.sync
        def _(eng):
            run('sp', eng, extra=final_waits)


def build_program(L=L_FULL):
    nc = bass.Bass("TRN2", target_bir_lowering=False)
    S = Sched()

    def din(name, shape):
        return nc.dram_tensor(name, list(shape), F32, kind="ExternalInput").ap()

    def dout(name, shape):
        return nc.dram_tensor(name, list(shape), F32, kind="ExternalOutput").ap()

    xT_d = din("xT", [128, 8, T])
    cT_d = din("cT", [128, 8, 2])
    wada_d = din("wada", [L_FULL, 48, 128, 8, 128])
    bada_d = din("bada", [128, L_FULL, 48])
    gmix_d = din("gmix", [128, L_FULL, 8])
    gffn_d = din("gffn", [128, L_FULL, 8])
    gq_d = din("gq", [128, L_FULL, 3])
    gkv_d = din("gkv", [128, L_FULL, 2])
    gsgu_d = din("gsgu", [L_FULL, 128, 256])
    pscale_d = din("pscale", [128, L_FULL, 2])
    gfinal_d = din("gfinal", [128, 8])
    winA_d = din("winA", [L_FULL, 128, 8, 384])
    winB_d = din("winB", [L_FULL, 128, 8, 1152])
    wuq_d = din("wuq", [L_FULL, 128, 3, 1024])
    wukv_d = din("wukv", [L_FULL, 128, 2, 1024])
    wout_d = din("wout", [L_FULL, 128, 8, 1024])
    wpool_d = din("wpool", [L_FULL, 4, 64, 64])
    wsT_d = din("wsT", [L_FULL, 128, 4, 128])
    bsT_d = din("bsT", [L_FULL, 128, 2, 128])
    w1_d = din("w1", [L_FULL, NJ, 128, 8, 512])
    w2_d = din("w2", [L_FULL, NJ, 128, 4, 1024])
    cckv_d = din("cckv", [128, L_FULL, 2, 256])
    ckr_d = din("ckr", [64, L_FULL, 256])
    cos_d = din("cosT", [64, TS])
    sin_d = din("sinT", [64, TS])
    invw_d = din("invw", [128, 2])
    corrL_d = din("corrL", [128, 2, 8])
    corrR_d = din("corrR", [128, 2, 8])
    yT_d = dout("yT", [128, 8, T])
    ckvst_d = dout("ckvst", [L_FULL, 128, 2, 512])
    krst_d = dout("krst", [L_FULL, 64, 512])

    base0 = (nc.sbuf_base + 63) // 64 * 64
    top = nc.sbuf_top
    cur = [base0]
    ncnt = [0]

    def alloc(shape, dt, name=None):
        size = int(np.prod(shape[1:])) * (4 if dt == F32 else 2)
        size = (size + 31) // 32 * 32
        off = cur[0]
        cur[0] += size
        assert cur[0] <= top, ("SBUF overflow", name, cur[0] - top)
        ncnt[0] += 1
        t_ = nc.alloc_sbuf_tensor_at("%s_%d" % (name or "t", ncnt[0]), list(shape), dt, offset=off)
        offs[id(t_)] = off
        return t_

    offs = {}

    def alias(t_, shape, dt, name):
        ncnt[0] += 1
        return nc.alloc_sbuf_tensor_at("%s_%d" % (name, ncnt[0]), list(shape), dt, offset=offs[id(t_)])

    x = alloc([128, 8, T], F32, "x")
    ones = alloc([128, 128], BF16, "ones")
    gmix = alloc([128, L_FULL, 8], F32, "gmix")
    gffn = alloc([128, L_FULL, 8], F32, "gffn")
    gq = alloc([128, L_FULL, 3], F32, "gq")
    gkv = alloc([128, L_FULL, 2], F32, "gkv")
    pscale = alloc([128, L_FULL, 2], F32, "pscale")
    gfinal = alloc([128, 8], F32, "gfinal")
    bada = alloc([128, L_FULL, 48], F32, "bada")
    cT = alloc([128, 8, 2], F32, "cT")
    sT = alloc([128, 8, 2], BF16, "sT")
    invw = alloc([128, 2], F32, "invw")
    corrL = alloc([128, 2, 8], F32, "corrL")
    corrR = alloc([128, 2, 8], F32, "corrR")
    modp = [alloc([128, 48, 2], F32, "mod") for _ in range(2)]
    a1p = [alloc([128, 8, 2], F32, "a1") for _ in range(2)]
    a2p = [alloc([128, 8, 2], F32, "a2") for _ in range(2)]
    scr_base = cur[0]

    ps = [nc.alloc_psum_tensor("ps%d" % i, [128, 512], F32) for i in range(8)]
    psb = [Buf("ps%d" % i, psum=True) for i in range(8)]

    B = {}

    def bf(name):
        if name not in B:
            B[name] = Buf(name)
        return B[name]

    gbank_list = [0, 1, 2, 3]
    gctr = [0]

    def gb():
        i = gbank_list[gctr[0] % len(gbank_list)]
        gctr[0] += 1
        return ps[i], psb[i]

    def mm(out, lhsT, rhs, start, stop, reads, writes, skip=False):
        if skip:
            S.op('pe', lambda e: e.matmul(out, lhsT=lhsT, rhs=rhs, start=start, stop=stop, skip_group_check=True),
                 reads, writes)
        else:
            S.op('pe', lambda e: e.matmul(out, lhsT=lhsT, rhs=rhs, start=start, stop=stop), reads, writes)

    def act(out, in_, func, reads, writes, scale=1.0, bias=0.0):
        S.op('act', lambda e: e.activation(out=out, in_=in_, func=func, scale=scale, bias=bias), reads, writes)

    def tt(eng, out, in0, in1, op, reads, writes):
        S.op(eng, lambda e: e.tensor_tensor(out=out, in0=in0, in1=in1, op=op), reads, writes)

    def stt(out, in0, scalar, in1, op0, op1, reads, writes):
        S.op('dve', lambda e: e.scalar_tensor_tensor(out=out, in0=in0, scalar=scalar, in1=in1, op0=op0, op1=op1),
             reads, writes)

    def ts(eng, out, in0, s1, s2, op0, op1, reads, writes):
        if s2 is None:
            S.op(eng, lambda e: e.tensor_scalar(out=out, in0=in0, scalar1=s1, scalar2=None, op0=op0), reads, writes)
        else:
            S.op(eng, lambda e: e.tensor_scalar(out=out, in0=in0, scalar1=s1, scalar2=s2, op0=op0, op1=op1),
                 reads, writes)

    def cp(eng, out, in_, reads, writes):
        if eng == 'act':
            S.op('act', lambda e: e.activation(out=out, in_=in_, func=AF.Copy), reads, writes)
        else:
            S.op(eng, lambda e: e.tensor_copy(out=out, in_=in_), reads, writes)

    def memset(eng, ap, val, writes):
        S.op(eng, lambda e: e.memset(ap, val), (), writes)

    def ld(q, out, in_, key, writes=None):
        S.dma(q, lambda e: e.dma_start(out=out, in_=in_), key, (), writes if writes is not None else [key])

    def rstd_from_ps(psum_ap, n_feat, lnv, rstd, rd, wr_ln, wr_rs):
        act(lnv, psum_ap, AF.Ln, rd, [wr_ln], scale=1.0 / n_feat, bias=EPS)
        act(rstd, lnv, AF.Exp, [wr_ln], [wr_rs], scale=-0.5)

    bx = bf("x")
    for t in range(5):
        S.dma('sp', lambda e, t=t: e.dma_start(out=x[:, :, t * 512:(t + 1) * 512], in_=xT_d[:, :, t * 512:(t + 1) * 512]),
              bx, (), [bx])
    bconst = bf("const")
    for dst, src in [(gmix, gmix_d), (gffn, gffn_d), (gq, gq_d), (gkv, gkv_d), (pscale, pscale_d), (gfinal, gfinal_d),
                     (bada, bada_d), (cT, cT_d), (invw, invw_d), (corrL, corrL_d), (corrR, corrR_d)]:
        S.dma('sp', lambda e, dst=dst, src=src: e.dma_start(out=dst[:], in_=src), bconst, (), [bconst])
    bones = bf("ones")
    memset('dve', ones[:], 1.0, [bones])
    bsT_ = bf("sT")
    act(sT[:], cT[:], AF.Silu, [bconst], [bsT_])

    out_toks = []

    def tokrange(s):
        return s * SUB, (s + 1) * SUB

    for l in range(L):
        mod, a1, a2 = modp[l % 2], a1p[l % 2], a2p[l % 2]
        bmod = bf("mod%d" % (l % 2))

        def emit_mod(ll, ring, ringb, m0, m1):
            pm, pmb = ps[7], psb[7]
            for m in range(m0, m1):
                r, rb = ring[m % 4], ringb[m % 4]
                ld('pool', r[:], wada_d[ll, m], rb)
                for k in range(8):
                    mm(pm[:, 2 * m:2 * m + 2], r[:, k, :], sT[:, k, :], k == 0, k == 7, [rb, bsT_], [pmb])

        def finish_mod(ll):
            pm, pmb = ps[7], psb[7]
            md, aa1, aa2 = modp[ll % 2], a1p[ll % 2], a2p[ll % 2]
            bm = bf("mod%d" % (ll % 2))
            pmv = pm[:, 0:96].rearrange("p (m s) -> p m s", s=2)
            for s_ in range(2):
                tt('dve', md[:, :, s_], pmv[:, :, s_], bada[:, ll, :], ALU.add, [pmb, bconst], [bm])
            for s_ in range(2):
                stt(aa1[:, :, s_], md[:, 8:16, s_], 1.0, gmix[:, ll, :], ALU.add, ALU.mult, [bm, bconst], [bm])
                stt(aa2[:, :, s_], md[:, 32:40, s_], 1.0, gffn[:, ll, :], ALU.add, ALU.mult, [bm, bconst], [bm])

        if l == 0:
            cur[0] = scr_base
            ring0 = [alloc([128, 8, 128], BF16, "adar") for _ in range(4)]
            ringb0 = [bf("adar%d" % i) for i in range(4)]
            emit_mod(0, ring0, ringb0, 0, 48)
            finish_mod(0)
            S.barrier()

        def mtype(s):
            return 0 if s < 8 else 1

        def norm_tile(tok0, n, avec, bvec_off, mt, h, hb, tag, sq, tmpx, lnv, rstd, bsq=None, tmpx_extra=()):
            if bsq is None:
                bsq = bf(tag + "sq")
            bln, brs = bf(tag + "ln"), bf(tag + "rs")
            if lnv is rstd:
                bln = brs
            act(sq[:, :, 0:n], x[:, :, tok0:tok0 + n], AF.Square, [bx], [bsq])
            pb, pbb = gb()
            for k in range(8):
                mm(pb[:, 0:n], ones[:], sq[:, k, 0:n], k == 0, k == 7, [bones, bsq], [pbb])
            rstd_from_ps(pb[:, 0:n], float(D), lnv[:, 0:n], rstd[:, 0:n], [pbb], bln, brs)
            for c in range(8):
                tb = bf(tag + "tmpx%d" % (c % 2))
                tx = tmpx[c % 2]
                tt('dve', tx[:, 0:n], x[:, c, tok0:tok0 + n], rstd[:, 0:n], ALU.mult, [bx, brs], [tb] + list(tmpx_extra))
                sc_ap = avec[:, c, mt:mt + 1]
                bi_ap = mod[:, bvec_off + c, mt:mt + 1]
                o_ap, i_ap = h[:, c, 0:n], tx[:, 0:n]
                S.op('act', lambda e, sc_ap=sc_ap, bi_ap=bi_ap, o_ap=o_ap, i_ap=i_ap: e.activation(
                    out=o_ap, in_=i_ap, func=AF.Identity, scale=sc_ap, bias=bi_ap), [tb, bmod], [hb])

        def alloc_kv(nkey, with_hph):
            KV = {}
            KV['knT'] = alloc([128, 4, nkey], BF16, "knT")
            KV['krT'] = alloc([128, nkey], BF16, "krT")
            KV['V'] = alloc([128, nkey // 128, 512], BF16, "V")
            if with_hph:
                KV['hph'] = alloc([128, 2, 8, 16], F32, "hph")
            return KV

        def alloc_A(with_hp):
            TA = {}
            TA['wA'] = alloc([128, 8, 384], BF16, "wA")
            if with_hp:
                TA['whp'] = alloc([128, 8, 256], BF16, "whp")
            TA['wkv'] = alloc([128, 2, 1024], BF16, "wkv")
            TA['sq'] = alloc([128, 8, SUB], BF16, "sqA")
            TA['tmpx'] = [alloc([128, SUB], F32, "tmpxA") for _ in range(2)]
            TA['lnv'] = alloc([128, SUB], F32, "lnvA")
            TA['rstd'] = alloc([128, SUB], F32, "rstdA")
            TA['h1'] = alloc([128, 8, SUB], BF16, "h1A")
            TA['sqk'] = alloc([128, 2, SUB], BF16, "sqk")
            TA['lnk'] = alloc([128, SUB], F32, "lnk")
            TA['rsk'] = alloc([128, SUB], F32, "rsk")
            TA['ckvn'] = [alloc([128, 2, SUB], BF16, "ckvn") for _ in range(2)]
            if with_hp:
                TA['kt1'] = alloc([64, SUB], F32, "kt1")
                TA['kt2'] = alloc([64, SUB], F32, "kt2")
                TA['rope'] = [alloc([64, 2, SUB], F32, "ropeA") for _ in range(2)]
            else:
                TA['ckvst'] = alloc([128, 2, SUB], F32, "ckvst")
                TA['krst'] = alloc([64, SUB], F32, "krst")
            return TA

        bknT, bkrT, bV, bhph = bf("knT"), bf("krT"), bf("V"), bf("hph")
        bwA, bwhp, bwkv = bf("wA"), bf("whp"), bf("wkv")

        def load_A(TA, with_hp):
            ld('pool', TA['wA'][:], winA_d[l], bwA)
            ld('pool', TA['wkv'][:], wukv_d[l], bwkv)
            if with_hp:
                ld('pool', TA['whp'][:], winB_d[l, :, :, 0:256], bwhp)

        def kv_proj(KV, TA, cb, cbb, key0):
            wkv = TA['wkv']
            for hp_ in range(2):
                pb, pbb = gb()
                for hh in range(2):
                    h_ = 2 * hp_ + hh
                    for k in range(2):
                        mm(pb[:, hh * 256:(hh + 1) * 256], wkv[:, k, h_ * 128:(h_ + 1) * 128], cb[:, k, :],
                           k == 0, k == 1, [bwkv, cbb], [pbb])
                cp('act', KV['knT'][:, 2 * hp_:2 * hp_ + 2, key0:key0 + 256],
                   pb[:, 0:512].rearrange("p (h n) -> p h n", h=2), [pbb], [bknT])
            for tb_ in range(2):
                pb, pbb = gb()
                for k in range(2):
                    mm(pb[:, 0:512], cb[:, k, tb_ * 128:(tb_ + 1) * 128], wkv[:, k, 512:1024], k == 0, k == 1,
                       [bwkv, cbb], [pbb])
                cp('dve', KV['V'][:, key0 // 128 + tb_, :], pb[:, 0:512], [pbb], [bV])

        def phaseA_sub(s, KV, TA):
            tok0, tok1 = tokrange(s)
            mt = mtype(s)
            sample = s < 8
            key0 = 256 + tok0 if sample else (s - 8) * 256
            wA, h1A = TA['wA'], TA['h1']
            bh1 = bf("h1A")
            norm_tile(tok0, SUB, a1, 0, mt, h1A, bh1, "nA", TA['sq'], TA['tmpx'], TA['lnv'], TA['rstd'])
            pc, pcb = gb()
            for mc in range(2):
                for k in range(8):
                    mm(pc[:, mc * 256:(mc + 1) * 256], wA[:, k, mc * 128:(mc + 1) * 128], h1A[:, k, :], k == 0, k == 7,
                       [bwA, bh1], [pcb])
            pk, pkb = gb()
            nk = 2 if sample else 1
            for q_ in range(nk):
                for k in range(8):
                    mm(pk[0:64, q_ * 256:(q_ + 1) * 256], wA[:, k, 256 + q_ * 64:256 + (q_ + 1) * 64], h1A[:, k, :],
                       k == 0, k == 7, [bwA, bh1], [pkb])
            if sample:
                whp = TA['whp']
                ph, phb = gb()
                for mc in range(2):
                    for side in range(2):
                        c0 = 0 if side == 0 else SUB - 8
                        o0 = mc * 16 + side * 8
                        for k in range(8):
                            mm(ph[:, o0:o0 + 8], whp[:, k, mc * 128:(mc + 1) * 128], h1A[:, k, c0:c0 + 8], k == 0, k == 7,
                               [bwhp, bh1], [phb])
                cp('dve', KV['hph'][:, :, s, :], ph[:, 0:32].rearrange("p (m n) -> p m n", m=2), [phb], [bhph])
            pcv = pc[:, 0:512].rearrange("p (m n) -> p m n", m=2)
            sqk, lnk, rsk = TA['sqk'], TA['lnk'], TA['rsk']
            bsqk, blnk, brsk = bf("sqk"), bf("lnk"), bf("rsk")
            act(sqk[:], pcv, AF.Square, [pcb], [bsqk])
            pq, pqb = gb()
            for k in range(2):
                mm(pq[:, 0:256], ones[:], sqk[:, k, :], k == 0, k == 1, [bones, bsqk], [pqb])
            rstd_from_ps(pq[:, 0:256], 256.0, lnk[:], rsk[:], [pqb], blnk, brsk)
            cn, cnb = TA['ckvn'][s % 2], bf("ckvn%d" % (s % 2))
            if sample:
                for mc in range(2):
                    stt(cn[:, mc, :], pcv[:, mc, :], gkv[:, l, mc:mc + 1], rsk[:], ALU.mult, ALU.mult,
                        [pcb, brsk, bconst], [cnb])
            else:
                ckvst = TA['ckvst']
                bst = bf("ckvst")
                for mc in range(2):
                    stt(ckvst[:, mc, :], pcv[:, mc, :], gkv[:, l, mc:mc + 1], rsk[:], ALU.mult, ALU.mult,
                        [pcb, brsk, bconst], [bst])
                cp('act', cn[:], ckvst[:], [bst], [cnb])
                p0 = (s - 8) * 256
                out_toks.append(S.dma('sp', lambda e, p0=p0, l=l, ckvst=ckvst: e.dma_start(out=ckvst_d[l, :, :, p0:p0 + 256], in_=ckvst[:]),
                                      bst, [bst], []))
            krT = KV['krT']
            if sample:
                rp, rpb = TA['rope'][s % 2], bf("ropeA%d" % (s % 2))
                S.dma('sp', lambda e, rp=rp, tok0=tok0: e.dma_start(out=rp[:, 0, :], in_=cos_d[:, tok0:tok0 + SUB]), rpb, (), [rpb])
                S.dma('sp', lambda e, rp=rp, tok0=tok0: e.dma_start(out=rp[:, 1, :], in_=sin_d[:, tok0:tok0 + SUB]), rpb, (), [rpb])
                kt1, kt2 = TA['kt1'], TA['kt2']
                bk1, bk2 = bf("kt1"), bf("kt2")
                tt('dve', kt1[:], pk[0:64, 0:256], rp[:, 0, :], ALU.mult, [pkb, rpb], [bk1])
                tt('dve', kt2[:], pk[0:64, 256:512], rp[:, 1, :], ALU.mult, [pkb, rpb], [bk2])
                tt('dve', krT[0:64, key0:key0 + 256], kt1[:], kt2[:], ALU.add, [bk1, bk2], [bkrT])
            else:
                krst = TA['krst']
                bks = bf("krst")
                cp('dve', krst[:], pk[0:64, 0:256], [pkb], [bks])
                cp('act', krT[0:64, key0:key0 + 256], krst[:], [bks], [bkrT])
                p0 = (s - 8) * 256
                out_toks.append(S.dma('sp', lambda e, p0=p0, l=l, krst=krst: e.dma_start(out=krst_d[l, :, p0:p0 + 256], in_=krst[:]),
                                      bks, [bks], []))
            kv_proj(KV, TA, cn, cnb, key0)

        cur[0] = scr_base
        KVs = alloc_kv(NKEY, True)
        kv_end = cur[0]
        NA = 512
        wA = alloc([128, 8, 384], BF16, "wA")
        whp = alloc([128, 8, 256], BF16, "whp")
        wkvS = alloc([128, 2, 1024], BF16, "wkv")
        sqS = alloc([128, 8, NA], BF16, "sqS")
        tmpxS = [alloc([128, NA], F32, "tmpxS") for _ in range(2)]
        rstdS = alloc([128, NA], F32, "rstdS")
        h1S = [alloc([128, 8, NA], BF16, "h1S") for _ in range(2)]
        sqkS = alloc([128, 2, NA], BF16, "sqkS")
        rskS = alloc([128, NA], F32, "rskS")
        ckvnS = [alloc([128, 2, NA], BF16, "ckvnS") for _ in range(2)]
        kt1S = alloc([64, NA], F32, "kt1S")
        kt2S = alloc([64, NA], F32, "kt2S")
        ropeS = [alloc([64, 2, NA], F32, "ropeS") for _ in range(2)]
        ckvc = alloc([128, 2, 256], BF16, "ckvc")
        ckvstS = alloc([128, 2, NA], F32, "ckvstS")
        krstS = alloc([64, NA], F32, "krstS")
        TAs = {'wkv': wkvS}
        ld('pool', wA[:], winA_d[l], bwA)
        ld('pool', wkvS[:], wukv_d[l], bwkv)
        ld('pool', whp[:], winB_d[l, :, :, 0:256], bwhp)
        bckvc = bf("ckvc")
        ld('pool', ckvc[:], cckv_d[:, l], bckvc)
        if PADZERO:
            memset('dve', KVs['krT'][64:128, :], 0.0, [bkrT])
        ld('pool', KVs['krT'][0:64, 0:256], ckr_d[:, l, :], bf("krTc"), writes=[bkrT])
        gbank_list[:] = [0, 1, 2, 3, 4, 5, 6, 7]
        kv_proj(KVs, TAs, ckvc, bckvc, 0)

        def sa_norm(t):
            norm_tile(t * NA, NA, a1, 0, (0 if t < 4 else 1), h1S[t % 2], bf("h1S%d" % (t % 2)), "nS", sqS, tmpxS, rstdS, rstdS)

        def sa_proj(t):
            h1, bh1 = h1S[t % 2], bf("h1S%d" % (t % 2))
            st = {}
            pcs = []
            for mc in range(2):
                pc, pcb = gb()
                for k in range(8):
                    mm(pc[:, 0:NA], wA[:, k, mc * 128:(mc + 1) * 128], h1[:, k, :], k == 0, k == 7, [bwA, bh1], [pcb])
                pcs.append((pc, pcb))
            pks = []
            for q_ in range(2 if t < 4 else 1):
                pk, pkb = gb()
                for k in range(8):
                    mm(pk[0:64, 0:NA], wA[:, k, 256 + q_ * 64:256 + (q_ + 1) * 64], h1[:, k, :], k == 0, k == 7,
                       [bwA, bh1], [pkb])
                pks.append((pk, pkb))
            if t == 4:
                return pcs, pks
            ph, phb = gb()
            for sub in range(2):
                for mc in range(2):
                    for side in range(2):
                        c0 = sub * 256 + (0 if side == 0 else SUB - 8)
                        o0 = sub * 32 + mc * 16 + side * 8
                        for k in range(8):
                            mm(ph[:, o0:o0 + 8], whp[:, k, mc * 128:(mc + 1) * 128], h1[:, k, c0:c0 + 8], k == 0, k == 7,
                               [bwhp, bh1], [phb])
            for sub in range(2):
                cp('dve', KVs['hph'][:, :, 2 * t + sub, :],
                   ph[:, sub * 32:(sub + 1) * 32].rearrange("p (m n) -> p m n", m=2), [phb], [bhph])
            return pcs, pks

        def sa_chain(t, pcs, pks):
            tok0 = t * NA
            key0 = 256 + tok0
            bsqk, brsk = bf("sqkS"), bf("rskS")
            for mc in range(2):
                act(sqkS[:, mc, :], pcs[mc][0][:, 0:NA], AF.Square, [pcs[mc][1]], [bsqk])
            pq, pqb = gb()
            for k in range(2):
                mm(pq[:, 0:NA], ones[:], sqkS[:, k, :], k == 0, k == 1, [bones, bsqk], [pqb])
            rstd_from_ps(pq[:, 0:NA], 256.0, rskS[:], rskS[:], [pqb], brsk, brsk)
            cn, cnb = ckvnS[t % 2], bf("ckvnS%d" % (t % 2))
            if t < 4:
                for mc in range(2):
                    stt(cn[:, mc, :], pcs[mc][0][:, 0:NA], gkv[:, l, mc:mc + 1], rskS[:], ALU.mult, ALU.mult,
                        [pcs[mc][1], brsk, bconst], [cnb])
                rp, rpb = ropeS[t % 2], bf("ropeS%d" % (t % 2))
                S.dma('sp', lambda e, rp=rp, tok0=tok0: e.dma_start(out=rp[:, 0, :], in_=cos_d[:, tok0:tok0 + NA]), rpb, (), [rpb])
                S.dma('sp', lambda e, rp=rp, tok0=tok0: e.dma_start(out=rp[:, 1, :], in_=sin_d[:, tok0:tok0 + NA]), rpb, (), [rpb])
                bk1, bk2 = bf("kt1S"), bf("kt2S")
                tt('dve', kt1S[:], pks[0][0][0:64, 0:NA], rp[:, 0, :], ALU.mult, [pks[0][1], rpb], [bk1])
                tt('dve', kt2S[:], pks[1][0][0:64, 0:NA], rp[:, 1, :], ALU.mult, [pks[1][1], rpb], [bk2])
                tt('dve', KVs['krT'][0:64, key0:key0 + NA], kt1S[:], kt2S[:], ALU.add, [bk1, bk2], [bkrT])
            else:
                bst, bks = bf("ckvstS"), bf("krstS")
                for mc in range(2):
                    stt(ckvstS[:, mc, :], pcs[mc][0][:, 0:NA], gkv[:, l, mc:mc + 1], rskS[:], ALU.mult, ALU.mult,
                        [pcs[mc][1], brsk, bconst], [bst])
                cp('act', cn[:], ckvstS[:], [bst], [cnb])
                out_toks.append(S.dma('sp', lambda e, l=l, ckvstS=ckvstS: e.dma_start(out=ckvst_d[l], in_=ckvstS[:]), bst, [bst], []))
                cp('dve', krstS[:], pks[0][0][0:64, 0:NA], [pks[0][1]], [bks])
                cp('act', KVs['krT'][0:64, key0:key0 + NA], krstS[:], [bks], [bkrT])
                out_toks.append(S.dma('sp', lambda e, l=l, krstS=krstS: e.dma_start(out=krst_d[l], in_=krstS[:]), bks, [bks], []))
            for h_ in range(4):
                pb, pbb = gb()
                for k in range(2):
                    mm(pb[:, 0:NA], wkvS[:, k, h_ * 128:(h_ + 1) * 128], cn[:, k, :], k == 0, k == 1, [bwkv, cnb], [pbb])
                cp('act', KVs['knT'][:, h_, key0:key0 + NA], pb[:, 0:NA], [pbb], [bknT])
            for tb_ in range(4):
                pb, pbb = gb()
                for k in range(2):
                    mm(pb[:, 0:512], cn[:, k, tb_ * 128:(tb_ + 1) * 128], wkvS[:, k, 512:1024], k == 0, k == 1,
                       [bwkv, cnb], [pbb])
                cp('dve', KVs['V'][:, key0 // 128 + tb_, :], pb[:, 0:512], [pbb], [bV])

        sa_norm(0)
        for t in range(5):
            pcs, pks = sa_proj(t)
            if t + 1 < 5:
                sa_norm(t + 1)
            sa_chain(t, pcs, pks)
        gbank_list[:] = [0, 1, 2, 3]
        S.barrier()

        cur[0] = kv_end
        wB = alloc([128, 8, 1152], BF16, "wB")
        wq = alloc([128, 3, 1024], BF16, "wq")
        wo = alloc([128, 8, 1024], BF16, "wo")
        wpl = alloc([128, 2, 128], BF16, "wpl")
        wsT = alloc([128, 4, 128], BF16, "wsT")
        bsT = alloc([128, 2, 128], BF16 if BST_BF16 else F32, "bsT")
        gsg = alloc([128, 256], BF16, "gsg")
        yT = alloc([128, 8, SUB], BF16, "yT")
        sqB = None
        wmn = alloc([128, 2, SUB], F32, "wmn")
        tmpxB = [wmn[:, 0, :], wmn[:, 1, :]]
        rstdB = alloc([128, SUB], F32, "rstdB")
        lnvB = rstdB
        lnq, rsq = lnvB, rstdB
        h1B = alloc([128, 8, SUB], BF16, "h1B")
        P0 = alloc([128, 2, SUB + 16], F32, "P0")
        sqq = alias(P0, [128, 3, SUB], BF16, "sqq")
        S2 = alloc([128, 2, SUB + 16], F32, "S2")
        S4 = alloc([128, 2, SUB + 16], F32, "S4")
        pooled = alloc([128, 2, SUB], BF16, "pooled")
        cqg = alloc([128, 3, SUB], BF16, "cqg")
        qnT = alloc([128, 4, SUB], BF16, "qnT")
        qrT = alloc([128, 4, SUB], BF16, "qrT")
        ropeB = alloc([64, 2, SUB], F32, "ropeB")
        qt1f = alloc([128, SUB], F32, "qt1")
        qt1 = qt1f[0:64, :]
        qt2 = alloc([64, SUB], F32, "qt2")
        ug = pooled
        vss = alloc([128, 2], F32, "vss")
        vln = alloc([128, 2], F32, "vln")
        vrs = alloc([128, 2], F32, "vrs")
        vgA = alloc([128, 2, 256], BF16, "vgA")
        vgB = alloc([128, 2, 256], BF16, "vgB")
        gt = S2[:, :, 0:SUB]
        vt = S4[:, :, 0:SUB]
        PT = [alloc([128, 512], BF16, "PT") for _ in range(2)]
        denr = qt1f
        NPT = len(PT)

        bwB, bwq, bwo, bwsm = bf("wB"), bf("wq"), bf("wo"), bf("wsm")
        ld('pool', wB[:], winB_d[l], bwB)
        ld('pool', wq[:], wuq_d[l], bwq)
        memset('dve', wpl[:], 0.0, [bwsm])
        for g in range(4):
            c_, hf = g // 2, g % 2
            S.dma('pool', lambda e, g=g, c_=c_, hf=hf, l=l, wpl=wpl: e.dma_start(out=wpl[hf * 64:(hf + 1) * 64, c_, hf * 64:(hf + 1) * 64],
                                                                  in_=wpool_d[l, g]), bwsm, (), [bwsm])
        ld('pool', wsT[:], wsT_d[l], bwsm)
        if BST_BF16:
            S.dma('pool', lambda e, l=l, bsT=bsT: e.dma_start(out=bsT[:], in_=bsT_d[l]), bwsm, (), [bwsm])
        else:
            S.dma('sp', lambda e, l=l, bsT=bsT: e.dma_start(out=bsT[:], in_=bsT_d[l]), bf("gsgk"), (), [bf("gsgk"), bwsm])
        S.dma('pool', lambda e, l=l, gsg=gsg: e.dma_start(out=gsg[:], in_=gsgu_d[l]), bf("gsgk"), (), [bf("gsgk")])
        ld('pool', wo[:], wout_d[l], bwo)
        bvgA, bvgB = bf("vgA"), bf("vgB")
        memset('dve', vgA[:], 0.0, [bvgA])
        memset('dve', vgB[:], 0.0, [bvgB])
        if PADZERO:
            memset('dve', qrT[64:128, :, :], 0.0, [bf("qrT")])

        def normB(s):
            tok0, tok1 = tokrange(s)
            norm_tile(tok0, SUB, a1, 0, mtype(s), h1B, bf('h1B'), 'nB', h1B, tmpxB, lnvB, rstdB, bsq=bf('h1B'),
                      tmpx_extra=[bf('wmn')])

        def phaseB_sub(s, KV, prenormed=False, next_s=None):
            tok0, tok1 = tokrange(s)
            mt = mtype(s)
            sample = s < 8
            knT, krT, V = KV['knT'], KV['krT'], KV['V']
            bh1 = bf("h1B")
            byT = bf("yT")
            bwm = bf("wmn")
            blnq, brsq = bf("nBrs"), bf("nBrs")
            if not prenormed:
                normB(s)
            tbx = [bf("nBtmpx0"), bf("nBtmpx1")]
            pcq0, pcq0b = gb()
            for mc in range(2):
                for k in range(8):
                    mm(pcq0[:, mc * 256:(mc + 1) * 256], wB[:, k, 256 + mc * 128:256 + (mc + 1) * 128], h1B[:, k, :],
                       k == 0, k == 7, [bwB, bh1], [pcq0b])
            pcq1, pcq1b = gb()
            for k in range(8):
                mm(pcq1[:, 0:256], wB[:, k, 512:640], h1B[:, k, :], k == 0, k == 7, [bwB, bh1], [pcq1b])
            bsqq, bcqg = bf("P0"), bf("cqg")
            act(sqq[:, 0:2, :], pcq0[:, 0:512].rearrange("p (m n) -> p m n", m=2), AF.Square, [pcq0b], [bsqq])
            act(sqq[:, 2, :], pcq1[:, 0:256], AF.Square, [pcq1b], [bsqq])
            for mc in range(3):
                src = pcq0[:, mc * 256:(mc + 1) * 256] if mc < 2 else pcq1[:, 0:256]
                ts('dve', cqg[:, mc, :], src, gq[:, l, mc:mc + 1], None, ALU.mult, None,
                   [pcq0b if mc < 2 else pcq1b, bconst], [bcqg])
            pss, pssb = gb()
            for k in range(3):
                mm(pss[:, 0:256], ones[:], sqq[:, k, :], k == 0, k == 2, [bones, bsqq], [pssb])
            rstd_from_ps(pss[:, 0:256], 384.0, lnq[:], rsq[:], [pssb], blnq, brsq)
            yield 'cq'
            bqn, bqr = bf("qnT"), bf("qrT")
            for hp_ in range(2):
                pq, pqb = gb()
                for hh in range(2):
                    h_ = 2 * hp_ + hh
                    for k in range(3):
                        mm(pq[:, hh * 256:(hh + 1) * 256], wq[:, k, h_ * 128:(h_ + 1) * 128], cqg[:, k, :], k == 0, k == 2,
                           [bwq, bcqg], [pqb])
                tt('dve', qnT[:, 2 * hp_:2 * hp_ + 2, :], pq[:, 0:512].rearrange("p (h n) -> p h n", h=2),
                   rsq[:].unsqueeze(1).broadcast_to([128, 2, 256]), ALU.mult, [pqb, brsq], [bqn])
            brp2 = bf("ropeB2")
            if sample:
                brp = bf("ropeB")
                S.dma('sp', lambda e, tok0=tok0, ropeB=ropeB: e.dma_start(out=ropeB[:, 0, :], in_=cos_d[:, tok0:tok0 + SUB]), brp, (), [brp, brp2])
                S.dma('sp', lambda e, tok0=tok0, ropeB=ropeB: e.dma_start(out=ropeB[:, 1, :], in_=sin_d[:, tok0:tok0 + SUB]), brp, (), [brp, brp2])
                tt('dve', ropeB[:, 0, :], ropeB[:, 0, :], rsq[0:64, :], ALU.mult, [brp, brsq], [brp2])
                tt('dve', ropeB[:, 1, :], ropeB[:, 1, :], rsq[0:64, :], ALU.mult, [brp, brsq, brp2], [brp2])
            for h_ in range(4):
                pq, pqb = gb()
                nq = 2 if sample else 1
                for q_ in range(nq):
                    c0 = 512 + q_ * 256 + h_ * 64
                    for k in range(3):
                        mm(pq[0:64, q_ * 256:(q_ + 1) * 256], wq[:, k, c0:c0 + 64], cqg[:, k, :], k == 0, k == 2,
                           [bwq, bcqg], [pqb])
                if sample:
                    bq1, bq2 = bf("qt1"), bf("qt2")
                    tt('dve', qt1[:], pq[0:64, 0:256], ropeB[:, 0, :], ALU.mult, [pqb, brp2], [bq1])
                    tt('dve', qt2[:], pq[0:64, 256:512], ropeB[:, 1, :], ALU.mult, [pqb, brp2], [bq2])
                    tt('dve', qrT[0:64, h_, :], qt1[:], qt2[:], ALU.add, [bq1, bq2], [bqr])
                else:
                    tt('dve', qrT[0:64, h_, :], pq[0:64, 0:256], rsq[0:64, :], ALU.mult, [pqb, brsq], [bqr])

            yield 'q'

            def rest_front():
                php, phpb = gb()
                for mc in range(2):
                    for k in range(8):
                        mm(php[:, mc * 256:(mc + 1) * 256], wB[:, k, mc * 128:(mc + 1) * 128], h1B[:, k, :], k == 0, k == 7,
                           [bwB, bh1], [phpb])
                bP0, bS2, bS4, bpl = bf("P0"), bf("S2"), bf("S4"), bf("pooled")
                cp('dve', P0[:, :, 8:8 + SUB], php[:, 0:512].rearrange("p (m n) -> p m n", m=2), [phpb], [bP0])
                if sample and s > 0:
                    cp('dve', P0[:, :, 0:8], KV['hph'][:, :, s - 1, 8:16], [bhph], [bP0])
                else:
                    memset('dve', P0[:, :, 0:8], 0.0, [bP0])
                if sample and s < 7:
                    cp('dve', P0[:, :, 8 + SUB:16 + SUB], KV['hph'][:, :, s + 1, 0:8], [bhph], [bP0])
                else:
                    memset('dve', P0[:, :, 8 + SUB:16 + SUB], 0.0, [bP0])
                W_ = SUB + 16
                wmw = [bwm] + tbx
                tt('dve', S2[:, :, 1:W_], P0[:, :, 1:W_], P0[:, :, 0:W_ - 1], ALU.add, [bP0], [bS2])
                tt('dve', S4[:, :, 3:W_], S2[:, :, 3:W_], S2[:, :, 1:W_ - 2], ALU.add, [bS2], [bS4])
                ts('dve', wmn[0:64, 0, :], S2[0:64, 0, 8:8 + SUB], invw[0:64, 0:1], None, ALU.mult, None, [bS2, bconst], wmw)
                ts('dve', wmn[64:128, 0, :], S4[64:128, 0, 9:9 + SUB], invw[64:128, 0:1], None, ALU.mult, None, [bS4, bconst], wmw)
                bS8 = bf("S8")
                tt('dve', S2[:, 1, 7:W_], S4[:, 1, 7:W_], S4[:, 1, 3:W_ - 4], ALU.add, [bS4, bwm, bS2], [bS8, bS2])
                ts('dve', wmn[0:64, 1, :], S2[0:64, 1, 11:11 + SUB], invw[0:64, 1:2], None, ALU.mult, None, [bS8, bconst], wmw)
                bS16 = bf("S16")
                tt('dve', S4[64:128, 1, 15:W_], S2[64:128, 1, 15:W_], S2[64:128, 1, 7:W_ - 8], ALU.add, [bS8, bwm, bS4], [bS16, bS4])
                ts('dve', wmn[64:128, 1, :], S4[64:128, 1, 15:15 + SUB], invw[64:128, 1:2], None, ALU.mult, None,
                   [bS16, bconst], wmw)
                seq_start = (s == 0) or not sample
                seq_end = (s == 7) or not sample
                if seq_start:
                    tt('dve', wmn[:, :, 0:8], wmn[:, :, 0:8], corrL[:], ALU.mult, [bwm, bconst], wmw)
                if seq_end:
                    tt('dve', wmn[:, :, SUB - 8:SUB], wmn[:, :, SUB - 8:SUB], corrR[:], ALU.mult, [bwm, bconst], wmw)
                tt('dve', pooled[:], wmn[:], P0[:, :, 8:8 + SUB], ALU.subtract, [bwm, bP0], [bpl])
                ppl, pplb = gb()
                for mc in range(2):
                    mm(ppl[:, mc * 256:(mc + 1) * 256], wpl[:, mc, :], pooled[:, mc, :], True, True, [bwsm, bpl], [pplb])
                for mc in range(2):
                    ts('dve', yT[:, mc, :], ppl[:, mc * 256:(mc + 1) * 256], pscale[:, l, mc:mc + 1], None, ALU.mult, None,
                       [pplb, bconst], [byT])
                pu, pub = gb()
                for mc in range(2):
                    for k in range(8):
                        mm(pu[:, mc * 256:(mc + 1) * 256], wB[:, k, 640 + mc * 128:640 + (mc + 1) * 128], h1B[:, k, :],
                           k == 0, k == 7, [bwB, bh1], [pub])
                bug = bf("pooled")
                act(ug[:], pu[:, 0:512].rearrange("p (m n) -> p m n", m=2), AF.Gelu_apprx_tanh, [pub], [bug])
                pv, pvb = gb()
                for blk in range(2):
                    for k in range(8):
                        mm(pv[:, blk * 256:(blk + 1) * 256], h1B[:, k, blk * 128:(blk + 1) * 128], wB[:, k, 896:1152],
                           k == 0, k == 7, [bwB, bh1], [pvb])
                bvt, bvss, bvrs, bgt = bf("S4"), bf("vss"), bf("vrs"), bf("S2")
                act(vt[:], pv[:, 0:512].rearrange("p (m n) -> p m n", m=2), AF.Gelu_apprx_tanh, [pvb], [bvt, bf("S16")])
                for blk in range(2):
                    S.op('act', lambda e, blk=blk, gt=gt, vt=vt, vss=vss: e.activation(out=gt[:, 0, :], in_=vt[:, blk, :], func=AF.Square,
                                                                accum_out=vss[:, blk:blk + 1]), [bvt], [bvss, bgt, bf("S8")])
                act(vln[:], vss[:], AF.Ln, [bvss], [bf("vln")], scale=1.0 / 256.0, bias=EPS)
                act(vrs[:], vln[:], AF.Exp, [bf("vln")], [bvrs], scale=-0.5)
                for blk in range(2):
                    for hf, (vg, vgb) in enumerate([(vgA, bvgA), (vgB, bvgB)]):
                        o_ = vg[:, blk, :].rearrange("p (m h c) -> p m h c", m=2, h=2)[:, :, hf, :]
                        i0 = vt[:, blk, :].rearrange("p (m h c) -> p m h c", m=2, h=2)[:, :, hf, :]
                        i1 = gsg[:].rearrange("p (m h c) -> p m h c", m=2, h=2)[:, :, hf, :]
                        stt(o_, i0, vrs[:, blk:blk + 1], i1, ALU.mult, ALU.mult, [bvt, bvrs, bf("gsgk")], [vgb])
                pg, pgb = gb()
                for mc in range(2):
                    for blk in range(2):
                        o_ = pg[:, mc * 256 + blk * 128: mc * 256 + (blk + 1) * 128]
                        mm(o_, vgA[:, blk, mc * 128:(mc + 1) * 128], wsT[:, 2 * mc, :], True, False, [bvgA, bwsm], [pgb])
                        mm(o_, vgB[:, blk, mc * 128:(mc + 1) * 128], wsT[:, 2 * mc + 1, :], False, True, [bvgB, bwsm], [pgb])
                for mc in range(2):
                    tt('dve', gt[:, mc, :].rearrange("p (b n) -> p b n", b=2),
                       pg[:, mc * 256:(mc + 1) * 256].rearrange("p (b n) -> p b n", b=2),
                       bsT[:, mc, :].unsqueeze(1).broadcast_to([128, 2, 128]), ALU.add, [pgb, bwsm], [bgt, bf("S8")])
                tt('dve', yT[:, 6:8, :], gt[:], ug[:], ALU.mult, [bgt, bug], [byT])

            if sample:
                kbase, nkc = 0, 18
            else:
                kbase, nkc = 2304 + (s - 8) * 256, 2
            npair = nkc // 2
            pairs = [(h_, jp) for h_ in range(4) for jp in range(npair)]
            st_ = {}

            def emit_scores(i):
                h_, jp = pairs[i]
                psc, pscb = (ps[7], psb[7]) if i % 2 == 0 else gb()
                for sub in range(2):
                    j = 2 * jp + sub
                    k0 = kbase + j * 128
                    mm(psc[:, sub * 256:(sub + 1) * 256], knT[:, h_, k0:k0 + 128], qnT[:, h_, :], True, False,
                       [bknT, bqn], [pscb])
                    if ROPE128:
                        mm(psc[:, sub * 256:(sub + 1) * 256], krT[:, k0:k0 + 128], qrT[:, h_, :], False, True,
                           [bkrT, bqr], [pscb])
                    else:
                        mm(psc[:, sub * 256:(sub + 1) * 256], krT[0:64, k0:k0 + 128], qrT[0:64, h_, :], False, True,
                           [bkrT, bqr], [pscb])
                pt, ptb = PT[i % NPT], bf("PT%d" % (i % NPT))
                act(pt[:], psc[:, 0:512], AF.Exp, [pscb], [ptb], scale=SCALE)
                st_[i] = (pt, ptb)

            def emit_pv(i):
                h_, jp = pairs[i]
                pt, ptb = st_.pop(i)
                po, pob = ps[4 + (h_ % 2)], psb[4 + (h_ % 2)]
                for sub in range(2):
                    j = 2 * jp + sub
                    kc = (kbase // 128) + j
                    mm(po[:, 0:256], V[:, kc, h_ * 128:(h_ + 1) * 128], pt[:, sub * 256:(sub + 1) * 256],
                       j == 0, j == nkc - 1, [bV, ptb], [pob], skip=True)
                    mm(po[:, 256:512], ones[:], pt[:, sub * 256:(sub + 1) * 256], False, j == nkc - 1,
                       [bones, ptb], [pob], skip=True)
                if jp == npair - 1:
                    bdr = bf("qt1")
                    S.op('dve', lambda e, po=po, denr=denr: e.reciprocal(out=denr[:], in_=po[:, 256:512]), [pob], [bdr])
                    tt('dve', yT[:, 2 + h_, :], po[:, 0:256], denr[:], ALU.mult, [pob, bdr], [byT])

            rest_front()
            emit_scores(0)
            for i in range(len(pairs)):
                if i + 1 < len(pairs):
                    emit_scores(i + 1)
                emit_pv(i)
                if i == npair - 1 and next_s is not None:
                    normB(next_s)
            yield 'att'
            for mp in range(4):
                pw, pwb = gb()
                for mh in range(2):
                    m_ = 2 * mp + mh
                    for k in range(8):
                        mm(pw[:, mh * 256:(mh + 1) * 256], wo[:, k, m_ * 128:(m_ + 1) * 128], yT[:, k, :], k == 0, k == 7,
                           [bwo, byT], [pwb])
                for mh in range(2):
                    m_ = 2 * mp + mh
                    stt(x[:, m_, tok0:tok1], pw[:, mh * 256:(mh + 1) * 256], mod[:, 16 + m_, mt:mt + 1], x[:, m_, tok0:tok1],
                        ALU.mult, ALU.add, [pwb, bmod, bx], [bx])

        gbank_list[:] = [0, 1, 2, 3, 6]
        gens = {s: phaseB_sub(s, KVs, prenormed=(s > 0), next_s=(s + 1 if s < NSUB - 1 else None)) for s in range(NSUB)}
        next(gens[0])
        next(gens[0])
        for s in range(NSUB):
            next(gens[s])
            if s + 1 < NSUB:
                next(gens[s + 1])
            for _ in gens[s]:
                pass
            if s + 1 < NSUB:
                next(gens[s + 1])
        gbank_list[:] = [0, 1, 2, 3]
        S.barrier()

        cur[0] = scr_base
        h2 = alloc([128, 8, T], BF16, "h2")
        wslot = [(alloc([128, 8, 512], BF16, "w1s"), alloc([128, 4, 1024], BF16, "w2s")) for _ in range(2)]
        sqF = alloc([128, 8, 512], BF16, "sqF")
        tmpxF = [alloc([128, 512], F32, "tmpxF") for _ in range(2)]
        lnvF = alloc([128, 512], F32, "lnvF")
        rstdF = alloc([128, 512], F32, "rstdF")
        rl = [alloc([128, 512], F32, "rl") for _ in range(2)]
        hid = [alloc([128, 4, 512], BF16, "hid") for _ in range(2)]
        ybuf = alloc([128, 8, 512], F32, "ybuf")
        bh2 = [bf("h2_%d" % t) for t in range(5)]

        def load_w(j):
            w1s, w2s = wslot[j % 2]
            kb1, kb2 = bf("w1s%d" % (j % 2)), bf("w2s%d" % (j % 2))
            ld('pool', w1s[:], w1_d[l, j], kb1)
            ld('pool', w2s[:], w2_d[l, j], kb2)

        gbank_list[:] = [0, 1, 2, 3, 4, 5, 6]
        load_w(0)
        nxt = l + 1 < L
        if nxt:
            ringF = [alloc([128, 8, 128], BF16, "adarF") for _ in range(4)]
            ringFb = [bf("adarF%d" % i) for i in range(4)]

        def norm2(t):
            mt = 0 if t < 4 else 1
            norm_tile(t * 512, 512, a2, 24, mt, h2[:, :, t * 512:(t + 1) * 512], bh2[t], "nF", sqF, tmpxF, lnvF, rstdF)

        def ff1(i, j, t):
            w1s, kb1 = wslot[j % 2][0], bf("w1s%d" % (j % 2))
            hd, hdb = hid[i % 2], bf("hid%d" % (i % 2))
            for m_ in range(4):
                pb, pbb = gb()
                for k in range(8):
                    mm(pb[:, 0:512], w1s[:, k, m_ * 128:(m_ + 1) * 128], h2[:, k, t * 512:(t + 1) * 512], k == 0, k == 7,
                       [kb1, bh2[t]], [pbb])
                r_, rb_ = rl[m_ % 2], bf("rl%d" % (m_ % 2))
                act(r_[:], pb[:, 0:512], AF.Relu, [pbb], [rb_])
                tt('pool', hd[:, m_, :], r_[:], r_[:], ALU.mult, [rb_], [hdb])

        def ff2(i, j, t):
            w2s, kb2 = wslot[j % 2][1], bf("w2s%d" % (j % 2))
            hd, hdb = hid[i % 2], bf("hid%d" % (i % 2))
            mt = 0 if t < 4 else 1
            for m_ in range(8):
                pb, pbb = gb()
                for k in range(4):
                    mm(pb[:, 0:512], w2s[:, k, m_ * 128:(m_ + 1) * 128], hd[:, k, :], k == 0, k == 3, [kb2, hdb], [pbb])
                stt(x[:, m_, t * 512:(t + 1) * 512], pb[:, 0:512], mod[:, 40 + m_, mt:mt + 1],
                    x[:, m_, t * 512:(t + 1) * 512], ALU.mult, ALU.add, [pbb, bmod, bx], [bx])

        steps = [(j, t) for j in range(NJ) for t in range(5)]
        load_w(1)
        norm2(0)
        norm2(1)
        ff1(0, 0, 0)
        for i, (j, t) in enumerate(steps):
            if i + 1 < len(steps):
                ff1(i + 1, *steps[i + 1])
            ff2(i, j, t)
            if j == 0 and t + 2 < 5:
                norm2(t + 2)
            if t == 4:
                if j + 2 < NJ:
                    load_w(j + 2)
                if nxt:
                    emit_mod(l + 1, ringF, ringFb, j * 6, (j + 1) * 6)
        if nxt:
            finish_mod(l + 1)
        if l == L - 1:
            for t in range(5):
                bsq, bln, brs, byb = bf("fsq"), bf("fln"), bf("frs"), bf("ybuf")
                act(sqF[:], x[:, :, t * 512:(t + 1) * 512], AF.Square, [bx], [bsq])
                pb, pbb = gb()
                for k in range(8):
                    mm(pb[:, 0:512], ones[:], sqF[:, k, :], k == 0, k == 7, [bones, bsq], [pbb])
                rstd_from_ps(pb[:, 0:512], float(D), lnvF[:], rstdF[:], [pbb], bln, brs)
                for c in range(8):
                    stt(ybuf[:, c, :], x[:, c, t * 512:(t + 1) * 512], gfinal[:, c:c + 1], rstdF[:], ALU.mult, ALU.mult,
                        [bx, brs, bconst], [byb])
                out_toks.append(S.dma('sp', lambda e, t=t, ybuf=ybuf: e.dma_start(out=yT_d[:, :, t * 512:(t + 1) * 512], in_=ybuf[:]),
                                      byb, [byb], []))
        gbank_list[:] = [0, 1, 2, 3]
        S.barrier()

    with ExitStack() as st:
        with nc.Block() as block:
            S.emit(nc, block, st, final_waits=out_toks)
    return nc


def _pc(a, nchunk):
    return np.ascontiguousarray(a.reshape((nchunk, 128) + a.shape[1:]).swapaxes(0, 1))


def _vecs(g, nchunk):
    Ln = g.shape[0]
    return np.ascontiguousarray(g.reshape(Ln, nchunk, 128).transpose(2, 0, 1))


def _rope_tables():
    rows = TS // 64
    row = np.repeat(np.arange(rows, dtype=np.float32), 64)
    col = np.tile(np.arange(64, dtype=np.float32), rows)
    inv = (1.0 / (10000.0 ** (np.arange(16, dtype=np.float32) / 16))).astype(np.float32)
    ang = np.concatenate([row[:, None] * inv, col[:, None] * inv], axis=-1).astype(np.float32)
    cos, sin = np.cos(ang).astype(np.float32), np.sin(ang).astype(np.float32)
    cosT = np.concatenate([cos, cos], axis=1).T
    sinT = np.concatenate([-sin, sin], axis=1).T
    return np.ascontiguousarray(cosT), np.ascontiguousarray(sinT)


def _pool_tables():
    wins = (2, 4, 8, 16)
    invw = np.zeros((128, 2), np.float32)
    corrL = np.ones((128, 2, 8), np.float32)
    corrR = np.ones((128, 2, 8), np.float32)
    Ls = 256
    for g, w in enumerate(wins):
        c, hf = g // 2, g % 2
        sl = slice(hf * 64, (hf + 1) * 64)
        invw[sl, c] = 1.0 / w
        for i in range(8):
            t = i
            cnt = min(t + (w - w // 2), Ls) - max(t - w // 2, 0)
            corrL[sl, c, i] = w / cnt
            t = Ls - 8 + i
            cnt = min(t + (w - w // 2), Ls) - max(t - w // 2, 0)
            corrR[sl, c, i] = w / cnt
    return invw, corrL, corrR


def prepare_inputs(x_prompt, x_sample, cache_ckv, cache_krope, c, c_ctx, w_ada, b_ada, g_mix,
                   w_in, w_pool, pool_scale, g_q, w_uq, g_kv, w_ukv, g_sgu, w_s, b_s, w_out,
                   g_ffn, w_ff1, w_ff2, g_final):
    f = lambda a: np.asarray(a, dtype=np.float32)
    x_prompt, x_sample, cache_ckv, cache_krope, c, c_ctx = map(f, (x_prompt, x_sample, cache_ckv, cache_krope, c, c_ctx))
    w_ada, b_ada, g_mix, w_in, w_pool, pool_scale, g_q, w_uq, g_kv, w_ukv = map(
        f, (w_ada, b_ada, g_mix, w_in, w_pool, pool_scale, g_q, w_uq, g_kv, w_ukv))
    g_sgu, w_s, b_s, w_out, g_ffn, w_ff1, w_ff2, g_final = map(f, (g_sgu, w_s, b_s, w_out, g_ffn, w_ff1, w_ff2, g_final))
    Ln = w_in.shape[0]
    OFF_Q, OFF_KV, OFF_R, OFF_G = 256, 640, 896, 960
    shared = {}
    shared["wada"] = np.ascontiguousarray(w_ada.reshape(Ln, 8, 128, 48, 128).transpose(0, 3, 2, 1, 4))
    shared["bada"] = _vecs(b_ada, 48)
    shared["gmix"] = _vecs(g_mix, 8)
    shared["gffn"] = _vecs(g_ffn, 8)
    shared["gq"] = _vecs(g_q, 3)
    shared["gkv"] = _vecs(g_kv, 2)
    shared["gsgu"] = np.ascontiguousarray(np.broadcast_to(g_sgu[:, None, :], (Ln, 128, 256)))
    shared["pscale"] = _vecs(pool_scale, 2)
    shared["gfinal"] = np.ascontiguousarray(g_final.reshape(8, 128).T)
    kr = w_in[:, :, OFF_R:OFF_G]
    kr_swp = np.concatenate([kr[:, :, 32:64], kr[:, :, 0:32]], axis=2)
    winA = np.concatenate([w_in[:, :, OFF_KV:OFF_R], kr, kr_swp], axis=2)
    shared["winA"] = np.ascontiguousarray(winA.reshape(Ln, 8, 128, 384).transpose(0, 2, 1, 3))
    winB = np.concatenate([w_in[:, :, 0:OFF_Q], w_in[:, :, OFF_Q:OFF_KV], w_in[:, :, OFF_G:OFF_G + 512]], axis=2)
    shared["winB"] = np.ascontiguousarray(winB.reshape(Ln, 8, 128, 1152).transpose(0, 2, 1, 3))
    wq4 = w_uq.reshape(Ln, 384, 4, 192)
    qn = wq4[:, :, :, 0:128].reshape(Ln, 384, 512)
    qr = wq4[:, :, :, 128:192]
    qr_swp = np.concatenate([qr[..., 32:64], qr[..., 0:32]], axis=-1)
    wuq = np.concatenate([qn, qr.reshape(Ln, 384, 256), qr_swp.reshape(Ln, 384, 256)], axis=2)
    shared["wuq"] = np.ascontiguousarray(wuq.reshape(Ln, 3, 128, 1024).transpose(0, 2, 1, 3))
    wkv4 = w_ukv.reshape(Ln, 256, 4, 256)
    wukv = np.concatenate([wkv4[..., 0:128].reshape(Ln, 256, 512), wkv4[..., 128:256].reshape(Ln, 256, 512)], axis=2)
    shared["wukv"] = np.ascontiguousarray(wukv.reshape(Ln, 2, 128, 1024).transpose(0, 2, 1, 3))
    shared["wout"] = np.ascontiguousarray(w_out.reshape(Ln, 8, 128, 1024).transpose(0, 2, 1, 3))
    shared["wpool"] = np.ascontiguousarray(w_pool)
    shared["wsT"] = np.ascontiguousarray(w_s.transpose(0, 3, 1, 2))
    bsT = np.broadcast_to(b_s.reshape(Ln, 2, 2, 1, 128), (Ln, 2, 2, 64, 128))
    shared["bsT"] = np.ascontiguousarray(bsT.reshape(Ln, 2, 128, 128).transpose(0, 2, 1, 3))
    shared["w1"] = np.ascontiguousarray(w_ff1.reshape(Ln, 8, 128, NJ, 512).transpose(0, 3, 2, 1, 4))
    shared["w2"] = np.ascontiguousarray(w_ff2.reshape(Ln, NJ, 4, 128, 1024).transpose(0, 1, 3, 2, 4))
    cosT, sinT = _rope_tables()
    shared["cosT"], shared["sinT"] = cosT, sinT
    shared["invw"], shared["corrL"], shared["corrR"] = _pool_tables()
    in_maps = []
    for i in range(8):
        m = dict(shared)
        xs = np.concatenate([x_sample[i], x_prompt[2 * i], x_prompt[2 * i + 1]], axis=0)
        m["xT"] = _pc(np.ascontiguousarray(xs.T), 8)
        cc = np.stack([c[i], c_ctx], axis=1)
        m["cT"] = _pc(cc, 8)
        ck = cache_ckv[i].transpose(2, 0, 1)
        m["cckv"] = np.ascontiguousarray(ck.reshape(2, 128, Ln, 256).transpose(1, 2, 0, 3))
        m["ckr"] = np.ascontiguousarray(cache_krope[i].transpose(2, 0, 1))
        in_maps.append(m)
    return in_maps


_NC_CACHE = {}


def kernel(**inputs):
    in_maps = prepare_inputs(**inputs)
    if "nc" not in _NC_CACHE:
        _NC_CACHE["nc"] = build_program(L_FULL)
    nc = _NC_CACHE["nc"]
    res = run_bass_kernel_spmd(nc, in_maps, core_ids=list(range(8)))
    y_prompt = np.zeros((16, 256, D), np.float32)
    y_sample = np.zeros((8, TS, D), np.float32)
    state_ckv = np.zeros((16, L_FULL, 256, 256), np.float32)
    state_krope = np.zeros((16, L_FULL, 256, 64), np.float32)
    for i, r in enumerate(res.results):
        yT = np.asarray(r["yT"])
        y = yT.transpose(2, 1, 0).reshape(T, D)
        y_sample[i] = y[0:TS]
        y_prompt[2 * i] = y[TS:TS + 256]
        y_prompt[2 * i + 1] = y[TS + 256:TS + 512]
        ck = np.asarray(r["ckvst"])
        ck = ck.transpose(0, 3, 2, 1).reshape(L_FULL, 512, 256)
        kr = np.asarray(r["krst"]).transpose(0, 2, 1)
        for b in range(2):
            state_ckv[2 * i + b] = ck[:, b * 256:(b + 1) * 256, :]
            state_krope[2 * i + b] = kr[:, b * 256:(b + 1) * 256, :]
    return (y_prompt, y_sample, state_ckv, state_krope)
```

```python
import math
from contextlib import ExitStack

import numpy as np
import concourse.bass as bass
import concourse.mybir as mybir
from concourse.bass_utils import run_bass_kernel_spmd

F32, BF16 = mybir.dt.float32, mybir.dt.bfloat16
AF = mybir.ActivationFunctionType
ALU = mybir.AluOpType

D = 1024
L_FULL = 4
T = 2560
TS = 2048
NSUB = 10
SUB = 256
NKEY = 2816
EPS = 1e-6
SCALE = 1.0 / math.sqrt(192.0)
NJ = 8
REORDER = True
PADZERO = True
ROPE128 = True
BST_BF16 = True
ENGS = ['pe', 'act', 'dve', 'pool', 'sp']


class Buf:
    __slots__ = ('name', 'w', 'r', 'dcount', 'sem', 'psum')

    def __init__(self, name, psum=False):
        self.name = name
        self.psum = psum
        self.w = None
        self.r = []
        self.dcount = 0
        self.sem = None


class Op:
    __slots__ = ('eng', 'fn', 'deps', 'idx', 'signal', 'sigval', 'dkey', 'dval')

    def __init__(self, eng, fn, deps, idx, dkey=None, dval=0):
        self.eng = eng
        self.fn = fn
        self.deps = deps
        self.idx = idx
        self.signal = False
        self.sigval = 0
        self.dkey = dkey
        self.dval = dval


class Sched:
    def __init__(self):
        self.ops = {e: [] for e in ENGS}
        self.dkeys = []
        self.bar_deps = set()
        self.bar_gen = 0
        self.eng_gen = {e: 0 for e in ENGS}

    def _deps(self, eng, reads, writes, is_dma):
        deps = set()
        for b in reads:
            if b.w is not None:
                deps.add(b.w)
            if b.psum:
                deps.update(t for t in b.r if not (t[0] == 'e' and t[1] == eng))
        for b in writes:
            if b.w is not None and not (is_dma and b.w[0] == 'd'):
                deps.add(b.w)
            deps.update(b.r)
        if self.eng_gen[eng] != self.bar_gen:
            deps |= self.bar_deps
            self.eng_gen[eng] = self.bar_gen
        return deps

    def op(self, eng, fn, reads=(), writes=()):
        deps = self._deps(eng, reads, writes, False)
        o = Op(eng, fn, deps, len(self.ops[eng]))
        self.ops[eng].append(o)
        tok = ('e', eng, o.idx)
        if eng == 'pe':
            o.deps = {d for d in deps if not (d[0] == 'e' and d[1] == 'pe')}
        for b in reads:
            self._add_reader(b, tok)
        for b in writes:
            b.w = tok
            b.r = []
        return tok

    @staticmethod
    def _add_reader(b, tok):
        b.r = [t for t in b.r if not (t[0] == tok[0] and t[1] == tok[1])]
        b.r.append(tok)

    def dma(self, eng, fn, key, reads=(), writes=()):
        deps = self._deps(eng, reads, writes, True)
        if key.sem is None:
            key.sem = True
            self.dkeys.append(key)
        key.dcount += 16
        o = Op(eng, fn, deps, len(self.ops[eng]), dkey=key, dval=key.dcount)
        self.ops[eng].append(o)
        tok = ('d', key, key.dcount)
        for b in reads:
            self._add_reader(b, tok)
        for b in writes:
            b.w = tok
            b.r = []
        return tok

    def barrier(self):
        deps = set()
        for e in ENGS:
            for o in reversed(self.ops[e]):
                if o.dkey is None:
                    deps.add(('e', e, o.idx))
                    break
        for k in self.dkeys:
            if k.dcount:
                deps.add(('d', k, k.dcount))
        self.bar_deps = deps
        self.bar_gen += 1

    def emit(self, nc, block, stack, final_waits=()):
        for e in ENGS:
            for o in self.ops[e]:
                for d in o.deps:
                    if d[0] == 'e':
                        self.ops[d[1]][d[2]].signal = True
        for d in final_waits:
            if d[0] == 'e':
                self.ops[d[1]][d[2]].signal = True
        esem = {}
        for e in ENGS:
            n = 0
            for o in self.ops[e]:
                if o.signal:
                    n += 1
                    o.sigval = n
            if n:
                esem[e] = stack.enter_context(nc.semaphore('s_' + e))
        for k in self.dkeys:
            k.sem = stack.enter_context(nc.semaphore('d_' + k.name))
        sched = self

        def run(e, engobj, extra=()):
            seen = {}

            def wait(tok):
                if tok[0] == 'e':
                    sem = esem[tok[1]]
                    val = sched.ops[tok[1]][tok[2]].sigval
                    key = tok[1]
                else:
                    sem = tok[1].sem
                    val = tok[2]
                    key = id(tok[1])
                if seen.get(key, 0) >= val:
                    return
                seen[key] = val
                engobj.wait_ge(sem, val)

            for o in sched.ops[e]:
                for d in sorted(o.deps, key=lambda t: (t[0], t[1] if t[0] == 'e' else t[1].name, t[2])):
                    wait(d)
                ins = o.fn(engobj)
                if o.dkey is not None:
                    ins.then_inc(o.dkey.sem, 16)
                elif o.signal:
                    ins.then_inc(esem[e], 1)
            for d in extra:
                wait(d)

        @block.tensor
        def _(eng):
            run('pe', eng)

        @block.scalar
        def _(eng):
            run('act', eng)

        @block.vector
        def _(eng):
            run('dve', eng)

        @block.gpsimd
        def _(eng):
            run('pool', eng)

        @block.sync
        def _(eng):
            run('sp', eng, extra=final_waits)


def build_program(L=L_FULL):
    nc = bass.Bass("TRN2", target_bir_lowering=False)
    S = Sched()

    def din(name, shape):
        return nc.dram_tensor(name, list(shape), F32, kind="ExternalInput").ap()

    def dout(name, shape):
        return nc.dram_tensor(name, list(shape), F32, kind="ExternalOutput").ap()

    xT_d = din("xT", [128, 8, T])
    cT_d = din("cT", [128, 8, 2])
    wada_d = din("wada", [L_FULL, 48, 128, 8, 128])
    bada_d = din("bada", [128, L_FULL, 48])
    gmix_d = din("gmix", [128, L_FULL, 8])
    gffn_d = din("gffn", [128, L_FULL, 8])
    gq_d = din("gq", [128, L_FULL, 3])
    gkv_d = din("gkv", [128, L_FULL, 2])
    gsgu_d = din("gsgu", [L_FULL, 128, 256])
    pscale_d = din("pscale", [128, L_FULL, 2])
    gfinal_d = din("gfinal", [128, 8])
    winA_d = din("winA", [L_FULL, 128, 8, 384])
    winB_d = din("winB", [L_FULL, 128, 8, 1152])
    wuq_d = din("wuq", [L_FULL, 128, 3, 1024])
    wukv_d = din("wukv", [L_FULL, 128, 2, 1024])
    wout_d = din("wout", [L_FULL, 128, 8, 1024])
    wpool_d = din("wpool", [L_FULL, 4, 64, 64])
    wsT_d = din("wsT", [L_FULL, 128, 4, 128])
    bsT_d = din("bsT", [L_FULL, 128, 2, 128])
    w1_d = din("w1", [L_FULL, NJ, 128, 8, 512])
    w2_d = din("w2", [L_FULL, NJ, 128, 4, 1024])
    cckv_d = din("cckv", [128, L_FULL, 2, 256])
    ckr_d = din("ckr", [64, L_FULL, 256])
    cos_d = din("cosT", [64, TS])
    sin_d = din("sinT", [64, TS])
    invw_d = din("invw", [128, 2])
    corrL_d = din("corrL", [128, 2, 8])
    corrR_d = din("corrR", [128, 2, 8])
    yT_d = dout("yT", [128, 8, T])
    ckvst_d = dout("ckvst", [L_FULL, 128, 2, 512])
    krst_d = dout("krst", [L_FULL, 64, 512])

    base0 = (nc.sbuf_base + 63) // 64 * 64
    top = nc.sbuf_top
    cur = [base0]
    ncnt = [0]

    def alloc(shape, dt, name=None):
        size = int(np.prod(shape[1:])) * (4 if dt == F32 else 2)
        size = (size + 31) // 32 * 32
        off = cur[0]
        cur[0] += size
        assert cur[0] <= top, ("SBUF overflow", name, cur[0] - top)
        ncnt[0] += 1
        t_ = nc.alloc_sbuf_tensor_at("%s_%d" % (name or "t", ncnt[0]), list(shape), dt, offset=off)
        offs[id(t_)] = off
        return t_

    offs = {}

    def alias(t_, shape, dt, name):
        ncnt[0] += 1
        return nc.alloc_sbuf_tensor_at("%s_%d" % (name, ncnt[0]), list(shape), dt, offset=offs[id(t_)])

    x = alloc([128, 8, T], F32, "x")
    ones = alloc([128, 128], BF16, "ones")
    gmix = alloc([128, L_FULL, 8], F32, "gmix")
    gffn = alloc([128, L_FULL, 8], F32, "gffn")
    gq = alloc([128, L_FULL, 3], F32, "gq")
    gkv = alloc([128, L_FULL, 2], F32, "gkv")
    pscale = alloc([128, L_FULL, 2], F32, "pscale")
    gfinal = alloc([128, 8], F32, "gfinal")
    bada = alloc([128, L_FULL, 48], F32, "bada")
    cT = alloc([128, 8, 2], F32, "cT")
    sT = alloc([128, 8, 2], BF16, "sT")
    invw = alloc([128, 2], F32, "invw")
    corrL = alloc([128, 2, 8], F32, "corrL")
    corrR = alloc([128, 2, 8], F32, "corrR")
    modp = [alloc([128, 48, 2], F32, "mod") for _ in range(2)]
    a1p = [alloc([128, 8, 2], F32, "a1") for _ in range(2)]
    a2p = [alloc([128, 8, 2], F32, "a2") for _ in range(2)]
    scr_base = cur[0]

    ps = [nc.alloc_psum_tensor("ps%d" % i, [128, 512], F32) for i in range(8)]
    psb = [Buf("ps%d" % i, psum=True) for i in range(8)]

    B = {}

    def bf(name):
        if name not in B:
            B[name] = Buf(name)
        return B[name]

    gbank_list = [0, 1, 2, 3]
    gctr = [0]

    def gb():
        i = gbank_list[gctr[0] % len(gbank_list)]
        gctr[0] += 1
        return ps[i], psb[i]

    def mm(out, lhsT, rhs, start, stop, reads, writes, skip=False):
        if skip:
            S.op('pe', lambda e: e.matmul(out, lhsT=lhsT, rhs=rhs, start=start, stop=stop, skip_group_check=True),
                 reads, writes)
        else:
            S.op('pe', lambda e: e.matmul(out, lhsT=lhsT, rhs=rhs, start=start, stop=stop), reads, writes)

    def act(out, in_, func, reads, writes, scale=1.0, bias=0.0):
        S.op('act', lambda e: e.activation(out=out, in_=in_, func=func, scale=scale, bias=bias), reads, writes)

    def tt(eng, out, in0, in1, op, reads, writes):
        S.op(eng, lambda e: e.tensor_tensor(out=out, in0=in0, in1=in1, op=op), reads, writes)

    def stt(out, in0, scalar, in1, op0, op1, reads, writes):
        S.op('dve', lambda e: e.scalar_tensor_tensor(out=out, in0=in0, scalar=scalar, in1=in1, op0=op0, op1=op1),
             reads, writes)

    def ts(eng, out, in0, s1, s2, op0, op1, reads, writes):
        if s2 is None:
            S.op(eng, lambda e: e.tensor_scalar(out=out, in0=in0, scalar1=s1, scalar2=None, op0=op0), reads, writes)
        else:
            S.op(eng, lambda e: e.tensor_scalar(out=out, in0=in0, scalar1=s1, scalar2=s2, op0=op0, op1=op1),
                 reads, writes)

    def cp(eng, out, in_, reads, writes):
        if eng == 'act':
            S.op('act', lambda e: e.activation(out=out, in_=in_, func=AF.Copy), reads, writes)
        else:
            S.op(eng, lambda e: e.tensor_copy(out=out, in_=in_), reads, writes)

    def memset(eng, ap, val, writes):
        S.op(eng, lambda e: e.memset(ap, val), (), writes)

    def ld(q, out, in_, key, writes=None):
        S.dma(q, lambda e: e.dma_start(out=out, in_=in_), key, (), writes if writes is not None else [key])

    def rstd_from_ps(psum_ap, n_feat, lnv, rstd, rd, wr_ln, wr_rs):
        act(lnv, psum_ap, AF.Ln, rd, [wr_ln], scale=1.0 / n_feat, bias=EPS)
        act(rstd, lnv, AF.Exp, [wr_ln], [wr_rs], scale=-0.5)

    bx = bf("x")
    for t in range(5):
        S.dma('sp', lambda e, t=t: e.dma_start(out=x[:, :, t * 512:(t + 1) * 512], in_=xT_d[:, :, t * 512:(t + 1) * 512]),
              bx, (), [bx])
    bconst = bf("const")
    for dst, src in [(gmix, gmix_d), (gffn, gffn_d), (gq, gq_d), (gkv, gkv_d), (pscale, pscale_d), (gfinal, gfinal_d),
                     (bada, bada_d), (cT, cT_d), (invw, invw_d), (corrL, corrL_d), (corrR, corrR_d)]:
        S.dma('sp', lambda e, dst=dst, src=src: e.dma_start(out=dst[:], in_=src), bconst, (), [bconst])
    bones = bf("ones")
    memset('dve', ones[:], 1.0, [bones])
    bsT_ = bf("sT")
    act(sT[:], cT[:], AF.Silu, [bconst], [bsT_])

    out_toks = []

    def tokrange(s):
        return s * SUB, (s + 1) * SUB

    for l in range(L):
        mod, a1, a2 = modp[l % 2], a1p[l % 2], a2p[l % 2]
        bmod = bf("mod%d" % (l % 2))

        def emit_mod(ll, ring, ringb, m0, m1):
            pm, pmb = ps[7], psb[7]
            for m in range(m0, m1):
                r, rb = ring[m % 4], ringb[m % 4]
                ld('pool', r[:], wada_d[ll, m], rb)
                for k in range(8):
                    mm(pm[:, 2 * m:2 * m + 2], r[:, k, :], sT[:, k, :], k == 0, k == 7, [rb, bsT_], [pmb])

        def finish_mod(ll):
            pm, pmb = ps[7], psb[7]
            md, aa1, aa2 = modp[ll % 2], a1p[ll % 2], a2p[ll % 2]
            bm = bf("mod%d" % (ll % 2))
            pmv = pm[:, 0:96].rearrange("p (m s) -> p m s", s=2)
            for s_ in range(2):
                tt('dve', md[:, :, s_], pmv[:, :, s_], bada[:, ll, :], ALU.add, [pmb, bconst], [bm])
            for s_ in range(2):
                stt(aa1[:, :, s_], md[:, 8:16, s_], 1.0, gmix[:, ll, :], ALU.add, ALU.mult, [bm, bconst], [bm])
                stt(aa2[:, :, s_], md[:, 32:40, s_], 1.0, gffn[:, ll, :], ALU.add, ALU.mult, [bm, bconst], [bm])

        if l == 0:
            cur[0] = scr_base
            ring0 = [alloc([128, 8, 128], BF16, "adar") for _ in range(4)]
            ringb0 = [bf("adar%d" % i) for i in range(4)]
            emit_mod(0, ring0, ringb0, 0, 48)
            finish_mod(0)
            S.barrier()

        def mtype(s):
            return 0 if s < 8 else 1

        def norm_tile(tok0, n, avec, bvec_off, mt, h, hb, tag, sq, tmpx, lnv, rstd, bsq=None, tmpx_extra=()):
            if bsq is None:
                bsq = bf(tag + "sq")
            bln, brs = bf(tag + "ln"), bf(tag + "rs")
            if lnv is rstd:
                bln = brs
            act(sq[:, :, 0:n], x[:, :, tok0:tok0 + n], AF.Square, [bx], [bsq])
            pb, pbb = gb()
            for k in range(8):
                mm(pb[:, 0:n], ones[:], sq[:, k, 0:n], k == 0, k == 7, [bones, bsq], [pbb])
            rstd_from_ps(pb[:, 0:n], float(D), lnv[:, 0:n], rstd[:, 0:n], [pbb], bln, brs)
            for c in range(8):
                tb = bf(tag + "tmpx%d" % (c % 2))
                tx = tmpx[c % 2]
                tt('dve', tx[:, 0:n], x[:, c, tok0:tok0 + n], rstd[:, 0:n], ALU.mult, [bx, brs], [tb] + list(tmpx_extra))
                sc_ap = avec[:, c, mt:mt + 1]
                bi_ap = mod[:, bvec_off + c, mt:mt + 1]
                o_ap, i_ap = h[:, c, 0:n], tx[:, 0:n]
                S.op('act', lambda e, sc_ap=sc_ap, bi_ap=bi_ap, o_ap=o_ap, i_ap=i_ap: e.activation(
                    out=o_ap, in_=i_ap, func=AF.Identity, scale=sc_ap, bias=bi_ap), [tb, bmod], [hb])

        def alloc_kv(nkey, with_hph):
            KV = {}
            KV['knT'] = alloc([128, 4, nkey], BF16, "knT")
            KV['krT'] = alloc([128, nkey], BF16, "krT")
            KV['V'] = alloc([128, nkey // 128, 512], BF16, "V")
            if with_hph:
                KV['hph'] = alloc([128, 2, 8, 16], F32, "hph")
            return KV

        def alloc_A(with_hp):
            TA = {}
            TA['wA'] = alloc([128, 8, 384], BF16, "wA")
            if with_hp:
                TA['whp'] = alloc([128, 8, 256], BF16, "whp")
            TA['wkv'] = alloc([128, 2, 1024], BF16, "wkv")
            TA['sq'] = alloc([128, 8, SUB], BF16, "sqA")
            TA['tmpx'] = [alloc([128, SUB], F32, "tmpxA") for _ in range(2)]
            TA['lnv'] = alloc([128, SUB], F32, "lnvA")
            TA['rstd'] = alloc([128, SUB], F32, "rstdA")
            TA['h1'] = alloc([128, 8, SUB], BF16, "h1A")
            TA['sqk'] = alloc([128, 2, SUB], BF16, "sqk")
            TA['lnk'] = alloc([128, SUB], F32, "lnk")
            TA['rsk'] = alloc([128, SUB], F32, "rsk")
            TA['ckvn'] = [alloc([128, 2, SUB], BF16, "ckvn") for _ in range(2)]
            if with_hp:
                TA['kt1'] = alloc([64, SUB], F32, "kt1")
                TA['kt2'] = alloc([64, SUB], F32, "kt2")
                TA['rope'] = [alloc([64, 2, SUB], F32, "ropeA") for _ in range(2)]
            else:
                TA['ckvst'] = alloc([128, 2, SUB], F32, "ckvst")
                TA['krst'] = alloc([64, SUB], F32, "krst")
            return TA

        bknT, bkrT, bV, bhph = bf("knT"), bf("krT"), bf("V"), bf("hph")
        bwA, bwhp, bwkv = bf("wA"), bf("whp"), bf("wkv")

        def load_A(TA, with_hp):
            ld('pool', TA['wA'][:], winA_d[l], bwA)
            ld('pool', TA['wkv'][:], wukv_d[l], bwkv)
            if with_hp:
                ld('pool', TA['whp'][:], winB_d[l, :, :, 0:256], bwhp)

        def kv_proj(KV, TA, cb, cbb, key0):
            wkv = TA['wkv']
            for hp_ in range(2):
                pb, pbb = gb()
                for hh in range(2):
                    h_ = 2 * hp_ + hh
                    for k in range(2):
                        mm(pb[:, hh * 256:(hh + 1) * 256], wkv[:, k, h_ * 128:(h_ + 1) * 128], cb[:, k, :],
                           k == 0, k == 1, [bwkv, cbb], [pbb])
                cp('act', KV['knT'][:, 2 * hp_:2 * hp_ + 2, key0:key0 + 256],
                   pb[:, 0:512].rearrange("p (h n) -> p h n", h=2), [pbb], [bknT])
            for tb_ in range(2):
                pb, pbb = gb()
                for k in range(2):
                    mm(pb[:, 0:512], cb[:, k, tb_ * 128:(tb_ + 1) * 128], wkv[:, k, 512:1024], k == 0, k == 1,
                       [bwkv, cbb], [pbb])
                cp('dve', KV['V'][:, key0 // 128 + tb_, :], pb[:, 0:512], [pbb], [bV])

        def phaseA_sub(s, KV, TA):
            tok0, tok1 = tokrange(s)
            mt = mtype(s)
            sample = s < 8
            key0 = 256 + tok0 if sample else (s - 8) * 256
            wA, h1A = TA['wA'], TA['h1']
            bh1 = bf("h1A")
            norm_tile(tok0, SUB, a1, 0, mt, h1A, bh1, "nA", TA['sq'], TA['tmpx'], TA['lnv'], TA['rstd'])
            pc, pcb = gb()
            for mc in range(2):
                for k in range(8):
                    mm(pc[:, mc * 256:(mc + 1) * 256], wA[:, k, mc * 128:(mc + 1) * 128], h1A[:, k, :], k == 0, k == 7,
                       [bwA, bh1], [pcb])
            pk, pkb = gb()
            nk = 2 if sample else 1
            for q_ in range(nk):
                for k in range(8):
                    mm(pk[0:64, q_ * 256:(q_ + 1) * 256], wA[:, k, 256 + q_ * 64:256 + (q_ + 1) * 64], h1A[:, k, :],
                       k == 0, k == 7, [bwA, bh1], [pkb])
            if sample:
                whp = TA['whp']
                ph, phb = gb()
                for mc in range(2):
                    for side in range(2):
                        c0 = 0 if side == 0 else SUB - 8
                        o0 = mc * 16 + side * 8
                        for k in range(8):
                            mm(ph[:, o0:o0 + 8], whp[:, k, mc * 128:(mc + 1) * 128], h1A[:, k, c0:c0 + 8], k == 0, k == 7,
                               [bwhp, bh1], [phb])
                cp('dve', KV['hph'][:, :, s, :], ph[:, 0:32].rearrange("p (m n) -> p m n", m=2), [phb], [bhph])
            pcv = pc[:, 0:512].rearrange("p (m n) -> p m n", m=2)
            sqk, lnk, rsk = TA['sqk'], TA['lnk'], TA['rsk']
            bsqk, blnk, brsk = bf("sqk"), bf("lnk"), bf("rsk")
            act(sqk[:], pcv, AF.Square, [pcb], [bsqk])
            pq, pqb = gb()
            for k in range(2):
                mm(pq[:, 0:256], ones[:], sqk[:, k, :], k == 0, k == 1, [bones, bsqk], [pqb])
            rstd_from_ps(pq[:, 0:256], 256.0, lnk[:], rsk[:], [pqb], blnk, brsk)
            cn, cnb = TA['ckvn'][s % 2], bf("ckvn%d" % (s % 2))
            if sample:
                for mc in range(2):
                    stt(cn[:, mc, :], pcv[:, mc, :], gkv[:, l, mc:mc + 1], rsk[:], ALU.mult, ALU.mult,
                        [pcb, brsk, bconst], [cnb])
            else:
                ckvst = TA['ckvst']
                bst = bf("ckvst")
                for mc in range(2):
                    stt(ckvst[:, mc, :], pcv[:, mc, :], gkv[:, l, mc:mc + 1], rsk[:], ALU.mult, ALU.mult,
                        [pcb, brsk, bconst], [bst])
                cp('act', cn[:], ckvst[:], [bst], [cnb])
                p0 = (s - 8) * 256
                out_toks.append(S.dma('sp', lambda e, p0=p0, l=l, ckvst=ckvst: e.dma_start(out=ckvst_d[l, :, :, p0:p0 + 256], in_=ckvst[:]),
                                      bst, [bst], []))
            krT = KV['krT']
            if sample:
                rp, rpb = TA['rope'][s % 2], bf("ropeA%d" % (s % 2))
                S.dma('sp', lambda e, rp=rp, tok0=tok0: e.dma_start(out=rp[:, 0, :], in_=cos_d[:, tok0:tok0 + SUB]), rpb, (), [rpb])
                S.dma('sp', lambda e, rp=rp, tok0=tok0: e.dma_start(out=rp[:, 1, :], in_=sin_d[:, tok0:tok0 + SUB]), rpb, (), [rpb])
                kt1, kt2 = TA['kt1'], TA['kt2']
                bk1, bk2 = bf("kt1"), bf("kt2")
                tt('dve', kt1[:], pk[0:64, 0:256], rp[:, 0, :], ALU.mult, [pkb, rpb], [bk1])
                tt('dve', kt2[:], pk[0:64, 256:512], rp[:, 1, :], ALU.mult, [pkb, rpb], [bk2])
                tt('dve', krT[0:64, key0:key0 + 256], kt1[:], kt2[:], ALU.add, [bk1, bk2], [bkrT])
            else:
                krst = TA['krst']
                bks = bf("krst")
                cp('dve', krst[:], pk[0:64, 0:256], [pkb], [bks])
                cp('act', krT[0:64, key0:key0 + 256], krst[:], [bks], [bkrT])
                p0 = (s - 8) * 256
                out_toks.append(S.dma('sp', lambda e, p0=p0, l=l, krst=krst: e.dma_start(out=krst_d[l, :, p0:p0 + 256], in_=krst[:]),
                                      bks, [bks], []))
            kv_proj(KV, TA, cn, cnb, key0)

        cur[0] = scr_base
        KVs = alloc_kv(NKEY, True)
        kv_end = cur[0]
        NA = 512
        wA = alloc([128, 8, 384], BF16, "wA")
        whp = alloc([128, 8, 256], BF16, "whp")
        wkvS = alloc([128, 2, 1024], BF16, "wkv")
        sqS = alloc([128, 8, NA], BF16, "sqS")
        tmpxS = [alloc([128, NA], F32, "tmpxS") for _ in range(2)]
        rstdS = alloc([128, NA], F32, "rstdS")
        h1S = [alloc([128, 8, NA], BF16, "h1S") for _ in range(2)]
        sqkS = alloc([128, 2, NA], BF16, "sqkS")
        rskS = alloc([128, NA], F32, "rskS")
        ckvnS = [alloc([128, 2, NA], BF16, "ckvnS") for _ in range(2)]
        kt1S = alloc([64, NA], F32, "kt1S")
        kt2S = alloc([64, NA], F32, "kt2S")
        ropeS = [alloc([64, 2, NA], F32, "ropeS") for _ in range(2)]
        ckvc = alloc([128, 2, 256], BF16, "ckvc")
        ckvstS = alloc([128, 2, NA], F32, "ckvstS")
        krstS = alloc([64, NA], F32, "krstS")
        TAs = {'wkv': wkvS}
        ld('pool', wA[:], winA_d[l], bwA)
        ld('pool', wkvS[:], wukv_d[l], bwkv)
        ld('pool', whp[:], winB_d[l, :, :, 0:256], bwhp)
        bckvc = bf("ckvc")
        ld('pool', ckvc[:], cckv_d[:, l], bckvc)
        if PADZERO:
            memset('dve', KVs['krT'][64:128, :], 0.0, [bkrT])
        ld('pool', KVs['krT'][0:64, 0:256], ckr_d[:, l, :], bf("krTc"), writes=[bkrT])
        gbank_list[:] = [0, 1, 2, 3, 4, 5, 6, 7]
        kv_proj(KVs, TAs, ckvc, bckvc, 0)

        def sa_norm(t):
            norm_tile(t * NA, NA, a1, 0, (0 if t < 4 else 1), h1S[t % 2], bf("h1S%d" % (t % 2)), "nS", sqS, tmpxS, rstdS, rstdS)

        def sa_proj(t):
            h1, bh1 = h1S[t % 2], bf("h1S%d" % (t % 2))
            st = {}
            pcs = []
            for mc in range(2):
                pc, pcb = gb()
                for k in range(8):
                    mm(pc[:, 0:NA], wA[:, k, mc * 128:(mc + 1) * 128], h1[:, k, :], k == 0, k == 7, [bwA, bh1], [pcb])
                pcs.append((pc, pcb))
            pks = []
            for q_ in range(2 if t < 4 else 1):
                pk, pkb = gb()
                for k in range(8):
                    mm(pk[0:64, 0:NA], wA[:, k, 256 + q_ * 64:256 + (q_ + 1) * 64], h1[:, k, :], k == 0, k == 7,
                       [bwA, bh1], [pkb])
                pks.append((pk, pkb))
            if t == 4:
                return pcs, pks
            ph, phb = gb()
            for sub in range(2):
                for mc in range(2):
                    for side in range(2):
                        c0 = sub * 256 + (0 if side == 0 else SUB - 8)
                        o0 = sub * 32 + mc * 16 + side * 8
                        for k in range(8):
                            mm(ph[:, o0:o0 + 8], whp[:, k, mc * 128:(mc + 1) * 128], h1[:, k, c0:c0 + 8], k == 0, k == 7,
                               [bwhp, bh1], [phb])
            for sub in range(2):
                cp('dve', KVs['hph'][:, :, 2 * t + sub, :],
                   ph[:, sub * 32:(sub + 1) * 32].rearrange("p (m n) -> p m n", m=2), [phb], [bhph])
            return pcs, pks

        def sa_chain(t, pcs, pks):
            tok0 = t * NA
            key0 = 256 + tok0
            bsqk, brsk = bf("sqkS"), bf("rskS")
            for mc in range(2):
                act(sqkS[:, mc, :], pcs[mc][0][:, 0:NA], AF.Square, [pcs[mc][1]], [bsqk])
            pq, pqb = gb()
            for k in range(2):
                mm(pq[:, 0:NA], ones[:], sqkS[:, k, :], k == 0, k == 1, [bones, bsqk], [pqb])
            rstd_from_ps(pq[:, 0:NA], 256.0, rskS[:], rskS[:], [pqb], brsk, brsk)
            cn, cnb = ckvnS[t % 2], bf("ckvnS%d" % (t % 2))
            if t < 4:
                for mc in range(2):
                    stt(cn[:, mc, :], pcs[mc][0][:, 0:NA], gkv[:, l, mc:mc + 1], rskS[:], ALU.mult, ALU.mult,
                        [pcs[mc][1], brsk, bconst], [cnb])
                rp, rpb = ropeS[t % 2], bf("ropeS%d" % (t % 2))
                S.dma('sp', lambda e, rp=rp, tok0=tok0: e.dma_start(out=rp[:, 0, :], in_=cos_d[:, tok0:tok0 + NA]), rpb, (), [rpb])
                S.dma('sp', lambda e, rp=rp, tok0=tok0: e.dma_start(out=rp[:, 1, :], in_=sin_d[:, tok0:tok0 + NA]), rpb, (), [rpb])
                bk1, bk2 = bf("kt1S"), bf("kt2S")
                tt('dve', kt1S[:], pks[0][0][0:64, 0:NA], rp[:, 0, :], ALU.mult, [pks[0][1], rpb], [bk1])
                tt('dve', kt2S[:], pks[1][0][0:64, 0:NA], rp[:, 1, :], ALU.mult, [pks[1][1], rpb], [bk2])
                tt('dve', KVs['krT'][0:64, key0:key0 + NA], kt1S[:], kt2S[:], ALU.add, [bk1, bk2], [bkrT])
            else:
                bst, bks = bf("ckvstS"), bf("krstS")
                for mc in range(2):
                    stt(ckvstS[:, mc, :], pcs[mc][0][:, 0:NA], gkv[:, l, mc:mc + 1], rskS[:], ALU.mult, ALU.mult,
                        [pcs[mc][1], brsk, bconst], [bst])
                cp('act', cn[:], ckvstS[:], [bst], [cnb])
                out_toks.append(S.dma('sp', lambda e, l=l, ckvstS=ckvstS: e.dma_start(out=ckvst_d[l], in_=ckvstS[:]), bst, [bst], []))
                cp('dve', krstS[:], pks[0][0][0:64, 0:NA], [pks[0][1]], [bks])
                cp('act', KVs['krT'][0:64, key0:key0 + NA], krstS[:], [bks], [bkrT])
                out_toks.append(S.dma('sp', lambda e, l=l, krstS=krstS: e.dma_start(out=krst_d[l], in_=krstS[:]), bks, [bks], []))
            for h_ in range(4):
                pb, pbb = gb()
                for k in range(2):
                    mm(pb[:, 0:NA], wkvS[:, k, h_ * 128:(h_ + 1) * 128], cn[:, k, :], k == 0, k == 1, [bwkv, cnb], [pbb])
                cp('act', KVs['knT'][:, h_, key0:key0 + NA], pb[:, 0:NA], [pbb], [bknT])
            for tb_ in range(4):
                pb, pbb = gb()
                for k in range(2):
                    mm(pb[:, 0:512], cn[:, k, tb_ * 128:(tb_ + 1) * 128], wkvS[:, k, 512:1024], k == 0, k == 1,
                       [bwkv, cnb], [pbb])
                cp('dve', KVs['V'][:, key0 // 128 + tb_, :], pb[:, 0:512], [pbb], [bV])

        sa_norm(0)
        for t in range(5):
            pcs, pks = sa_proj(t)
            if t + 1 < 5:
                sa_norm(t + 1)
            sa_chain(t, pcs, pks)
        gbank_list[:] = [0, 1, 2, 3]
        S.barrier()

        cur[0] = kv_end
        wB = alloc([128, 8, 1152], BF16, "wB")
        wq = alloc([128, 3, 1024], BF16, "wq")
        wo = alloc([128, 8, 1024], BF16, "wo")
        wpl = alloc([128, 2, 128], BF16, "wpl")
        wsT = alloc([128, 4, 128], BF16, "wsT")
        bsT = alloc([128, 2, 128], BF16 if BST_BF16 else F32, "bsT")
        gsg = alloc([128, 256], BF16, "gsg")
        yT = alloc([128, 8, SUB], BF16, "yT")
        sqB = None
        wmn = alloc([128, 2, SUB], F32, "wmn")
        tmpxB = [wmn[:, 0, :], wmn[:, 1, :]]
        rstdB = alloc([128, SUB], F32, "rstdB")
        lnvB = rstdB
        lnq, rsq = lnvB, rstdB
        h1B = alloc([128, 8, SUB], BF16, "h1B")
        P0 = alloc([128, 2, SUB + 16], F32, "P0")
        sqq = alias(P0, [128, 3, SUB], BF16, "sqq")
        S2 = alloc([128, 2, SUB + 16], F32, "S2")
        S4 = alloc([128, 2, SUB + 16], F32, "S4")
        pooled = alloc([128, 2, SUB], BF16, "pooled")
        cqg = alloc([128, 3, SUB], BF16, "cqg")
        qnT = alloc([128, 4, SUB], BF16, "qnT")
        qrT = alloc([128, 4, SUB], BF16, "qrT")
        ropeB = alloc([64, 2, SUB], F32, "ropeB")
        qt1f = alloc([128, SUB], F32, "qt1")
        qt1 = qt1f[0:64, :]
        qt2 = alloc([64, SUB], F32, "qt2")
        ug = pooled
        vss = alloc([128, 2], F32, "vss")
        vln = alloc([128, 2], F32, "vln")
        vrs = alloc([128, 2], F32, "vrs")
        vgA = alloc([128, 2, 256], BF16, "vgA")
        vgB = alloc([128, 2, 256], BF16, "vgB")
        gt = S2[:, :, 0:SUB]
        vt = S4[:, :, 0:SUB]
        PT = [alloc([128, 512], BF16, "PT") for _ in range(2)]
        denr = qt1f
        NPT = len(PT)

        bwB, bwq, bwo, bwsm = bf("wB"), bf("wq"), bf("wo"), bf("wsm")
        ld('pool', wB[:], winB_d[l], bwB)
        ld('pool', wq[:], wuq_d[l], bwq)
        memset('dve', wpl[:], 0.0, [bwsm])
        for g in range(4):
            c_, hf = g // 2, g % 2
            S.dma('pool', lambda e, g=g, c_=c_, hf=hf, l=l, wpl=wpl: e.dma_start(out=wpl[hf * 64:(hf + 1) * 64, c_, hf * 64:(hf + 1) * 64],
                                                                  in_=wpool_d[l, g]), bwsm, (), [bwsm])
        ld('pool', wsT[:], wsT_d[l], bwsm)
        if BST_BF16:
            S.dma('pool', lambda e, l=l, bsT=bsT: e.dma_start(out=bsT[:], in_=bsT_d[l]), bwsm, (), [bwsm])
        else:
            S.dma('sp', lambda e, l=l, bsT=bsT: e.dma_start(out=bsT[:], in_=bsT_d[l]), bf("gsgk"), (), [bf("gsgk"), bwsm])
        S.dma('pool', lambda e, l=l, gsg=gsg: e.dma_start(out=gsg[:], in_=gsgu_d[l]), bf("gsgk"), (), [bf("gsgk")])
        ld('pool', wo[:], wout_d[l], bwo)
        bvgA, bvgB = bf("vgA"), bf("vgB")
        memset('dve', vgA[:], 0.0, [bvgA])
        memset('dve', vgB[:], 0.0, [bvgB])
        if PADZERO:
            memset('dve', qrT[64:128, :, :], 0.0, [bf("qrT")])

        def normB(s):
            tok0, tok1 = tokrange(s)
            norm_tile(tok0, SUB, a1, 0, mtype(s), h1B, bf('h1B'), 'nB', h1B, tmpxB, lnvB, rstdB, bsq=bf('h1B'),
                      tmpx_extra=[bf('wmn')])

        def phaseB_sub(s, KV, prenormed=False, next_s=None):
            tok0, tok1 = tokrange(s)
            mt = mtype(s)
            sample = s < 8
            knT, krT, V = KV['knT'], KV['krT'], KV['V']
            bh1 = bf("h1B")
            byT = bf("yT")
            bwm = bf("wmn")
            blnq, brsq = bf("nBrs"), bf("nBrs")
            if not prenormed:
                normB(s)
            tbx = [bf("nBtmpx0"), bf("nBtmpx1")]
            pcq0, pcq0b = gb()
            for mc in range(2):
                for k in range(8):
                    mm(pcq0[:, mc * 256:(mc + 1) * 256], wB[:, k, 256 + mc * 128:256 + (mc + 1) * 128], h1B[:, k, :],
                       k == 0, k == 7, [bwB, bh1], [pcq0b])
            pcq1, pcq1b = gb()
            for k in range(8):
                mm(pcq1[:, 0:256], wB[:, k, 512:640], h1B[:, k, :], k == 0, k == 7, [bwB, bh1], [pcq1b])
            bsqq, bcqg = bf("P0"), bf("cqg")
            act(sqq[:, 0:2, :], pcq0[:, 0:512].rearrange("p (m n) -> p m n", m=2), AF.Square, [pcq0b], [bsqq])
            act(sqq[:, 2, :], pcq1[:, 0:256], AF.Square, [pcq1b], [bsqq])
            for mc in range(3):
                src = pcq0[:, mc * 256:(mc + 1) * 256] if mc < 2 else pcq1[:, 0:256]
                ts('dve', cqg[:, mc, :], src, gq[:, l, mc:mc + 1], None, ALU.mult, None,
                   [pcq0b if mc < 2 else pcq1b, bconst], [bcqg])
            pss, pssb = gb()
            for k in range(3):
                mm(pss[:, 0:256], ones[:], sqq[:, k, :], k == 0, k == 2, [bones, bsqq], [pssb])
            rstd_from_ps(pss[:, 0:256], 384.0, lnq[:], rsq[:], [pssb], blnq, brsq)
            yield 'cq'
            bqn, bqr = bf("qnT"), bf("qrT")
            for hp_ in range(2):
                pq, pqb = gb()
                for hh in range(2):
                    h_ = 2 * hp_ + hh
                    for k in range(3):
                        mm(pq[:, hh * 256:(hh + 1) * 256], wq[:, k, h_ * 128:(h_ + 1) * 128], cqg[:, k, :], k == 0, k == 2,
                           [bwq, bcqg], [pqb])
                tt('dve', qnT[:, 2 * hp_:2 * hp_ + 2, :], pq[:, 0:512].rearrange("p (h n) -> p h n", h=2),
                   rsq[:].unsqueeze(1).broadcast_to([128, 2, 256]), ALU.mult, [pqb, brsq], [bqn])
            brp2 = bf("ropeB2")
            if sample:
                brp = bf("ropeB")
                S.dma('sp', lambda e, tok0=tok0, ropeB=ropeB: e.dma_start(out=ropeB[:, 0, :], in_=cos_d[:, tok0:tok0 + SUB]), brp, (), [brp, brp2])
                S.dma('sp', lambda e, tok0=tok0, ropeB=ropeB: e.dma_start(out=ropeB[:, 1, :], in_=sin_d[:, tok0:tok0 + SUB]), brp, (), [brp, brp2])
                tt('dve', ropeB[:, 0, :], ropeB[:, 0, :], rsq[0:64, :], ALU.mult, [brp, brsq], [brp2])
                tt('dve', ropeB[:, 1, :], ropeB[:, 1, :], rsq[0:64, :], ALU.mult, [brp, brsq, brp2], [brp2])
            for h_ in range(4):
                pq, pqb = gb()
                nq = 2 if sample else 1
                for q_ in range(nq):
                    c0 = 512 + q_ * 256 + h_ * 64
                    for k in range(3):
                        mm(pq[0:64, q_ * 256:(q_ + 1) * 256], wq[:, k, c0:c0 + 64], cqg[:, k, :], k == 0, k == 2,
                           [bwq, bcqg], [pqb])
                if sample:
                    bq1, bq2 = bf("qt1"), bf("qt2")
                    tt('dve', qt1[:], pq[0:64, 0:256], ropeB[:, 0, :], ALU.mult, [pqb, brp2], [bq1])
                    tt('dve', qt2[:], pq[0:64, 256:512], ropeB[:, 1, :], ALU.mult, [pqb, brp2], [bq2])
                    tt('dve', qrT[0:64, h_, :], qt1[:], qt2[:], ALU.add, [bq1, bq2], [bqr])
                else:
                    tt('dve', qrT[0:64, h_, :], pq[0:64, 0:256], rsq[0:64, :], ALU.mult, [pqb, brsq], [bqr])

            yield 'q'

            def rest_front():
                php, phpb = gb()
                for mc in range(2):
                    for k in range(8):
                        mm(php[:, mc * 256:(mc + 1) * 256], wB[:, k, mc * 128:(mc + 1) * 128], h1B[:, k, :], k == 0, k == 7,
                           [bwB, bh1], [phpb])
                bP0, bS2, bS4, bpl = bf("P0"), bf("S2"), bf("S4"), bf("pooled")
                cp('dve', P0[:, :, 8:8 + SUB], php[:, 0:512].rearrange("p (m n) -> p m n", m=2), [phpb], [bP0])
                if sample and s > 0:
                    cp('dve', P0[:, :, 0:8], KV['hph'][:, :, s - 1, 8:16], [bhph], [bP0])
                else:
                    memset('dve', P0[:, :, 0:8], 0.0, [bP0])
                if sample and s < 7:
                    cp('dve', P0[:, :, 8 + SUB:16 + SUB], KV['hph'][:, :, s + 1, 0:8], [bhph], [bP0])
                else:
                    memset('dve', P0[:, :, 8 + SUB:16 + SUB], 0.0, [bP0])
                W_ = SUB + 16
                wmw = [bwm] + tbx
                tt('dve', S2[:, :, 1:W_], P0[:, :, 1:W_], P0[:, :, 0:W_ - 1], ALU.add, [bP0], [bS2])
                tt('dve', S4[:, :, 3:W_], S2[:, :, 3:W_], S2[:, :, 1:W_ - 2], ALU.add, [bS2], [bS4])
                ts('dve', wmn[0:64, 0, :], S2[0:64, 0, 8:8 + SUB], invw[0:64, 0:1], None, ALU.mult, None, [bS2, bconst], wmw)
                ts('dve', wmn[64:128, 0, :], S4[64:128, 0, 9:9 + SUB], invw[64:128, 0:1], None, ALU.mult, None, [bS4, bconst], wmw)
                bS8 = bf("S8")
                tt('dve', S2[:, 1, 7:W_], S4[:, 1, 7:W_], S4[:, 1, 3:W_ - 4], ALU.add, [bS4, bwm, bS2], [bS8, bS2])
                ts('dve', wmn[0:64, 1, :], S2[0:64, 1, 11:11 + SUB], invw[0:64, 1:2], None, ALU.mult, None, [bS8, bconst], wmw)
                bS16 = bf("S16")
                tt('dve', S4[64:128, 1, 15:W_], S2[64:128, 1, 15:W_], S2[64:128, 1, 7:W_ - 8], ALU.add, [bS8, bwm, bS4], [bS16, bS4])
                ts('dve', wmn[64:128, 1, :], S4[64:128, 1, 15:15 + SUB], invw[64:128, 1:2], None, ALU.mult, None,
                   [bS16, bconst], wmw)
                seq_start = (s == 0) or not sample
                seq_end = (s == 7) or not sample
                if seq_start:
                    tt('dve', wmn[:, :, 0:8], wmn[:, :, 0:8], corrL[:], ALU.mult, [bwm, bconst], wmw)
                if seq_end:
                    tt('dve', wmn[:, :, SUB - 8:SUB], wmn[:, :, SUB - 8:SUB], corrR[:], ALU.mult, [bwm, bconst], wmw)
                tt('dve', pooled[:], wmn[:], P0[:, :, 8:8 + SUB], ALU.subtract, [bwm, bP0], [bpl])
                ppl, pplb = gb()
                for mc in range(2):
                    mm(ppl[:, mc * 256:(mc + 1) * 256], wpl[:, mc, :], pooled[:, mc, :], True, True, [bwsm, bpl], [pplb])
                for mc in range(2):
                    ts('dve', yT[:, mc, :], ppl[:, mc * 256:(mc + 1) * 256], pscale[:, l, mc:mc + 1], None, ALU.mult, None,
                       [pplb, bconst], [byT])
                pu, pub = gb()
                for mc in range(2):
                    for k in range(8):
                        mm(pu[:, mc * 256:(mc + 1) * 256], wB[:, k, 640 + mc * 128:640 + (mc + 1) * 128], h1B[:, k, :],
                           k == 0, k == 7, [bwB, bh1], [pub])
                bug = bf("pooled")
                act(ug[:], pu[:, 0:512].rearrange("p (m n) -> p m n", m=2), AF.Gelu_apprx_tanh, [pub], [bug])
                pv, pvb = gb()
                for blk in range(2):
                    for k in range(8):
                        mm(pv[:, blk * 256:(blk + 1) * 256], h1B[:, k, blk * 128:(blk + 1) * 128], wB[:, k, 896:1152],
                           k == 0, k == 7, [bwB, bh1], [pvb])
                bvt, bvss, bvrs, bgt = bf("S4"), bf("vss"), bf("vrs"), bf("S2")
                act(vt[:], pv[:, 0:512].rearrange("p (m n) -> p m n", m=2), AF.Gelu_apprx_tanh, [pvb], [bvt, bf("S16")])
                for blk in range(2):
                    S.op('act', lambda e, blk=blk, gt=gt, vt=vt, vss=vss: e.activation(out=gt[:, 0, :], in_=vt[:, blk, :], func=AF.Square,
                                                                accum_out=vss[:, blk:blk + 1]), [bvt], [bvss, bgt, bf("S8")])
                act(vln[:], vss[:], AF.Ln, [bvss], [bf("vln")], scale=1.0 / 256.0, bias=EPS)
                act(vrs[:], vln[:], AF.Exp, [bf("vln")], [bvrs], scale=-0.5)
                for blk in range(2):
                    for hf, (vg, vgb) in enumerate([(vgA, bvgA), (vgB, bvgB)]):
                        o_ = vg[:, blk, :].rearrange("p (m h c) -> p m h c", m=2, h=2)[:, :, hf, :]
                        i0 = vt[:, blk, :].rearrange("p (m h c) -> p m h c", m=2, h=2)[:, :, hf, :]
                        i1 = gsg[:].rearrange("p (m h c) -> p m h c", m=2, h=2)[:, :, hf, :]
                        stt(o_, i0, vrs[:, blk:blk + 1], i1, ALU.mult, ALU.mult, [bvt, bvrs, bf("gsgk")], [vgb])
                pg, pgb = gb()
                for mc in range(2):
                    for blk in range(2):
                        o_ = pg[:, mc * 256 + blk * 128: mc * 256 + (blk + 1) * 128]
                        mm(o_, vgA[:, blk, mc * 128:(mc + 1) * 128], wsT[:, 2 * mc, :], True, False, [bvgA, bwsm], [pgb])
                        mm(o_, vgB[:, blk, mc * 128:(mc + 1) * 128], wsT[:, 2 * mc + 1, :], False, True, [bvgB, bwsm], [pgb])
                for mc in range(2):
                    tt('dve', gt[:, mc, :].rearrange("p (b n) -> p b n", b=2),
                       pg[:, mc * 256:(mc + 1) * 256].rearrange("p (b n) -> p b n", b=2),
                       bsT[:, mc, :].unsqueeze(1).broadcast_to([128, 2, 128]), ALU.add, [pgb, bwsm], [bgt, bf("S8")])
                tt('dve', yT[:, 6:8, :], gt[:], ug[:], ALU.mult, [bgt, bug], [byT])

            if sample:
                kbase, nkc = 0, 18
            else:
                kbase, nkc = 2304 + (s - 8) * 256, 2
            npair = nkc // 2
            pairs = [(h_, jp) for h_ in range(4) for jp in range(npair)]
            st_ = {}

            def emit_scores(i):
                h_, jp = pairs[i]
                psc, pscb = (ps[7], psb[7]) if i % 2 == 0 else gb()
                for sub in range(2):
                    j = 2 * jp + sub
                    k0 = kbase + j * 128
                    mm(psc[:, sub * 256:(sub + 1) * 256], knT[:, h_, k0:k0 + 128], qnT[:, h_, :], True, False,
                       [bknT, bqn], [pscb])
                    if ROPE128:
                        mm(psc[:, sub * 256:(sub + 1) * 256], krT[:, k0:k0 + 128], qrT[:, h_, :], False, True,
                           [bkrT, bqr], [pscb])
                    else:
                        mm(psc[:, sub * 256:(sub + 1) * 256], krT[0:64, k0:k0 + 128], qrT[0:64, h_, :], False, True,
                           [bkrT, bqr], [pscb])
                pt, ptb = PT[i % NPT], bf("PT%d" % (i % NPT))
                act(pt[:], psc[:, 0:512], AF.Exp, [pscb], [ptb], scale=SCALE)
                st_[i] = (pt, ptb)

            def emit_pv(i):
                h_, jp = pairs[i]
                pt, ptb = st_.pop(i)
                po, pob = ps[4 + (h_ % 2)], psb[4 + (h_ % 2)]
                for sub in range(2):
                    j = 2 * jp + sub
                    kc = (kbase // 128) + j
                    mm(po[:, 0:256], V[:, kc, h_ * 128:(h_ + 1) * 128], pt[:, sub * 256:(sub + 1) * 256],
                       j == 0, j == nkc - 1, [bV, ptb], [pob], skip=True)
                    mm(po[:, 256:512], ones[:], pt[:, sub * 256:(sub + 1) * 256], False, j == nkc - 1,
                       [bones, ptb], [pob], skip=True)
                if jp == npair - 1:
                    bdr = bf("qt1")
                    S.op('dve', lambda e, po=po, denr=denr: e.reciprocal(out=denr[:], in_=po[:, 256:512]), [pob], [bdr])
                    tt('dve', yT[:, 2 + h_, :], po[:, 0:256], denr[:], ALU.mult, [pob, bdr], [byT])

            rest_front()
            emit_scores(0)
            for i in range(len(pairs)):
                if i + 1 < len(pairs):
                    emit_scores(i + 1)
                emit_pv(i)
                if i == npair - 1 and next_s is not None:
                    normB(next_s)
            yield 'att'
            for mp in range(4):
                pw, pwb = gb()
                for mh in range(2):
                    m_ = 2 * mp + mh
                    for k in range(8):
                        mm(pw[:, mh * 256:(mh + 1) * 256], wo[:, k, m_ * 128:(m_ + 1) * 128], yT[:, k, :], k == 0, k == 7,
                           [bwo, byT], [pwb])
                for mh in range(2):
                    m_ = 2 * mp + mh
                    stt(x[:, m_, tok0:tok1], pw[:, mh * 256:(mh + 1) * 256], mod[:, 16 + m_, mt:mt + 1], x[:, m_, tok0:tok1],
                        ALU.mult, ALU.add, [pwb, bmod, bx], [bx])

        gbank_list[:] = [0, 1, 2, 3, 6]
        gens = {s: phaseB_sub(s, KVs, prenormed=(s > 0), next_s=(s + 1 if s < NSUB - 1 else None)) for s in range(NSUB)}
        next(gens[0])
        next(gens[0])
        for s in range(NSUB):
            next(gens[s])
            if s + 1 < NSUB:
                next(gens[s + 1])
            for _ in gens[s]:
                pass
            if s + 1 < NSUB:
                next(gens[s + 1])
        gbank_list[:] = [0, 1, 2, 3]
        S.barrier()

        cur[0] = scr_base
        h2 = alloc([128, 8, T], BF16, "h2")
        wslot = [(alloc([128, 8, 512], BF16, "w1s"), alloc([128, 4, 1024], BF16, "w2s")) for _ in range(2)]
        sqF = alloc([128, 8, 512], BF16, "sqF")
        tmpxF = [alloc([128, 512], F32, "tmpxF") for _ in range(2)]
        lnvF = alloc([128, 512], F32, "lnvF")
        rstdF = alloc([128, 512], F32, "rstdF")
        rl = [alloc([128, 512], F32, "rl") for _ in range(2)]
        hid = [alloc([128, 4, 512], BF16, "hid") for _ in range(2)]
        ybuf = alloc([128, 8, 512], F32, "ybuf")
        bh2 = [bf("h2_%d" % t) for t in range(5)]

        def load_w(j):
            w1s, w2s = wslot[j % 2]
            kb1, kb2 = bf("w1s%d" % (j % 2)), bf("w2s%d" % (j % 2))
            ld('pool', w1s[:], w1_d[l, j], kb1)
            ld('pool', w2s[:], w2_d[l, j], kb2)

        gbank_list[:] = [0, 1, 2, 3, 4, 5, 6]
        load_w(0)
        nxt = l + 1 < L
        if nxt:
            ringF = [alloc([128, 8, 128], BF16, "adarF") for _ in range(4)]
            ringFb = [bf("adarF%d" % i) for i in range(4)]

        def norm2(t):
            mt = 0 if t < 4 else 1
            norm_tile(t * 512, 512, a2, 24, mt, h2[:, :, t * 512:(t + 1) * 512], bh2[t], "nF", sqF, tmpxF, lnvF, rstdF)

        def ff1(i, j, t):
            w1s, kb1 = wslot[j % 2][0], bf("w1s%d" % (j % 2))
            hd, hdb = hid[i % 2], bf("hid%d" % (i % 2))
            for m_ in range(4):
                pb, pbb = gb()
                for k in range(8):
                    mm(pb[:, 0:512], w1s[:, k, m_ * 128:(m_ + 1) * 128], h2[:, k, t * 512:(t + 1) * 512], k == 0, k == 7,
                       [kb1, bh2[t]], [pbb])
                r_, rb_ = rl[m_ % 2], bf("rl%d" % (m_ % 2))
                act(r_[:], pb[:, 0:512], AF.Relu, [pbb], [rb_])
                act(hd[:, m_, :], r_[:], AF.Square, [rb_], [hdb])

        def ff2(i, j, t):
            w2s, kb2 = wslot[j % 2][1], bf("w2s%d" % (j % 2))
            hd, hdb = hid[i % 2], bf("hid%d" % (i % 2))
            mt = 0 if t < 4 else 1
            for m_ in range(8):
                pb, pbb = gb()
                for k in range(4):
                    mm(pb[:, 0:512], w2s[:, k, m_ * 128:(m_ + 1) * 128], hd[:, k, :], k == 0, k == 3, [kb2, hdb], [pbb])
                stt(x[:, m_, t * 512:(t + 1) * 512], pb[:, 0:512], mod[:, 40 + m_, mt:mt + 1],
                    x[:, m_, t * 512:(t + 1) * 512], ALU.mult, ALU.add, [pbb, bmod, bx], [bx])

        steps = [(j, t) for j in range(NJ) for t in range(5)]
        load_w(1)
        norm2(0)
        norm2(1)
        ff1(0, 0, 0)
        for i, (j, t) in enumerate(steps):
            if i + 1 < len(steps):
                ff1(i + 1, *steps[i + 1])
            ff2(i, j, t)
            if j == 0 and t + 2 < 5:
                norm2(t + 2)
            if t == 4:
                if j + 2 < NJ:
                    load_w(j + 2)
                if nxt:
                    emit_mod(l + 1, ringF, ringFb, j * 6, (j + 1) * 6)
        if nxt:
            finish_mod(l + 1)
        if l == L - 1:
            for t in range(5):
                bsq, bln, brs, byb = bf("fsq"), bf("fln"), bf("frs"), bf("ybuf")
                act(sqF[:], x[:, :, t * 512:(t + 1) * 512], AF.Square, [bx], [bsq])
                pb, pbb = gb()
                for k in range(8):
                    mm(pb[:, 0:512], ones[:], sqF[:, k, :], k == 0, k == 7, [bones, bsq], [pbb])
                rstd_from_ps(pb[:, 0:512], float(D), lnvF[:], rstdF[:], [pbb], bln, brs)
                for c in range(8):
                    stt(ybuf[:, c, :], x[:, c, t * 512:(t + 1) * 512], gfinal[:, c:c + 1], rstdF[:], ALU.mult, ALU.mult,
                        [bx, brs, bconst], [byb])
                out_toks.append(S.dma('sp', lambda e, t=t, ybuf=ybuf: e.dma_start(out=yT_d[:, :, t * 512:(t + 1) * 512], in_=ybuf[:]),
                                      byb, [byb], []))
        gbank_list[:] = [0, 1, 2, 3]
        S.barrier()

    with ExitStack() as st:
        with nc.Block() as block:
            S.emit(nc, block, st, final_waits=out_toks)
    return nc


def _pc(a, nchunk):
    return np.ascontiguousarray(a.reshape((nchunk, 128) + a.shape[1:]).swapaxes(0, 1))


def _vecs(g, nchunk):
    Ln = g.shape[0]
    return np.ascontiguousarray(g.reshape(Ln, nchunk, 128).transpose(2, 0, 1))


def _rope_tables():
    rows = TS // 64
    row = np.repeat(np.arange(rows, dtype=np.float32), 64)
    col = np.tile(np.arange(64, dtype=np.float32), rows)
    inv = (1.0 / (10000.0 ** (np.arange(16, dtype=np.float32) / 16))).astype(np.float32)
    ang = np.concatenate([row[:, None] * inv, col[:, None] * inv], axis=-1).astype(np.float32)
    cos, sin = np.cos(ang).astype(np.float32), np.sin(ang).astype(np.float32)
    cosT = np.concatenate([cos, cos], axis=1).T
    sinT = np.concatenate([-sin, sin], axis=1).T
    return np.ascontiguousarray(cosT), np.ascontiguousarray(sinT)


def _pool_tables():
    wins = (2, 4, 8, 16)
    invw = np.zeros((128, 2), np.float32)
    corrL = np.ones((128, 2, 8), np.float32)
    corrR = np.ones((128, 2, 8), np.float32)
    Ls = 256
    for g, w in enumerate(wins):
        c, hf = g // 2, g % 2
        sl = slice(hf * 64, (hf + 1) * 64)
        invw[sl, c] = 1.0 / w
        for i in range(8):
            t = i
            cnt = min(t + (w - w // 2), Ls) - max(t - w // 2, 0)
            corrL[sl, c, i] = w / cnt
            t = Ls - 8 + i
            cnt = min(t + (w - w // 2), Ls) - max(t - w // 2, 0)
            corrR[sl, c, i] = w / cnt
    return invw, corrL, corrR


def prepare_inputs(x_prompt, x_sample, cache_ckv, cache_krope, c, c_ctx, w_ada, b_ada, g_mix,
                   w_in, w_pool, pool_scale, g_q, w_uq, g_kv, w_ukv, g_sgu, w_s, b_s, w_out,
                   g_ffn, w_ff1, w_ff2, g_final):
    f = lambda a: np.asarray(a, dtype=np.float32)
    x_prompt, x_sample, cache_ckv, cache_krope, c, c_ctx = map(f, (x_prompt, x_sample, cache_ckv, cache_krope, c, c_ctx))
    w_ada, b_ada, g_mix, w_in, w_pool, pool_scale, g_q, w_uq, g_kv, w_ukv = map(
        f, (w_ada, b_ada, g_mix, w_in, w_pool, pool_scale, g_q, w_uq, g_kv, w_ukv))
    g_sgu, w_s, b_s, w_out, g_ffn, w_ff1, w_ff2, g_final = map(f, (g_sgu, w_s, b_s, w_out, g_ffn, w_ff1, w_ff2, g_final))
    Ln = w_in.shape[0]
    OFF_Q, OFF_KV, OFF_R, OFF_G = 256, 640, 896, 960
    shared = {}
    shared["wada"] = np.ascontiguousarray(w_ada.reshape(Ln, 8, 128, 48, 128).transpose(0, 3, 2, 1, 4))
    shared["bada"] = _vecs(b_ada, 48)
    shared["gmix"] = _vecs(g_mix, 8)
    shared["gffn"] = _vecs(g_ffn, 8)
    shared["gq"] = _vecs(g_q, 3)
    shared["gkv"] = _vecs(g_kv, 2)
    shared["gsgu"] = np.ascontiguousarray(np.broadcast_to(g_sgu[:, None, :], (Ln, 128, 256)))
    shared["pscale"] = _vecs(pool_scale, 2)
    shared["gfinal"] = np.ascontiguousarray(g_final.reshape(8, 128).T)
    kr = w_in[:, :, OFF_R:OFF_G]
    kr_swp = np.concatenate([kr[:, :, 32:64], kr[:, :, 0:32]], axis=2)
    winA = np.concatenate([w_in[:, :, OFF_KV:OFF_R], kr, kr_swp], axis=2)
    shared["winA"] = np.ascontiguousarray(winA.reshape(Ln, 8, 128, 384).transpose(0, 2, 1, 3))
    winB = np.concatenate([w_in[:, :, 0:OFF_Q], w_in[:, :, OFF_Q:OFF_KV], w_in[:, :, OFF_G:OFF_G + 512]], axis=2)
    shared["winB"] = np.ascontiguousarray(winB.reshape(Ln, 8, 128, 1152).transpose(0, 2, 1, 3))
    wq4 = w_uq.reshape(Ln, 384, 4, 192)
    qn = wq4[:, :, :, 0:128].reshape(Ln, 384, 512)
    qr = wq4[:, :, :, 128:192]
    qr_swp = np.concatenate([qr[..., 32:64], qr[..., 0:32]], axis=-1)
    wuq = np.concatenate([qn, qr.reshape(Ln, 384, 256), qr_swp.reshape(Ln, 384, 256)], axis=2)
    shared["wuq"] = np.ascontiguousarray(wuq.reshape(Ln, 3, 128, 1024).transpose(0, 2, 1, 3))
    wkv4 = w_ukv.reshape(Ln, 256, 4, 256)
    wukv = np.concatenate([wkv4[..., 0:128].reshape(Ln, 256, 512), wkv4[..., 128:256].reshape(Ln, 256, 512)], axis=2)
    shared["wukv"] = np.ascontiguousarray(wukv.reshape(Ln, 2, 128, 1024).transpose(0, 2, 1, 3))
    shared["wout"] = np.ascontiguousarray(w_out.reshape(Ln, 8, 128, 1024).transpose(0, 2, 1, 3))
    shared["wpool"] = np.ascontiguousarray(w_pool)
    shared["wsT"] = np.ascontiguousarray(w_s.transpose(0, 3, 1, 2))
    bsT = np.broadcast_to(b_s.reshape(Ln, 2, 2, 1, 128), (Ln, 2, 2, 64, 128))
    shared["bsT"] = np.ascontiguousarray(bsT.reshape(Ln, 2, 128, 128).transpose(0, 2, 1, 3))
    shared["w1"] = np.ascontiguousarray(w_ff1.reshape(Ln, 8, 128, NJ, 512).transpose(0, 3, 2, 1, 4))
    shared["w2"] = np.ascontiguousarray(w_ff2.reshape(Ln, NJ, 4, 128, 1024).transpose(0, 1, 3, 2, 4))
    cosT, sinT = _rope_tables()
    shared["cosT"], shared["sinT"] = cosT, sinT
    shared["invw"], shared["corrL"], shared["corrR"] = _pool_tables()
    in_maps = []
    for i in range(8):
        m = dict(shared)
        xs = np.concatenate([x_sample[i], x_prompt[2 * i], x_prompt[2 * i + 1]], axis=0)
        m["xT"] = _pc(np.ascontiguousarray(xs.T), 8)
        cc = np.stack([c[i], c_ctx], axis=1)
        m["cT"] = _pc(cc, 8)
        ck = cache_ckv[i].transpose(2, 0, 1)
        m["cckv"] = np.ascontiguousarray(ck.reshape(2, 128, Ln, 256).transpose(1, 2, 0, 3))
        m["ckr"] = np.ascontiguousarray(cache_krope[i].transpose(2, 0, 1))
        in_maps.append(m)
    return in_maps


_NC_CACHE = {}


def kernel(**inputs):
    in_maps = prepare_inputs(**inputs)
    if "nc" not in _NC_CACHE:
        _NC_CACHE["nc"] = build_program(L_FULL)
    nc = _NC_CACHE["nc"]
    res = run_bass_kernel_spmd(nc, in_maps, core_ids=list(range(8)))
    y_prompt = np.zeros((16, 256, D), np.float32)
    y_sample = np.zeros((8, TS, D), np.float32)
    state_ckv = np.zeros((16, L_FULL, 256, 256), np.float32)
    state_krope = np.zeros((16, L_FULL, 256, 64), np.float32)
    for i, r in enumerate(res.results):
        yT = np.asarray(r["yT"])
        y = yT.transpose(2, 1, 0).reshape(T, D)
        y_sample[i] = y[0:TS]
        y_prompt[2 * i] = y[TS:TS + 256]
        y_prompt[2 * i + 1] = y[TS + 256:TS + 512]
        ck = np.asarray(r["ckvst"])
        ck = ck.transpose(0, 3, 2, 1).reshape(L_FULL, 512, 256)
        kr = np.asarray(r["krst"]).transpose(0, 2, 1)
        for b in range(2):
            state_ckv[2 * i + b] = ck[:, b * 256:(b + 1) * 256, :]
            state_krope[2 * i + b] = kr[:, b * 256:(b + 1) * 256, :]
    return (y_prompt, y_sample, state_ckv, state_krope)
```

```python
import math
from contextlib import ExitStack

import numpy as np
import concourse.bass as bass
import concourse.mybir as mybir
from concourse.bass_utils import run_bass_kernel_spmd

F32, BF16 = mybir.dt.float32, mybir.dt.bfloat16
AF = mybir.ActivationFunctionType
ALU = mybir.AluOpType

D = 1024
L_FULL = 4
T = 2560
TS = 2048
NSUB = 10
SUB = 256
NKEY = 2816
EPS = 1e-6
SCALE = 1.0 / math.sqrt(192.0)
NJ = 8
REORDER = True
PADZERO = True
ROPE128 = True
BST_BF16 = True
ENGS = ['pe', 'act', 'dve', 'pool', 'sp']


class Buf:
    __slots__ = ('name', 'w', 'r', 'dcount', 'sem', 'psum')

    def __init__(self, name, psum=False):
        self.name = name
        self.psum = psum
        self.w = None
        self.r = []
        self.dcount = 0
        self.sem = None


class Op:
    __slots__ = ('eng', 'fn', 'deps', 'idx', 'signal', 'sigval', 'dkey', 'dval')

    def __init__(self, eng, fn, deps, idx, dkey=None, dval=0):
        self.eng = eng
        self.fn = fn
        self.deps = deps
        self.idx = idx
        self.signal = False
        self.sigval = 0
        self.dkey = dkey
        self.dval = dval


class Sched:
    def __init__(self):
        self.ops = {e: [] for e in ENGS}
        self.dkeys = []
        self.bar_deps = set()
        self.bar_gen = 0
        self.eng_gen = {e: 0 for e in ENGS}

    def _deps(self, eng, reads, writes, is_dma):
        deps = set()
        for b in reads:
            if b.w is not None:
                deps.add(b.w)
            if b.psum:
                deps.update(t for t in b.r if not (t[0] == 'e' and t[1] == eng))
        for b in writes:
            if b.w is not None and not (is_dma and b.w[0] == 'd'):
                deps.add(b.w)
            deps.update(b.r)
        if self.eng_gen[eng] != self.bar_gen:
            deps |= self.bar_deps
            self.eng_gen[eng] = self.bar_gen
        return deps

    def op(self, eng, fn, reads=(), writes=()):
        deps = self._deps(eng, reads, writes, False)
        o = Op(eng, fn, deps, len(self.ops[eng]))
        self.ops[eng].append(o)
        tok = ('e', eng, o.idx)
        if eng == 'pe':
            o.deps = {d for d in deps if not (d[0] == 'e' and d[1] == 'pe')}
        for b in reads:
            self._add_reader(b, tok)
        for b in writes:
            b.w = tok
            b.r = []
        return tok

    @staticmethod
    def _add_reader(b, tok):
        b.r = [t for t in b.r if not (t[0] == tok[0] and t[1] == tok[1])]
        b.r.append(tok)

    def dma(self, eng, fn, key, reads=(), writes=()):
        deps = self._deps(eng, reads, writes, True)
        if key.sem is None:
            key.sem = True
            self.dkeys.append(key)
        key.dcount += 16
        o = Op(eng, fn, deps, len(self.ops[eng]), dkey=key, dval=key.dcount)
        self.ops[eng].append(o)
        tok = ('d', key, key.dcount)
        for b in reads:
            self._add_reader(b, tok)
        for b in writes:
            b.w = tok
            b.r = []
        return tok

    def barrier(self):
        deps = set()
        for e in ENGS:
            for o in reversed(self.ops[e]):
                if o.dkey is None:
                    deps.add(('e', e, o.idx))
                    break
        for k in self.dkeys:
            if k.dcount:
                deps.add(('d', k, k.dcount))
        self.bar_deps = deps
        self.bar_gen += 1

    def emit(self, nc, block, stack, final_waits=()):
        for e in ENGS:
            for o in self.ops[e]:
                for d in o.deps:
                    if d[0] == 'e':
                        self.ops[d[1]][d[2]].signal = True
        for d in final_waits:
            if d[0] == 'e':
                self.ops[d[1]][d[2]].signal = True
        esem = {}
        for e in ENGS:
            n = 0
            for o in self.ops[e]:
                if o.signal:
                    n += 1
                    o.sigval = n
            if n:
                esem[e] = stack.enter_context(nc.semaphore('s_' + e))
        for k in self.dkeys:
            k.sem = stack.enter_context(nc.semaphore('d_' + k.name))
        sched = self

        def run(e, engobj, extra=()):
            seen = {}

            def wait(tok):
                if tok[0] == 'e':
                    sem = esem[tok[1]]
                    val = sched.ops[tok[1]][tok[2]].sigval
                    key = tok[1]
                else:
                    sem = tok[1].sem
                    val = tok[2]
                    key = id(tok[1])
                if seen.get(key, 0) >= val:
                    return
                seen[key] = val
                engobj.wait_ge(sem, val)

            for o in sched.ops[e]:
                for d in sorted(o.deps, key=lambda t: (t[0], t[1] if t[0] == 'e' else t[1].name, t[2])):
                    wait(d)
                ins = o.fn(engobj)
                if o.dkey is not None:
                    ins.then_inc(o.dkey.sem, 16)
                elif o.signal:
                    ins.then_inc(esem[e], 1)
            for d in extra:
                wait(d)

        @block.tensor
        def _(eng):
            run('pe', eng)

        @block.scalar
        def _(eng):
            run('act', eng)

        @block.vector
        def _(eng):
            run('dve', eng)

        @block.gpsimd
        def _(eng):
            run('pool', eng)

        @block.sync
        def _(eng):
            run('sp', eng, extra=final_waits)


def build_program(L=L_FULL):
    nc = bass.Bass("TRN2", target_bir_lowering=False)
    S = Sched()

    def din(name, shape):
        return nc.dram_tensor(name, list(shape), F32, kind="ExternalInput").ap()

    def dout(name, shape):
        return nc.dram_tensor(name, list(shape), F32, kind="ExternalOutput").ap()

    xT_d = din("xT", [128, 8, T])
    cT_d = din("cT", [128, 8, 2])
    wada_d = din("wada", [L_FULL, 48, 128, 8, 128])
    bada_d = din("bada", [128, L_FULL, 48])
    gmix_d = din("gmix", [128, L_FULL, 8])
    gffn_d = din("gffn", [128, L_FULL, 8])
    gq_d = din("gq", [128, L_FULL, 3])
    gkv_d = din("gkv", [128, L_FULL, 2])
    gsgu_d = din("gsgu", [L_FULL, 128, 256])
    pscale_d = din("pscale", [128, L_FULL, 2])
    gfinal_d = din("gfinal", [128, 8])
    winA_d = din("winA", [L_FULL, 128, 8, 384])
    winB_d = din("winB", [L_FULL, 128, 8, 1152])
    wuq_d = din("wuq", [L_FULL, 128, 3, 1024])
    wukv_d = din("wukv", [L_FULL, 128, 2, 1024])
    wout_d = din("wout", [L_FULL, 128, 8, 1024])
    wpool_d = din("wpool", [L_FULL, 4, 64, 64])
    wsT_d = din("wsT", [L_FULL, 128, 4, 128])
    bsT_d = din("bsT", [L_FULL, 128, 2, 128])
    w1_d = din("w1", [L_FULL, NJ, 128, 8, 512])
    w2_d = din("w2", [L_FULL, NJ, 128, 4, 1024])
    cckv_d = din("cckv", [128, L_FULL, 2, 256])
    ckr_d = din("ckr", [64, L_FULL, 256])
    cos_d = din("cosT", [64, TS])
    sin_d = din("sinT", [64, TS])
    invw_d = din("invw", [128, 2])
    corrL_d = din("corrL", [128, 2, 8])
    corrR_d = din("corrR", [128, 2, 8])
    yT_d = dout("yT", [128, 8, T])
    ckvst_d = dout("ckvst", [L_FULL, 128, 2, 512])
    krst_d = dout("krst", [L_FULL, 64, 512])

    base0 = (nc.sbuf_base + 63) // 64 * 64
    top = nc.sbuf_top
    cur = [base0]
    ncnt = [0]

    def alloc(shape, dt, name=None):
        size = int(np.prod(shape[1:])) * (4 if dt == F32 else 2)
        size = (size + 31) // 32 * 32
        off = cur[0]
        cur[0] += size
        assert cur[0] <= top, ("SBUF overflow", name, cur[0] - top)
        ncnt[0] += 1
        t_ = nc.alloc_sbuf_tensor_at("%s_%d" % (name or "t", ncnt[0]), list(shape), dt, offset=off)
        offs[id(t_)] = off
        return t_

    offs = {}

    def alias(t_, shape, dt, name):
        ncnt[0] += 1
        return nc.alloc_sbuf_tensor_at("%s_%d" % (name, ncnt[0]), list(shape), dt, offset=offs[id(t_)])

    x = alloc([128, 8, T], F32, "x")
    ones = alloc([128, 128], BF16, "ones")
    gmix = alloc([128, L_FULL, 8], F32, "gmix")
    gffn = alloc([128, L_FULL, 8], F32, "gffn")
    gq = alloc([128, L_FULL, 3], F32, "gq")
    gkv = alloc([128, L_FULL, 2], F32, "gkv")
    pscale = alloc([128, L_FULL, 2], F32, "pscale")
    gfinal = alloc([128, 8], F32, "gfinal")
    bada = alloc([128, L_FULL, 48], F32, "bada")
    cT = alloc([128, 8, 2], F32, "cT")
    sT = alloc([128, 8, 2], BF16, "sT")
    invw = alloc([128, 2], F32, "invw")
    corrL = alloc([128, 2, 8], F32, "corrL")
    corrR = alloc([128, 2, 8], F32, "corrR")
    modp = [alloc([128, 48, 2], F32, "mod") for _ in range(2)]
    a1p = [alloc([128, 8, 2], F32, "a1") for _ in range(2)]
    a2p = [alloc([128, 8, 2], F32, "a2") for _ in range(2)]
    scr_base = cur[0]

    ps = [nc.alloc_psum_tensor("ps%d" % i, [128, 512], F32) for i in range(8)]
    psb = [Buf("ps%d" % i, psum=True) for i in range(8)]

    B = {}

    def bf(name):
        if name not in B:
            B[name] = Buf(name)
        return B[name]

    gbank_list = [0, 1, 2, 3]
    gctr = [0]

    def gb():
        i = gbank_list[gctr[0] % len(gbank_list)]
        gctr[0] += 1
        return ps[i], psb[i]

    def mm(out, lhsT, rhs, start, stop, reads, writes, skip=False):
        if skip:
            S.op('pe', lambda e: e.matmul(out, lhsT=lhsT, rhs=rhs, start=start, stop=stop, skip_group_check=True),
                 reads, writes)
        else:
            S.op('pe', lambda e: e.matmul(out, lhsT=lhsT, rhs=rhs, start=start, stop=stop), reads, writes)

    def act(out, in_, func, reads, writes, scale=1.0, bias=0.0):
        S.op('act', lambda e: e.activation(out=out, in_=in_, func=func, scale=scale, bias=bias), reads, writes)

    def tt(eng, out, in0, in1, op, reads, writes):
        S.op(eng, lambda e: e.tensor_tensor(out=out, in0=in0, in1=in1, op=op), reads, writes)

    def stt(out, in0, scalar, in1, op0, op1, reads, writes):
        S.op('dve', lambda e: e.scalar_tensor_tensor(out=out, in0=in0, scalar=scalar, in1=in1, op0=op0, op1=op1),
             reads, writes)

    def ts(eng, out, in0, s1, s2, op0, op1, reads, writes):
        if s2 is None:
            S.op(eng, lambda e: e.tensor_scalar(out=out, in0=in0, scalar1=s1, scalar2=None, op0=op0), reads, writes)
        else:
            S.op(eng, lambda e: e.tensor_scalar(out=out, in0=in0, scalar1=s1, scalar2=s2, op0=op0, op1=op1),
                 reads, writes)

    def cp(eng, out, in_, reads, writes):
        if eng == 'act':
            S.op('act', lambda e: e.activation(out=out, in_=in_, func=AF.Copy), reads, writes)
        else:
            S.op(eng, lambda e: e.tensor_copy(out=out, in_=in_), reads, writes)

    def memset(eng, ap, val, writes):
        S.op(eng, lambda e: e.memset(ap, val), (), writes)

    def ld(q, out, in_, key, writes=None):
        S.dma(q, lambda e: e.dma_start(out=out, in_=in_), key, (), writes if writes is not None else [key])

    def rstd_from_ps(psum_ap, n_feat, lnv, rstd, rd, wr_ln, wr_rs):
        act(lnv, psum_ap, AF.Ln, rd, [wr_ln], scale=1.0 / n_feat, bias=EPS)
        act(rstd, lnv, AF.Exp, [wr_ln], [wr_rs], scale=-0.5)

    bx = bf("x")
    for t in range(5):
        S.dma('sp', lambda e, t=t: e.dma_start(out=x[:, :, t * 512:(t + 1) * 512], in_=xT_d[:, :, t * 512:(t + 1) * 512]),
              bx, (), [bx])
    bconst = bf("const")
    for dst, src in [(gmix, gmix_d), (gffn, gffn_d), (gq, gq_d), (gkv, gkv_d), (pscale, pscale_d), (gfinal, gfinal_d),
                     (bada, bada_d), (cT, cT_d), (invw, invw_d), (corrL, corrL_d), (corrR, corrR_d)]:
        S.dma('sp', lambda e, dst=dst, src=src: e.dma_start(out=dst[:], in_=src), bconst, (), [bconst])
    bones = bf("ones")
    memset('dve', ones[:], 1.0, [bones])
    bsT_ = bf("sT")
    act(sT[:], cT[:], AF.Silu, [bconst], [bsT_])

    out_toks = []

    def tokrange(s):
        return s * SUB, (s + 1) * SUB

    for l in range(L):
        mod, a1, a2 = modp[l % 2], a1p[l % 2], a2p[l % 2]
        bmod = bf("mod%d" % (l % 2))

        def emit_mod_dma(ll, ring, ringb, m0, m1):
            for m in range(m0, m1):
                ld('pool', ring[m % len(ring)][:], wada_d[ll, m], ringb[m % len(ring)])

        def emit_mod_mm(ll, ring, ringb, m0, m1):
            pm, pmb = ps[7], psb[7]
            for m in range(m0, m1):
                r, rb = ring[m % len(ring)], ringb[m % len(ring)]
                for k in range(8):
                    mm(pm[:, 2 * m:2 * m + 2], r[:, k, :], sT[:, k, :], k == 0, k == 7, [rb, bsT_], [pmb])

        def emit_mod(ll, ring, ringb, m0, m1):
            for m in range(m0, m1):
                emit_mod_dma(ll, ring, ringb, m, m + 1)
                emit_mod_mm(ll, ring, ringb, m, m + 1)

        def finish_mod(ll):
            pm, pmb = ps[7], psb[7]
            md, aa1, aa2 = modp[ll % 2], a1p[ll % 2], a2p[ll % 2]
            bm = bf("mod%d" % (ll % 2))
            pmv = pm[:, 0:96].rearrange("p (m s) -> p m s", s=2)
            for s_ in range(2):
                tt('dve', md[:, :, s_], pmv[:, :, s_], bada[:, ll, :], ALU.add, [pmb, bconst], [bm])
            for s_ in range(2):
                stt(aa1[:, :, s_], md[:, 8:16, s_], 1.0, gmix[:, ll, :], ALU.add, ALU.mult, [bm, bconst], [bm])
                stt(aa2[:, :, s_], md[:, 32:40, s_], 1.0, gffn[:, ll, :], ALU.add, ALU.mult, [bm, bconst], [bm])

        if l == 0:
            cur[0] = scr_base
            ring0 = [alloc([128, 8, 128], BF16, "adar") for _ in range(4)]
            ringb0 = [bf("adar%d" % i) for i in range(4)]
            emit_mod(0, ring0, ringb0, 0, 48)
            finish_mod(0)
            S.barrier()

        def mtype(s):
            return 0 if s < 8 else 1

        def norm_tile(tok0, n, avec, bvec_off, mt, h, hb, tag, sq, tmpx, lnv, rstd, bsq=None, tmpx_extra=()):
            if bsq is None:
                bsq = bf(tag + "sq")
            bln, brs = bf(tag + "ln"), bf(tag + "rs")
            if lnv is rstd:
                bln = brs
            act(sq[:, :, 0:n], x[:, :, tok0:tok0 + n], AF.Square, [bx], [bsq])
            pb, pbb = gb()
            for k in range(8):
                mm(pb[:, 0:n], ones[:], sq[:, k, 0:n], k == 0, k == 7, [bones, bsq], [pbb])
            rstd_from_ps(pb[:, 0:n], float(D), lnv[:, 0:n], rstd[:, 0:n], [pbb], bln, brs)
            for c in range(8):
                tb = bf(tag + "tmpx%d" % (c % 2))
                tx = tmpx[c % 2]
                tt('dve', tx[:, 0:n], x[:, c, tok0:tok0 + n], rstd[:, 0:n], ALU.mult, [bx, brs], [tb] + list(tmpx_extra))
                sc_ap = avec[:, c, mt:mt + 1]
                bi_ap = mod[:, bvec_off + c, mt:mt + 1]
                o_ap, i_ap = h[:, c, 0:n], tx[:, 0:n]
                S.op('act', lambda e, sc_ap=sc_ap, bi_ap=bi_ap, o_ap=o_ap, i_ap=i_ap: e.activation(
                    out=o_ap, in_=i_ap, func=AF.Identity, scale=sc_ap, bias=bi_ap), [tb, bmod], [hb])

        def alloc_kv(nkey, with_hph):
            KV = {}
            KV['knT'] = alloc([128, 4, nkey], BF16, "knT")
            KV['krT'] = alloc([128, nkey], BF16, "krT")
            KV['V'] = alloc([128, nkey // 128, 512], BF16, "V")
            if with_hph:
                KV['hph'] = alloc([128, 2, 8, 16], F32, "hph")
            return KV

        def alloc_A(with_hp):
            TA = {}
            TA['wA'] = alloc([128, 8, 384], BF16, "wA")
            if with_hp:
                TA['whp'] = alloc([128, 8, 256], BF16, "whp")
            TA['wkv'] = alloc([128, 2, 1024], BF16, "wkv")
            TA['sq'] = alloc([128, 8, SUB], BF16, "sqA")
            TA['tmpx'] = [alloc([128, SUB], F32, "tmpxA") for _ in range(2)]
            TA['lnv'] = alloc([128, SUB], F32, "lnvA")
            TA['rstd'] = alloc([128, SUB], F32, "rstdA")
            TA['h1'] = alloc([128, 8, SUB], BF16, "h1A")
            TA['sqk'] = alloc([128, 2, SUB], BF16, "sqk")
            TA['lnk'] = alloc([128, SUB], F32, "lnk")
            TA['rsk'] = alloc([128, SUB], F32, "rsk")
            TA['ckvn'] = [alloc([128, 2, SUB], BF16, "ckvn") for _ in range(2)]
            if with_hp:
                TA['kt1'] = alloc([64, SUB], F32, "kt1")
                TA['kt2'] = alloc([64, SUB], F32, "kt2")
                TA['rope'] = [alloc([64, 2, SUB], F32, "ropeA") for _ in range(2)]
            else:
                TA['ckvst'] = alloc([128, 2, SUB], F32, "ckvst")
                TA['krst'] = alloc([64, SUB], F32, "krst")
            return TA

        bknT, bkrT, bV, bhph = bf("knT"), bf("krT"), bf("V"), bf("hph")
        bwA, bwhp, bwkv = bf("wA"), bf("whp"), bf("wkv")

        def load_A(TA, with_hp):
            ld('pool', TA['wA'][:], winA_d[l], bwA)
            ld('pool', TA['wkv'][:], wukv_d[l], bwkv)
            if with_hp:
                ld('pool', TA['whp'][:], winB_d[l, :, :, 0:256], bwhp)

        def kv_proj(KV, TA, cb, cbb, key0):
            wkv = TA['wkv']
            for hp_ in range(2):
                pb, pbb = gb()
                for hh in range(2):
                    h_ = 2 * hp_ + hh
                    for k in range(2):
                        mm(pb[:, hh * 256:(hh + 1) * 256], wkv[:, k, h_ * 128:(h_ + 1) * 128], cb[:, k, :],
                           k == 0, k == 1, [bwkv, cbb], [pbb])
                cp('act', KV['knT'][:, 2 * hp_:2 * hp_ + 2, key0:key0 + 256],
                   pb[:, 0:512].rearrange("p (h n) -> p h n", h=2), [pbb], [bknT])
            for tb_ in range(2):
                pb, pbb = gb()
                for k in range(2):
                    mm(pb[:, 0:512], cb[:, k, tb_ * 128:(tb_ + 1) * 128], wkv[:, k, 512:1024], k == 0, k == 1,
                       [bwkv, cbb], [pbb])
                cp('dve', KV['V'][:, key0 // 128 + tb_, :], pb[:, 0:512], [pbb], [bV])

        def phaseA_sub(s, KV, TA):
            tok0, tok1 = tokrange(s)
            mt = mtype(s)
            sample = s < 8
            key0 = 256 + tok0 if sample else (s - 8) * 256
            wA, h1A = TA['wA'], TA['h1']
            bh1 = bf("h1A")
            norm_tile(tok0, SUB, a1, 0, mt, h1A, bh1, "nA", TA['sq'], TA['tmpx'], TA['lnv'], TA['rstd'])
            pc, pcb = gb()
            for mc in range(2):
                for k in range(8):
                    mm(pc[:, mc * 256:(mc + 1) * 256], wA[:, k, mc * 128:(mc + 1) * 128], h1A[:, k, :], k == 0, k == 7,
                       [bwA, bh1], [pcb])
            pk, pkb = gb()
            nk = 2 if sample else 1
            for q_ in range(nk):
                for k in range(8):
                    mm(pk[0:64, q_ * 256:(q_ + 1) * 256], wA[:, k, 256 + q_ * 64:256 + (q_ + 1) * 64], h1A[:, k, :],
                       k == 0, k == 7, [bwA, bh1], [pkb])
            if sample:
                whp = TA['whp']
                ph, phb = gb()
                for mc in range(2):
                    for side in range(2):
                        c0 = 0 if side == 0 else SUB - 8
                        o0 = mc * 16 + side * 8
                        for k in range(8):
                            mm(ph[:, o0:o0 + 8], whp[:, k, mc * 128:(mc + 1) * 128], h1A[:, k, c0:c0 + 8], k == 0, k == 7,
                               [bwhp, bh1], [phb])
                cp('dve', KV['hph'][:, :, s, :], ph[:, 0:32].rearrange("p (m n) -> p m n", m=2), [phb], [bhph])
            pcv = pc[:, 0:512].rearrange("p (m n) -> p m n", m=2)
            sqk, lnk, rsk = TA['sqk'], TA['lnk'], TA['rsk']
            bsqk, blnk, brsk = bf("sqk"), bf("lnk"), bf("rsk")
            act(sqk[:], pcv, AF.Square, [pcb], [bsqk])
            pq, pqb = gb()
            for k in range(2):
                mm(pq[:, 0:256], ones[:], sqk[:, k, :], k == 0, k == 1, [bones, bsqk], [pqb])
            rstd_from_ps(pq[:, 0:256], 256.0, lnk[:], rsk[:], [pqb], blnk, brsk)
            cn, cnb = TA['ckvn'][s % 2], bf("ckvn%d" % (s % 2))
            if sample:
                for mc in range(2):
                    stt(cn[:, mc, :], pcv[:, mc, :], gkv[:, l, mc:mc + 1], rsk[:], ALU.mult, ALU.mult,
                        [pcb, brsk, bconst], [cnb])
            else:
                ckvst = TA['ckvst']
                bst = bf("ckvst")
                for mc in range(2):
                    stt(ckvst[:, mc, :], pcv[:, mc, :], gkv[:, l, mc:mc + 1], rsk[:], ALU.mult, ALU.mult,
                        [pcb, brsk, bconst], [bst])
                cp('act', cn[:], ckvst[:], [bst], [cnb])
                p0 = (s - 8) * 256
                out_toks.append(S.dma('sp', lambda e, p0=p0, l=l, ckvst=ckvst: e.dma_start(out=ckvst_d[l, :, :, p0:p0 + 256], in_=ckvst[:]),
                                      bst, [bst], []))
            krT = KV['krT']
            if sample:
                rp, rpb = TA['rope'][s % 2], bf("ropeA%d" % (s % 2))
                S.dma('sp', lambda e, rp=rp, tok0=tok0: e.dma_start(out=rp[:, 0, :], in_=cos_d[:, tok0:tok0 + SUB]), rpb, (), [rpb])
                S.dma('sp', lambda e, rp=rp, tok0=tok0: e.dma_start(out=rp[:, 1, :], in_=sin_d[:, tok0:tok0 + SUB]), rpb, (), [rpb])
                kt1, kt2 = TA['kt1'], TA['kt2']
                bk1, bk2 = bf("kt1"), bf("kt2")
                tt('dve', kt1[:], pk[0:64, 0:256], rp[:, 0, :], ALU.mult, [pkb, rpb], [bk1])
                tt('dve', kt2[:], pk[0:64, 256:512], rp[:, 1, :], ALU.mult, [pkb, rpb], [bk2])
                tt('dve', krT[0:64, key0:key0 + 256], kt1[:], kt2[:], ALU.add, [bk1, bk2], [bkrT])
            else:
                krst = TA['krst']
                bks = bf("krst")
                cp('dve', krst[:], pk[0:64, 0:256], [pkb], [bks])
                cp('act', krT[0:64, key0:key0 + 256], krst[:], [bks], [bkrT])
                p0 = (s - 8) * 256
                out_toks.append(S.dma('sp', lambda e, p0=p0, l=l, krst=krst: e.dma_start(out=krst_d[l, :, p0:p0 + 256], in_=krst[:]),
                                      bks, [bks], []))
            kv_proj(KV, TA, cn, cnb, key0)

        cur[0] = scr_base
        KVs = alloc_kv(NKEY, True)
        kv_end = cur[0]
        NA = 512
        wA = alloc([128, 8, 384], BF16, "wA")
        whp = alloc([128, 8, 256], BF16, "whp")
        wkvS = alloc([128, 2, 1024], BF16, "wkv")
        sqS = alloc([128, 8, NA], BF16, "sqS")
        tmpxS = [alloc([128, NA], F32, "tmpxS") for _ in range(2)]
        rstdS = alloc([128, NA], F32, "rstdS")
        h1S = [alloc([128, 8, NA], BF16, "h1S") for _ in range(2)]
        sqkS = alloc([128, 2, NA], BF16, "sqkS")
        rskS = alloc([128, NA], F32, "rskS")
        ckvnS = [alloc([128, 2, NA], BF16, "ckvnS") for _ in range(2)]
        kt1S = alloc([64, NA], F32, "kt1S")
        kt2S = alloc([64, NA], F32, "kt2S")
        ropeS = [alloc([64, 2, NA], F32, "ropeS") for _ in range(2)]
        ckvc = alloc([128, 2, 256], BF16, "ckvc")
        ckvstS = alloc([128, 2, NA], F32, "ckvstS")
        krstS = alloc([64, NA], F32, "krstS")
        TAs = {'wkv': wkvS}
        ld('pool', wA[:], winA_d[l], bwA)
        ld('pool', wkvS[:], wukv_d[l], bwkv)
        ld('pool', whp[:], winB_d[l, :, :, 0:256], bwhp)
        bckvc = bf("ckvc")
        ld('pool', ckvc[:], cckv_d[:, l], bckvc)
        if PADZERO:
            memset('dve', KVs['krT'][64:128, :], 0.0, [bkrT])
        ld('pool', KVs['krT'][0:64, 0:256], ckr_d[:, l, :], bf("krTc"), writes=[bkrT])
        gbank_list[:] = [0, 1, 2, 3, 4, 5, 6, 7]
        kv_proj(KVs, TAs, ckvc, bckvc, 0)

        def sa_norm(t):
            norm_tile(t * NA, NA, a1, 0, (0 if t < 4 else 1), h1S[t % 2], bf("h1S%d" % (t % 2)), "nS", sqS, tmpxS, rstdS, rstdS)

        def sa_proj(t):
            h1, bh1 = h1S[t % 2], bf("h1S%d" % (t % 2))
            st = {}
            pcs = []
            for mc in range(2):
                pc, pcb = gb()
                for k in range(8):
                    mm(pc[:, 0:NA], wA[:, k, mc * 128:(mc + 1) * 128], h1[:, k, :], k == 0, k == 7, [bwA, bh1], [pcb])
                pcs.append((pc, pcb))
            pks = []
            for q_ in range(2 if t < 4 else 1):
                pk, pkb = gb()
                for k in range(8):
                    mm(pk[0:64, 0:NA], wA[:, k, 256 + q_ * 64:256 + (q_ + 1) * 64], h1[:, k, :], k == 0, k == 7,
                       [bwA, bh1], [pkb])
                pks.append((pk, pkb))
            if t == 4:
                return pcs, pks
            ph, phb = gb()
            for sub in range(2):
                for mc in range(2):
                    for side in range(2):
                        c0 = sub * 256 + (0 if side == 0 else SUB - 8)
                        o0 = sub * 32 + mc * 16 + side * 8
                        for k in range(8):
                            mm(ph[:, o0:o0 + 8], whp[:, k, mc * 128:(mc + 1) * 128], h1[:, k, c0:c0 + 8], k == 0, k == 7,
                               [bwhp, bh1], [phb])
            for sub in range(2):
                cp('dve', KVs['hph'][:, :, 2 * t + sub, :],
                   ph[:, sub * 32:(sub + 1) * 32].rearrange("p (m n) -> p m n", m=2), [phb], [bhph])
            return pcs, pks

        def sa_chain(t, pcs, pks):
            tok0 = t * NA
            key0 = 256 + tok0
            bsqk, brsk = bf("sqkS"), bf("rskS")
            for mc in range(2):
                act(sqkS[:, mc, :], pcs[mc][0][:, 0:NA], AF.Square, [pcs[mc][1]], [bsqk])
            pq, pqb = gb()
            for k in range(2):
                mm(pq[:, 0:NA], ones[:], sqkS[:, k, :], k == 0, k == 1, [bones, bsqk], [pqb])
            rstd_from_ps(pq[:, 0:NA], 256.0, rskS[:], rskS[:], [pqb], brsk, brsk)
            cn, cnb = ckvnS[t % 2], bf("ckvnS%d" % (t % 2))
            if t < 4:
                for mc in range(2):
                    stt(cn[:, mc, :], pcs[mc][0][:, 0:NA], gkv[:, l, mc:mc + 1], rskS[:], ALU.mult, ALU.mult,
                        [pcs[mc][1], brsk, bconst], [cnb])
                rp, rpb = ropeS[t % 2], bf("ropeS%d" % (t % 2))
                S.dma('sp', lambda e, rp=rp, tok0=tok0: e.dma_start(out=rp[:, 0, :], in_=cos_d[:, tok0:tok0 + NA]), rpb, (), [rpb])
                S.dma('sp', lambda e, rp=rp, tok0=tok0: e.dma_start(out=rp[:, 1, :], in_=sin_d[:, tok0:tok0 + NA]), rpb, (), [rpb])
                bk1, bk2 = bf("kt1S"), bf("kt2S")
                tt('dve', kt1S[:], pks[0][0][0:64, 0:NA], rp[:, 0, :], ALU.mult, [pks[0][1], rpb], [bk1])
                tt('dve', kt2S[:], pks[1][0][0:64, 0:NA], rp[:, 1, :], ALU.mult, [pks[1][1], rpb], [bk2])
                tt('dve', KVs['krT'][0:64, key0:key0 + NA], kt1S[:], kt2S[:], ALU.add, [bk1, bk2], [bkrT])
            else:
                bst, bks = bf("ckvstS"), bf("krstS")
                for mc in range(2):
                    stt(ckvstS[:, mc, :], pcs[mc][0][:, 0:NA], gkv[:, l, mc:mc + 1], rskS[:], ALU.mult, ALU.mult,
                        [pcs[mc][1], brsk, bconst], [bst])
                cp('act', cn[:], ckvstS[:], [bst], [cnb])
                out_toks.append(S.dma('sp', lambda e, l=l, ckvstS=ckvstS: e.dma_start(out=ckvst_d[l], in_=ckvstS[:]), bst, [bst], []))
                cp('dve', krstS[:], pks[0][0][0:64, 0:NA], [pks[0][1]], [bks])
                cp('act', KVs['krT'][0:64, key0:key0 + NA], krstS[:], [bks], [bkrT])
                out_toks.append(S.dma('sp', lambda e, l=l, krstS=krstS: e.dma_start(out=krst_d[l], in_=krstS[:]), bks, [bks], []))
            for h_ in range(4):
                pb, pbb = gb()
                for k in range(2):
                    mm(pb[:, 0:NA], wkvS[:, k, h_ * 128:(h_ + 1) * 128], cn[:, k, :], k == 0, k == 1, [bwkv, cnb], [pbb])
                cp('act', KVs['knT'][:, h_, key0:key0 + NA], pb[:, 0:NA], [pbb], [bknT])
            for tb_ in range(4):
                pb, pbb = gb()
                for k in range(2):
                    mm(pb[:, 0:512], cn[:, k, tb_ * 128:(tb_ + 1) * 128], wkvS[:, k, 512:1024], k == 0, k == 1,
                       [bwkv, cnb], [pbb])
                cp('dve', KVs['V'][:, key0 // 128 + tb_, :], pb[:, 0:512], [pbb], [bV])

        sa_norm(0)
        for t in range(5):
            pcs, pks = sa_proj(t)
            if t + 1 < 5:
                sa_norm(t + 1)
            sa_chain(t, pcs, pks)
        gbank_list[:] = [0, 1, 2, 3]
        S.barrier()

        cur[0] = kv_end
        wB = alloc([128, 8, 1152], BF16, "wB")
        wq = alloc([128, 3, 1024], BF16, "wq")
        wo = alloc([128, 8, 1024], BF16, "wo")
        wpl = alloc([128, 2, 128], BF16, "wpl")
        wsT = alloc([128, 4, 128], BF16, "wsT")
        bsT = alloc([128, 2, 128], BF16 if BST_BF16 else F32, "bsT")
        gsg = alloc([128, 256], BF16, "gsg")
        yT = alloc([128, 8, SUB], BF16, "yT")
        sqB = None
        wmn = alloc([128, 2, SUB], F32, "wmn")
        tmpxB = [wmn[:, 0, :], wmn[:, 1, :]]
        rstdB = alloc([128, SUB], F32, "rstdB")
        lnvB = rstdB
        lnq, rsq = lnvB, rstdB
        h1B = alloc([128, 8, SUB], BF16, "h1B")
        P0 = alloc([128, 2, SUB + 16], F32, "P0")
        sqq = alias(P0, [128, 3, SUB], BF16, "sqq")
        S2 = alloc([128, 2, SUB + 16], F32, "S2")
        S4 = alloc([128, 2, SUB + 16], F32, "S4")
        pooled = alloc([128, 2, SUB], BF16, "pooled")
        cqg = alloc([128, 3, SUB], BF16, "cqg")
        qnT = alloc([128, 4, SUB], BF16, "qnT")
        qrT = alloc([128, 4, SUB], BF16, "qrT")
        ropeB = alloc([64, 2, SUB], F32, "ropeB")
        qt1f = alloc([128, SUB], F32, "qt1")
        qt1 = qt1f[0:64, :]
        qt2 = alloc([64, SUB], F32, "qt2")
        ug = pooled
        vss = alloc([128, 2], F32, "vss")
        vln = alloc([128, 2], F32, "vln")
        vrs = alloc([128, 2], F32, "vrs")
        vgA = alloc([128, 2, 256], BF16, "vgA")
        vgB = alloc([128, 2, 256], BF16, "vgB")
        gt = S2[:, :, 0:SUB]
        vt = S4[:, :, 0:SUB]
        PT = [alloc([128, 512], BF16, "PT") for _ in range(2)]
        denr = qt1f
        NPT = len(PT)

        bwB, bwq, bwo, bwsm = bf("wB"), bf("wq"), bf("wo"), bf("wsm")
        ld('pool', wB[:], winB_d[l], bwB)
        ld('pool', wq[:], wuq_d[l], bwq)
        memset('dve', wpl[:], 0.0, [bwsm])
        for g in range(4):
            c_, hf = g // 2, g % 2
            S.dma('pool', lambda e, g=g, c_=c_, hf=hf, l=l, wpl=wpl: e.dma_start(out=wpl[hf * 64:(hf + 1) * 64, c_, hf * 64:(hf + 1) * 64],
                                                                  in_=wpool_d[l, g]), bwsm, (), [bwsm])
        ld('pool', wsT[:], wsT_d[l], bwsm)
        if BST_BF16:
            S.dma('pool', lambda e, l=l, bsT=bsT: e.dma_start(out=bsT[:], in_=bsT_d[l]), bwsm, (), [bwsm])
        else:
            S.dma('sp', lambda e, l=l, bsT=bsT: e.dma_start(out=bsT[:], in_=bsT_d[l]), bf("gsgk"), (), [bf("gsgk"), bwsm])
        S.dma('pool', lambda e, l=l, gsg=gsg: e.dma_start(out=gsg[:], in_=gsgu_d[l]), bf("gsgk"), (), [bf("gsgk")])
        ld('pool', wo[:], wout_d[l], bwo)
        bvgA, bvgB = bf("vgA"), bf("vgB")
        memset('dve', vgA[:], 0.0, [bvgA])
        memset('dve', vgB[:], 0.0, [bvgB])
        if PADZERO:
            memset('dve', qrT[64:128, :, :], 0.0, [bf("qrT")])

        def normB(s):
            tok0, tok1 = tokrange(s)
            norm_tile(tok0, SUB, a1, 0, mtype(s), h1B, bf('h1B'), 'nB', h1B, tmpxB, lnvB, rstdB, bsq=bf('h1B'),
                      tmpx_extra=[bf('wmn')])

        def phaseB_sub(s, KV, prenormed=False, next_s=None):
            tok0, tok1 = tokrange(s)
            mt = mtype(s)
            sample = s < 8
            knT, krT, V = KV['knT'], KV['krT'], KV['V']
            bh1 = bf("h1B")
            byT = bf("yT")
            bwm = bf("wmn")
            blnq, brsq = bf("nBrs"), bf("nBrs")
            if not prenormed:
                normB(s)
            tbx = [bf("nBtmpx0"), bf("nBtmpx1")]
            pcq0, pcq0b = gb()
            for mc in range(2):
                for k in range(8):
                    mm(pcq0[:, mc * 256:(mc + 1) * 256], wB[:, k, 256 + mc * 128:256 + (mc + 1) * 128], h1B[:, k, :],
                       k == 0, k == 7, [bwB, bh1], [pcq0b])
            pcq1, pcq1b = gb()
            for k in range(8):
                mm(pcq1[:, 0:256], wB[:, k, 512:640], h1B[:, k, :], k == 0, k == 7, [bwB, bh1], [pcq1b])
            bsqq, bcqg = bf("P0"), bf("cqg")
            act(sqq[:, 0:2, :], pcq0[:, 0:512].rearrange("p (m n) -> p m n", m=2), AF.Square, [pcq0b], [bsqq])
            act(sqq[:, 2, :], pcq1[:, 0:256], AF.Square, [pcq1b], [bsqq])
            for mc in range(3):
                src = pcq0[:, mc * 256:(mc + 1) * 256] if mc < 2 else pcq1[:, 0:256]
                ts('dve', cqg[:, mc, :], src, gq[:, l, mc:mc + 1], None, ALU.mult, None,
                   [pcq0b if mc < 2 else pcq1b, bconst], [bcqg])
            pss, pssb = gb()
            for k in range(3):
                mm(pss[:, 0:256], ones[:], sqq[:, k, :], k == 0, k == 2, [bones, bsqq], [pssb])
            rstd_from_ps(pss[:, 0:256], 384.0, lnq[:], rsq[:], [pssb], blnq, brsq)
            yield 'cq'
            bqn, bqr = bf("qnT"), bf("qrT")
            for hp_ in range(2):
                pq, pqb = gb()
                for hh in range(2):
                    h_ = 2 * hp_ + hh
                    for k in range(3):
                        mm(pq[:, hh * 256:(hh + 1) * 256], wq[:, k, h_ * 128:(h_ + 1) * 128], cqg[:, k, :], k == 0, k == 2,
                           [bwq, bcqg], [pqb])
                tt('dve', qnT[:, 2 * hp_:2 * hp_ + 2, :], pq[:, 0:512].rearrange("p (h n) -> p h n", h=2),
                   rsq[:].unsqueeze(1).broadcast_to([128, 2, 256]), ALU.mult, [pqb, brsq], [bqn])
            brp2 = bf("ropeB2")
            if sample:
                brp = bf("ropeB")
                S.dma('sp', lambda e, tok0=tok0, ropeB=ropeB: e.dma_start(out=ropeB[:, 0, :], in_=cos_d[:, tok0:tok0 + SUB]), brp, (), [brp, brp2])
                S.dma('sp', lambda e, tok0=tok0, ropeB=ropeB: e.dma_start(out=ropeB[:, 1, :], in_=sin_d[:, tok0:tok0 + SUB]), brp, (), [brp, brp2])
                tt('dve', ropeB[:, 0, :], ropeB[:, 0, :], rsq[0:64, :], ALU.mult, [brp, brsq], [brp2])
                tt('dve', ropeB[:, 1, :], ropeB[:, 1, :], rsq[0:64, :], ALU.mult, [brp, brsq, brp2], [brp2])
            for h_ in range(4):
                pq, pqb = gb()
                nq = 2 if sample else 1
                for q_ in range(nq):
                    c0 = 512 + q_ * 256 + h_ * 64
                    for k in range(3):
                        mm(pq[0:64, q_ * 256:(q_ + 1) * 256], wq[:, k, c0:c0 + 64], cqg[:, k, :], k == 0, k == 2,
                           [bwq, bcqg], [pqb])
                if sample:
                    bq1, bq2 = bf("qt1"), bf("qt2")
                    tt('dve', qt1[:], pq[0:64, 0:256], ropeB[:, 0, :], ALU.mult, [pqb, brp2], [bq1])
                    tt('dve', qt2[:], pq[0:64, 256:512], ropeB[:, 1, :], ALU.mult, [pqb, brp2], [bq2])
                    tt('dve', qrT[0:64, h_, :], qt1[:], qt2[:], ALU.add, [bq1, bq2], [bqr])
                else:
                    tt('dve', qrT[0:64, h_, :], pq[0:64, 0:256], rsq[0:64, :], ALU.mult, [pqb, brsq], [bqr])

            yield 'q'

            def rest_front():
                php, phpb = gb()
                for mc in range(2):
                    for k in range(8):
                        mm(php[:, mc * 256:(mc + 1) * 256], wB[:, k, mc * 128:(mc + 1) * 128], h1B[:, k, :], k == 0, k == 7,
                           [bwB, bh1], [phpb])
                bP0, bS2, bS4, bpl = bf("P0"), bf("S2"), bf("S4"), bf("pooled")
                cp('dve', P0[:, :, 8:8 + SUB], php[:, 0:512].rearrange("p (m n) -> p m n", m=2), [phpb], [bP0])
                if sample and s > 0:
                    cp('dve', P0[:, :, 0:8], KV['hph'][:, :, s - 1, 8:16], [bhph], [bP0])
                else:
                    memset('dve', P0[:, :, 0:8], 0.0, [bP0])
                if sample and s < 7:
                    cp('dve', P0[:, :, 8 + SUB:16 + SUB], KV['hph'][:, :, s + 1, 0:8], [bhph], [bP0])
                else:
                    memset('dve', P0[:, :, 8 + SUB:16 + SUB], 0.0, [bP0])
                W_ = SUB + 16
                wmw = [bwm] + tbx
                tt('dve', S2[:, :, 1:W_], P0[:, :, 1:W_], P0[:, :, 0:W_ - 1], ALU.add, [bP0], [bS2])
                tt('dve', S4[:, :, 3:W_], S2[:, :, 3:W_], S2[:, :, 1:W_ - 2], ALU.add, [bS2], [bS4])
                ts('dve', wmn[0:64, 0, :], S2[0:64, 0, 8:8 + SUB], invw[0:64, 0:1], None, ALU.mult, None, [bS2, bconst], wmw)
                ts('dve', wmn[64:128, 0, :], S4[64:128, 0, 9:9 + SUB], invw[64:128, 0:1], None, ALU.mult, None, [bS4, bconst], wmw)
                bS8 = bf("S8")
                tt('dve', S2[:, 1, 7:W_], S4[:, 1, 7:W_], S4[:, 1, 3:W_ - 4], ALU.add, [bS4, bwm, bS2], [bS8, bS2])
                ts('dve', wmn[0:64, 1, :], S2[0:64, 1, 11:11 + SUB], invw[0:64, 1:2], None, ALU.mult, None, [bS8, bconst], wmw)
                bS16 = bf("S16")
                tt('dve', S4[64:128, 1, 15:W_], S2[64:128, 1, 15:W_], S2[64:128, 1, 7:W_ - 8], ALU.add, [bS8, bwm, bS4], [bS16, bS4])
                ts('dve', wmn[64:128, 1, :], S4[64:128, 1, 15:15 + SUB], invw[64:128, 1:2], None, ALU.mult, None,
                   [bS16, bconst], wmw)
                seq_start = (s == 0) or not sample
                seq_end = (s == 7) or not sample
                if seq_start:
                    tt('dve', wmn[:, :, 0:8], wmn[:, :, 0:8], corrL[:], ALU.mult, [bwm, bconst], wmw)
                if seq_end:
                    tt('dve', wmn[:, :, SUB - 8:SUB], wmn[:, :, SUB - 8:SUB], corrR[:], ALU.mult, [bwm, bconst], wmw)
                tt('dve', pooled[:], wmn[:], P0[:, :, 8:8 + SUB], ALU.subtract, [bwm, bP0], [bpl])
                ppl, pplb = gb()
                for mc in range(2):
                    mm(ppl[:, mc * 256:(mc + 1) * 256], wpl[:, mc, :], pooled[:, mc, :], True, True, [bwsm, bpl], [pplb])
                for mc in range(2):
                    ts('dve', yT[:, mc, :], ppl[:, mc * 256:(mc + 1) * 256], pscale[:, l, mc:mc + 1], None, ALU.mult, None,
                       [pplb, bconst], [byT])
                pu, pub = gb()
                for mc in range(2):
                    for k in range(8):
                        mm(pu[:, mc * 256:(mc + 1) * 256], wB[:, k, 640 + mc * 128:640 + (mc + 1) * 128], h1B[:, k, :],
                           k == 0, k == 7, [bwB, bh1], [pub])
                bug = bf("pooled")
                act(ug[:], pu[:, 0:512].rearrange("p (m n) -> p m n", m=2), AF.Gelu_apprx_tanh, [pub], [bug])
                pv, pvb = gb()
                for blk in range(2):
                    for k in range(8):
                        mm(pv[:, blk * 256:(blk + 1) * 256], h1B[:, k, blk * 128:(blk + 1) * 128], wB[:, k, 896:1152],
                           k == 0, k == 7, [bwB, bh1], [pvb])
                bvt, bvss, bvrs, bgt = bf("S4"), bf("vss"), bf("vrs"), bf("S2")
                act(vt[:], pv[:, 0:512].rearrange("p (m n) -> p m n", m=2), AF.Gelu_apprx_tanh, [pvb], [bvt, bf("S16")])
                for blk in range(2):
                    S.op('act', lambda e, blk=blk, gt=gt, vt=vt, vss=vss: e.activation(out=gt[:, 0, :], in_=vt[:, blk, :], func=AF.Square,
                                                                accum_out=vss[:, blk:blk + 1]), [bvt], [bvss, bgt, bf("S8")])
                act(vln[:], vss[:], AF.Ln, [bvss], [bf("vln")], scale=1.0 / 256.0, bias=EPS)
                act(vrs[:], vln[:], AF.Exp, [bf("vln")], [bvrs], scale=-0.5)
                for blk in range(2):
                    for hf, (vg, vgb) in enumerate([(vgA, bvgA), (vgB, bvgB)]):
                        o_ = vg[:, blk, :].rearrange("p (m h c) -> p m h c", m=2, h=2)[:, :, hf, :]
                        i0 = vt[:, blk, :].rearrange("p (m h c) -> p m h c", m=2, h=2)[:, :, hf, :]
                        i1 = gsg[:].rearrange("p (m h c) -> p m h c", m=2, h=2)[:, :, hf, :]
                        stt(o_, i0, vrs[:, blk:blk + 1], i1, ALU.mult, ALU.mult, [bvt, bvrs, bf("gsgk")], [vgb])
                pg, pgb = gb()
                for mc in range(2):
                    for blk in range(2):
                        o_ = pg[:, mc * 256 + blk * 128: mc * 256 + (blk + 1) * 128]
                        mm(o_, vgA[:, blk, mc * 128:(mc + 1) * 128], wsT[:, 2 * mc, :], True, False, [bvgA, bwsm], [pgb])
                        mm(o_, vgB[:, blk, mc * 128:(mc + 1) * 128], wsT[:, 2 * mc + 1, :], False, True, [bvgB, bwsm], [pgb])
                for mc in range(2):
                    tt('dve', gt[:, mc, :].rearrange("p (b n) -> p b n", b=2),
                       pg[:, mc * 256:(mc + 1) * 256].rearrange("p (b n) -> p b n", b=2),
                       bsT[:, mc, :].unsqueeze(1).broadcast_to([128, 2, 128]), ALU.add, [pgb, bwsm], [bgt, bf("S8")])
                tt('dve', yT[:, 6:8, :], gt[:], ug[:], ALU.mult, [bgt, bug], [byT])

            if sample:
                kbase, nkc = 0, 18
            else:
                kbase, nkc = 2304 + (s - 8) * 256, 2
            npair = nkc // 2
            pairs = [(h_, jp) for h_ in range(4) for jp in range(npair)]
            st_ = {}

            def emit_scores(i):
                h_, jp = pairs[i]
                psc, pscb = (ps[7], psb[7]) if i % 2 == 0 else gb()
                for sub in range(2):
                    j = 2 * jp + sub
                    k0 = kbase + j * 128
                    mm(psc[:, sub * 256:(sub + 1) * 256], knT[:, h_, k0:k0 + 128], qnT[:, h_, :], True, False,
                       [bknT, bqn], [pscb])
                    if ROPE128:
                        mm(psc[:, sub * 256:(sub + 1) * 256], krT[:, k0:k0 + 128], qrT[:, h_, :], False, True,
                           [bkrT, bqr], [pscb])
                    else:
                        mm(psc[:, sub * 256:(sub + 1) * 256], krT[0:64, k0:k0 + 128], qrT[0:64, h_, :], False, True,
                           [bkrT, bqr], [pscb])
                pt, ptb = PT[i % NPT], bf("PT%d" % (i % NPT))
                act(pt[:], psc[:, 0:512], AF.Exp, [pscb], [ptb], scale=SCALE)
                st_[i] = (pt, ptb)

            def emit_pv(i):
                h_, jp = pairs[i]
                pt, ptb = st_.pop(i)
                po, pob = ps[4 + (h_ % 2)], psb[4 + (h_ % 2)]
                for sub in range(2):
                    j = 2 * jp + sub
                    kc = (kbase // 128) + j
                    mm(po[:, 0:256], V[:, kc, h_ * 128:(h_ + 1) * 128], pt[:, sub * 256:(sub + 1) * 256],
                       j == 0, j == nkc - 1, [bV, ptb], [pob], skip=True)
                    mm(po[:, 256:512], ones[:], pt[:, sub * 256:(sub + 1) * 256], False, j == nkc - 1,
                       [bones, ptb], [pob], skip=True)
                if jp == npair - 1:
                    bdr = bf("qt1")
                    S.op('dve', lambda e, po=po, denr=denr: e.reciprocal(out=denr[:], in_=po[:, 256:512]), [pob], [bdr])
                    tt('dve', yT[:, 2 + h_, :], po[:, 0:256], denr[:], ALU.mult, [pob, bdr], [byT])

            rest_front()
            emit_scores(0)
            for i in range(len(pairs)):
                if i + 1 < len(pairs):
                    emit_scores(i + 1)
                emit_pv(i)
                if i == npair - 1 and next_s is not None:
                    normB(next_s)
            yield 'att'
            for mp in range(4):
                pw, pwb = gb()
                for mh in range(2):
                    m_ = 2 * mp + mh
                    for k in range(8):
                        mm(pw[:, mh * 256:(mh + 1) * 256], wo[:, k, m_ * 128:(m_ + 1) * 128], yT[:, k, :], k == 0, k == 7,
                           [bwo, byT], [pwb])
                for mh in range(2):
                    m_ = 2 * mp + mh
                    stt(x[:, m_, tok0:tok1], pw[:, mh * 256:(mh + 1) * 256], mod[:, 16 + m_, mt:mt + 1], x[:, m_, tok0:tok1],
                        ALU.mult, ALU.add, [pwb, bmod, bx], [bx])

        gbank_list[:] = [0, 1, 2, 3, 6]
        gens = {s: phaseB_sub(s, KVs, prenormed=(s > 0), next_s=(s + 1 if s < NSUB - 1 else None)) for s in range(NSUB)}
        next(gens[0])
        next(gens[0])
        for s in range(NSUB):
            next(gens[s])
            if s + 1 < NSUB:
                next(gens[s + 1])
            for _ in gens[s]:
                pass
            if s + 1 < NSUB:
                next(gens[s + 1])
        gbank_list[:] = [0, 1, 2, 3]
        S.barrier()

        cur[0] = scr_base
        h2 = alloc([128, 8, T], BF16, "h2")
        wslot = [(alloc([128, 8, 512], BF16, "w1s"), alloc([128, 4, 1024], BF16, "w2s")) for _ in range(2)]
        sqF = alloc([128, 8, 512], BF16, "sqF")
        tmpxF = [alloc([128, 512], F32, "tmpxF") for _ in range(2)]
        lnvF = alloc([128, 512], F32, "lnvF")
        rstdF = alloc([128, 512], F32, "rstdF")
        rl = [alloc([128, 512], F32, "rl") for _ in range(2)]
        hid = [alloc([128, 4, 512], BF16, "hid") for _ in range(2)]
        ybuf = alloc([128, 8, 512], F32, "ybuf") if l == L - 1 else None
        bh2 = [bf("h2_%d" % t) for t in range(5)]

        def load_w(j):
            w1s, w2s = wslot[j % 2]
            kb1, kb2 = bf("w1s%d" % (j % 2)), bf("w2s%d" % (j % 2))
            ld('pool', w1s[:], w1_d[l, j], kb1)
            ld('pool', w2s[:], w2_d[l, j], kb2)

        gbank_list[:] = [0, 1, 2, 3, 4, 5, 6]
        load_w(0)
        nxt = l + 1 < L
        if nxt:
            ringF = [alloc([128, 8, 128], BF16, "adarF") for _ in range(6)]
            ringFb = [bf("adarF%d" % i) for i in range(6)]

        def norm2(t):
            mt = 0 if t < 4 else 1
            norm_tile(t * 512, 512, a2, 24, mt, h2[:, :, t * 512:(t + 1) * 512], bh2[t], "nF", sqF, tmpxF, lnvF, rstdF)

        def ff1(i, j, t):
            w1s, kb1 = wslot[j % 2][0], bf("w1s%d" % (j % 2))
            hd, hdb = hid[i % 2], bf("hid%d" % (i % 2))
            for m_ in range(4):
                pb, pbb = gb()
                for k in range(8):
                    mm(pb[:, 0:512], w1s[:, k, m_ * 128:(m_ + 1) * 128], h2[:, k, t * 512:(t + 1) * 512], k == 0, k == 7,
                       [kb1, bh2[t]], [pbb])
                r_, rb_ = rl[m_ % 2], bf("rl%d" % (m_ % 2))
                act(r_[:], pb[:, 0:512], AF.Relu, [pbb], [rb_])
                act(hd[:, m_, :], r_[:], AF.Square, [rb_], [hdb])

        def ff2(i, j, t):
            w2s, kb2 = wslot[j % 2][1], bf("w2s%d" % (j % 2))
            hd, hdb = hid[i % 2], bf("hid%d" % (i % 2))
            mt = 0 if t < 4 else 1
            for m_ in range(8):
                pb, pbb = gb()
                for k in range(4):
                    mm(pb[:, 0:512], w2s[:, k, m_ * 128:(m_ + 1) * 128], hd[:, k, :], k == 0, k == 3, [kb2, hdb], [pbb])
                stt(x[:, m_, t * 512:(t + 1) * 512], pb[:, 0:512], mod[:, 40 + m_, mt:mt + 1],
                    x[:, m_, t * 512:(t + 1) * 512], ALU.mult, ALU.add, [pbb, bmod, bx], [bx])

        steps = [(j, t) for j in range(NJ) for t in range(5)]
        load_w(1)
        if nxt:
            emit_mod_dma(l + 1, ringF, ringFb, 0, 6)
        norm2(0)
        norm2(1)
        ff1(0, 0, 0)
        for i, (j, t) in enumerate(steps):
            if i + 1 < len(steps):
                ff1(i + 1, *steps[i + 1])
            ff2(i, j, t)
            if j == 0 and t + 2 < 5:
                norm2(t + 2)
            if t == 4:
                if j + 2 < NJ:
                    load_w(j + 2)
                if nxt:
                    emit_mod_mm(l + 1, ringF, ringFb, j * 6, (j + 1) * 6)
                    if j + 1 < NJ:
                        emit_mod_dma(l + 1, ringF, ringFb, (j + 1) * 6, (j + 2) * 6)
        if nxt:
            finish_mod(l + 1)
        if l == L - 1:
            for t in range(5):
                bsq, bln, brs, byb = bf("fsq"), bf("fln"), bf("frs"), bf("ybuf")
                act(sqF[:], x[:, :, t * 512:(t + 1) * 512], AF.Square, [bx], [bsq])
                pb, pbb = gb()
                for k in range(8):
                    mm(pb[:, 0:512], ones[:], sqF[:, k, :], k == 0, k == 7, [bones, bsq], [pbb])
                rstd_from_ps(pb[:, 0:512], float(D), lnvF[:], rstdF[:], [pbb], bln, brs)
                for c in range(8):
                    stt(ybuf[:, c, :], x[:, c, t * 512:(t + 1) * 512], gfinal[:, c:c + 1], rstdF[:], ALU.mult, ALU.mult,
                        [bx, brs, bconst], [byb])
                out_toks.append(S.dma('sp', lambda e, t=t, ybuf=ybuf: e.dma_start(out=yT_d[:, :, t * 512:(t + 1) * 512], in_=ybuf[:]),
                                      byb, [byb], []))
        gbank_list[:] = [0, 1, 2, 3]
        S.barrier()

    with ExitStack() as st:
        with nc.Block() as block:
            S.emit(nc, block, st, final_waits=out_toks)
    return nc


def _pc(a, nchunk):
    return np.ascontiguousarray(a.reshape((nchunk, 128) + a.shape[1:]).swapaxes(0, 1))


def _vecs(g, nchunk):
    Ln = g.shape[0]
    return np.ascontiguousarray(g.reshape(Ln, nchunk, 128).transpose(2, 0, 1))


def _rope_tables():
    rows = TS // 64
    row = np.repeat(np.arange(rows, dtype=np.float32), 64)
    col = np.tile(np.arange(64, dtype=np.float32), rows)
    inv = (1.0 / (10000.0 ** (np.arange(16, dtype=np.float32) / 16))).astype(np.float32)
    ang = np.concatenate([row[:, None] * inv, col[:, None] * inv], axis=-1).astype(np.float32)
    cos, sin = np.cos(ang).astype(np.float32), np.sin(ang).astype(np.float32)
    cosT = np.concatenate([cos, cos], axis=1).T
    sinT = np.concatenate([-sin, sin], axis=1).T
    return np.ascontiguousarray(cosT), np.ascontiguousarray(sinT)


def _pool_tables():
    wins = (2, 4, 8, 16)
    invw = np.zeros((128, 2), np.float32)
    corrL = np.ones((128, 2, 8), np.float32)
    corrR = np.ones((128, 2, 8), np.float32)
    Ls = 256
    for g, w in enumerate(wins):
        c, hf = g // 2, g % 2
        sl = slice(hf * 64, (hf + 1) * 64)
        invw[sl, c] = 1.0 / w
        for i in range(8):
            t = i
            cnt = min(t + (w - w // 2), Ls) - max(t - w // 2, 0)
            corrL[sl, c, i] = w / cnt
            t = Ls - 8 + i
            cnt = min(t + (w - w // 2), Ls) - max(t - w // 2, 0)
            corrR[sl, c, i] = w / cnt
    return invw, corrL, corrR


def prepare_inputs(x_prompt, x_sample, cache_ckv, cache_krope, c, c_ctx, w_ada, b_ada, g_mix,
                   w_in, w_pool, pool_scale, g_q, w_uq, g_kv, w_ukv, g_sgu, w_s, b_s, w_out,
                   g_ffn, w_ff1, w_ff2, g_final):
    f = lambda a: np.asarray(a, dtype=np.float32)
    x_prompt, x_sample, cache_ckv, cache_krope, c, c_ctx = map(f, (x_prompt, x_sample, cache_ckv, cache_krope, c, c_ctx))
    w_ada, b_ada, g_mix, w_in, w_pool, pool_scale, g_q, w_uq, g_kv, w_ukv = map(
        f, (w_ada, b_ada, g_mix, w_in, w_pool, pool_scale, g_q, w_uq, g_kv, w_ukv))
    g_sgu, w_s, b_s, w_out, g_ffn, w_ff1, w_ff2, g_final = map(f, (g_sgu, w_s, b_s, w_out, g_ffn, w_ff1, w_ff2, g_final))
    Ln = w_in.shape[0]
    OFF_Q, OFF_KV, OFF_R, OFF_G = 256, 640, 896, 960
    shared = {}
    shared["wada"] = np.ascontiguousarray(w_ada.reshape(Ln, 8, 128, 48, 128).transpose(0, 3, 2, 1, 4))
    shared["bada"] = _vecs(b_ada, 48)
    shared["gmix"] = _vecs(g_mix, 8)
    shared["gffn"] = _vecs(g_ffn, 8)
    shared["gq"] = _vecs(g_q, 3)
    shared["gkv"] = _vecs(g_kv, 2)
    shared["gsgu"] = np.ascontiguousarray(np.broadcast_to(g_sgu[:, None, :], (Ln, 128, 256)))
    shared["pscale"] = _vecs(pool_scale, 2)
    shared["gfinal"] = np.ascontiguousarray(g_final.reshape(8, 128).T)
    kr = w_in[:, :, OFF_R:OFF_G]
    kr_swp = np.concatenate([kr[:, :, 32:64], kr[:, :, 0:32]], axis=2)
    winA = np.concatenate([w_in[:, :, OFF_KV:OFF_R], kr, kr_swp], axis=2)
    shared["winA"] = np.ascontiguousarray(winA.reshape(Ln, 8, 128, 384).transpose(0, 2, 1, 3))
    winB = np.concatenate([w_in[:, :, 0:OFF_Q], w_in[:, :, OFF_Q:OFF_KV], w_in[:, :, OFF_G:OFF_G + 512]], axis=2)
    shared["winB"] = np.ascontiguousarray(winB.reshape(Ln, 8, 128, 1152).transpose(0, 2, 1, 3))
    wq4 = w_uq.reshape(Ln, 384, 4, 192)
    qn = wq4[:, :, :, 0:128].reshape(Ln, 384, 512)
    qr = wq4[:, :, :, 128:192]
    qr_swp = np.concatenate([qr[..., 32:64], qr[..., 0:32]], axis=-1)
    wuq = np.concatenate([qn, qr.reshape(Ln, 384, 256), qr_swp.reshape(Ln, 384, 256)], axis=2)
    shared["wuq"] = np.ascontiguousarray(wuq.reshape(Ln, 3, 128, 1024).transpose(0, 2, 1, 3))
    wkv4 = w_ukv.reshape(Ln, 256, 4, 256)
    wukv = np.concatenate([wkv4[..., 0:128].reshape(Ln, 256, 512), wkv4[..., 128:256].reshape(Ln, 256, 512)], axis=2)
    shared["wukv"] = np.ascontiguousarray(wukv.reshape(Ln, 2, 128, 1024).transpose(0, 2, 1, 3))
    shared["wout"] = np.ascontiguousarray(w_out.reshape(Ln, 8, 128, 1024).transpose(0, 2, 1, 3))
    shared["wpool"] = np.ascontiguousarray(w_pool)
    shared["wsT"] = np.ascontiguousarray(w_s.transpose(0, 3, 1, 2))
    bsT = np.broadcast_to(b_s.reshape(Ln, 2, 2, 1, 128), (Ln, 2, 2, 64, 128))
    shared["bsT"] = np.ascontiguousarray(bsT.reshape(Ln, 2, 128, 128).transpose(0, 2, 1, 3))
    shared["w1"] = np.ascontiguousarray(w_ff1.reshape(Ln, 8, 128, NJ, 512).transpose(0, 3, 2, 1, 4))
    shared["w2"] = np.ascontiguousarray(w_ff2.reshape(Ln, NJ, 4, 128, 1024).transpose(0, 1, 3, 2, 4))
    cosT, sinT = _rope_tables()
    shared["cosT"], shared["sinT"] = cosT, sinT
    shared["invw"], shared["corrL"], shared["corrR"] = _pool_tables()
    in_maps = []
    for i in range(8):
        m = dict(shared)
        xs = np.concatenate([x_sample[i], x_prompt[2 * i], x_prompt[2 * i + 1]], axis=0)
        m["xT"] = _pc(np.ascontiguousarray(xs.T), 8)
        cc = np.stack([c[i], c_ctx], axis=1)
        m["cT"] = _pc(cc, 8)
        ck = cache_ckv[i].transpose(2, 0, 1)
        m["cckv"] = np.ascontiguousarray(ck.reshape(2, 128, Ln, 256).transpose(1, 2, 0, 3))
        m["ckr"] = np.ascontiguousarray(cache_krope[i].transpose(2, 0, 1))
        in_maps.append(m)
    return in_maps


_NC_CACHE = {}


def kernel(**inputs):
    in_maps = prepare_inputs(**inputs)
    if "nc" not in _NC_CACHE:
        _NC_CACHE["nc"] = build_program(L_FULL)
    nc = _NC_CACHE["nc"]
    res = run_bass_kernel_spmd(nc, in_maps, core_ids=list(range(8)))
    y_prompt = np.zeros((16, 256, D), np.float32)
    y_sample = np.zeros((8, TS, D), np.float32)
    state_ckv = np.zeros((16, L_FULL, 256, 256), np.float32)
    state_krope = np.zeros((16, L_FULL, 256, 64), np.float32)
    for i, r in enumerate(res.results):
        yT = np.asarray(r["yT"])
        y = yT.transpose(2, 1, 0).reshape(T, D)
        y_sample[i] = y[0:TS]
        y_prompt[2 * i] = y[TS:TS + 256]
        y_prompt[2 * i + 1] = y[TS + 256:TS + 512]
        ck = np.asarray(r["ckvst"])
        ck = ck.transpose(0, 3, 2, 1).reshape(L_FULL, 512, 256)
        kr = np.asarray(r["krst"]).transpose(0, 2, 1)
        for b in range(2):
            state_ckv[2 * i + b] = ck[:, b * 256:(b + 1) * 256, :]
            state_krope[2 * i + b] = kr[:, b * 256:(b + 1) * 256, :]
    return (y_prompt, y_sample, state_ckv, state_krope)
```

```python
import math
from contextlib import ExitStack

import numpy as np
import concourse.bass as bass
import concourse.mybir as mybir
from concourse.bass_utils import run_bass_kernel_spmd

F32, BF16 = mybir.dt.float32, mybir.dt.bfloat16
AF = mybir.ActivationFunctionType
ALU = mybir.AluOpType

D = 1024
L_FULL = 4
T = 2560
TS = 2048
NSUB = 10
SUB = 256
NKEY = 2816
EPS = 1e-6
SCALE = 1.0 / math.sqrt(192.0)
NJ = 8
REORDER = True
PADZERO = True
ROPE128 = True
BST_BF16 = True
ENGS = ['pe', 'act', 'dve', 'pool', 'sp']


class Buf:
    __slots__ = ('name', 'w', 'r', 'dcount', 'sem', 'psum')

    def __init__(self, name, psum=False):
        self.name = name
        self.psum = psum
        self.w = None
        self.r = []
        self.dcount = 0
        self.sem = None


class Op:
    __slots__ = ('eng', 'fn', 'deps', 'idx', 'signal', 'sigval', 'dkey', 'dval')

    def __init__(self, eng, fn, deps, idx, dkey=None, dval=0):
        self.eng = eng
        self.fn = fn
        self.deps = deps
        self.idx = idx
        self.signal = False
        self.sigval = 0
        self.dkey = dkey
        self.dval = dval


class Sched:
    def __init__(self):
        self.ops = {e: [] for e in ENGS}
        self.dkeys = []
        self.bar_deps = set()
        self.bar_gen = 0
        self.eng_gen = {e: 0 for e in ENGS}

    def _deps(self, eng, reads, writes, is_dma):
        deps = set()
        for b in reads:
            if b.w is not None:
                deps.add(b.w)
            if b.psum:
                deps.update(t for t in b.r if not (t[0] == 'e' and t[1] == eng))
        for b in writes:
            if b.w is not None and not (is_dma and b.w[0] == 'd'):
                deps.add(b.w)
            deps.update(b.r)
        if self.eng_gen[eng] != self.bar_gen:
            deps |= self.bar_deps
            self.eng_gen[eng] = self.bar_gen
        return deps

    def op(self, eng, fn, reads=(), writes=()):
        deps = self._deps(eng, reads, writes, False)
        o = Op(eng, fn, deps, len(self.ops[eng]))
        self.ops[eng].append(o)
        tok = ('e', eng, o.idx)
        if eng == 'pe':
            o.deps = {d for d in deps if not (d[0] == 'e' and d[1] == 'pe')}
        for b in reads:
            self._add_reader(b, tok)
        for b in writes:
            b.w = tok
            b.r = []
        return tok

    @staticmethod
    def _add_reader(b, tok):
        b.r = [t for t in b.r if not (t[0] == tok[0] and t[1] == tok[1])]
        b.r.append(tok)

    def dma(self, eng, fn, key, reads=(), writes=()):
        deps = self._deps(eng, reads, writes, True)
        if key.sem is None:
            key.sem = True
            self.dkeys.append(key)
        key.dcount += 16
        o = Op(eng, fn, deps, len(self.ops[eng]), dkey=key, dval=key.dcount)
        self.ops[eng].append(o)
        tok = ('d', key, key.dcount)
        for b in reads:
            self._add_reader(b, tok)
        for b in writes:
            b.w = tok
            b.r = []
        return tok

    def barrier(self):
        deps = set()
        for e in ENGS:
            for o in reversed(self.ops[e]):
                if o.dkey is None:
                    deps.add(('e', e, o.idx))
                    break
        for k in self.dkeys:
            if k.dcount:
                deps.add(('d', k, k.dcount))
        self.bar_deps = deps
        self.bar_gen += 1

    def emit(self, nc, block, stack, final_waits=()):
        for e in ENGS:
            for o in self.ops[e]:
                for d in o.deps:
                    if d[0] == 'e':
                        self.ops[d[1]][d[2]].signal = True
        for d in final_waits:
            if d[0] == 'e':
                self.ops[d[1]][d[2]].signal = True
        esem = {}
        for e in ENGS:
            n = 0
            for o in self.ops[e]:
                if o.signal:
                    n += 1
                    o.sigval = n
            if n:
                esem[e] = stack.enter_context(nc.semaphore('s_' + e))
        for k in self.dkeys:
            k.sem = stack.enter_context(nc.semaphore('d_' + k.name))
        sched = self

        def run(e, engobj, extra=()):
            seen = {}

            def wait(tok):
                if tok[0] == 'e':
                    sem = esem[tok[1]]
                    val = sched.ops[tok[1]][tok[2]].sigval
                    key = tok[1]
                else:
                    sem = tok[1].sem
                    val = tok[2]
                    key = id(tok[1])
                if seen.get(key, 0) >= val:
                    return
                seen[key] = val
                engobj.wait_ge(sem, val)

            for o in sched.ops[e]:
                for d in sorted(o.deps, key=lambda t: (t[0], t[1] if t[0] == 'e' else t[1].name, t[2])):
                    wait(d)
                ins = o.fn(engobj)
                if o.dkey is not None:
                    ins.then_inc(o.dkey.sem, 16)
                elif o.signal:
                    ins.then_inc(esem[e], 1)
            for d in extra:
                wait(d)

        @block.tensor
        def _(eng):
            run('pe', eng)

        @block.scalar
        def _(eng):
            run('act', eng)

        @block.vector
        def _(eng):
            run('dve', eng)

        @block.gpsimd
        def _(eng):
            run('pool', eng)

        @block.sync
        def _(eng):
            run('sp', eng, extra=final_waits)


def build_program(L=L_FULL):
    nc = bass.Bass("TRN2", target_bir_lowering=False)
    S = Sched()

    def din(name, shape):
        return nc.dram_tensor(name, list(shape), F32, kind="ExternalInput").ap()

    def dout(name, shape):
        return nc.dram_tensor(name, list(shape), F32, kind="ExternalOutput").ap()

    xT_d = din("xT", [128, 8, T])
    cT_d = din("cT", [128, 8, 2])
    wada_d = din("wada", [L_FULL, 48, 128, 8, 128])
    bada_d = din("bada", [128, L_FULL, 48])
    gmix_d = din("gmix", [128, L_FULL, 8])
    gffn_d = din("gffn", [128, L_FULL, 8])
    gq_d = din("gq", [128, L_FULL, 3])
    gkv_d = din("gkv", [128, L_FULL, 2])
    gsgu_d = din("gsgu", [L_FULL, 128, 256])
    pscale_d = din("pscale", [128, L_FULL, 2])
    gfinal_d = din("gfinal", [128, 8])
    winA_d = din("winA", [L_FULL, 128, 8, 384])
    winB_d = din("winB", [L_FULL, 128, 8, 1152])
    wuq_d = din("wuq", [L_FULL, 128, 3, 1024])
    wukv_d = din("wukv", [L_FULL, 128, 2, 1024])
    wout_d = din("wout", [L_FULL, 128, 8, 1024])
    wpool_d = din("wpool", [L_FULL, 4, 64, 64])
    wsT_d = din("wsT", [L_FULL, 128, 4, 128])
    bsT_d = din("bsT", [L_FULL, 128, 2, 128])
    w1_d = din("w1", [L_FULL, NJ, 128, 8, 512])
    w2_d = din("w2", [L_FULL, NJ, 128, 4, 1024])
    cckv_d = din("cckv", [128, L_FULL, 2, 256])
    ckr_d = din("ckr", [64, L_FULL, 256])
    cos_d = din("cosT", [64, TS])
    sin_d = din("sinT", [64, TS])
    invw_d = din("invw", [128, 2])
    corrL_d = din("corrL", [128, 2, 8])
    corrR_d = din("corrR", [128, 2, 8])
    yT_d = dout("yT", [128, 8, T])
    ckvst_d = dout("ckvst", [L_FULL, 128, 2, 512])
    krst_d = dout("krst", [L_FULL, 64, 512])

    base0 = (nc.sbuf_base + 63) // 64 * 64
    top = nc.sbuf_top
    cur = [base0]
    ncnt = [0]

    def alloc(shape, dt, name=None):
        size = int(np.prod(shape[1:])) * (4 if dt == F32 else 2)
        size = (size + 31) // 32 * 32
        off = cur[0]
        cur[0] += size
        assert cur[0] <= top, ("SBUF overflow", name, cur[0] - top)
        ncnt[0] += 1
        t_ = nc.alloc_sbuf_tensor_at("%s_%d" % (name or "t", ncnt[0]), list(shape), dt, offset=off)
        offs[id(t_)] = off
        return t_

    offs = {}

    def alias(t_, shape, dt, name):
        ncnt[0] += 1
        return nc.alloc_sbuf_tensor_at("%s_%d" % (name, ncnt[0]), list(shape), dt, offset=offs[id(t_)])

    x = alloc([128, 8, T], F32, "x")
    ones = alloc([128, 128], BF16, "ones")
    gmix = alloc([128, L_FULL, 8], F32, "gmix")
    gffn = alloc([128, L_FULL, 8], F32, "gffn")
    gq = alloc([128, L_FULL, 3], F32, "gq")
    gkv = alloc([128, L_FULL, 2], F32, "gkv")
    pscale = alloc([128, L_FULL, 2], F32, "pscale")
    gfinal = alloc([128, 8], F32, "gfinal")
    bada = alloc([128, L_FULL, 48], F32, "bada")
    cT = alloc([128, 8, 2], F32, "cT")
    sT = alloc([128, 8, 2], BF16, "sT")
    invw = alloc([128, 2], F32, "invw")
    corrL = alloc([128, 2, 8], F32, "corrL")
    corrR = alloc([128, 2, 8], F32, "corrR")
    modp = [alloc([128, 48, 2], F32, "mod") for _ in range(2)]
    a1p = [alloc([128, 8, 2], F32, "a1") for _ in range(2)]
    a2p = [alloc([128, 8, 2], F32, "a2") for _ in range(2)]
    scr_base = cur[0]

    ps = [nc.alloc_psum_tensor("ps%d" % i, [128, 512], F32) for i in range(8)]
    psb = [Buf("ps%d" % i, psum=True) for i in range(8)]

    B = {}

    def bf(name):
        if name not in B:
            B[name] = Buf(name)
        return B[name]

    gbank_list = [0, 1, 2, 3]
    gctr = [0]

    def gb():
        i = gbank_list[gctr[0] % len(gbank_list)]
        gctr[0] += 1
        return ps[i], psb[i]

    def mm(out, lhsT, rhs, start, stop, reads, writes, skip=False):
        if skip:
            S.op('pe', lambda e: e.matmul(out, lhsT=lhsT, rhs=rhs, start=start, stop=stop, skip_group_check=True),
                 reads, writes)
        else:
            S.op('pe', lambda e: e.matmul(out, lhsT=lhsT, rhs=rhs, start=start, stop=stop), reads, writes)

    def act(out, in_, func, reads, writes, scale=1.0, bias=0.0):
        S.op('act', lambda e: e.activation(out=out, in_=in_, func=func, scale=scale, bias=bias), reads, writes)

    def tt(eng, out, in0, in1, op, reads, writes):
        S.op(eng, lambda e: e.tensor_tensor(out=out, in0=in0, in1=in1, op=op), reads, writes)

    def stt(out, in0, scalar, in1, op0, op1, reads, writes):
        S.op('dve', lambda e: e.scalar_tensor_tensor(out=out, in0=in0, scalar=scalar, in1=in1, op0=op0, op1=op1),
             reads, writes)

    def ts(eng, out, in0, s1, s2, op0, op1, reads, writes):
        if s2 is None:
            S.op(eng, lambda e: e.tensor_scalar(out=out, in0=in0, scalar1=s1, scalar2=None, op0=op0), reads, writes)
        else:
            S.op(eng, lambda e: e.tensor_scalar(out=out, in0=in0, scalar1=s1, scalar2=s2, op0=op0, op1=op1),
                 reads, writes)

    def cp(eng, out, in_, reads, writes):
        if eng == 'act':
            S.op('act', lambda e: e.activation(out=out, in_=in_, func=AF.Copy), reads, writes)
        else:
            S.op(eng, lambda e: e.tensor_copy(out=out, in_=in_), reads, writes)

    def memset(eng, ap, val, writes):
        S.op(eng, lambda e: e.memset(ap, val), (), writes)

    def ld(q, out, in_, key, writes=None):
        S.dma(q, lambda e: e.dma_start(out=out, in_=in_), key, (), writes if writes is not None else [key])

    def rstd_from_ps(psum_ap, n_feat, lnv, rstd, rd, wr_ln, wr_rs):
        act(lnv, psum_ap, AF.Ln, rd, [wr_ln], scale=1.0 / n_feat, bias=EPS)
        act(rstd, lnv, AF.Exp, [wr_ln], [wr_rs], scale=-0.5)

    bx = bf("x")
    for t in range(5):
        S.dma('sp', lambda e, t=t: e.dma_start(out=x[:, :, t * 512:(t + 1) * 512], in_=xT_d[:, :, t * 512:(t + 1) * 512]),
              bx, (), [bx])
    bconst = bf("const")
    for dst, src in [(gmix, gmix_d), (gffn, gffn_d), (gq, gq_d), (gkv, gkv_d), (pscale, pscale_d), (gfinal, gfinal_d),
                     (bada, bada_d), (cT, cT_d), (invw, invw_d), (corrL, corrL_d), (corrR, corrR_d)]:
        S.dma('sp', lambda e, dst=dst, src=src: e.dma_start(out=dst[:], in_=src), bconst, (), [bconst])
    bones = bf("ones")
    memset('dve', ones[:], 1.0, [bones])
    bsT_ = bf("sT")
    act(sT[:], cT[:], AF.Silu, [bconst], [bsT_])

    out_toks = []

    def tokrange(s):
        return s * SUB, (s + 1) * SUB

    for l in range(L):
        mod, a1, a2 = modp[l % 2], a1p[l % 2], a2p[l % 2]
        bmod = bf("mod%d" % (l % 2))

        def emit_mod_dma(ll, ring, ringb, m0, m1):
            for m in range(m0, m1):
                ld('pool', ring[m % len(ring)][:], wada_d[ll, m], ringb[m % len(ring)])

        def emit_mod_mm(ll, ring, ringb, m0, m1):
            pm, pmb = ps[7], psb[7]
            for m in range(m0, m1):
                r, rb = ring[m % len(ring)], ringb[m % len(ring)]
                for k in range(8):
                    mm(pm[:, 2 * m:2 * m + 2], r[:, k, :], sT[:, k, :], k == 0, k == 7, [rb, bsT_], [pmb])

        def emit_mod(ll, ring, ringb, m0, m1):
            for m in range(m0, m1):
                emit_mod_dma(ll, ring, ringb, m, m + 1)
                emit_mod_mm(ll, ring, ringb, m, m + 1)

        def finish_mod(ll):
            pm, pmb = ps[7], psb[7]
            md, aa1, aa2 = modp[ll % 2], a1p[ll % 2], a2p[ll % 2]
            bm = bf("mod%d" % (ll % 2))
            pmv = pm[:, 0:96].rearrange("p (m s) -> p m s", s=2)
            for s_ in range(2):
                tt('dve', md[:, :, s_], pmv[:, :, s_], bada[:, ll, :], ALU.add, [pmb, bconst], [bm])
            for s_ in range(2):
                stt(aa1[:, :, s_], md[:, 8:16, s_], 1.0, gmix[:, ll, :], ALU.add, ALU.mult, [bm, bconst], [bm])
                stt(aa2[:, :, s_], md[:, 32:40, s_], 1.0, gffn[:, ll, :], ALU.add, ALU.mult, [bm, bconst], [bm])

        if l == 0:
            cur[0] = scr_base
            ring0 = [alloc([128, 8, 128], BF16, "adar") for _ in range(4)]
            ringb0 = [bf("adar%d" % i) for i in range(4)]
            emit_mod(0, ring0, ringb0, 0, 48)
            finish_mod(0)
            S.barrier()

        def mtype(s):
            return 0 if s < 8 else 1

        def norm_tile(tok0, n, avec, bvec_off, mt, h, hb, tag, sq, tmpx, lnv, rstd, bsq=None, tmpx_extra=()):
            if bsq is None:
                bsq = bf(tag + "sq")
            bln, brs = bf(tag + "ln"), bf(tag + "rs")
            if lnv is rstd:
                bln = brs
            act(sq[:, :, 0:n], x[:, :, tok0:tok0 + n], AF.Square, [bx], [bsq])
            pb, pbb = gb()
            for k in range(8):
                mm(pb[:, 0:n], ones[:], sq[:, k, 0:n], k == 0, k == 7, [bones, bsq], [pbb])
            rstd_from_ps(pb[:, 0:n], float(D), lnv[:, 0:n], rstd[:, 0:n], [pbb], bln, brs)
            for c in range(8):
                tb = bf(tag + "tmpx%d" % (c % 2))
                tx = tmpx[c % 2]
                tt('dve', tx[:, 0:n], x[:, c, tok0:tok0 + n], rstd[:, 0:n], ALU.mult, [bx, brs], [tb] + list(tmpx_extra))
                sc_ap = avec[:, c, mt:mt + 1]
                bi_ap = mod[:, bvec_off + c, mt:mt + 1]
                o_ap, i_ap = h[:, c, 0:n], tx[:, 0:n]
                S.op('act', lambda e, sc_ap=sc_ap, bi_ap=bi_ap, o_ap=o_ap, i_ap=i_ap: e.activation(
                    out=o_ap, in_=i_ap, func=AF.Identity, scale=sc_ap, bias=bi_ap), [tb, bmod], [hb])

        def alloc_kv(nkey, with_hph):
            KV = {}
            KV['knT'] = alloc([128, 4, nkey], BF16, "knT")
            KV['krT'] = alloc([128, nkey], BF16, "krT")
            KV['V'] = alloc([128, nkey // 128, 512], BF16, "V")
            if with_hph:
                KV['hph'] = alloc([128, 2, 8, 16], F32, "hph")
            return KV

        def alloc_A(with_hp):
            TA = {}
            TA['wA'] = alloc([128, 8, 384], BF16, "wA")
            if with_hp:
                TA['whp'] = alloc([128, 8, 256], BF16, "whp")
            TA['wkv'] = alloc([128, 2, 1024], BF16, "wkv")
            TA['sq'] = alloc([128, 8, SUB], BF16, "sqA")
            TA['tmpx'] = [alloc([128, SUB], F32, "tmpxA") for _ in range(2)]
            TA['lnv'] = alloc([128, SUB], F32, "lnvA")
            TA['rstd'] = alloc([128, SUB], F32, "rstdA")
            TA['h1'] = alloc([128, 8, SUB], BF16, "h1A")
            TA['sqk'] = alloc([128, 2, SUB], BF16, "sqk")
            TA['lnk'] = alloc([128, SUB], F32, "lnk")
            TA['rsk'] = alloc([128, SUB], F32, "rsk")
            TA['ckvn'] = [alloc([128, 2, SUB], BF16, "ckvn") for _ in range(2)]
            if with_hp:
                TA['kt1'] = alloc([64, SUB], F32, "kt1")
                TA['kt2'] = alloc([64, SUB], F32, "kt2")
                TA['rope'] = [alloc([64, 2, SUB], F32, "ropeA") for _ in range(2)]
            else:
                TA['ckvst'] = alloc([128, 2, SUB], F32, "ckvst")
                TA['krst'] = alloc([64, SUB], F32, "krst")
            return TA

        bknT, bkrT, bV, bhph = bf("knT"), bf("krT"), bf("V"), bf("hph")
        bwA, bwhp, bwkv = bf("wA"), bf("whp"), bf("wkv")

        def load_A(TA, with_hp):
            ld('pool', TA['wA'][:], winA_d[l], bwA)
            ld('pool', TA['wkv'][:], wukv_d[l], bwkv)
            if with_hp:
                ld('pool', TA['whp'][:], winB_d[l, :, :, 0:256], bwhp)

        def kv_proj(KV, TA, cb, cbb, key0):
            wkv = TA['wkv']
            for hp_ in range(2):
                pb, pbb = gb()
                for hh in range(2):
                    h_ = 2 * hp_ + hh
                    for k in range(2):
                        mm(pb[:, hh * 256:(hh + 1) * 256], wkv[:, k, h_ * 128:(h_ + 1) * 128], cb[:, k, :],
                           k == 0, k == 1, [bwkv, cbb], [pbb])
                cp('act', KV['knT'][:, 2 * hp_:2 * hp_ + 2, key0:key0 + 256],
                   pb[:, 0:512].rearrange("p (h n) -> p h n", h=2), [pbb], [bknT])
            for tb_ in range(2):
                pb, pbb = gb()
                for k in range(2):
                    mm(pb[:, 0:512], cb[:, k, tb_ * 128:(tb_ + 1) * 128], wkv[:, k, 512:1024], k == 0, k == 1,
                       [bwkv, cbb], [pbb])
                cp('dve', KV['V'][:, key0 // 128 + tb_, :], pb[:, 0:512], [pbb], [bV])

        def phaseA_sub(s, KV, TA):
            tok0, tok1 = tokrange(s)
            mt = mtype(s)
            sample = s < 8
            key0 = 256 + tok0 if sample else (s - 8) * 256
            wA, h1A = TA['wA'], TA['h1']
            bh1 = bf("h1A")
            norm_tile(tok0, SUB, a1, 0, mt, h1A, bh1, "nA", TA['sq'], TA['tmpx'], TA['lnv'], TA['rstd'])
            pc, pcb = gb()
            for mc in range(2):
                for k in range(8):
                    mm(pc[:, mc * 256:(mc + 1) * 256], wA[:, k, mc * 128:(mc + 1) * 128], h1A[:, k, :], k == 0, k == 7,
                       [bwA, bh1], [pcb])
            pk, pkb = gb()
            nk = 2 if sample else 1
            for q_ in range(nk):
                for k in range(8):
                    mm(pk[0:64, q_ * 256:(q_ + 1) * 256], wA[:, k, 256 + q_ * 64:256 + (q_ + 1) * 64], h1A[:, k, :],
                       k == 0, k == 7, [bwA, bh1], [pkb])
            if sample:
                whp = TA['whp']
                ph, phb = gb()
                for mc in range(2):
                    for side in range(2):
                        c0 = 0 if side == 0 else SUB - 8
                        o0 = mc * 16 + side * 8
                        for k in range(8):
                            mm(ph[:, o0:o0 + 8], whp[:, k, mc * 128:(mc + 1) * 128], h1A[:, k, c0:c0 + 8], k == 0, k == 7,
                               [bwhp, bh1], [phb])
                cp('dve', KV['hph'][:, :, s, :], ph[:, 0:32].rearrange("p (m n) -> p m n", m=2), [phb], [bhph])
            pcv = pc[:, 0:512].rearrange("p (m n) -> p m n", m=2)
            sqk, lnk, rsk = TA['sqk'], TA['lnk'], TA['rsk']
            bsqk, blnk, brsk = bf("sqk"), bf("lnk"), bf("rsk")
            act(sqk[:], pcv, AF.Square, [pcb], [bsqk])
            pq, pqb = gb()
            for k in range(2):
                mm(pq[:, 0:256], ones[:], sqk[:, k, :], k == 0, k == 1, [bones, bsqk], [pqb])
            rstd_from_ps(pq[:, 0:256], 256.0, lnk[:], rsk[:], [pqb], blnk, brsk)
            cn, cnb = TA['ckvn'][s % 2], bf("ckvn%d" % (s % 2))
            if sample:
                for mc in range(2):
                    stt(cn[:, mc, :], pcv[:, mc, :], gkv[:, l, mc:mc + 1], rsk[:], ALU.mult, ALU.mult,
                        [pcb, brsk, bconst], [cnb])
            else:
                ckvst = TA['ckvst']
                bst = bf("ckvst")
                for mc in range(2):
                    stt(ckvst[:, mc, :], pcv[:, mc, :], gkv[:, l, mc:mc + 1], rsk[:], ALU.mult, ALU.mult,
                        [pcb, brsk, bconst], [bst])
                cp('act', cn[:], ckvst[:], [bst], [cnb])
                p0 = (s - 8) * 256
                out_toks.append(S.dma('sp', lambda e, p0=p0, l=l, ckvst=ckvst: e.dma_start(out=ckvst_d[l, :, :, p0:p0 + 256], in_=ckvst[:]),
                                      bst, [bst], []))
            krT = KV['krT']
            if sample:
                rp, rpb = TA['rope'][s % 2], bf("ropeA%d" % (s % 2))
                S.dma('sp', lambda e, rp=rp, tok0=tok0: e.dma_start(out=rp[:, 0, :], in_=cos_d[:, tok0:tok0 + SUB]), rpb, (), [rpb])
                S.dma('sp', lambda e, rp=rp, tok0=tok0: e.dma_start(out=rp[:, 1, :], in_=sin_d[:, tok0:tok0 + SUB]), rpb, (), [rpb])
                kt1, kt2 = TA['kt1'], TA['kt2']
                bk1, bk2 = bf("kt1"), bf("kt2")
                tt('dve', kt1[:], pk[0:64, 0:256], rp[:, 0, :], ALU.mult, [pkb, rpb], [bk1])
                tt('dve', kt2[:], pk[0:64, 256:512], rp[:, 1, :], ALU.mult, [pkb, rpb], [bk2])
                tt('dve', krT[0:64, key0:key0 + 256], kt1[:], kt2[:], ALU.add, [bk1, bk2], [bkrT])
            else:
                krst = TA['krst']
                bks = bf("krst")
                cp('dve', krst[:], pk[0:64, 0:256], [pkb], [bks])
                cp('act', krT[0:64, key0:key0 + 256], krst[:], [bks], [bkrT])
                p0 = (s - 8) * 256
                out_toks.append(S.dma('sp', lambda e, p0=p0, l=l, krst=krst: e.dma_start(out=krst_d[l, :, p0:p0 + 256], in_=krst[:]),
                                      bks, [bks], []))
            kv_proj(KV, TA, cn, cnb, key0)

        cur[0] = scr_base
        KVs = alloc_kv(NKEY, True)
        kv_end = cur[0]
        NA = 512
        wA = alloc([128, 8, 384], BF16, "wA")
        whp = alloc([128, 8, 256], BF16, "whp")
        wkvS = alloc([128, 2, 1024], BF16, "wkv")
        sqS = alloc([128, 8, NA], BF16, "sqS")
        tmpxS = [alloc([128, NA], F32, "tmpxS") for _ in range(2)]
        rstdS = alloc([128, NA], F32, "rstdS")
        h1S = [alloc([128, 8, NA], BF16, "h1S") for _ in range(2)]
        sqkS = alloc([128, 2, NA], BF16, "sqkS")
        rskS = alloc([128, NA], F32, "rskS")
        ckvnS = [alloc([128, 2, NA], BF16, "ckvnS") for _ in range(2)]
        kt1S = alloc([64, NA], F32, "kt1S")
        kt2S = alloc([64, NA], F32, "kt2S")
        ropeS = [alloc([64, 2, NA], F32, "ropeS") for _ in range(2)]
        ckvc = alloc([128, 2, 256], BF16, "ckvc")
        ckvstS = alloc([128, 2, NA], F32, "ckvstS")
        krstS = alloc([64, NA], F32, "krstS")
        TAs = {'wkv': wkvS}
        ld('pool', wA[:], winA_d[l], bwA)
        ld('pool', wkvS[:], wukv_d[l], bwkv)
        ld('pool', whp[:], winB_d[l, :, :, 0:256], bwhp)
        bckvc = bf("ckvc")
        ld('pool', ckvc[:], cckv_d[:, l], bckvc)
        if PADZERO:
            memset('dve', KVs['krT'][64:128, :], 0.0, [bkrT])
        ld('pool', KVs['krT'][0:64, 0:256], ckr_d[:, l, :], bf("krTc"), writes=[bkrT])
        gbank_list[:] = [0, 1, 2, 3, 4, 5, 6, 7]
        kv_proj(KVs, TAs, ckvc, bckvc, 0)

        def sa_norm(t):
            norm_tile(t * NA, NA, a1, 0, (0 if t < 4 else 1), h1S[t % 2], bf("h1S%d" % (t % 2)), "nS", sqS, tmpxS, rstdS, rstdS)

        def sa_proj(t):
            h1, bh1 = h1S[t % 2], bf("h1S%d" % (t % 2))
            st = {}
            pcs = []
            for mc in range(2):
                pc, pcb = gb()
                for k in range(8):
                    mm(pc[:, 0:NA], wA[:, k, mc * 128:(mc + 1) * 128], h1[:, k, :], k == 0, k == 7, [bwA, bh1], [pcb])
                pcs.append((pc, pcb))
            pks = []
            for q_ in range(2 if t < 4 else 1):
                pk, pkb = gb()
                for k in range(8):
                    mm(pk[0:64, 0:NA], wA[:, k, 256 + q_ * 64:256 + (q_ + 1) * 64], h1[:, k, :], k == 0, k == 7,
                       [bwA, bh1], [pkb])
                pks.append((pk, pkb))
            if t == 4:
                return pcs, pks
            ph, phb = gb()
            for sub in range(2):
                for mc in range(2):
                    for side in range(2):
                        c0 = sub * 256 + (0 if side == 0 else SUB - 8)
                        o0 = sub * 32 + mc * 16 + side * 8
                        for k in range(8):
                            mm(ph[:, o0:o0 + 8], whp[:, k, mc * 128:(mc + 1) * 128], h1[:, k, c0:c0 + 8], k == 0, k == 7,
                               [bwhp, bh1], [phb])
            for sub in range(2):
                cp('dve', KVs['hph'][:, :, 2 * t + sub, :],
                   ph[:, sub * 32:(sub + 1) * 32].rearrange("p (m n) -> p m n", m=2), [phb], [bhph])
            return pcs, pks

        def sa_chain(t, pcs, pks):
            tok0 = t * NA
            key0 = 256 + tok0
            bsqk, brsk = bf("sqkS"), bf("rskS")
            for mc in range(2):
                act(sqkS[:, mc, :], pcs[mc][0][:, 0:NA], AF.Square, [pcs[mc][1]], [bsqk])
            pq, pqb = gb()
            for k in range(2):
                mm(pq[:, 0:NA], ones[:], sqkS[:, k, :], k == 0, k == 1, [bones, bsqk], [pqb])
            rstd_from_ps(pq[:, 0:NA], 256.0, rskS[:], rskS[:], [pqb], brsk, brsk)
            cn, cnb = ckvnS[t % 2], bf("ckvnS%d" % (t % 2))
            if t < 4:
                for mc in range(2):
                    stt(cn[:, mc, :], pcs[mc][0][:, 0:NA], gkv[:, l, mc:mc + 1], rskS[:], ALU.mult, ALU.mult,
                        [pcs[mc][1], brsk, bconst], [cnb])
                rp, rpb = ropeS[t % 2], bf("ropeS%d" % (t % 2))
                S.dma('sp', lambda e, rp=rp, tok0=tok0: e.dma_start(out=rp[:, 0, :], in_=cos_d[:, tok0:tok0 + NA]), rpb, (), [rpb])
                S.dma('sp', lambda e, rp=rp, tok0=tok0: e.dma_start(out=rp[:, 1, :], in_=sin_d[:, tok0:tok0 + NA]), rpb, (), [rpb])
                bk1, bk2 = bf("kt1S"), bf("kt2S")
                tt('dve', kt1S[:], pks[0][0][0:64, 0:NA], rp[:, 0, :], ALU.mult, [pks[0][1], rpb], [bk1])
                tt('dve', kt2S[:], pks[1][0][0:64, 0:NA], rp[:, 1, :], ALU.mult, [pks[1][1], rpb], [bk2])
                tt('dve', KVs['krT'][0:64, key0:key0 + NA], kt1S[:], kt2S[:], ALU.add, [bk1, bk2], [bkrT])
            else:
                bst, bks = bf("ckvstS"), bf("krstS")
                for mc in range(2):
                    stt(ckvstS[:, mc, :], pcs[mc][0][:, 0:NA], gkv[:, l, mc:mc + 1], rskS[:], ALU.mult, ALU.mult,
                        [pcs[mc][1], brsk, bconst], [bst])
                cp('act', cn[:], ckvstS[:], [bst], [cnb])
                out_toks.append(S.dma('sp', lambda e, l=l, ckvstS=ckvstS: e.dma_start(out=ckvst_d[l], in_=ckvstS[:]), bst, [bst], []))
                cp('dve', krstS[:], pks[0][0][0:64, 0:NA], [pks[0][1]], [bks])
                cp('act', KVs['krT'][0:64, key0:key0 + NA], krstS[:], [bks], [bkrT])
                out_toks.append(S.dma('sp', lambda e, l=l, krstS=krstS: e.dma_start(out=krst_d[l], in_=krstS[:]), bks, [bks], []))
            for h_ in range(4):
                pb, pbb = gb()
                for k in range(2):
                    mm(pb[:, 0:NA], wkvS[:, k, h_ * 128:(h_ + 1) * 128], cn[:, k, :], k == 0, k == 1, [bwkv, cnb], [pbb])
                cp('act', KVs['knT'][:, h_, key0:key0 + NA], pb[:, 0:NA], [pbb], [bknT])
            for tb_ in range(4):
                pb, pbb = gb()
                for k in range(2):
                    mm(pb[:, 0:512], cn[:, k, tb_ * 128:(tb_ + 1) * 128], wkvS[:, k, 512:1024], k == 0, k == 1,
                       [bwkv, cnb], [pbb])
                cp('dve', KVs['V'][:, key0 // 128 + tb_, :], pb[:, 0:512], [pbb], [bV])

        sa_norm(0)
        for t in range(5):
            pcs, pks = sa_proj(t)
            if t + 1 < 5:
                sa_norm(t + 1)
            sa_chain(t, pcs, pks)
        gbank_list[:] = [0, 1, 2, 3]
        S.barrier()

        cur[0] = kv_end
        wB = alloc([128, 8, 1152], BF16, "wB")
        wq = alloc([128, 3, 1024], BF16, "wq")
        wo = alloc([128, 8, 1024], BF16, "wo")
        wpl = alloc([128, 2, 128], BF16, "wpl")
        wsT = alloc([128, 4, 128], BF16, "wsT")
        bsT = alloc([128, 2, 128], BF16 if BST_BF16 else F32, "bsT")
        gsg = alloc([128, 256], BF16, "gsg")
        yT = alloc([128, 8, SUB], BF16, "yT")
        sqB = None
        wmn = alloc([128, 2, SUB], F32, "wmn")
        tmpxB = [wmn[:, 0, :], wmn[:, 1, :]]
        rstdB = alloc([128, SUB], F32, "rstdB")
        lnvB = rstdB
        lnq, rsq = lnvB, rstdB
        h1B = alloc([128, 8, SUB], BF16, "h1B")
        P0 = alloc([128, 2, SUB + 16], F32, "P0")
        sqq = alias(P0, [128, 3, SUB], BF16, "sqq")
        S2 = alloc([128, 2, SUB + 16], F32, "S2")
        S4 = alloc([128, 2, SUB + 16], F32, "S4")
        pooled = alloc([128, 2, SUB], BF16, "pooled")
        cqg = alloc([128, 3, SUB], BF16, "cqg")
        qnT = alloc([128, 4, SUB], BF16, "qnT")
        qrT = alloc([128, 4, SUB], BF16, "qrT")
        ropeB = alloc([64, 2, SUB], F32, "ropeB")
        qt1f = alloc([128, SUB], F32, "qt1")
        qt1 = qt1f[0:64, :]
        qt2 = alloc([64, SUB], F32, "qt2")
        ug = pooled
        vss = alloc([128, 2], F32, "vss")
        vln = alloc([128, 2], F32, "vln")
        vrs = alloc([128, 2], F32, "vrs")
        vgA = alloc([128, 2, 256], BF16, "vgA")
        vgB = alloc([128, 2, 256], BF16, "vgB")
        gt = S2[:, :, 0:SUB]
        vt = S4[:, :, 0:SUB]
        PT = [alloc([128, 512], BF16, "PT") for _ in range(2)]
        denr = qt1f
        NPT = len(PT)

        bwB, bwq, bwo, bwsm = bf("wB"), bf("wq"), bf("wo"), bf("wsm")
        ld('pool', wB[:], winB_d[l], bwB)
        ld('pool', wq[:], wuq_d[l], bwq)
        memset('dve', wpl[:], 0.0, [bwsm])
        for g in range(4):
            c_, hf = g // 2, g % 2
            S.dma('pool', lambda e, g=g, c_=c_, hf=hf, l=l, wpl=wpl: e.dma_start(out=wpl[hf * 64:(hf + 1) * 64, c_, hf * 64:(hf + 1) * 64],
                                                                  in_=wpool_d[l, g]), bwsm, (), [bwsm])
        ld('pool', wsT[:], wsT_d[l], bwsm)
        if BST_BF16:
            S.dma('pool', lambda e, l=l, bsT=bsT: e.dma_start(out=bsT[:], in_=bsT_d[l]), bwsm, (), [bwsm])
        else:
            S.dma('sp', lambda e, l=l, bsT=bsT: e.dma_start(out=bsT[:], in_=bsT_d[l]), bf("gsgk"), (), [bf("gsgk"), bwsm])
        S.dma('pool', lambda e, l=l, gsg=gsg: e.dma_start(out=gsg[:], in_=gsgu_d[l]), bf("gsgk"), (), [bf("gsgk")])
        ld('pool', wo[:], wout_d[l], bwo)
        bvgA, bvgB = bf("vgA"), bf("vgB")
        memset('dve', vgA[:], 0.0, [bvgA])
        memset('dve', vgB[:], 0.0, [bvgB])
        if PADZERO:
            memset('dve', qrT[64:128, :, :], 0.0, [bf("qrT")])

        def normB(s):
            tok0, tok1 = tokrange(s)
            norm_tile(tok0, SUB, a1, 0, mtype(s), h1B, bf('h1B'), 'nB', h1B, tmpxB, lnvB, rstdB, bsq=bf('h1B'),
                      tmpx_extra=[bf('wmn')])

        def phaseB_sub(s, KV, prenormed=False, next_s=None):
            tok0, tok1 = tokrange(s)
            mt = mtype(s)
            sample = s < 8
            knT, krT, V = KV['knT'], KV['krT'], KV['V']
            bh1 = bf("h1B")
            byT = bf("yT")
            bwm = bf("wmn")
            blnq, brsq = bf("nBrs"), bf("nBrs")
            if not prenormed:
                normB(s)
            tbx = [bf("nBtmpx0"), bf("nBtmpx1")]
            pcq0, pcq0b = gb()
            for mc in range(2):
                for k in range(8):
                    mm(pcq0[:, mc * 256:(mc + 1) * 256], wB[:, k, 256 + mc * 128:256 + (mc + 1) * 128], h1B[:, k, :],
                       k == 0, k == 7, [bwB, bh1], [pcq0b])
            pcq1, pcq1b = gb()
            for k in range(8):
                mm(pcq1[:, 0:256], wB[:, k, 512:640], h1B[:, k, :], k == 0, k == 7, [bwB, bh1], [pcq1b])
            bsqq, bcqg = bf("P0"), bf("cqg")
            act(sqq[:, 0:2, :], pcq0[:, 0:512].rearrange("p (m n) -> p m n", m=2), AF.Square, [pcq0b], [bsqq])
            act(sqq[:, 2, :], pcq1[:, 0:256], AF.Square, [pcq1b], [bsqq])
            for mc in range(3):
                src = pcq0[:, mc * 256:(mc + 1) * 256] if mc < 2 else pcq1[:, 0:256]
                ts('dve', cqg[:, mc, :], src, gq[:, l, mc:mc + 1], None, ALU.mult, None,
                   [pcq0b if mc < 2 else pcq1b, bconst], [bcqg])
            pss, pssb = gb()
            for k in range(3):
                mm(pss[:, 0:256], ones[:], sqq[:, k, :], k == 0, k == 2, [bones, bsqq], [pssb])
            rstd_from_ps(pss[:, 0:256], 384.0, lnq[:], rsq[:], [pssb], blnq, brsq)
            yield 'cq'
            bqn, bqr = bf("qnT"), bf("qrT")
            for hp_ in range(2):
                pq, pqb = gb()
                for hh in range(2):
                    h_ = 2 * hp_ + hh
                    for k in range(3):
                        mm(pq[:, hh * 256:(hh + 1) * 256], wq[:, k, h_ * 128:(h_ + 1) * 128], cqg[:, k, :], k == 0, k == 2,
                           [bwq, bcqg], [pqb])
                tt('dve', qnT[:, 2 * hp_:2 * hp_ + 2, :], pq[:, 0:512].rearrange("p (h n) -> p h n", h=2),
                   rsq[:].unsqueeze(1).broadcast_to([128, 2, 256]), ALU.mult, [pqb, brsq], [bqn])
            brp2 = bf("ropeB2")
            if sample:
                brp = bf("ropeB")
                S.dma('sp', lambda e, tok0=tok0, ropeB=ropeB: e.dma_start(out=ropeB[:, 0, :], in_=cos_d[:, tok0:tok0 + SUB]), brp, (), [brp, brp2])
                S.dma('sp', lambda e, tok0=tok0, ropeB=ropeB: e.dma_start(out=ropeB[:, 1, :], in_=sin_d[:, tok0:tok0 + SUB]), brp, (), [brp, brp2])
                tt('dve', ropeB[:, 0, :], ropeB[:, 0, :], rsq[0:64, :], ALU.mult, [brp, brsq], [brp2])
                tt('dve', ropeB[:, 1, :], ropeB[:, 1, :], rsq[0:64, :], ALU.mult, [brp, brsq, brp2], [brp2])
            for h_ in range(4):
                pq, pqb = gb()
                nq = 2 if sample else 1
                for q_ in range(nq):
                    c0 = 512 + q_ * 256 + h_ * 64
                    for k in range(3):
                        mm(pq[0:64, q_ * 256:(q_ + 1) * 256], wq[:, k, c0:c0 + 64], cqg[:, k, :], k == 0, k == 2,
                           [bwq, bcqg], [pqb])
                if sample:
                    bq1, bq2 = bf("qt1"), bf("qt2")
                    tt('dve', qt1[:], pq[0:64, 0:256], ropeB[:, 0, :], ALU.mult, [pqb, brp2], [bq1])
                    tt('dve', qt2[:], pq[0:64, 256:512], ropeB[:, 1, :], ALU.mult, [pqb, brp2], [bq2])
                    tt('dve', qrT[0:64, h_, :], qt1[:], qt2[:], ALU.add, [bq1, bq2], [bqr])
                else:
                    tt('dve', qrT[0:64, h_, :], pq[0:64, 0:256], rsq[0:64, :], ALU.mult, [pqb, brsq], [bqr])

            yield 'q'

            def rest_front():
                php, phpb = gb()
                for mc in range(2):
                    for k in range(8):
                        mm(php[:, mc * 256:(mc + 1) * 256], wB[:, k, mc * 128:(mc + 1) * 128], h1B[:, k, :], k == 0, k == 7,
                           [bwB, bh1], [phpb])
                bP0, bS2, bS4, bpl = bf("P0"), bf("S2"), bf("S4"), bf("pooled")
                cp('dve', P0[:, :, 8:8 + SUB], php[:, 0:512].rearrange("p (m n) -> p m n", m=2), [phpb], [bP0])
                if sample and s > 0:
                    cp('dve', P0[:, :, 0:8], KV['hph'][:, :, s - 1, 8:16], [bhph], [bP0])
                else:
                    memset('dve', P0[:, :, 0:8], 0.0, [bP0])
                if sample and s < 7:
                    cp('dve', P0[:, :, 8 + SUB:16 + SUB], KV['hph'][:, :, s + 1, 0:8], [bhph], [bP0])
                else:
                    memset('dve', P0[:, :, 8 + SUB:16 + SUB], 0.0, [bP0])
                W_ = SUB + 16
                wmw = [bwm] + tbx
                tt('dve', S2[:, :, 1:W_], P0[:, :, 1:W_], P0[:, :, 0:W_ - 1], ALU.add, [bP0], [bS2])
                tt('dve', S4[:, :, 3:W_], S2[:, :, 3:W_], S2[:, :, 1:W_ - 2], ALU.add, [bS2], [bS4])
                ts('dve', wmn[0:64, 0, :], S2[0:64, 0, 8:8 + SUB], invw[0:64, 0:1], None, ALU.mult, None, [bS2, bconst], wmw)
                ts('dve', wmn[64:128, 0, :], S4[64:128, 0, 9:9 + SUB], invw[64:128, 0:1], None, ALU.mult, None, [bS4, bconst], wmw)
                bS8 = bf("S8")
                tt('dve', S2[:, 1, 7:W_], S4[:, 1, 7:W_], S4[:, 1, 3:W_ - 4], ALU.add, [bS4, bwm, bS2], [bS8, bS2])
                ts('dve', wmn[0:64, 1, :], S2[0:64, 1, 11:11 + SUB], invw[0:64, 1:2], None, ALU.mult, None, [bS8, bconst], wmw)
                bS16 = bf("S16")
                tt('dve', S4[64:128, 1, 15:W_], S2[64:128, 1, 15:W_], S2[64:128, 1, 7:W_ - 8], ALU.add, [bS8, bwm, bS4], [bS16, bS4])
                ts('dve', wmn[64:128, 1, :], S4[64:128, 1, 15:15 + SUB], invw[64:128, 1:2], None, ALU.mult, None,
                   [bS16, bconst], wmw)
                seq_start = (s == 0) or not sample
                seq_end = (s == 7) or not sample
                if seq_start:
                    tt('dve', wmn[:, :, 0:8], wmn[:, :, 0:8], corrL[:], ALU.mult, [bwm, bconst], wmw)
                if seq_end:
                    tt('dve', wmn[:, :, SUB - 8:SUB], wmn[:, :, SUB - 8:SUB], corrR[:], ALU.mult, [bwm, bconst], wmw)
                tt('dve', pooled[:], wmn[:], P0[:, :, 8:8 + SUB], ALU.subtract, [bwm, bP0], [bpl])
                ppl, pplb = gb()
                for mc in range(2):
                    mm(ppl[:, mc * 256:(mc + 1) * 256], wpl[:, mc, :], pooled[:, mc, :], True, True, [bwsm, bpl], [pplb])
                for mc in range(2):
                    ts('dve', yT[:, mc, :], ppl[:, mc * 256:(mc + 1) * 256], pscale[:, l, mc:mc + 1], None, ALU.mult, None,
                       [pplb, bconst], [byT])
                pu, pub = gb()
                for mc in range(2):
                    for k in range(8):
                        mm(pu[:, mc * 256:(mc + 1) * 256], wB[:, k, 640 + mc * 128:640 + (mc + 1) * 128], h1B[:, k, :],
                           k == 0, k == 7, [bwB, bh1], [pub])
                bug = bf("pooled")
                act(ug[:], pu[:, 0:512].rearrange("p (m n) -> p m n", m=2), AF.Gelu_apprx_tanh, [pub], [bug])
                pv, pvb = gb()
                for blk in range(2):
                    for k in range(8):
                        mm(pv[:, blk * 256:(blk + 1) * 256], h1B[:, k, blk * 128:(blk + 1) * 128], wB[:, k, 896:1152],
                           k == 0, k == 7, [bwB, bh1], [pvb])
                bvt, bvss, bvrs, bgt = bf("S4"), bf("vss"), bf("vrs"), bf("S2")
                act(vt[:], pv[:, 0:512].rearrange("p (m n) -> p m n", m=2), AF.Gelu_apprx_tanh, [pvb], [bvt, bf("S16")])
                for blk in range(2):
                    S.op('act', lambda e, blk=blk, gt=gt, vt=vt, vss=vss: e.activation(out=gt[:, 0, :], in_=vt[:, blk, :], func=AF.Square,
                                                                accum_out=vss[:, blk:blk + 1]), [bvt], [bvss, bgt, bf("S8")])
                act(vln[:], vss[:], AF.Ln, [bvss], [bf("vln")], scale=1.0 / 256.0, bias=EPS)
                act(vrs[:], vln[:], AF.Exp, [bf("vln")], [bvrs], scale=-0.5)
                for blk in range(2):
                    for hf, (vg, vgb) in enumerate([(vgA, bvgA), (vgB, bvgB)]):
                        o_ = vg[:, blk, :].rearrange("p (m h c) -> p m h c", m=2, h=2)[:, :, hf, :]
                        i0 = vt[:, blk, :].rearrange("p (m h c) -> p m h c", m=2, h=2)[:, :, hf, :]
                        i1 = gsg[:].rearrange("p (m h c) -> p m h c", m=2, h=2)[:, :, hf, :]
                        stt(o_, i0, vrs[:, blk:blk + 1], i1, ALU.mult, ALU.mult, [bvt, bvrs, bf("gsgk")], [vgb])
                pg, pgb = gb()
                for mc in range(2):
                    for blk in range(2):
                        o_ = pg[:, mc * 256 + blk * 128: mc * 256 + (blk + 1) * 128]
                        mm(o_, vgA[:, blk, mc * 128:(mc + 1) * 128], wsT[:, 2 * mc, :], True, False, [bvgA, bwsm], [pgb])
                        mm(o_, vgB[:, blk, mc * 128:(mc + 1) * 128], wsT[:, 2 * mc + 1, :], False, True, [bvgB, bwsm], [pgb])
                for mc in range(2):
                    tt('dve', gt[:, mc, :].rearrange("p (b n) -> p b n", b=2),
                       pg[:, mc * 256:(mc + 1) * 256].rearrange("p (b n) -> p b n", b=2),
                       bsT[:, mc, :].unsqueeze(1).broadcast_to([128, 2, 128]), ALU.add, [pgb, bwsm], [bgt, bf("S8")])
                tt('dve', yT[:, 6:8, :], gt[:], ug[:], ALU.mult, [bgt, bug], [byT])

            if sample:
                kbase, nkc = 0, 18
            else:
                kbase, nkc = 2304 + (s - 8) * 256, 2
            npair = nkc // 2
            pairs = [(h_, jp) for h_ in range(4) for jp in range(npair)]
            st_ = {}

            def emit_scores(i):
                h_, jp = pairs[i]
                psc, pscb = (ps[7], psb[7]) if i % 2 == 0 else gb()
                for sub in range(2):
                    j = 2 * jp + sub
                    k0 = kbase + j * 128
                    mm(psc[:, sub * 256:(sub + 1) * 256], knT[:, h_, k0:k0 + 128], qnT[:, h_, :], True, False,
                       [bknT, bqn], [pscb])
                    if ROPE128:
                        mm(psc[:, sub * 256:(sub + 1) * 256], krT[:, k0:k0 + 128], qrT[:, h_, :], False, True,
                           [bkrT, bqr], [pscb])
                    else:
                        mm(psc[:, sub * 256:(sub + 1) * 256], krT[0:64, k0:k0 + 128], qrT[0:64, h_, :], False, True,
                           [bkrT, bqr], [pscb])
                pt, ptb = PT[i % NPT], bf("PT%d" % (i % NPT))
                act(pt[:], psc[:, 0:512], AF.Exp, [pscb], [ptb], scale=SCALE)
                st_[i] = (pt, ptb)

            def emit_pv(i):
                h_, jp = pairs[i]
                pt, ptb = st_.pop(i)
                po, pob = ps[4 + (h_ % 2)], psb[4 + (h_ % 2)]
                for sub in range(2):
                    j = 2 * jp + sub
                    kc = (kbase // 128) + j
                    mm(po[:, 0:256], V[:, kc, h_ * 128:(h_ + 1) * 128], pt[:, sub * 256:(sub + 1) * 256],
                       j == 0, j == nkc - 1, [bV, ptb], [pob], skip=True)
                    mm(po[:, 256:512], ones[:], pt[:, sub * 256:(sub + 1) * 256], False, j == nkc - 1,
                       [bones, ptb], [pob], skip=True)
                if jp == npair - 1:
                    bdr = bf("qt1")
                    S.op('dve', lambda e, po=po, denr=denr: e.reciprocal(out=denr[:], in_=po[:, 256:512]), [pob], [bdr])
                    tt('dve', yT[:, 2 + h_, :], po[:, 0:256], denr[:], ALU.mult, [pob, bdr], [byT])

            rest_front()
            emit_scores(0)
            for i in range(len(pairs)):
                if i + 1 < len(pairs):
                    emit_scores(i + 1)
                emit_pv(i)
                if i == npair - 1 and next_s is not None:
                    normB(next_s)
            yield 'att'
            for mp in range(4):
                pw, pwb = gb()
                for mh in range(2):
                    m_ = 2 * mp + mh
                    for k in range(8):
                        mm(pw[:, mh * 256:(mh + 1) * 256], wo[:, k, m_ * 128:(m_ + 1) * 128], yT[:, k, :], k == 0, k == 7,
                           [bwo, byT], [pwb])
                for mh in range(2):
                    m_ = 2 * mp + mh
                    stt(x[:, m_, tok0:tok1], pw[:, mh * 256:(mh + 1) * 256], mod[:, 16 + m_, mt:mt + 1], x[:, m_, tok0:tok1],
                        ALU.mult, ALU.add, [pwb, bmod, bx], [bx])

        gbank_list[:] = [0, 1, 2, 3, 6]
        gens = {s: phaseB_sub(s, KVs, prenormed=(s > 0), next_s=(s + 1 if s < NSUB - 1 else None)) for s in range(NSUB)}
        next(gens[0])
        next(gens[0])
        for s in range(NSUB):
            next(gens[s])
            if s + 1 < NSUB:
                next(gens[s + 1])
            for _ in gens[s]:
                pass
            if s + 1 < NSUB:
                next(gens[s + 1])
        gbank_list[:] = [0, 1, 2, 3]
        S.barrier()

        cur[0] = scr_base
        h2 = alloc([128, 8, T], BF16, "h2")
        wslot = [(alloc([128, 8, 512], BF16, "w1s"), alloc([128, 4, 1024], BF16, "w2s")) for _ in range(2)]
        sqF = alloc([128, 8, 512], BF16, "sqF")
        lnvF = alloc([128, 512], F32, "lnvF")
        rstdF = lnvF
        rstd2 = alloc([128, T], F32, "rstd2")
        b2bf = alloc([128, 8, 2], BF16, "b2bf")
        cbuf = [alloc([128, 4, 2], F32, "cbuf") for _ in range(2)]
        rl = [alloc([128, 512], F32, "rl") for _ in range(2)]
        hid = [alloc([128, 4, 512], BF16, "hid") for _ in range(2)]
        ybuf = alloc([128, 8, 512], F32, "ybuf") if l == L - 1 else None
        bh2 = [bf("h2_%d" % t) for t in range(5)]

        def load_w(j):
            w1s, w2s = wslot[j % 2]
            kb1, kb2 = bf("w1s%d" % (j % 2)), bf("w2s%d" % (j % 2))
            ld('pool', w1s[:], w1_d[l, j], kb1)
            ld('pool', w2s[:], w2_d[l, j], kb2)

        gbank_list[:] = [0, 1, 2, 3, 4, 5, 6]
        load_w(0)
        nxt = l + 1 < L
        if nxt:
            ringF = [alloc([128, 8, 128], BF16, "adarF") for _ in range(6)]
            ringFb = [bf("adarF%d" % i) for i in range(6)]

        bb2 = bf("b2bf")
        cp('dve', b2bf[:], mod[:, 24:32, :], [bmod], [bb2])
        brs2 = [bf("rstd2_%d" % t) for t in range(5)]

        def norm2(t):
            mt = 0 if t < 4 else 1
            sl = slice(t * 512, (t + 1) * 512)
            bsq = bf("nFsq")
            act(sqF[:], x[:, :, sl], AF.Square, [bx], [bsq])
            for c in range(8):
                if c % 2 == 0:
                    ts('dve', h2[:, c, sl], x[:, c, sl], a2[:, c, mt:mt + 1], None, ALU.mult, None, [bx, bmod], [bh2[t]])
                else:
                    sc_ap, o_ap, i_ap = a2[:, c, mt:mt + 1], h2[:, c, sl], x[:, c, sl]
                    S.op('act', lambda e, sc_ap=sc_ap, o_ap=o_ap, i_ap=i_ap: e.activation(
                        out=o_ap, in_=i_ap, func=AF.Copy, scale=sc_ap), [bx, bmod], [bh2[t]])
            pb, pbb = gb()
            for k in range(8):
                mm(pb[:, 0:512], ones[:], sqF[:, k, :], k == 0, k == 7, [bones, bsq], [pbb])
            rstd_from_ps(pb[:, 0:512], float(D), rstd2[:, sl], rstd2[:, sl], [pbb], brs2[t], brs2[t])

        def slab_bias(j):
            w1s, kb1 = wslot[j % 2][0], bf("w1s%d" % (j % 2))
            pm, pmb = ps[7], psb[7]
            for m_ in range(4):
                for k in range(8):
                    mm(pm[:, 128 + 2 * m_:130 + 2 * m_], w1s[:, k, m_ * 128:(m_ + 1) * 128], b2bf[:, k, :], k == 0, k == 7,
                       [kb1, bb2], [pmb])
            cp('dve', cbuf[j % 2][:], pm[:, 128:136].rearrange("p (m s) -> p m s", s=2), [pmb], [bf("cbuf%d" % (j % 2))])

        def ff1(i, j, t):
            w1s, kb1 = wslot[j % 2][0], bf("w1s%d" % (j % 2))
            hd, hdb = hid[i % 2], bf("hid%d" % (i % 2))
            for m_ in range(4):
                pb, pbb = gb()
                for k in range(8):
                    mm(pb[:, 0:512], w1s[:, k, m_ * 128:(m_ + 1) * 128], h2[:, k, t * 512:(t + 1) * 512], k == 0, k == 7,
                       [kb1, bh2[t]], [pbb])
                r_, rb_ = rl[m_ % 2], bf("rl%d" % (m_ % 2))
                mt = 0 if t < 4 else 1
                tt('dve', r_[:], pb[:, 0:512], rstd2[:, t * 512:(t + 1) * 512], ALU.mult, [pbb, brs2[t]], [rb_])
                act(r_[:], r_[:], AF.Relu, [rb_, bf("cbuf%d" % (j % 2))], [rb_], bias=cbuf[j % 2][:, m_, mt:mt + 1])
                act(hd[:, m_, :], r_[:], AF.Square, [rb_], [hdb])

        def ff2(i, j, t):
            w2s, kb2 = wslot[j % 2][1], bf("w2s%d" % (j % 2))
            hd, hdb = hid[i % 2], bf("hid%d" % (i % 2))
            mt = 0 if t < 4 else 1
            for m_ in range(8):
                pb, pbb = gb()
                for k in range(4):
                    mm(pb[:, 0:512], w2s[:, k, m_ * 128:(m_ + 1) * 128], hd[:, k, :], k == 0, k == 3, [kb2, hdb], [pbb])
                stt(x[:, m_, t * 512:(t + 1) * 512], pb[:, 0:512], mod[:, 40 + m_, mt:mt + 1],
                    x[:, m_, t * 512:(t + 1) * 512], ALU.mult, ALU.add, [pbb, bmod, bx], [bx])

        steps = [(j, t) for j in range(NJ) for t in range(5)]
        load_w(1)
        if nxt:
            emit_mod_dma(l + 1, ringF, ringFb, 0, 6)
        norm2(0)
        norm2(1)
        slab_bias(0)
        ff1(0, 0, 0)
        for i, (j, t) in enumerate(steps):
            if i + 1 < len(steps):
                if steps[i + 1][1] == 0:
                    slab_bias(steps[i + 1][0])
                ff1(i + 1, *steps[i + 1])
            ff2(i, j, t)
            if j == 0 and t + 2 < 5:
                norm2(t + 2)
            if t == 4:
                if j + 2 < NJ:
                    load_w(j + 2)
                if nxt:
                    emit_mod_mm(l + 1, ringF, ringFb, j * 6, (j + 1) * 6)
                    if j + 1 < NJ:
                        emit_mod_dma(l + 1, ringF, ringFb, (j + 1) * 6, (j + 2) * 6)
        if nxt:
            finish_mod(l + 1)
        if l == L - 1:
            for t in range(5):
                bsq, bln, brs, byb = bf("fsq"), bf("fln"), bf("frs"), bf("ybuf")
                act(sqF[:], x[:, :, t * 512:(t + 1) * 512], AF.Square, [bx], [bsq])
                pb, pbb = gb()
                for k in range(8):
                    mm(pb[:, 0:512], ones[:], sqF[:, k, :], k == 0, k == 7, [bones, bsq], [pbb])
                rstd_from_ps(pb[:, 0:512], float(D), lnvF[:], rstdF[:], [pbb], brs, brs)
                for c in range(8):
                    stt(ybuf[:, c, :], x[:, c, t * 512:(t + 1) * 512], gfinal[:, c:c + 1], rstdF[:], ALU.mult, ALU.mult,
                        [bx, brs, bconst], [byb])
                out_toks.append(S.dma('sp', lambda e, t=t, ybuf=ybuf: e.dma_start(out=yT_d[:, :, t * 512:(t + 1) * 512], in_=ybuf[:]),
                                      byb, [byb], []))
        gbank_list[:] = [0, 1, 2, 3]
        S.barrier()

    with ExitStack() as st:
        with nc.Block() as block:
            S.emit(nc, block, st, final_waits=out_toks)
    return nc


def _pc(a, nchunk):
    return np.ascontiguousarray(a.reshape((nchunk, 128) + a.shape[1:]).swapaxes(0, 1))


def _vecs(g, nchunk):
    Ln = g.shape[0]
    return np.ascontiguousarray(g.reshape(Ln, nchunk, 128).transpose(2, 0, 1))


def _rope_tables():
    rows = TS // 64
    row = np.repeat(np.arange(rows, dtype=np.float32), 64)
    col = np.tile(np.arange(64, dtype=np.float32), rows)
    inv = (1.0 / (10000.0 ** (np.arange(16, dtype=np.float32) / 16))).astype(np.float32)
    ang = np.concatenate([row[:, None] * inv, col[:, None] * inv], axis=-1).astype(np.float32)
    cos, sin = np.cos(ang).astype(np.float32), np.sin(ang).astype(np.float32)
    cosT = np.concatenate([cos, cos], axis=1).T
    sinT = np.concatenate([-sin, sin], axis=1).T
    return np.ascontiguousarray(cosT), np.ascontiguousarray(sinT)


def _pool_tables():
    wins = (2, 4, 8, 16)
    invw = np.zeros((128, 2), np.float32)
    corrL = np.ones((128, 2, 8), np.float32)
    corrR = np.ones((128, 2, 8), np.float32)
    Ls = 256
    for g, w in enumerate(wins):
        c, hf = g // 2, g % 2
        sl = slice(hf * 64, (hf + 1) * 64)
        invw[sl, c] = 1.0 / w
        for i in range(8):
            t = i
            cnt = min(t + (w - w // 2), Ls) - max(t - w // 2, 0)
            corrL[sl, c, i] = w / cnt
            t = Ls - 8 + i
            cnt = min(t + (w - w // 2), Ls) - max(t - w // 2, 0)
            corrR[sl, c, i] = w / cnt
    return invw, corrL, corrR


def prepare_inputs(x_prompt, x_sample, cache_ckv, cache_krope, c, c_ctx, w_ada, b_ada, g_mix,
                   w_in, w_pool, pool_scale, g_q, w_uq, g_kv, w_ukv, g_sgu, w_s, b_s, w_out,
                   g_ffn, w_ff1, w_ff2, g_final):
    f = lambda a: np.asarray(a, dtype=np.float32)
    x_prompt, x_sample, cache_ckv, cache_krope, c, c_ctx = map(f, (x_prompt, x_sample, cache_ckv, cache_krope, c, c_ctx))
    w_ada, b_ada, g_mix, w_in, w_pool, pool_scale, g_q, w_uq, g_kv, w_ukv = map(
        f, (w_ada, b_ada, g_mix, w_in, w_pool, pool_scale, g_q, w_uq, g_kv, w_ukv))
    g_sgu, w_s, b_s, w_out, g_ffn, w_ff1, w_ff2, g_final = map(f, (g_sgu, w_s, b_s, w_out, g_ffn, w_ff1, w_ff2, g_final))
    Ln = w_in.shape[0]
    OFF_Q, OFF_KV, OFF_R, OFF_G = 256, 640, 896, 960
    shared = {}
    shared["wada"] = np.ascontiguousarray(w_ada.reshape(Ln, 8, 128, 48, 128).transpose(0, 3, 2, 1, 4))
    shared["bada"] = _vecs(b_ada, 48)
    shared["gmix"] = _vecs(g_mix, 8)
    shared["gffn"] = _vecs(g_ffn, 8)
    shared["gq"] = _vecs(g_q, 3)
    shared["gkv"] = _vecs(g_kv, 2)
    shared["gsgu"] = np.ascontiguousarray(np.broadcast_to(g_sgu[:, None, :], (Ln, 128, 256)))
    shared["pscale"] = _vecs(pool_scale, 2)
    shared["gfinal"] = np.ascontiguousarray(g_final.reshape(8, 128).T)
    kr = w_in[:, :, OFF_R:OFF_G]
    kr_swp = np.concatenate([kr[:, :, 32:64], kr[:, :, 0:32]], axis=2)
    winA = np.concatenate([w_in[:, :, OFF_KV:OFF_R], kr, kr_swp], axis=2)
    shared["winA"] = np.ascontiguousarray(winA.reshape(Ln, 8, 128, 384).transpose(0, 2, 1, 3))
    winB = np.concatenate([w_in[:, :, 0:OFF_Q], w_in[:, :, OFF_Q:OFF_KV], w_in[:, :, OFF_G:OFF_G + 512]], axis=2)
    shared["winB"] = np.ascontiguousarray(winB.reshape(Ln, 8, 128, 1152).transpose(0, 2, 1, 3))
    wq4 = w_uq.reshape(Ln, 384, 4, 192)
    qn = wq4[:, :, :, 0:128].reshape(Ln, 384, 512)
    qr = wq4[:, :, :, 128:192]
    qr_swp = np.concatenate([qr[..., 32:64], qr[..., 0:32]], axis=-1)
    wuq = np.concatenate([qn, qr.reshape(Ln, 384, 256), qr_swp.reshape(Ln, 384, 256)], axis=2)
    shared["wuq"] = np.ascontiguousarray(wuq.reshape(Ln, 3, 128, 1024).transpose(0, 2, 1, 3))
    wkv4 = w_ukv.reshape(Ln, 256, 4, 256)
    wukv = np.concatenate([wkv4[..., 0:128].reshape(Ln, 256, 512), wkv4[..., 128:256].reshape(Ln, 256, 512)], axis=2)
    shared["wukv"] = np.ascontiguousarray(wukv.reshape(Ln, 2, 128, 1024).transpose(0, 2, 1, 3))
    shared["wout"] = np.ascontiguousarray(w_out.reshape(Ln, 8, 128, 1024).transpose(0, 2, 1, 3))
    shared["wpool"] = np.ascontiguousarray(w_pool)
    shared["wsT"] = np.ascontiguousarray(w_s.transpose(0, 3, 1, 2))
    bsT = np.broadcast_to(b_s.reshape(Ln, 2, 2, 1, 128), (Ln, 2, 2, 64, 128))
    shared["bsT"] = np.ascontiguousarray(bsT.reshape(Ln, 2, 128, 128).transpose(0, 2, 1, 3))
    shared["w1"] = np.ascontiguousarray(w_ff1.reshape(Ln, 8, 128, NJ, 512).transpose(0, 3, 2, 1, 4))
    shared["w2"] = np.ascontiguousarray(w_ff2.reshape(Ln, NJ, 4, 128, 1024).transpose(0, 1, 3, 2, 4))
    cosT, sinT = _rope_tables()
    shared["cosT"], shared["sinT"] = cosT, sinT
    shared["invw"], shared["corrL"], shared["corrR"] = _pool_tables()
    in_maps = []
    for i in range(8):
        m = dict(shared)
        xs = np.concatenate([x_sample[i], x_prompt[2 * i], x_prompt[2 * i + 1]], axis=0)
        m["xT"] = _pc(np.ascontiguousarray(xs.T), 8)
        cc = np.stack([c[i], c_ctx], axis=1)
        m["cT"] = _pc(cc, 8)
        ck = cache_ckv[i].transpose(2, 0, 1)
        m["cckv"] = np.ascontiguousarray(ck.reshape(2, 128, Ln, 256).transpose(1, 2, 0, 3))
        m["ckr"] = np.ascontiguousarray(cache_krope[i].transpose(2, 0, 1))
        in_maps.append(m)
    return in_maps


_NC_CACHE = {}


def kernel(**inputs):
    in_maps = prepare_inputs(**inputs)
    if "nc" not in _NC_CACHE:
        _NC_CACHE["nc"] = build_program(L_FULL)
    nc = _NC_CACHE["nc"]
    res = run_bass_kernel_spmd(nc, in_maps, core_ids=list(range(8)))
    y_prompt = np.zeros((16, 256, D), np.float32)
    y_sample = np.zeros((8, TS, D), np.float32)
    state_ckv = np.zeros((16, L_FULL, 256, 256), np.float32)
    state_krope = np.zeros((16, L_FULL, 256, 64), np.float32)
    for i, r in enumerate(res.results):
        yT = np.asarray(r["yT"])
        y = yT.transpose(2, 1, 0).reshape(T, D)
        y_sample[i] = y[0:TS]
        y_prompt[2 * i] = y[TS:TS + 256]
        y_prompt[2 * i + 1] = y[TS + 256:TS + 512]
        ck = np.asarray(r["ckvst"])
        ck = ck.transpose(0, 3, 2, 1).reshape(L_FULL, 512, 256)
        kr = np.asarray(r["krst"]).transpose(0, 2, 1)
        for b in range(2):
            state_ckv[2 * i + b] = ck[:, b * 256:(b + 1) * 256, :]
            state_krope[2 * i + b] = kr[:, b * 256:(b + 1) * 256, :]
    return (y_prompt, y_sample, state_ckv, state_krope)
```

```python
import math
from contextlib import ExitStack

import numpy as np
import concourse.bass as bass
import concourse.mybir as mybir
from concourse.bass_utils import run_bass_kernel_spmd

F32, BF16 = mybir.dt.float32, mybir.dt.bfloat16
AF = mybir.ActivationFunctionType
ALU = mybir.AluOpType

D = 1024
L_FULL = 4
T = 2560
TS = 2048
NSUB = 10
SUB = 256
NKEY = 2816
EPS = 1e-6
SCALE = 1.0 / math.sqrt(192.0)
NJ = 8
REORDER = True
PADZERO = True
ROPE128 = True
BST_BF16 = True
ENGS = ['pe', 'act', 'dve', 'pool', 'sp']


class Buf:
    __slots__ = ('name', 'w', 'r', 'dcount', 'sem', 'psum')

    def __init__(self, name, psum=False):
        self.name = name
        self.psum = psum
        self.w = None
        self.r = []
        self.dcount = 0
        self.sem = None


class Op:
    __slots__ = ('eng', 'fn', 'deps', 'idx', 'signal', 'sigval', 'dkey', 'dval')

    def __init__(self, eng, fn, deps, idx, dkey=None, dval=0):
        self.eng = eng
        self.fn = fn
        self.deps = deps
        self.idx = idx
        self.signal = False
        self.sigval = 0
        self.dkey = dkey
        self.dval = dval


class Sched:
    def __init__(self):
        self.ops = {e: [] for e in ENGS}
        self.dkeys = []
        self.bar_deps = set()
        self.bar_gen = 0
        self.eng_gen = {e: 0 for e in ENGS}

    def _deps(self, eng, reads, writes, is_dma):
        deps = set()
        for b in reads:
            if b.w is not None:
                deps.add(b.w)
            if b.psum:
                deps.update(t for t in b.r if not (t[0] == 'e' and t[1] == eng))
        for b in writes:
            if b.w is not None and not (is_dma and b.w[0] == 'd'):
                deps.add(b.w)
            deps.update(b.r)
        if self.eng_gen[eng] != self.bar_gen:
            deps |= self.bar_deps
            self.eng_gen[eng] = self.bar_gen
        return deps

    def op(self, eng, fn, reads=(), writes=()):
        deps = self._deps(eng, reads, writes, False)
        o = Op(eng, fn, deps, len(self.ops[eng]))
        self.ops[eng].append(o)
        tok = ('e', eng, o.idx)
        if eng == 'pe':
            o.deps = {d for d in deps if not (d[0] == 'e' and d[1] == 'pe')}
        for b in reads:
            self._add_reader(b, tok)
        for b in writes:
            b.w = tok
            b.r = []
        return tok

    @staticmethod
    def _add_reader(b, tok):
        b.r = [t for t in b.r if not (t[0] == tok[0] and t[1] == tok[1])]
        b.r.append(tok)

    def dma(self, eng, fn, key, reads=(), writes=()):
        deps = self._deps(eng, reads, writes, True)
        if key.sem is None:
            key.sem = True
            self.dkeys.append(key)
        key.dcount += 16
        o = Op(eng, fn, deps, len(self.ops[eng]), dkey=key, dval=key.dcount)
        self.ops[eng].append(o)
        tok = ('d', key, key.dcount)
        for b in reads:
            self._add_reader(b, tok)
        for b in writes:
            b.w = tok
            b.r = []
        return tok

    def barrier(self):
        deps = set()
        for e in ENGS:
            for o in reversed(self.ops[e]):
                if o.dkey is None:
                    deps.add(('e', e, o.idx))
                    break
        for k in self.dkeys:
            if k.dcount:
                deps.add(('d', k, k.dcount))
        self.bar_deps = deps
        self.bar_gen += 1

    def emit(self, nc, block, stack, final_waits=()):
        for e in ENGS:
            for o in self.ops[e]:
                for d in o.deps:
                    if d[0] == 'e':
                        self.ops[d[1]][d[2]].signal = True
        for d in final_waits:
            if d[0] == 'e':
                self.ops[d[1]][d[2]].signal = True
        esem = {}
        for e in ENGS:
            n = 0
            for o in self.ops[e]:
                if o.signal:
                    n += 1
                    o.sigval = n
            if n:
                esem[e] = stack.enter_context(nc.semaphore('s_' + e))
        for k in self.dkeys:
            k.sem = stack.enter_context(nc.semaphore('d_' + k.name))
        sched = self

        def run(e, engobj, extra=()):
            seen = {}

            def wait(tok):
                if tok[0] == 'e':
                    sem = esem[tok[1]]
                    val = sched.ops[tok[1]][tok[2]].sigval
                    key = tok[1]
                else:
                    sem = tok[1].sem
                    val = tok[2]
                    key = id(tok[1])
                if seen.get(key, 0) >= val:
                    return
                seen[key] = val
                engobj.wait_ge(sem, val)

            for o in sched.ops[e]:
                for d in sorted(o.deps, key=lambda t: (t[0], t[1] if t[0] == 'e' else t[1].name, t[2])):
                    wait(d)
                ins = o.fn(engobj)
                if o.dkey is not None:
                    ins.then_inc(o.dkey.sem, 16)
                elif o.signal:
                    ins.then_inc(esem[e], 1)
            for d in extra:
                wait(d)

        @block.tensor
        def _(eng):
            run('pe', eng)

        @block.scalar
        def _(eng):
            run('act', eng)

        @block.vector
        def _(eng):
            run('dve', eng)

        @block.gpsimd
        def _(eng):
            run('pool', eng)

        @block.sync
        def _(eng):
            run('sp', eng, extra=final_waits)


def build_program(L=L_FULL):
    nc = bass.Bass("TRN2", target_bir_lowering=False)
    S = Sched()

    def din(name, shape):
        return nc.dram_tensor(name, list(shape), F32, kind="ExternalInput").ap()

    def dout(name, shape):
        return nc.dram_tensor(name, list(shape), F32, kind="ExternalOutput").ap()

    xT_d = din("xT", [128, 8, T])
    cT_d = din("cT", [128, 8, 2])
    wada_d = din("wada", [L_FULL, 48, 128, 8, 128])
    bada_d = din("bada", [128, L_FULL, 48])
    gmix_d = din("gmix", [128, L_FULL, 8])
    gffn_d = din("gffn", [128, L_FULL, 8])
    gq_d = din("gq", [128, L_FULL, 3])
    gkv_d = din("gkv", [128, L_FULL, 2])
    gsgu_d = din("gsgu", [L_FULL, 128, 256])
    pscale_d = din("pscale", [128, L_FULL, 2])
    gfinal_d = din("gfinal", [128, 8])
    winA_d = din("winA", [L_FULL, 128, 8, 384])
    winB_d = din("winB", [L_FULL, 128, 8, 1152])
    wuq_d = din("wuq", [L_FULL, 128, 3, 1024])
    wukv_d = din("wukv", [L_FULL, 128, 2, 1024])
    wout_d = din("wout", [L_FULL, 128, 8, 1024])
    wpool_d = din("wpool", [L_FULL, 4, 64, 64])
    wsT_d = din("wsT", [L_FULL, 128, 4, 128])
    bsT_d = din("bsT", [L_FULL, 128, 2, 128])
    w1_d = din("w1", [L_FULL, NJ, 128, 8, 512])
    w2_d = din("w2", [L_FULL, NJ, 128, 4, 1024])
    cckv_d = din("cckv", [128, L_FULL, 2, 256])
    ckr_d = din("ckr", [64, L_FULL, 256])
    cos_d = din("cosT", [64, TS])
    sin_d = din("sinT", [64, TS])
    invw_d = din("invw", [128, 2])
    corrL_d = din("corrL", [128, 2, 8])
    corrR_d = din("corrR", [128, 2, 8])
    yT_d = dout("yT", [128, 8, T])
    ckvst_d = dout("ckvst", [L_FULL, 128, 2, 512])
    krst_d = dout("krst", [L_FULL, 64, 512])

    base0 = (nc.sbuf_base + 63) // 64 * 64
    top = nc.sbuf_top
    cur = [base0]
    ncnt = [0]

    def alloc(shape, dt, name=None):
        size = int(np.prod(shape[1:])) * (4 if dt == F32 else 2)
        size = (size + 31) // 32 * 32
        off = cur[0]
        cur[0] += size
        assert cur[0] <= top, ("SBUF overflow", name, cur[0] - top)
        ncnt[0] += 1
        t_ = nc.alloc_sbuf_tensor_at("%s_%d" % (name or "t", ncnt[0]), list(shape), dt, offset=off)
        offs[id(t_)] = off
        return t_

    offs = {}

    def alias(t_, shape, dt, name):
        ncnt[0] += 1
        return nc.alloc_sbuf_tensor_at("%s_%d" % (name, ncnt[0]), list(shape), dt, offset=offs[id(t_)])

    x = alloc([128, 8, T], F32, "x")
    ones = alloc([128, 128], BF16, "ones")
    gmix = alloc([128, L_FULL, 8], F32, "gmix")
    gffn = alloc([128, L_FULL, 8], F32, "gffn")
    gq = alloc([128, L_FULL, 3], F32, "gq")
    gkv = alloc([128, L_FULL, 2], F32, "gkv")
    pscale = alloc([128, L_FULL, 2], F32, "pscale")
    gfinal = alloc([128, 8], F32, "gfinal")
    bada = alloc([128, L_FULL, 48], F32, "bada")
    cT = alloc([128, 8, 2], F32, "cT")
    sT = alloc([128, 8, 2], BF16, "sT")
    invw = alloc([128, 2], F32, "invw")
    corrL = alloc([128, 2, 8], F32, "corrL")
    corrR = alloc([128, 2, 8], F32, "corrR")
    modp = [alloc([128, 48, 2], F32, "mod") for _ in range(2)]
    a1p = [alloc([128, 8, 2], F32, "a1") for _ in range(2)]
    a2p = [alloc([128, 8, 2], F32, "a2") for _ in range(2)]
    scr_base = cur[0]

    ps = [nc.alloc_psum_tensor("ps%d" % i, [128, 512], F32) for i in range(8)]
    psb = [Buf("ps%d" % i, psum=True) for i in range(8)]

    B = {}

    def bf(name):
        if name not in B:
            B[name] = Buf(name)
        return B[name]

    gbank_list = [0, 1, 2, 3]
    gctr = [0]

    def gb():
        i = gbank_list[gctr[0] % len(gbank_list)]
        gctr[0] += 1
        return ps[i], psb[i]

    def mm(out, lhsT, rhs, start, stop, reads, writes, skip=False):
        if skip:
            S.op('pe', lambda e: e.matmul(out, lhsT=lhsT, rhs=rhs, start=start, stop=stop, skip_group_check=True),
                 reads, writes)
        else:
            S.op('pe', lambda e: e.matmul(out, lhsT=lhsT, rhs=rhs, start=start, stop=stop), reads, writes)

    def act(out, in_, func, reads, writes, scale=1.0, bias=0.0):
        S.op('act', lambda e: e.activation(out=out, in_=in_, func=func, scale=scale, bias=bias), reads, writes)

    def tt(eng, out, in0, in1, op, reads, writes):
        S.op(eng, lambda e: e.tensor_tensor(out=out, in0=in0, in1=in1, op=op), reads, writes)

    def stt(out, in0, scalar, in1, op0, op1, reads, writes):
        S.op('dve', lambda e: e.scalar_tensor_tensor(out=out, in0=in0, scalar=scalar, in1=in1, op0=op0, op1=op1),
             reads, writes)

    def ts(eng, out, in0, s1, s2, op0, op1, reads, writes):
        if s2 is None:
            S.op(eng, lambda e: e.tensor_scalar(out=out, in0=in0, scalar1=s1, scalar2=None, op0=op0), reads, writes)
        else:
            S.op(eng, lambda e: e.tensor_scalar(out=out, in0=in0, scalar1=s1, scalar2=s2, op0=op0, op1=op1),
                 reads, writes)

    def cp(eng, out, in_, reads, writes):
        if eng == 'act':
            S.op('act', lambda e: e.activation(out=out, in_=in_, func=AF.Copy), reads, writes)
        else:
            S.op(eng, lambda e: e.tensor_copy(out=out, in_=in_), reads, writes)

    def memset(eng, ap, val, writes):
        S.op(eng, lambda e: e.memset(ap, val), (), writes)

    def ld(q, out, in_, key, writes=None):
        S.dma(q, lambda e: e.dma_start(out=out, in_=in_), key, (), writes if writes is not None else [key])

    def rstd_from_ps(psum_ap, n_feat, lnv, rstd, rd, wr_ln, wr_rs):
        act(lnv, psum_ap, AF.Ln, rd, [wr_ln], scale=1.0 / n_feat, bias=EPS)
        act(rstd, lnv, AF.Exp, [wr_ln], [wr_rs], scale=-0.5)

    bx = bf("x")
    for t in range(5):
        S.dma('sp', lambda e, t=t: e.dma_start(out=x[:, :, t * 512:(t + 1) * 512], in_=xT_d[:, :, t * 512:(t + 1) * 512]),
              bx, (), [bx])
    bconst = bf("const")
    for dst, src in [(gmix, gmix_d), (gffn, gffn_d), (gq, gq_d), (gkv, gkv_d), (pscale, pscale_d), (gfinal, gfinal_d),
                     (bada, bada_d), (cT, cT_d), (invw, invw_d), (corrL, corrL_d), (corrR, corrR_d)]:
        S.dma('sp', lambda e, dst=dst, src=src: e.dma_start(out=dst[:], in_=src), bconst, (), [bconst])
    bones = bf("ones")
    memset('dve', ones[:], 1.0, [bones])
    bsT_ = bf("sT")
    act(sT[:], cT[:], AF.Silu, [bconst], [bsT_])

    out_toks = []

    def tokrange(s):
        return s * SUB, (s + 1) * SUB

    for l in range(L):
        mod, a1, a2 = modp[l % 2], a1p[l % 2], a2p[l % 2]
        bmod = bf("mod%d" % (l % 2))

        def emit_mod_dma(ll, ring, ringb, m0, m1):
            for m in range(m0, m1):
                ld('pool', ring[m % len(ring)][:], wada_d[ll, m], ringb[m % len(ring)])

        def emit_mod_mm(ll, ring, ringb, m0, m1):
            pm, pmb = ps[7], psb[7]
            for m in range(m0, m1):
                r, rb = ring[m % len(ring)], ringb[m % len(ring)]
                for k in range(8):
                    mm(pm[:, 2 * m:2 * m + 2], r[:, k, :], sT[:, k, :], k == 0, k == 7, [rb, bsT_], [pmb])

        def emit_mod(ll, ring, ringb, m0, m1):
            for m in range(m0, m1):
                emit_mod_dma(ll, ring, ringb, m, m + 1)
                emit_mod_mm(ll, ring, ringb, m, m + 1)

        def finish_mod(ll):
            pm, pmb = ps[7], psb[7]
            md, aa1, aa2 = modp[ll % 2], a1p[ll % 2], a2p[ll % 2]
            bm = bf("mod%d" % (ll % 2))
            pmv = pm[:, 0:96].rearrange("p (m s) -> p m s", s=2)
            for s_ in range(2):
                tt('dve', md[:, :, s_], pmv[:, :, s_], bada[:, ll, :], ALU.add, [pmb, bconst], [bm])
            for s_ in range(2):
                stt(aa1[:, :, s_], md[:, 8:16, s_], 1.0, gmix[:, ll, :], ALU.add, ALU.mult, [bm, bconst], [bm])
                stt(aa2[:, :, s_], md[:, 32:40, s_], 1.0, gffn[:, ll, :], ALU.add, ALU.mult, [bm, bconst], [bm])

        if l == 0:
            cur[0] = scr_base
            ring0 = [alloc([128, 8, 128], BF16, "adar") for _ in range(4)]
            ringb0 = [bf("adar%d" % i) for i in range(4)]
            emit_mod(0, ring0, ringb0, 0, 48)
            finish_mod(0)
            S.barrier()

        def mtype(s):
            return 0 if s < 8 else 1

        def norm_tile(tok0, n, avec, bvec_off, mt, h, hb, tag, sq, tmpx, lnv, rstd, bsq=None, tmpx_extra=()):
            if bsq is None:
                bsq = bf(tag + "sq")
            bln, brs = bf(tag + "ln"), bf(tag + "rs")
            if lnv is rstd:
                bln = brs
            act(sq[:, :, 0:n], x[:, :, tok0:tok0 + n], AF.Square, [bx], [bsq])
            pb, pbb = gb()
            for k in range(8):
                mm(pb[:, 0:n], ones[:], sq[:, k, 0:n], k == 0, k == 7, [bones, bsq], [pbb])
            rstd_from_ps(pb[:, 0:n], float(D), lnv[:, 0:n], rstd[:, 0:n], [pbb], bln, brs)
            for c in range(8):
                tb = bf(tag + "tmpx%d" % (c % 2))
                tx = tmpx[c % 2]
                tt('dve', tx[:, 0:n], x[:, c, tok0:tok0 + n], rstd[:, 0:n], ALU.mult, [bx, brs], [tb] + list(tmpx_extra))
                sc_ap = avec[:, c, mt:mt + 1]
                bi_ap = mod[:, bvec_off + c, mt:mt + 1]
                o_ap, i_ap = h[:, c, 0:n], tx[:, 0:n]
                S.op('act', lambda e, sc_ap=sc_ap, bi_ap=bi_ap, o_ap=o_ap, i_ap=i_ap: e.activation(
                    out=o_ap, in_=i_ap, func=AF.Identity, scale=sc_ap, bias=bi_ap), [tb, bmod], [hb])

        def alloc_kv(nkey, with_hph):
            KV = {}
            KV['knT'] = alloc([128, 4, nkey], BF16, "knT")
            KV['krT'] = alloc([128, nkey], BF16, "krT")
            KV['V'] = alloc([128, nkey // 128, 512], BF16, "V")
            if with_hph:
                KV['hph'] = alloc([128, 2, 8, 16], F32, "hph")
            return KV

        def alloc_A(with_hp):
            TA = {}
            TA['wA'] = alloc([128, 8, 384], BF16, "wA")
            if with_hp:
                TA['whp'] = alloc([128, 8, 256], BF16, "whp")
            TA['wkv'] = alloc([128, 2, 1024], BF16, "wkv")
            TA['sq'] = alloc([128, 8, SUB], BF16, "sqA")
            TA['tmpx'] = [alloc([128, SUB], F32, "tmpxA") for _ in range(2)]
            TA['lnv'] = alloc([128, SUB], F32, "lnvA")
            TA['rstd'] = alloc([128, SUB], F32, "rstdA")
            TA['h1'] = alloc([128, 8, SUB], BF16, "h1A")
            TA['sqk'] = alloc([128, 2, SUB], BF16, "sqk")
            TA['lnk'] = alloc([128, SUB], F32, "lnk")
            TA['rsk'] = alloc([128, SUB], F32, "rsk")
            TA['ckvn'] = [alloc([128, 2, SUB], BF16, "ckvn") for _ in range(2)]
            if with_hp:
                TA['kt1'] = alloc([64, SUB], F32, "kt1")
                TA['kt2'] = alloc([64, SUB], F32, "kt2")
                TA['rope'] = [alloc([64, 2, SUB], F32, "ropeA") for _ in range(2)]
            else:
                TA['ckvst'] = alloc([128, 2, SUB], F32, "ckvst")
                TA['krst'] = alloc([64, SUB], F32, "krst")
            return TA

        bknT, bkrT, bV, bhph = bf("knT"), bf("krT"), bf("V"), bf("hph")
        bwA, bwhp, bwkv = bf("wA"), bf("whp"), bf("wkv")

        def load_A(TA, with_hp):
            ld('pool', TA['wA'][:], winA_d[l], bwA)
            ld('pool', TA['wkv'][:], wukv_d[l], bwkv)
            if with_hp:
                ld('pool', TA['whp'][:], winB_d[l, :, :, 0:256], bwhp)

        def kv_proj(KV, TA, cb, cbb, key0):
            wkv = TA['wkv']
            for hp_ in range(2):
                pb, pbb = gb()
                for hh in range(2):
                    h_ = 2 * hp_ + hh
                    for k in range(2):
                        mm(pb[:, hh * 256:(hh + 1) * 256], wkv[:, k, h_ * 128:(h_ + 1) * 128], cb[:, k, :],
                           k == 0, k == 1, [bwkv, cbb], [pbb])
                cp('act', KV['knT'][:, 2 * hp_:2 * hp_ + 2, key0:key0 + 256],
                   pb[:, 0:512].rearrange("p (h n) -> p h n", h=2), [pbb], [bknT])
            for tb_ in range(2):
                pb, pbb = gb()
                for k in range(2):
                    mm(pb[:, 0:512], cb[:, k, tb_ * 128:(tb_ + 1) * 128], wkv[:, k, 512:1024], k == 0, k == 1,
                       [bwkv, cbb], [pbb])
                cp('dve', KV['V'][:, key0 // 128 + tb_, :], pb[:, 0:512], [pbb], [bV])

        def phaseA_sub(s, KV, TA):
            tok0, tok1 = tokrange(s)
            mt = mtype(s)
            sample = s < 8
            key0 = 256 + tok0 if sample else (s - 8) * 256
            wA, h1A = TA['wA'], TA['h1']
            bh1 = bf("h1A")
            norm_tile(tok0, SUB, a1, 0, mt, h1A, bh1, "nA", TA['sq'], TA['tmpx'], TA['lnv'], TA['rstd'])
            pc, pcb = gb()
            for mc in range(2):
                for k in range(8):
                    mm(pc[:, mc * 256:(mc + 1) * 256], wA[:, k, mc * 128:(mc + 1) * 128], h1A[:, k, :], k == 0, k == 7,
                       [bwA, bh1], [pcb])
            pk, pkb = gb()
            nk = 2 if sample else 1
            for q_ in range(nk):
                for k in range(8):
                    mm(pk[0:64, q_ * 256:(q_ + 1) * 256], wA[:, k, 256 + q_ * 64:256 + (q_ + 1) * 64], h1A[:, k, :],
                       k == 0, k == 7, [bwA, bh1], [pkb])
            if sample:
                whp = TA['whp']
                ph, phb = gb()
                for mc in range(2):
                    for side in range(2):
                        c0 = 0 if side == 0 else SUB - 8
                        o0 = mc * 16 + side * 8
                        for k in range(8):
                            mm(ph[:, o0:o0 + 8], whp[:, k, mc * 128:(mc + 1) * 128], h1A[:, k, c0:c0 + 8], k == 0, k == 7,
                               [bwhp, bh1], [phb])
                cp('dve', KV['hph'][:, :, s, :], ph[:, 0:32].rearrange("p (m n) -> p m n", m=2), [phb], [bhph])
            pcv = pc[:, 0:512].rearrange("p (m n) -> p m n", m=2)
            sqk, lnk, rsk = TA['sqk'], TA['lnk'], TA['rsk']
            bsqk, blnk, brsk = bf("sqk"), bf("lnk"), bf("rsk")
            act(sqk[:], pcv, AF.Square, [pcb], [bsqk])
            pq, pqb = gb()
            for k in range(2):
                mm(pq[:, 0:256], ones[:], sqk[:, k, :], k == 0, k == 1, [bones, bsqk], [pqb])
            rstd_from_ps(pq[:, 0:256], 256.0, lnk[:], rsk[:], [pqb], blnk, brsk)
            cn, cnb = TA['ckvn'][s % 2], bf("ckvn%d" % (s % 2))
            if sample:
                for mc in range(2):
                    stt(cn[:, mc, :], pcv[:, mc, :], gkv[:, l, mc:mc + 1], rsk[:], ALU.mult, ALU.mult,
                        [pcb, brsk, bconst], [cnb])
            else:
                ckvst = TA['ckvst']
                bst = bf("ckvst")
                for mc in range(2):
                    stt(ckvst[:, mc, :], pcv[:, mc, :], gkv[:, l, mc:mc + 1], rsk[:], ALU.mult, ALU.mult,
                        [pcb, brsk, bconst], [bst])
                cp('act', cn[:], ckvst[:], [bst], [cnb])
                p0 = (s - 8) * 256
                out_toks.append(S.dma('sp', lambda e, p0=p0, l=l, ckvst=ckvst: e.dma_start(out=ckvst_d[l, :, :, p0:p0 + 256], in_=ckvst[:]),
                                      bst, [bst], []))
            krT = KV['krT']
            if sample:
                rp, rpb = TA['rope'][s % 2], bf("ropeA%d" % (s % 2))
                S.dma('sp', lambda e, rp=rp, tok0=tok0: e.dma_start(out=rp[:, 0, :], in_=cos_d[:, tok0:tok0 + SUB]), rpb, (), [rpb])
                S.dma('sp', lambda e, rp=rp, tok0=tok0: e.dma_start(out=rp[:, 1, :], in_=sin_d[:, tok0:tok0 + SUB]), rpb, (), [rpb])
                kt1, kt2 = TA['kt1'], TA['kt2']
                bk1, bk2 = bf("kt1"), bf("kt2")
                tt('dve', kt1[:], pk[0:64, 0:256], rp[:, 0, :], ALU.mult, [pkb, rpb], [bk1])
                tt('dve', kt2[:], pk[0:64, 256:512], rp[:, 1, :], ALU.mult, [pkb, rpb], [bk2])
                tt('dve', krT[0:64, key0:key0 + 256], kt1[:], kt2[:], ALU.add, [bk1, bk2], [bkrT])
            else:
                krst = TA['krst']
                bks = bf("krst")
                cp('dve', krst[:], pk[0:64, 0:256], [pkb], [bks])
                cp('act', krT[0:64, key0:key0 + 256], krst[:], [bks], [bkrT])
                p0 = (s - 8) * 256
                out_toks.append(S.dma('sp', lambda e, p0=p0, l=l, krst=krst: e.dma_start(out=krst_d[l, :, p0:p0 + 256], in_=krst[:]),
                                      bks, [bks], []))
            kv_proj(KV, TA, cn, cnb, key0)

        cur[0] = scr_base
        KVs = alloc_kv(NKEY, True)
        kv_end = cur[0]
        NA = 512
        wA = alloc([128, 8, 384], BF16, "wA")
        whp = alloc([128, 8, 256], BF16, "whp")
        wkvS = alloc([128, 2, 1024], BF16, "wkv")
        sqS = alloc([128, 8, NA], BF16, "sqS")
        tmpxS = [alloc([128, NA], F32, "tmpxS") for _ in range(2)]
        rstdS = alloc([128, NA], F32, "rstdS")
        h1S = [alloc([128, 8, NA], BF16, "h1S") for _ in range(2)]
        sqkS = alloc([128, 2, NA], BF16, "sqkS")
        rskS = alloc([128, NA], F32, "rskS")
        ckvnS = [alloc([128, 2, NA], BF16, "ckvnS") for _ in range(2)]
        kt1S = alloc([64, NA], F32, "kt1S")
        kt2S = alloc([64, NA], F32, "kt2S")
        ropeS = [alloc([64, 2, NA], F32, "ropeS") for _ in range(2)]
        ckvc = alloc([128, 2, 256], BF16, "ckvc")
        ckvstS = alloc([128, 2, NA], F32, "ckvstS")
        krstS = alloc([64, NA], F32, "krstS")
        TAs = {'wkv': wkvS}
        ld('pool', wA[:], winA_d[l], bwA)
        ld('pool', wkvS[:], wukv_d[l], bwkv)
        ld('pool', whp[:], winB_d[l, :, :, 0:256], bwhp)
        bckvc = bf("ckvc")
        ld('pool', ckvc[:], cckv_d[:, l], bckvc)
        if PADZERO:
            memset('dve', KVs['krT'][64:128, :], 0.0, [bkrT])
        ld('pool', KVs['krT'][0:64, 0:256], ckr_d[:, l, :], bf("krTc"), writes=[bkrT])
        gbank_list[:] = [0, 1, 2, 3, 4, 5, 6, 7]
        kv_proj(KVs, TAs, ckvc, bckvc, 0)

        def sa_norm(t):
            norm_tile(t * NA, NA, a1, 0, (0 if t < 4 else 1), h1S[t % 2], bf("h1S%d" % (t % 2)), "nS", sqS, tmpxS, rstdS, rstdS)

        def sa_proj(t):
            h1, bh1 = h1S[t % 2], bf("h1S%d" % (t % 2))
            st = {}
            pcs = []
            for mc in range(2):
                pc, pcb = gb()
                for k in range(8):
                    mm(pc[:, 0:NA], wA[:, k, mc * 128:(mc + 1) * 128], h1[:, k, :], k == 0, k == 7, [bwA, bh1], [pcb])
                pcs.append((pc, pcb))
            pks = []
            for q_ in range(2 if t < 4 else 1):
                pk, pkb = gb()
                for k in range(8):
                    mm(pk[0:64, 0:NA], wA[:, k, 256 + q_ * 64:256 + (q_ + 1) * 64], h1[:, k, :], k == 0, k == 7,
                       [bwA, bh1], [pkb])
                pks.append((pk, pkb))
            if t == 4:
                return pcs, pks
            ph, phb = gb()
            for sub in range(2):
                for mc in range(2):
                    for side in range(2):
                        c0 = sub * 256 + (0 if side == 0 else SUB - 8)
                        o0 = sub * 32 + mc * 16 + side * 8
                        for k in range(8):
                            mm(ph[:, o0:o0 + 8], whp[:, k, mc * 128:(mc + 1) * 128], h1[:, k, c0:c0 + 8], k == 0, k == 7,
                               [bwhp, bh1], [phb])
            for sub in range(2):
                cp('dve', KVs['hph'][:, :, 2 * t + sub, :],
                   ph[:, sub * 32:(sub + 1) * 32].rearrange("p (m n) -> p m n", m=2), [phb], [bhph])
            return pcs, pks

        def sa_chain(t, pcs, pks):
            tok0 = t * NA
            key0 = 256 + tok0
            bsqk, brsk = bf("sqkS"), bf("rskS")
            for mc in range(2):
                act(sqkS[:, mc, :], pcs[mc][0][:, 0:NA], AF.Square, [pcs[mc][1]], [bsqk])
            pq, pqb = gb()
            for k in range(2):
                mm(pq[:, 0:NA], ones[:], sqkS[:, k, :], k == 0, k == 1, [bones, bsqk], [pqb])
            rstd_from_ps(pq[:, 0:NA], 256.0, rskS[:], rskS[:], [pqb], brsk, brsk)
            cn, cnb = ckvnS[t % 2], bf("ckvnS%d" % (t % 2))
            if t < 4:
                for mc in range(2):
                    stt(cn[:, mc, :], pcs[mc][0][:, 0:NA], gkv[:, l, mc:mc + 1], rskS[:], ALU.mult, ALU.mult,
                        [pcs[mc][1], brsk, bconst], [cnb])
                rp, rpb = ropeS[t % 2], bf("ropeS%d" % (t % 2))
                S.dma('sp', lambda e, rp=rp, tok0=tok0: e.dma_start(out=rp[:, 0, :], in_=cos_d[:, tok0:tok0 + NA]), rpb, (), [rpb])
                S.dma('sp', lambda e, rp=rp, tok0=tok0: e.dma_start(out=rp[:, 1, :], in_=sin_d[:, tok0:tok0 + NA]), rpb, (), [rpb])
                bk1, bk2 = bf("kt1S"), bf("kt2S")
                tt('dve', kt1S[:], pks[0][0][0:64, 0:NA], rp[:, 0, :], ALU.mult, [pks[0][1], rpb], [bk1])
                tt('dve', kt2S[:], pks[1][0][0:64, 0:NA], rp[:, 1, :], ALU.mult, [pks[1][1], rpb], [bk2])
                tt('dve', KVs['krT'][0:64, key0:key0 + NA], kt1S[:], kt2S[:], ALU.add, [bk1, bk2], [bkrT])
            else:
                bst, bks = bf("ckvstS"), bf("krstS")
                for mc in range(2):
                    stt(ckvstS[:, mc, :], pcs[mc][0][:, 0:NA], gkv[:, l, mc:mc + 1], rskS[:], ALU.mult, ALU.mult,
                        [pcs[mc][1], brsk, bconst], [bst])
                cp('act', cn[:], ckvstS[:], [bst], [cnb])
                out_toks.append(S.dma('sp', lambda e, l=l, ckvstS=ckvstS: e.dma_start(out=ckvst_d[l], in_=ckvstS[:]), bst, [bst], []))
                cp('dve', krstS[:], pks[0][0][0:64, 0:NA], [pks[0][1]], [bks])
                cp('act', KVs['krT'][0:64, key0:key0 + NA], krstS[:], [bks], [bkrT])
                out_toks.append(S.dma('sp', lambda e, l=l, krstS=krstS: e.dma_start(out=krst_d[l], in_=krstS[:]), bks, [bks], []))
            for h_ in range(4):
                pb, pbb = gb()
                for k in range(2):
                    mm(pb[:, 0:NA], wkvS[:, k, h_ * 128:(h_ + 1) * 128], cn[:, k, :], k == 0, k == 1, [bwkv, cnb], [pbb])
                cp('act', KVs['knT'][:, h_, key0:key0 + NA], pb[:, 0:NA], [pbb], [bknT])
            for tb_ in range(4):
                pb, pbb = gb()
                for k in range(2):
                    mm(pb[:, 0:512], cn[:, k, tb_ * 128:(tb_ + 1) * 128], wkvS[:, k, 512:1024], k == 0, k == 1,
                       [bwkv, cnb], [pbb])
                cp('dve', KVs['V'][:, key0 // 128 + tb_, :], pb[:, 0:512], [pbb], [bV])

        sa_norm(0)
        for t in range(5):
            pcs, pks = sa_proj(t)
            if t + 1 < 5:
                sa_norm(t + 1)
            sa_chain(t, pcs, pks)
        gbank_list[:] = [0, 1, 2, 3]
        S.barrier()

        cur[0] = kv_end
        wB = alloc([128, 8, 1152], BF16, "wB")
        wq = alloc([128, 3, 1024], BF16, "wq")
        wo = alloc([128, 8, 1024], BF16, "wo")
        wpl = alloc([128, 2, 128], BF16, "wpl")
        wsT = alloc([128, 4, 128], BF16, "wsT")
        bsT = alloc([128, 2, 128], BF16 if BST_BF16 else F32, "bsT")
        gsg = alloc([128, 256], BF16, "gsg")
        yT = alloc([128, 8, SUB], BF16, "yT")
        sqB = None
        wmn = alloc([128, 2, SUB], F32, "wmn")
        tmpxB = [wmn[:, 0, :], wmn[:, 1, :]]
        rstdB = alloc([128, SUB], F32, "rstdB")
        lnvB = rstdB
        lnq, rsq = lnvB, rstdB
        h1B = alloc([128, 8, SUB], BF16, "h1B")
        P0 = alloc([128, 2, SUB + 16], F32, "P0")
        sqq = alias(P0, [128, 3, SUB], BF16, "sqq")
        S2 = alloc([128, 2, SUB + 16], F32, "S2")
        S4 = alloc([128, 2, SUB + 16], F32, "S4")
        pooled = alloc([128, 2, SUB], BF16, "pooled")
        cqg = alloc([128, 3, SUB], BF16, "cqg")
        qnT = alloc([128, 4, SUB], BF16, "qnT")
        qrT = alloc([128, 4, SUB], BF16, "qrT")
        ropeB = alloc([64, 2, SUB], F32, "ropeB")
        qt1f = alloc([128, SUB], F32, "qt1")
        qt1 = qt1f[0:64, :]
        qt2 = alloc([64, SUB], F32, "qt2")
        ug = pooled
        vss = alloc([128, 2], F32, "vss")
        vln = alloc([128, 2], F32, "vln")
        vrs = alloc([128, 2], F32, "vrs")
        vgA = alloc([128, 2, 256], BF16, "vgA")
        vgB = alloc([128, 2, 256], BF16, "vgB")
        gt = S2[:, :, 0:SUB]
        vt = S4[:, :, 0:SUB]
        PT = [alloc([128, 512], BF16, "PT") for _ in range(2)]
        denr = qt1f
        NPT = len(PT)

        bwB, bwq, bwo, bwsm = bf("wB"), bf("wq"), bf("wo"), bf("wsm")
        ld('pool', wB[:], winB_d[l], bwB)
        ld('pool', wq[:], wuq_d[l], bwq)
        memset('dve', wpl[:], 0.0, [bwsm])
        for g in range(4):
            c_, hf = g // 2, g % 2
            S.dma('pool', lambda e, g=g, c_=c_, hf=hf, l=l, wpl=wpl: e.dma_start(out=wpl[hf * 64:(hf + 1) * 64, c_, hf * 64:(hf + 1) * 64],
                                                                  in_=wpool_d[l, g]), bwsm, (), [bwsm])
        ld('pool', wsT[:], wsT_d[l], bwsm)
        if BST_BF16:
            S.dma('pool', lambda e, l=l, bsT=bsT: e.dma_start(out=bsT[:], in_=bsT_d[l]), bwsm, (), [bwsm])
        else:
            S.dma('sp', lambda e, l=l, bsT=bsT: e.dma_start(out=bsT[:], in_=bsT_d[l]), bf("gsgk"), (), [bf("gsgk"), bwsm])
        S.dma('pool', lambda e, l=l, gsg=gsg: e.dma_start(out=gsg[:], in_=gsgu_d[l]), bf("gsgk"), (), [bf("gsgk")])
        ld('pool', wo[:], wout_d[l], bwo)
        bvgA, bvgB = bf("vgA"), bf("vgB")
        memset('dve', vgA[:], 0.0, [bvgA])
        memset('dve', vgB[:], 0.0, [bvgB])
        if PADZERO:
            memset('dve', qrT[64:128, :, :], 0.0, [bf("qrT")])

        def normB(s):
            tok0, tok1 = tokrange(s)
            norm_tile(tok0, SUB, a1, 0, mtype(s), h1B, bf('h1B'), 'nB', h1B, tmpxB, lnvB, rstdB, bsq=bf('h1B'),
                      tmpx_extra=[bf('wmn')])

        def phaseB_sub(s, KV, prenormed=False, next_s=None):
            tok0, tok1 = tokrange(s)
            mt = mtype(s)
            sample = s < 8
            knT, krT, V = KV['knT'], KV['krT'], KV['V']
            bh1 = bf("h1B")
            byT = bf("yT")
            bwm = bf("wmn")
            blnq, brsq = bf("nBrs"), bf("nBrs")
            if not prenormed:
                normB(s)
            tbx = [bf("nBtmpx0"), bf("nBtmpx1")]
            pcq0, pcq0b = gb()
            for mc in range(2):
                for k in range(8):
                    mm(pcq0[:, mc * 256:(mc + 1) * 256], wB[:, k, 256 + mc * 128:256 + (mc + 1) * 128], h1B[:, k, :],
                       k == 0, k == 7, [bwB, bh1], [pcq0b])
            pcq1, pcq1b = gb()
            for k in range(8):
                mm(pcq1[:, 0:256], wB[:, k, 512:640], h1B[:, k, :], k == 0, k == 7, [bwB, bh1], [pcq1b])
            bsqq, bcqg = bf("P0"), bf("cqg")
            act(sqq[:, 0:2, :], pcq0[:, 0:512].rearrange("p (m n) -> p m n", m=2), AF.Square, [pcq0b], [bsqq])
            act(sqq[:, 2, :], pcq1[:, 0:256], AF.Square, [pcq1b], [bsqq])
            for mc in range(3):
                src_ = pcq0[:, mc * 256:(mc + 1) * 256] if mc < 2 else pcq1[:, 0:256]
                sc_ap, o_ap = gq[:, l, mc:mc + 1], cqg[:, mc, :]
                S.op('act', lambda e, sc_ap=sc_ap, o_ap=o_ap, src_=src_: e.activation(out=o_ap, in_=src_, func=AF.Copy, scale=sc_ap),
                     [pcq0b if mc < 2 else pcq1b, bconst], [bcqg])
            pss, pssb = gb()
            for k in range(3):
                mm(pss[:, 0:256], ones[:], sqq[:, k, :], k == 0, k == 2, [bones, bsqq], [pssb])
            rstd_from_ps(pss[:, 0:256], 384.0, lnq[:], rsq[:], [pssb], blnq, brsq)
            yield 'cq'
            bqn, bqr = bf("qnT"), bf("qrT")
            for hp_ in range(2):
                pq, pqb = gb()
                for hh in range(2):
                    h_ = 2 * hp_ + hh
                    for k in range(3):
                        mm(pq[:, hh * 256:(hh + 1) * 256], wq[:, k, h_ * 128:(h_ + 1) * 128], cqg[:, k, :], k == 0, k == 2,
                           [bwq, bcqg], [pqb])
                tt('dve', qnT[:, 2 * hp_:2 * hp_ + 2, :], pq[:, 0:512].rearrange("p (h n) -> p h n", h=2),
                   rsq[:].unsqueeze(1).broadcast_to([128, 2, 256]), ALU.mult, [pqb, brsq], [bqn])
            brp2 = bf("ropeB2")
            if sample:
                brp = bf("ropeB")
                S.dma('sp', lambda e, tok0=tok0, ropeB=ropeB: e.dma_start(out=ropeB[:, 0, :], in_=cos_d[:, tok0:tok0 + SUB]), brp, (), [brp, brp2])
                S.dma('sp', lambda e, tok0=tok0, ropeB=ropeB: e.dma_start(out=ropeB[:, 1, :], in_=sin_d[:, tok0:tok0 + SUB]), brp, (), [brp, brp2])
                tt('dve', ropeB[:, 0, :], ropeB[:, 0, :], rsq[0:64, :], ALU.mult, [brp, brsq], [brp2])
                tt('dve', ropeB[:, 1, :], ropeB[:, 1, :], rsq[0:64, :], ALU.mult, [brp, brsq, brp2], [brp2])
            for h_ in range(4):
                pq, pqb = gb()
                nq = 2 if sample else 1
                for q_ in range(nq):
                    c0 = 512 + q_ * 256 + h_ * 64
                    for k in range(3):
                        mm(pq[0:64, q_ * 256:(q_ + 1) * 256], wq[:, k, c0:c0 + 64], cqg[:, k, :], k == 0, k == 2,
                           [bwq, bcqg], [pqb])
                if sample:
                    bq1, bq2 = bf("qt1"), bf("qt2")
                    tt('dve', qt1[:], pq[0:64, 0:256], ropeB[:, 0, :], ALU.mult, [pqb, brp2], [bq1])
                    tt('dve', qt2[:], pq[0:64, 256:512], ropeB[:, 1, :], ALU.mult, [pqb, brp2], [bq2])
                    tt('dve', qrT[0:64, h_, :], qt1[:], qt2[:], ALU.add, [bq1, bq2], [bqr])
                else:
                    tt('dve', qrT[0:64, h_, :], pq[0:64, 0:256], rsq[0:64, :], ALU.mult, [pqb, brsq], [bqr])

            yield 'q'

            def rest_front():
                php, phpb = gb()
                for mc in range(2):
                    for k in range(8):
                        mm(php[:, mc * 256:(mc + 1) * 256], wB[:, k, mc * 128:(mc + 1) * 128], h1B[:, k, :], k == 0, k == 7,
                           [bwB, bh1], [phpb])
                bP0, bS2, bS4, bpl = bf("P0"), bf("S2"), bf("S4"), bf("pooled")
                cp('dve', P0[:, :, 8:8 + SUB], php[:, 0:512].rearrange("p (m n) -> p m n", m=2), [phpb], [bP0])
                if sample and s > 0:
                    cp('dve', P0[:, :, 0:8], KV['hph'][:, :, s - 1, 8:16], [bhph], [bP0])
                else:
                    memset('dve', P0[:, :, 0:8], 0.0, [bP0])
                if sample and s < 7:
                    cp('dve', P0[:, :, 8 + SUB:16 + SUB], KV['hph'][:, :, s + 1, 0:8], [bhph], [bP0])
                else:
                    memset('dve', P0[:, :, 8 + SUB:16 + SUB], 0.0, [bP0])
                W_ = SUB + 16
                wmw = [bwm] + tbx
                tt('dve', S2[:, :, 1:W_], P0[:, :, 1:W_], P0[:, :, 0:W_ - 1], ALU.add, [bP0], [bS2])
                tt('dve', S4[:, :, 3:W_], S2[:, :, 3:W_], S2[:, :, 1:W_ - 2], ALU.add, [bS2], [bS4])
                ts('dve', wmn[0:64, 0, :], S2[0:64, 0, 8:8 + SUB], invw[0:64, 0:1], None, ALU.mult, None, [bS2, bconst], wmw)
                ts('dve', wmn[64:128, 0, :], S4[64:128, 0, 9:9 + SUB], invw[64:128, 0:1], None, ALU.mult, None, [bS4, bconst], wmw)
                bS8 = bf("S8")
                tt('dve', S2[:, 1, 7:W_], S4[:, 1, 7:W_], S4[:, 1, 3:W_ - 4], ALU.add, [bS4, bwm, bS2], [bS8, bS2])
                ts('dve', wmn[0:64, 1, :], S2[0:64, 1, 11:11 + SUB], invw[0:64, 1:2], None, ALU.mult, None, [bS8, bconst], wmw)
                bS16 = bf("S16")
                tt('dve', S4[64:128, 1, 15:W_], S2[64:128, 1, 15:W_], S2[64:128, 1, 7:W_ - 8], ALU.add, [bS8, bwm, bS4], [bS16, bS4])
                ts('dve', wmn[64:128, 1, :], S4[64:128, 1, 15:15 + SUB], invw[64:128, 1:2], None, ALU.mult, None,
                   [bS16, bconst], wmw)
                seq_start = (s == 0) or not sample
                seq_end = (s == 7) or not sample
                if seq_start:
                    tt('dve', wmn[:, :, 0:8], wmn[:, :, 0:8], corrL[:], ALU.mult, [bwm, bconst], wmw)
                if seq_end:
                    tt('dve', wmn[:, :, SUB - 8:SUB], wmn[:, :, SUB - 8:SUB], corrR[:], ALU.mult, [bwm, bconst], wmw)
                tt('dve', pooled[:], wmn[:], P0[:, :, 8:8 + SUB], ALU.subtract, [bwm, bP0], [bpl])
                ppl, pplb = gb()
                for mc in range(2):
                    mm(ppl[:, mc * 256:(mc + 1) * 256], wpl[:, mc, :], pooled[:, mc, :], True, True, [bwsm, bpl], [pplb])
                for mc in range(2):
                    ts('dve', yT[:, mc, :], ppl[:, mc * 256:(mc + 1) * 256], pscale[:, l, mc:mc + 1], None, ALU.mult, None,
                       [pplb, bconst], [byT])
                pu, pub = gb()
                for mc in range(2):
                    for k in range(8):
                        mm(pu[:, mc * 256:(mc + 1) * 256], wB[:, k, 640 + mc * 128:640 + (mc + 1) * 128], h1B[:, k, :],
                           k == 0, k == 7, [bwB, bh1], [pub])
                bug = bf("pooled")
                act(ug[:], pu[:, 0:512].rearrange("p (m n) -> p m n", m=2), AF.Gelu_apprx_tanh, [pub], [bug])
                pv, pvb = gb()
                for blk in range(2):
                    for k in range(8):
                        mm(pv[:, blk * 256:(blk + 1) * 256], h1B[:, k, blk * 128:(blk + 1) * 128], wB[:, k, 896:1152],
                           k == 0, k == 7, [bwB, bh1], [pvb])
                bvt, bvss, bvrs, bgt = bf("S4"), bf("vss"), bf("vrs"), bf("S2")
                act(vt[:], pv[:, 0:512].rearrange("p (m n) -> p m n", m=2), AF.Gelu_apprx_tanh, [pvb], [bvt, bf("S16")])
                for blk in range(2):
                    S.op('act', lambda e, blk=blk, gt=gt, vt=vt, vss=vss: e.activation(out=gt[:, 0, :], in_=vt[:, blk, :], func=AF.Square,
                                                                accum_out=vss[:, blk:blk + 1]), [bvt], [bvss, bgt, bf("S8")])
                act(vln[:], vss[:], AF.Ln, [bvss], [bf("vln")], scale=1.0 / 256.0, bias=EPS)
                act(vrs[:], vln[:], AF.Exp, [bf("vln")], [bvrs], scale=-0.5)
                for blk in range(2):
                    for hf, (vg, vgb) in enumerate([(vgA, bvgA), (vgB, bvgB)]):
                        o_ = vg[:, blk, :].rearrange("p (m h c) -> p m h c", m=2, h=2)[:, :, hf, :]
                        i0 = vt[:, blk, :].rearrange("p (m h c) -> p m h c", m=2, h=2)[:, :, hf, :]
                        i1 = gsg[:].rearrange("p (m h c) -> p m h c", m=2, h=2)[:, :, hf, :]
                        stt(o_, i0, vrs[:, blk:blk + 1], i1, ALU.mult, ALU.mult, [bvt, bvrs, bf("gsgk")], [vgb])
                pg, pgb = gb()
                for mc in range(2):
                    for blk in range(2):
                        o_ = pg[:, mc * 256 + blk * 128: mc * 256 + (blk + 1) * 128]
                        mm(o_, vgA[:, blk, mc * 128:(mc + 1) * 128], wsT[:, 2 * mc, :], True, False, [bvgA, bwsm], [pgb])
                        mm(o_, vgB[:, blk, mc * 128:(mc + 1) * 128], wsT[:, 2 * mc + 1, :], False, True, [bvgB, bwsm], [pgb])
                for mc in range(2):
                    tt('dve', gt[:, mc, :].rearrange("p (b n) -> p b n", b=2),
                       pg[:, mc * 256:(mc + 1) * 256].rearrange("p (b n) -> p b n", b=2),
                       bsT[:, mc, :].unsqueeze(1).broadcast_to([128, 2, 128]), ALU.add, [pgb, bwsm], [bgt, bf("S8")])
                tt('dve', yT[:, 6:8, :], gt[:], ug[:], ALU.mult, [bgt, bug], [byT])

            if sample:
                kbase, nkc = 0, 18
            else:
                kbase, nkc = 2304 + (s - 8) * 256, 2
            npair = nkc // 2
            pairs = [(h_, jp) for h_ in range(4) for jp in range(npair)]
            st_ = {}

            def emit_scores(i):
                h_, jp = pairs[i]
                psc, pscb = (ps[7], psb[7]) if i % 2 == 0 else gb()
                for sub in range(2):
                    j = 2 * jp + sub
                    k0 = kbase + j * 128
                    mm(psc[:, sub * 256:(sub + 1) * 256], knT[:, h_, k0:k0 + 128], qnT[:, h_, :], True, False,
                       [bknT, bqn], [pscb])
                    if ROPE128:
                        mm(psc[:, sub * 256:(sub + 1) * 256], krT[:, k0:k0 + 128], qrT[:, h_, :], False, True,
                           [bkrT, bqr], [pscb])
                    else:
                        mm(psc[:, sub * 256:(sub + 1) * 256], krT[0:64, k0:k0 + 128], qrT[0:64, h_, :], False, True,
                           [bkrT, bqr], [pscb])
                pt, ptb = PT[i % NPT], bf("PT%d" % (i % NPT))
                act(pt[:], psc[:, 0:512], AF.Exp, [pscb], [ptb], scale=SCALE)
                st_[i] = (pt, ptb)

            def emit_pv(i):
                h_, jp = pairs[i]
                pt, ptb = st_.pop(i)
                po, pob = ps[4 + (h_ % 2)], psb[4 + (h_ % 2)]
                for sub in range(2):
                    j = 2 * jp + sub
                    kc = (kbase // 128) + j
                    mm(po[:, 0:256], V[:, kc, h_ * 128:(h_ + 1) * 128], pt[:, sub * 256:(sub + 1) * 256],
                       j == 0, j == nkc - 1, [bV, ptb], [pob], skip=True)
                    mm(po[:, 256:512], ones[:], pt[:, sub * 256:(sub + 1) * 256], False, j == nkc - 1,
                       [bones, ptb], [pob], skip=True)
                if jp == npair - 1:
                    bdr = bf("qt1")
                    S.op('dve', lambda e, po=po, denr=denr: e.reciprocal(out=denr[:], in_=po[:, 256:512]), [pob], [bdr])
                    tt('dve', yT[:, 2 + h_, :], po[:, 0:256], denr[:], ALU.mult, [pob, bdr], [byT])

            rest_front()
            emit_scores(0)
            for i in range(len(pairs)):
                if i + 1 < len(pairs):
                    emit_scores(i + 1)
                emit_pv(i)
                if i == npair - 1 and next_s is not None:
                    normB(next_s)
            yield 'att'
            for mp in range(4):
                pw, pwb = gb()
                for mh in range(2):
                    m_ = 2 * mp + mh
                    for k in range(8):
                        mm(pw[:, mh * 256:(mh + 1) * 256], wo[:, k, m_ * 128:(m_ + 1) * 128], yT[:, k, :], k == 0, k == 7,
                           [bwo, byT], [pwb])
                for mh in range(2):
                    m_ = 2 * mp + mh
                    stt(x[:, m_, tok0:tok1], pw[:, mh * 256:(mh + 1) * 256], mod[:, 16 + m_, mt:mt + 1], x[:, m_, tok0:tok1],
                        ALU.mult, ALU.add, [pwb, bmod, bx], [bx])

        gbank_list[:] = [0, 1, 2, 3, 6]
        gens = {s: phaseB_sub(s, KVs, prenormed=(s > 0), next_s=(s + 1 if s < NSUB - 1 else None)) for s in range(NSUB)}
        next(gens[0])
        next(gens[0])
        for s in range(NSUB):
            next(gens[s])
            if s + 1 < NSUB:
                next(gens[s + 1])
            for _ in gens[s]:
                pass
            if s + 1 < NSUB:
                next(gens[s + 1])
        gbank_list[:] = [0, 1, 2, 3]
        S.barrier()

        cur[0] = scr_base
        h2 = alloc([128, 8, T], BF16, "h2")
        wslot = [(alloc([128, 8, 512], BF16, "w1s"), alloc([128, 4, 1024], BF16, "w2s")) for _ in range(2)]
        sqF = alloc([128, 8, 512], BF16, "sqF")
        lnvF = alloc([128, 512], F32, "lnvF")
        rstdF = lnvF
        rstd2 = alloc([128, T], F32, "rstd2")
        b2bf = alloc([128, 8, 2], BF16, "b2bf")
        cbuf = [alloc([128, 4, 2], F32, "cbuf") for _ in range(2)]
        rl = [alloc([128, 512], F32, "rl") for _ in range(2)]
        hid = [alloc([128, 4, 512], BF16, "hid") for _ in range(2)]
        ybuf = alloc([128, 8, 512], F32, "ybuf") if l == L - 1 else None
        bh2 = [bf("h2_%d" % t) for t in range(5)]

        def load_w(j):
            w1s, w2s = wslot[j % 2]
            kb1, kb2 = bf("w1s%d" % (j % 2)), bf("w2s%d" % (j % 2))
            ld('pool', w1s[:], w1_d[l, j], kb1)
            ld('pool', w2s[:], w2_d[l, j], kb2)

        gbank_list[:] = [0, 1, 2, 3, 4, 5, 6]
        load_w(0)
        nxt = l + 1 < L
        if nxt:
            ringF = [alloc([128, 8, 128], BF16, "adarF") for _ in range(6)]
            ringFb = [bf("adarF%d" % i) for i in range(6)]

        bb2 = bf("b2bf")
        cp('dve', b2bf[:], mod[:, 24:32, :], [bmod], [bb2])
        brs2 = [bf("rstd2_%d" % t) for t in range(5)]

        def norm2(t):
            mt = 0 if t < 4 else 1
            sl = slice(t * 512, (t + 1) * 512)
            bsq = bf("nFsq")
            act(sqF[:], x[:, :, sl], AF.Square, [bx], [bsq])
            for c in range(8):
                if c % 2 == 0:
                    ts('dve', h2[:, c, sl], x[:, c, sl], a2[:, c, mt:mt + 1], None, ALU.mult, None, [bx, bmod], [bh2[t]])
                else:
                    sc_ap, o_ap, i_ap = a2[:, c, mt:mt + 1], h2[:, c, sl], x[:, c, sl]
                    S.op('act', lambda e, sc_ap=sc_ap, o_ap=o_ap, i_ap=i_ap: e.activation(
                        out=o_ap, in_=i_ap, func=AF.Copy, scale=sc_ap), [bx, bmod], [bh2[t]])
            pb, pbb = gb()
            for k in range(8):
                mm(pb[:, 0:512], ones[:], sqF[:, k, :], k == 0, k == 7, [bones, bsq], [pbb])
            rstd_from_ps(pb[:, 0:512], float(D), rstd2[:, sl], rstd2[:, sl], [pbb], brs2[t], brs2[t])

        def slab_bias(j):
            w1s, kb1 = wslot[j % 2][0], bf("w1s%d" % (j % 2))
            pm, pmb = ps[7], psb[7]
            for m_ in range(4):
                for k in range(8):
                    mm(pm[:, 128 + 2 * m_:130 + 2 * m_], w1s[:, k, m_ * 128:(m_ + 1) * 128], b2bf[:, k, :], k == 0, k == 7,
                       [kb1, bb2], [pmb])
            cp('dve', cbuf[j % 2][:], pm[:, 128:136].rearrange("p (m s) -> p m s", s=2), [pmb], [bf("cbuf%d" % (j % 2))])

        def ff1(i, j, t):
            w1s, kb1 = wslot[j % 2][0], bf("w1s%d" % (j % 2))
            hd, hdb = hid[i % 2], bf("hid%d" % (i % 2))
            for m_ in range(4):
                pb, pbb = gb()
                for k in range(8):
                    mm(pb[:, 0:512], w1s[:, k, m_ * 128:(m_ + 1) * 128], h2[:, k, t * 512:(t + 1) * 512], k == 0, k == 7,
                       [kb1, bh2[t]], [pbb])
                r_, rb_ = rl[m_ % 2], bf("rl%d" % (m_ % 2))
                mt = 0 if t < 4 else 1
                tt('dve', r_[:], pb[:, 0:512], rstd2[:, t * 512:(t + 1) * 512], ALU.mult, [pbb, brs2[t]], [rb_])
                act(r_[:], r_[:], AF.Relu, [rb_, bf("cbuf%d" % (j % 2))], [rb_], bias=cbuf[j % 2][:, m_, mt:mt + 1])
                act(hd[:, m_, :], r_[:], AF.Square, [rb_], [hdb])

        def ff2(i, j, t):
            w2s, kb2 = wslot[j % 2][1], bf("w2s%d" % (j % 2))
            hd, hdb = hid[i % 2], bf("hid%d" % (i % 2))
            mt = 0 if t < 4 else 1
            for m_ in range(8):
                pb, pbb = gb()
                for k in range(4):
                    mm(pb[:, 0:512], w2s[:, k, m_ * 128:(m_ + 1) * 128], hd[:, k, :], k == 0, k == 3, [kb2, hdb], [pbb])
                stt(x[:, m_, t * 512:(t + 1) * 512], pb[:, 0:512], mod[:, 40 + m_, mt:mt + 1],
                    x[:, m_, t * 512:(t + 1) * 512], ALU.mult, ALU.add, [pbb, bmod, bx], [bx])

        steps = [(j, t) for j in range(NJ) for t in range(5)]
        load_w(1)
        if nxt:
            emit_mod_dma(l + 1, ringF, ringFb, 0, 6)
        norm2(0)
        norm2(1)
        slab_bias(0)
        ff1(0, 0, 0)
        for i, (j, t) in enumerate(steps):
            if i + 1 < len(steps):
                if steps[i + 1][1] == 0:
                    slab_bias(steps[i + 1][0])
                ff1(i + 1, *steps[i + 1])
            ff2(i, j, t)
            if j == 0 and t + 2 < 5:
                norm2(t + 2)
            if t == 4:
                if j + 2 < NJ:
                    load_w(j + 2)
                if nxt:
                    emit_mod_mm(l + 1, ringF, ringFb, j * 6, (j + 1) * 6)
                    if j + 1 < NJ:
                        emit_mod_dma(l + 1, ringF, ringFb, (j + 1) * 6, (j + 2) * 6)
        if nxt:
            finish_mod(l + 1)
        if l == L - 1:
            for t in range(5):
                bsq, bln, brs, byb = bf("fsq"), bf("fln"), bf("frs"), bf("ybuf")
                act(sqF[:], x[:, :, t * 512:(t + 1) * 512], AF.Square, [bx], [bsq])
                pb, pbb = gb()
                for k in range(8):
                    mm(pb[:, 0:512], ones[:], sqF[:, k, :], k == 0, k == 7, [bones, bsq], [pbb])
                rstd_from_ps(pb[:, 0:512], float(D), lnvF[:], rstdF[:], [pbb], brs, brs)
                for c in range(8):
                    stt(ybuf[:, c, :], x[:, c, t * 512:(t + 1) * 512], gfinal[:, c:c + 1], rstdF[:], ALU.mult, ALU.mult,
                        [bx, brs, bconst], [byb])
                out_toks.append(S.dma('sp', lambda e, t=t, ybuf=ybuf: e.dma_start(out=yT_d[:, :, t * 512:(t + 1) * 512], in_=ybuf[:]),
                                      byb, [byb], []))
        gbank_list[:] = [0, 1, 2, 3]
        S.barrier()

    with ExitStack() as st:
        with nc.Block() as block:
            S.emit(nc, block, st, final_waits=out_toks)
    return nc


def _pc(a, nchunk):
    return np.ascontiguousarray(a.reshape((nchunk, 128) + a.shape[1:]).swapaxes(0, 1))


def _vecs(g, nchunk):
    Ln = g.shape[0]
    return np.ascontiguousarray(g.reshape(Ln, nchunk, 128).transpose(2, 0, 1))


def _rope_tables():
    rows = TS // 64
    row = np.repeat(np.arange(rows, dtype=np.float32), 64)
    col = np.tile(np.arange(64, dtype=np.float32), rows)
    inv = (1.0 / (10000.0 ** (np.arange(16, dtype=np.float32) / 16))).astype(np.float32)
    ang = np.concatenate([row[:, None] * inv, col[:, None] * inv], axis=-1).astype(np.float32)
    cos, sin = np.cos(ang).astype(np.float32), np.sin(ang).astype(np.float32)
    cosT = np.concatenate([cos, cos], axis=1).T
    sinT = np.concatenate([-sin, sin], axis=1).T
    return np.ascontiguousarray(cosT), np.ascontiguousarray(sinT)


def _pool_tables():
    wins = (2, 4, 8, 16)
    invw = np.zeros((128, 2), np.float32)
    corrL = np.ones((128, 2, 8), np.float32)
    corrR = np.ones((128, 2, 8), np.float32)
    Ls = 256
    for g, w in enumerate(wins):
        c, hf = g // 2, g % 2
        sl = slice(hf * 64, (hf + 1) * 64)
        invw[sl, c] = 1.0 / w
        for i in range(8):
            t = i
            cnt = min(t + (w - w // 2), Ls) - max(t - w // 2, 0)
            corrL[sl, c, i] = w / cnt
            t = Ls - 8 + i
            cnt = min(t + (w - w // 2), Ls) - max(t - w // 2, 0)
            corrR[sl, c, i] = w / cnt
    return invw, corrL, corrR


def prepare_inputs(x_prompt, x_sample, cache_ckv, cache_krope, c, c_ctx, w_ada, b_ada, g_mix,
                   w_in, w_pool, pool_scale, g_q, w_uq, g_kv, w_ukv, g_sgu, w_s, b_s, w_out,
                   g_ffn, w_ff1, w_ff2, g_final):
    f = lambda a: np.asarray(a, dtype=np.float32)
    x_prompt, x_sample, cache_ckv, cache_krope, c, c_ctx = map(f, (x_prompt, x_sample, cache_ckv, cache_krope, c, c_ctx))
    w_ada, b_ada, g_mix, w_in, w_pool, pool_scale, g_q, w_uq, g_kv, w_ukv = map(
        f, (w_ada, b_ada, g_mix, w_in, w_pool, pool_scale, g_q, w_uq, g_kv, w_ukv))
    g_sgu, w_s, b_s, w_out, g_ffn, w_ff1, w_ff2, g_final = map(f, (g_sgu, w_s, b_s, w_out, g_ffn, w_ff1, w_ff2, g_final))
    Ln = w_in.shape[0]
    OFF_Q, OFF_KV, OFF_R, OFF_G = 256, 640, 896, 960
    shared = {}
    shared["wada"] = np.ascontiguousarray(w_ada.reshape(Ln, 8, 128, 48, 128).transpose(0, 3, 2, 1, 4))
    shared["bada"] = _vecs(b_ada, 48)
    shared["gmix"] = _vecs(g_mix, 8)
    shared["gffn"] = _vecs(g_ffn, 8)
    shared["gq"] = _vecs(g_q, 3)
    shared["gkv"] = _vecs(g_kv, 2)
    shared["gsgu"] = np.ascontiguousarray(np.broadcast_to(g_sgu[:, None, :], (Ln, 128, 256)))
    shared["pscale"] = _vecs(pool_scale, 2)
    shared["gfinal"] = np.ascontiguousarray(g_final.reshape(8, 128).T)
    kr = w_in[:, :, OFF_R:OFF_G]
    kr_swp = np.concatenate([kr[:, :, 32:64], kr[:, :, 0:32]], axis=2)
    winA = np.concatenate([w_in[:, :, OFF_KV:OFF_R], kr, kr_swp], axis=2)
    shared["winA"] = np.ascontiguousarray(winA.reshape(Ln, 8, 128, 384).transpose(0, 2, 1, 3))
    winB = np.concatenate([w_in[:, :, 0:OFF_Q], w_in[:, :, OFF_Q:OFF_KV], w_in[:, :, OFF_G:OFF_G + 512]], axis=2)
    shared["winB"] = np.ascontiguousarray(winB.reshape(Ln, 8, 128, 1152).transpose(0, 2, 1, 3))
    wq4 = w_uq.reshape(Ln, 384, 4, 192)
    qn = wq4[:, :, :, 0:128].reshape(Ln, 384, 512)
    qr = wq4[:, :, :, 128:192]
    qr_swp = np.concatenate([qr[..., 32:64], qr[..., 0:32]], axis=-1)
    wuq = np.concatenate([qn, qr.reshape(Ln, 384, 256), qr_swp.reshape(Ln, 384, 256)], axis=2)
    shared["wuq"] = np.ascontiguousarray(wuq.reshape(Ln, 3, 128, 1024).transpose(0, 2, 1, 3))
    wkv4 = w_ukv.reshape(Ln, 256, 4, 256)
    wukv = np.concatenate([wkv4[..., 0:128].reshape(Ln, 256, 512), wkv4[..., 128:256].reshape(Ln, 256, 512)], axis=2)
    shared["wukv"] = np.ascontiguousarray(wukv.reshape(Ln, 2, 128, 1024).transpose(0, 2, 1, 3))
    shared["wout"] = np.ascontiguousarray(w_out.reshape(Ln, 8, 128, 1024).transpose(0, 2, 1, 3))
    shared["wpool"] = np.ascontiguousarray(w_pool)
    shared["wsT"] = np.ascontiguousarray(w_s.transpose(0, 3, 1, 2))
    bsT = np.broadcast_to(b_s.reshape(Ln, 2, 2, 1, 128), (Ln, 2, 2, 64, 128))
    shared["bsT"] = np.ascontiguousarray(bsT.reshape(Ln, 2, 128, 128).transpose(0, 2, 1, 3))
    shared["w1"] = np.ascontiguousarray(w_ff1.reshape(Ln, 8, 128, NJ, 512).transpose(0, 3, 2, 1, 4))
    shared["w2"] = np.ascontiguousarray(w_ff2.reshape(Ln, NJ, 4, 128, 1024).transpose(0, 1, 3, 2, 4))
    cosT, sinT = _rope_tables()
    shared["cosT"], shared["sinT"] = cosT, sinT
    shared["invw"], shared["corrL"], shared["corrR"] = _pool_tables()
    in_maps = []
    for i in range(8):
        m = dict(shared)
        xs = np.concatenate([x_sample[i], x_prompt[2 * i], x_prompt[2 * i + 1]], axis=0)
        m["xT"] = _pc(np.ascontiguousarray(xs.T), 8)
        cc = np.stack([c[i], c_ctx], axis=1)
        m["cT"] = _pc(cc, 8)
        ck = cache_ckv[i].transpose(2, 0, 1)
        m["cckv"] = np.ascontiguousarray(ck.reshape(2, 128, Ln, 256).transpose(1, 2, 0, 3))
        m["ckr"] = np.ascontiguousarray(cache_krope[i].transpose(2, 0, 1))
        in_maps.append(m)
    return in_maps


_NC_CACHE = {}


def kernel(**inputs):
    in_maps = prepare_inputs(**inputs)
    if "nc" not in _NC_CACHE:
        _NC_CACHE["nc"] = build_program(L_FULL)
    nc = _NC_CACHE["nc"]
    res = run_bass_kernel_spmd(nc, in_maps, core_ids=list(range(8)))
    y_prompt = np.zeros((16, 256, D), np.float32)
    y_sample = np.zeros((8, TS, D), np.float32)
    state_ckv = np.zeros((16, L_FULL, 256, 256), np.float32)
    state_krope = np.zeros((16, L_FULL, 256, 64), np.float32)
    for i, r in enumerate(res.results):
        yT = np.asarray(r["yT"])
        y = yT.transpose(2, 1, 0).reshape(T, D)
        y_sample[i] = y[0:TS]
        y_prompt[2 * i] = y[TS:TS + 256]
        y_prompt[2 * i + 1] = y[TS + 256:TS + 512]
        ck = np.asarray(r["ckvst"])
        ck = ck.transpose(0, 3, 2, 1).reshape(L_FULL, 512, 256)
        kr = np.asarray(r["krst"]).transpose(0, 2, 1)
        for b in range(2):
            state_ckv[2 * i + b] = ck[:, b * 256:(b + 1) * 256, :]
            state_krope[2 * i + b] = kr[:, b * 256:(b + 1) * 256, :]
    return (y_prompt, y_sample, state_ckv, state_krope)
```
